# Optimizing a Trainium2 kernel written in Bass

```python
import math
import jax, jax.numpy as jnp
from jax import lax
import numpy as np

D_MODEL = 1024
BATCH = 8
SEQ = 2048
DEPTH = 2

S5_WIDTH = D_MODEL // 2
S5_GROUP = 16
S5_GROUPS = S5_WIDTH // S5_GROUP
S5_STATE = 64
S5_DT_MIN = 0.001
S5_DT_MAX = 0.1
S5_DIRS = 2
HY_WIDTH = D_MODEL
HY_ORDER = 2
HY_SHORT = 3
HY_BANDS = 16
HY_EMB = 1 + 2 * HY_BANDS
HY_FFN = 64
HY_DIRS = 2
HY_FILTERS = HY_ORDER * HY_DIRS * HY_WIDTH
HY_DECAY_TARGET = 1e-2
HY_FAST_PCT = 0.3
HY_SLOW_PCT = 1.5
HY_SHIFT = 0.05
HY_EPS = 1e-6
N_BRANCH = 2
RMS_EPS = 1e-6
IN_COLS = 2 * S5_WIDTH + (HY_ORDER + 1) * HY_WIDTH + HY_WIDTH + N_BRANCH * D_MODEL
SPLITS = [S5_WIDTH, 2 * S5_WIDTH, 2 * S5_WIDTH + (HY_ORDER + 1) * HY_WIDTH,
          2 * S5_WIDTH + (HY_ORDER + 2) * HY_WIDTH]

kernel_name = "hybrid_s5_hyena_gated_encoder"


def rmsnorm(x, w):
    xf = x.astype(jnp.float32)
    y = xf * lax.rsqrt(jnp.mean(xf * xf, axis=-1, keepdims=True) + RMS_EPS)
    return (y * w.astype(jnp.float32)).astype(x.dtype)


def _cmul(ar, ai, br, bi):
    return ar * br - ai * bi, ar * bi + ai * br


def _s5_combine(e1, e2):
    a1r, a1i, b1r, b1i = e1
    a2r, a2i, b2r, b2i = e2
    ar, ai = _cmul(a1r, a1i, a2r, a2i)
    tr, ti = _cmul(a2r, a2i, b1r, b1i)
    return ar, ai, tr + b2r, ti + b2i


def s5_direction(u, lam_re, lam_im, log_step, b_re, b_im, c_re, c_im, reverse):
    dt = jnp.exp(log_step)[:, None]
    mag = jnp.exp(lam_re * dt)
    ang = lam_im * dt
    lbar_re, lbar_im = mag * jnp.cos(ang), mag * jnp.sin(ang)
    nr, ni = lbar_re - 1.0, lbar_im
    den = lam_re * lam_re + lam_im * lam_im
    kr = (nr * lam_re + ni * lam_im) / den
    ki = (ni * lam_re - nr * lam_im) / den
    bb_re, bb_im = _cmul(kr[..., None], ki[..., None], b_re, b_im)
    bu_re = jnp.einsum('blgh,gph->blgp', u, bb_re)
    bu_im = jnp.einsum('blgh,gph->blgp', u, bb_im)
    L = u.shape[1]
    a_re = jnp.broadcast_to(lbar_re, (1, L) + lbar_re.shape)
    a_im = jnp.broadcast_to(lbar_im, (1, L) + lbar_im.shape)
    _, _, s_re, s_im = lax.associative_scan(_s5_combine, (a_re, a_im, bu_re, bu_im),
                                            reverse=reverse, axis=1)
    return (jnp.einsum('ghp,blgp->blgh', c_re, s_re)
            - jnp.einsum('ghp,blgp->blgh', c_im, s_im))


def s5_mixer(u, gate, lam_re, lam_im, log_step, b_re, b_im, c_re, c_im, d, w_glu, b_glu):
    bsz, L, _ = u.shape
    f32 = jnp.float32
    uf = u.astype(f32).reshape(bsz, L, S5_GROUPS, S5_GROUP)
    y = uf * d.astype(f32).reshape(S5_GROUPS, S5_GROUP)
    for direction, rev in enumerate((False, True)):
        y = y + s5_direction(uf, lam_re[direction].astype(f32), lam_im[direction].astype(f32),
                             log_step[direction].astype(f32), b_re[direction].astype(f32),
                             b_im[direction].astype(f32), c_re[direction].astype(f32),
                             c_im[direction].astype(f32), rev)
    y = y.reshape(bsz, L, S5_WIDTH).astype(u.dtype)
    y = jax.nn.gelu(y)
    y = y * jax.nn.sigmoid(y @ w_glu + b_glu)
    return y * jax.nn.silu(gate)


def hyena_filters(L, w1, b1, freq, w2, b2, w3, decay):
    f32 = jnp.float32
    t = jnp.linspace(0.0, 1.0, L, dtype=f32)[:, None]
    pos = jnp.arange(L, dtype=f32)[:, None]
    bands = jnp.linspace(1e-4, HY_BANDS - 1, HY_BANDS, dtype=f32)[None, :]
    ang = bands * pos * (2.0 * math.pi / L)
    feats = jnp.concatenate([t, jnp.cos(ang), -jnp.sin(ang)], axis=-1)
    fr = freq.astype(f32)
    h = jnp.sin(fr * (feats @ w1.astype(f32) + b1.astype(f32)))
    h = jnp.sin(fr * (h @ w2.astype(f32) + b2.astype(f32)))
    h = h @ w3.astype(f32)
    h = h * (jnp.exp(-t * jnp.abs(decay.astype(f32))) + HY_SHIFT)
    h = h.reshape(L, HY_ORDER, HY_DIRS, HY_WIDTH)
    h_fwd, h_bwd = h[:, :, 0], h[:, :, 1]
    k = jnp.concatenate([h_fwd, jnp.zeros_like(h_fwd[:1]), jnp.flip(h_bwd[1:], axis=0)], axis=0)
    k = k * lax.rsqrt(jnp.sum(k * k, axis=0, keepdims=True) + HY_EPS)
    return jnp.fft.rfft(k, axis=0)


def hyena_mixer(proj, gate, conv_w, conv_b, w1, b1, freq, w2, b2, w3, decay, d):
    f32 = jnp.float32
    bsz, L, _ = proj.shape
    pad = HY_SHORT // 2
    pp = jnp.pad(proj, ((0, 0), (pad, pad), (0, 0)))
    sc = conv_b
    for j in range(HY_SHORT):
        sc = sc + pp[:, j:j + L] * conv_w[j]
    v, x1, x2 = jnp.split(sc, HY_ORDER + 1, axis=-1)
    kf = hyena_filters(L, w1, b1, freq, w2, b2, w3, decay)
    z = v.astype(f32)
    for o, g in enumerate((x1, x2)):
        zf = jnp.fft.rfft(z, n=2 * L, axis=1)
        conv = jnp.fft.irfft(zf * kf[:, o][None], n=2 * L, axis=1)[:, :L]
        z = g.astype(f32) * (conv + d[o].astype(f32) * z)
    return z.astype(proj.dtype) * jax.nn.silu(gate)


def setup_inputs(seed: int = 0) -> dict:
    key = jax.random.key(seed)
    ks = jax.random.split(key, 32)
    f32 = jnp.float32
    nrm = lambda k, shape, s: jax.random.normal(k, shape, f32) * s
    x = nrm(ks[0], (BATCH, SEQ, D_MODEL), 1.0)
    norm_w = 1.0 + nrm(ks[1], (DEPTH, D_MODEL), 0.02)
    w_in = nrm(ks[2], (DEPTH, D_MODEL, IN_COLS), D_MODEL ** -0.5)
    n_idx = jnp.arange(S5_STATE, dtype=f32)
    s5_lam_re = -0.5 + nrm(ks[3], (DEPTH, S5_DIRS, S5_GROUPS, S5_STATE), 0.01)
    s5_lam_im = math.pi * n_idx + nrm(ks[4], (DEPTH, S5_DIRS, S5_GROUPS, S5_STATE), 0.01)
    s5_log_step = (math.log(S5_DT_MIN) + jax.random.uniform(ks[5], (DEPTH, S5_DIRS, S5_GROUPS), f32)
                   * (math.log(S5_DT_MAX) - math.log(S5_DT_MIN)))
    s5_b_re = nrm(ks[6], (DEPTH, S5_DIRS, S5_GROUPS, S5_STATE, S5_GROUP), (2 * S5_GROUP) ** -0.5)
    s5_b_im = nrm(ks[7], (DEPTH, S5_DIRS, S5_GROUPS, S5_STATE, S5_GROUP), (2 * S5_GROUP) ** -0.5)
    s5_c_re = nrm(ks[8], (DEPTH, S5_DIRS, S5_GROUPS, S5_GROUP, S5_STATE), S5_STATE ** -0.5)
    s5_c_im = nrm(ks[9], (DEPTH, S5_DIRS, S5_GROUPS, S5_GROUP, S5_STATE), S5_STATE ** -0.5)
    s5_d = nrm(ks[10], (DEPTH, S5_WIDTH), 1.0)
    s5_w_glu = nrm(ks[11], (DEPTH, S5_WIDTH, S5_WIDTH), S5_WIDTH ** -0.5)
    s5_b_glu = nrm(ks[12], (DEPTH, S5_WIDTH), 0.01)
    hy_conv_w = nrm(ks[13], (DEPTH, HY_SHORT, (HY_ORDER + 1) * HY_WIDTH), HY_SHORT ** -0.5)
    hy_conv_b = nrm(ks[14], (DEPTH, (HY_ORDER + 1) * HY_WIDTH), 0.01)
    hy_w1 = nrm(ks[15], (DEPTH, HY_EMB, HY_FFN), HY_EMB ** -0.5)
    hy_b1 = nrm(ks[16], (DEPTH, HY_FFN), 0.1)
    hy_freq = 1.0 + nrm(ks[17], (DEPTH, HY_FFN), 0.02)
    hy_w2 = nrm(ks[18], (DEPTH, HY_FFN, HY_FFN), HY_FFN ** -0.5)
    hy_b2 = nrm(ks[19], (DEPTH, HY_FFN), 0.1)
    hy_w3 = nrm(ks[20], (DEPTH, HY_FFN, HY_FILTERS), HY_FFN ** -0.5)
    min_decay = math.log(HY_DECAY_TARGET) / HY_SLOW_PCT
    max_decay = math.log(HY_DECAY_TARGET) / HY_FAST_PCT
    base_decay = jnp.tile(jnp.linspace(min_decay, max_decay, HY_WIDTH, dtype=f32), HY_ORDER * HY_DIRS)
    hy_decay = base_decay[None, :] + nrm(ks[21], (DEPTH, HY_FILTERS), 0.01)
    hy_d = nrm(ks[22], (DEPTH, HY_ORDER, HY_WIDTH), 0.5)
    w_branch_s5 = nrm(ks[23], (DEPTH, S5_WIDTH, D_MODEL), S5_WIDTH ** -0.5)
    w_branch_hy = nrm(ks[24], (DEPTH, HY_WIDTH, D_MODEL), HY_WIDTH ** -0.5)
    w_out = nrm(ks[25], (DEPTH, D_MODEL, D_MODEL), D_MODEL ** -0.5)
    final_norm_w = 1.0 + nrm(ks[26], (D_MODEL,), 0.02)
    return {"x": x, "norm_w": norm_w, "w_in": w_in,
            "s5_lam_re": s5_lam_re, "s5_lam_im": s5_lam_im, "s5_log_step": s5_log_step,
            "s5_b_re": s5_b_re, "s5_b_im": s5_b_im, "s5_c_re": s5_c_re, "s5_c_im": s5_c_im,
            "s5_d": s5_d, "s5_w_glu": s5_w_glu, "s5_b_glu": s5_b_glu,
            "hy_conv_w": hy_conv_w, "hy_conv_b": hy_conv_b, "hy_w1": hy_w1, "hy_b1": hy_b1,
            "hy_freq": hy_freq, "hy_w2": hy_w2, "hy_b2": hy_b2, "hy_w3": hy_w3,
            "hy_decay": hy_decay, "hy_d": hy_d,
            "w_branch_s5": w_branch_s5, "w_branch_hy": w_branch_hy, "w_out": w_out,
            "final_norm_w": final_norm_w}


def reference(x, norm_w, w_in, s5_lam_re, s5_lam_im, s5_log_step, s5_b_re, s5_b_im, s5_c_re, s5_c_im,
              s5_d, s5_w_glu, s5_b_glu, hy_conv_w, hy_conv_b, hy_w1, hy_b1, hy_freq, hy_w2, hy_b2,
              hy_w3, hy_decay, hy_d, w_branch_s5, w_branch_hy, w_out, final_norm_w):
    h = x
    for l in range(DEPTH):
        xn = rmsnorm(h, norm_w[l])
        proj = xn @ w_in[l]
        u_s5, g_s5, p_hy, g_hy, m_logits = jnp.split(proj, SPLITS, axis=-1)
        y_s5 = s5_mixer(u_s5, g_s5, s5_lam_re[l], s5_lam_im[l], s5_log_step[l], s5_b_re[l], s5_b_im[l],
                        s5_c_re[l], s5_c_im[l], s5_d[l], s5_w_glu[l], s5_b_glu[l]) @ w_branch_s5[l]
        y_hy = hyena_mixer(p_hy, g_hy, hy_conv_w[l], hy_conv_b[l], hy_w1[l], hy_b1[l], hy_freq[l],
                           hy_w2[l], hy_b2[l], hy_w3[l], hy_decay[l], hy_d[l]) @ w_branch_hy[l]
        m = jax.nn.sigmoid(m_logits)
        merged = m[..., :D_MODEL] * y_s5 + m[..., D_MODEL:] * y_hy
        h = h + merged @ w_out[l]
    return rmsnorm(h, final_norm_w)
```

```python
import math
from contextlib import ExitStack
import numpy as np
import concourse.bass as bass
import concourse.mybir as mybir
from concourse.bass_utils import run_bass_kernel_spmd

F32 = mybir.dt.float32
BF16 = mybir.dt.bfloat16
ALU = mybir.AluOpType
AF = mybir.ActivationFunctionType

D = 1024; L = 2048; DEPTH = 2; NCORES = 8
INC = 7168
NDS = 48
MAGIC = 12582912.0
TWO_PI = 2.0 * math.pi


class KB:
    def __init__(self, nc, es):
        self.nc = nc
        self.engs = {'pe': nc.tensor, 'act': nc.scalar, 'dve': nc.vector, 'pool': nc.gpsimd, 'sp': nc.sync}
        self.csem = {e: es.enter_context(nc.semaphore('c_' + e)) for e in ('pe', 'act', 'dve', 'pool')}
        self.cnt = {e: 0 for e in self.csem}
        self.dsem = [es.enter_context(nc.semaphore('d%d' % i)) for i in range(NDS)]
        self.dcnt = [0] * NDS
        self.dnext = 0
        self.waited = {e: {} for e in self.engs}
        self.lastw = {}
        self.readers = {}
        self.psn = 0
        self.nins = 0

    def _sem(self, sid):
        return self.csem[sid[1]] if sid[0] == 'c' else self.dsem[sid[1]]

    def _wait(self, e, sid, val):
        if e == 'pe' and sid == ('c', 'pe'):
            return
        w = self.waited[e]
        if w.get(sid, 0) >= val:
            return
        self.engs[e].wait_ge(self._sem(sid), val)
        w[sid] = val

    def _deps(self, e, reads, writes):
        for t in reads:
            if t in self.lastw:
                self._wait(e, *self.lastw[t])
        for t in writes:
            if t in self.lastw:
                self._wait(e, *self.lastw[t])
            for sid, val in self.readers.get(t, {}).items():
                self._wait(e, sid, val)

    def _record(self, sid, val, reads, writes):
        for t in writes:
            self.lastw[t] = (sid, val)
            self.readers[t] = {}
        for t in reads:
            r = self.readers.setdefault(t, {})
            if r.get(sid, 0) < val:
                r[sid] = val

    def op(self, e, fn, reads=(), writes=(), inc=True):
        self._deps(e, reads, writes)
        ins = fn(self.engs[e])
        self.nins += 1
        if e == 'pe' and not inc:
            val = self.cnt['pe'] + 1
        else:
            self.cnt[e] += 1
            val = self.cnt[e]
            ins.then_inc(self.csem[e], 1)
        self._record(('c', e), val, reads, writes)

    def dma(self, q, out, in_, reads=(), writes=()):
        i = self.dnext
        self.dnext = (self.dnext + 1) % NDS
        sid = ('d', i)
        if self.dcnt[i] > 0:
            self._wait(q, sid, self.dcnt[i])
        self._deps(q, reads, writes)
        self.engs[q].dma_start(out=out, in_=in_).then_inc(self.dsem[i], 16)
        self.nins += 1
        self.dcnt[i] += 16
        self._record(sid, self.dcnt[i], reads, writes)

    def barrier(self, scratch):
        for i in range(NDS):
            if self.dcnt[i]:
                self._wait('pool', ('d', i), self.dcnt[i])
        for e in ('pe', 'act', 'dve'):
            if self.cnt[e]:
                self._wait('pool', ('c', e), self.cnt[e])
        self.op('pool', lambda g: g.memset(scratch, 0.0), writes=('__bar',))
        val = self.cnt['pool']
        for e in ('pe', 'act', 'dve', 'sp'):
            self._wait(e, ('c', 'pool'), val)
        self.lastw.clear()
        self.readers.clear()

    def ps(self):
        b = self.psn % 8
        self.psn += 1
        return b


def fft_consts():
    N = 4096
    n1 = np.arange(32); k1 = np.arange(32); n2 = np.arange(64); k2 = np.arange(64)
    ang = -2 * np.pi * np.outer(n1, k1 + 0.5) / 64.0
    g = np.stack([np.cos(ang), np.sin(ang)], -1)
    angh = -2 * np.pi * np.outer(n1 + 32, k1 + 0.5) / 64.0
    gh = -np.stack([np.cos(angh), np.sin(angh)], -1)
    G1 = np.zeros((4, 32, 4, 32, 2)); G1hi = np.zeros((4, 32, 4, 32, 2))
    for cq in range(4):
        G1[cq, :, cq] = g; G1hi[cq, :, cq] = gh
    a = -2 * np.pi * (n2[:, None, None] * (k1[None, :, None] + 0.5) / 4096.0 + n2[:, None, None] * k2[None, None, :] / 64.0)
    twr, twi = np.cos(a), np.sin(a)
    T = np.zeros((64, 32, 2, 64, 2))
    T[:, :, 0, :, 0] = twr; T[:, :, 0, :, 1] = twi
    T[:, :, 1, :, 0] = -twi; T[:, :, 1, :, 1] = twr
    T = np.concatenate([T, T], 0).reshape(128, 32 * 2 * 128)
    e = 2 * np.pi * np.outer(k2, n2) / 64.0
    er, ei = np.cos(e), np.sin(e)
    Ga = np.zeros((64, 2, 64, 2)); Gb = np.zeros((64, 2, 64, 2))
    for rp, s in ((0, 1.0), (1, -1.0)):
        Ga[:, rp, :, 0] = s * er; Ga[:, rp, :, 1] = s * ei
        Gb[:, rp, :, 0] = -ei; Gb[:, rp, :, 1] = er
    phi = 2 * np.pi * (k1[:, None, None] + 0.5) * (64 * n1[None, None, :] + n2[None, :, None]) / 4096.0
    H = np.zeros((4, 32, 64, 2, 4, 32))
    for cq in range(4):
        H[cq, :, :, 0, cq, :] = (2.0 / N) * np.cos(phi)
        H[cq, :, :, 1, cq, :] = -(2.0 / N) * np.sin(phi)
    Sw = np.zeros((64, 2, 64, 2))
    for k in range(64):
        Sw[k, 0, k, 1] = 1; Sw[k, 1, k, 0] = 1
    small = np.concatenate([G1.reshape(128, 256), G1hi.reshape(128, 256), Ga.reshape(128, 128),
                            Gb.reshape(128, 128), Sw.reshape(128, 128)], 1)
    return (small.astype(np.float32), T.astype(np.float32), H.reshape(128, 64 * 2 * 128).astype(np.float32))


def hy_feats():
    f32 = np.float32
    t = np.linspace(0.0, 1.0, L, dtype=f32)
    bands = np.linspace(1e-4, 15, 16, dtype=f32)
    def feats(pos, tt):
        angv = bands[None, :] * pos[:, None].astype(f32) * f32(2.0 * math.pi / L)
        return np.concatenate([tt[:, None], np.cos(angv), -np.sin(angv)], -1).astype(f32)
    posF = np.arange(L)
    posR = (L - np.arange(L)) % L
    fF = feats(posF, t); fR = feats(posR, t[posR])
    ft = np.stack([fF.T, fR.T], 0).astype(f32)
    tt = np.stack([np.broadcast_to(t, (128, L)), np.broadcast_to(t[posR], (128, L))], 0).astype(f32)
    return ft, tt


def s5_kv():
    j = np.arange(8)
    dbl = 8.0 * 2.0 ** np.arange(8)
    f = np.concatenate([7 - j, j + 1, j - 7, dbl])
    b = np.concatenate([j, 8 - j, -j, dbl])
    kv = np.stack([f, b], 0).astype(np.float32)
    return np.broadcast_to(kv[None], (128, 2, 32)).copy()


def s5_masks():
    jj = np.repeat(np.arange(8), 16)
    mf = (jj[None, :] >= jj[:, None]).astype(np.float32)
    mb = (jj[:, None] >= jj[None, :]).astype(np.float32)
    return np.concatenate([mf, mb], 1)


def build(dump=(), layers=DEPTH, stop_after=None):
    nc = bass.Bass("TRN2", target_bir_lowering=False)
    dt_in = {}

    def din(name, shape, dt=F32):
        dt_in[name] = nc.dram_tensor(name, list(shape), dt, kind="ExternalInput").ap()
        return dt_in[name]

    def dscr(name, shape, dt=F32):
        kind = "ExternalOutput" if name in dump else "Internal"
        return nc.dram_tensor(name, list(shape), dt, kind=kind).ap()

    xT = din("xT", [8, 128, L])
    normw = din("normw", [DEPTH + 1, 128, 8])
    w_in = din("w_in", [DEPTH, D, INC])
    w_glu = din("w_glu", [DEPTH, 512, 512])
    b_glu = din("b_glu", [DEPTH, 128, 4])
    s5d = din("s5d", [DEPTH, 128, 4])
    w_bs5 = din("w_bs5", [DEPTH, 512, D])
    w_bhy = din("w_bhy", [DEPTH, D, D])
    w_out = din("w_out", [DEPTH, D, D])
    convw = din("convw", [DEPTH, 128, 3, 24])
    convb = din("convb", [DEPTH, 128, 24])
    hyd = din("hyd", [DEPTH, 128, 2, 8])
    s5lam = din("s5lam", [DEPTH, 128, 3, 32])
    s5B = din("s5B", [DEPTH, 128, 2, 32, 16])
    s5C = din("s5C", [DEPTH, 128, 2, 32, 16])
    kvt = din("kvt", [128, 2, 32])
    masks = din("masks", [128, 256])
    ident = din("ident", [128, 128])
    hw1 = din("hw1", [DEPTH, 33, 64])
    hw2 = din("hw2", [DEPTH, 64, 64])
    hw3 = din("hw3", [DEPTH, 64, 4096])
    hvec = din("hvec", [DEPTH, 64, 3])
    hdec = din("hdec", [DEPTH, 128, 32])
    feats = din("feats", [2, 33, L])
    ttab = din("ttab", [2, 128, L])
    fsmall = din("fsmall", [128, 896])
    fT = din("fT", [128, 8192])
    fH = din("fH", [128, 16384])

    outT = nc.dram_tensor("outT", [8, 128, L], F32, kind="ExternalOutput").ap()

    hbuf = dscr("hbuf", [8, 128, L])
    ud = dscr("ud", [4, 128, L], BF16)
    yd = dscr("yd", [4, 128, L], BF16)
    ys5d = dscr("ys5d", [8, 128, L], BF16)
    yhyd = dscr("yhyd", [8, 128, L], BF16)
    binD = dscr("binD", [128, 32 * 2 * 2 * 64], BF16)
    coutD = dscr("coutD", [128, 2 * 16 * 2 * 128], BF16)
    mgD = dscr("mgD", [128, 32 * 128], BF16)
    coefD = dscr("coefD", [128, 3 * 2 * 16 * 8])
    khatD = dscr("khatD", [2, 8, 2, 128, 4096], BF16)

    _uid = [0]

    def SBT(name, shape, dt):
        _uid[0] += 1
        return nc.sbuf_tensor("%s_%d" % (name, _uid[0]), shape, dt)

    es0 = ExitStack()
    with es0:
        kb = KB(nc, es0)
        E0 = es0.enter_context
        psum = E0(nc.psum_tensor("psum", [128, 8, 512], F32))
        barscr = E0(SBT("barscr", [128, 8], F32))
        G = {}

        def PS(b, n=512, p0=0, p1=128, off=0):
            return psum[p0:p1, b, off:off + n]

        def PS4(g):
            return psum[:, 4 * g:4 * g + 4, :].rearrange("p b n -> p (b n)")

        def pst(b):
            return 'ps%d' % b

        def pst4(g):
            return tuple('ps%d' % (4 * g + i) for i in range(4))

        def range_reduce(eng, x_ap, tmp_ap, rd, wr_tmp):
            kb.op(eng, lambda v: v.tensor_scalar(out=tmp_ap, in0=x_ap, scalar1=float(1.0 / TWO_PI), scalar2=MAGIC,
                                                 op0=ALU.mult, op1=ALU.add), reads=rd, writes=wr_tmp)
            kb.op(eng, lambda v: v.tensor_scalar(out=tmp_ap, in0=tmp_ap, scalar1=-MAGIC, scalar2=None, op0=ALU.add),
                  reads=wr_tmp, writes=wr_tmp)
            kb.op(eng, lambda v: v.scalar_tensor_tensor(out=x_ap, in0=tmp_ap, scalar=float(-TWO_PI), in1=x_ap,
                                                        op0=ALU.mult, op1=ALU.add), reads=wr_tmp + rd, writes=rd)
            kb.op(eng, lambda v: v.tensor_scalar(out=x_ap, in0=x_ap, scalar1=3.14159, scalar2=-3.14159,
                                                 op0=ALU.min, op1=ALU.max), reads=rd, writes=rd)

        def phase_norm(src, wrow, dst_xn, dst_out):
            with ExitStack() as es:
                E = es.enter_context
                h = E(SBT("n_h", [128, 8, L], F32))
                sq = E(SBT("n_sq", [128, 2, 8, 512], BF16))
                ones = E(SBT("n_ones", [128, 128], BF16))
                nw = E(SBT("n_w", [128, 8], F32))
                rt = E(SBT("n_rt", [128, 2, 512], F32))
                rstd = E(SBT("n_rstd", [128, L], F32))
                epsb = E(SBT("n_eps", [128, 1], F32))
                ob = E(SBT("n_ob", [128, 2, L], F32)) if dst_out is not None else None
                kb.op('pool', lambda g: g.memset(ones[:], 1.0), writes=('ones',))
                kb.op('pool', lambda g: g.memset(epsb[:], 1e-6), writes=('epsb',))
                kb.dma('sp', nw[:], normw[wrow], writes=('nw',))
                for c in range(8):
                    kb.dma('sp' if c % 2 == 0 else 'act', h[:, c, :], src[c], writes=('h%d' % c,))
                for tt in range(4):
                    ts = slice(tt * 512, (tt + 1) * 512)
                    s = tt % 2
                    kb.op('act', lambda a: a.activation(out=sq[:, s], in_=h[:, :, ts], func=AF.Square),
                          reads=tuple('h%d' % c for c in range(8)), writes=('sq%d' % s,))
                    b = kb.ps()
                    for c in range(8):
                        kb.op('pe', lambda p: p.matmul(PS(b), ones[:], sq[:, s, c, :], start=(c == 0), stop=(c == 7)),
                              reads=('ones', 'sq%d' % s), writes=(pst(b),), inc=(c == 7))
                    kb.op('act', lambda a: a.activation(out=rt[:, s, :], in_=PS(b), func=AF.Sqrt, bias=epsb[:, 0:1],
                                                        scale=float(1.0 / D)),
                          reads=(pst(b), 'epsb'), writes=('rt%d' % s,))
                    kb.op('dve', lambda v: v.reciprocal(out=rstd[:, ts], in_=rt[:, s, :]), reads=('rt%d' % s,),
                          writes=('rstd%d' % tt,))
                for c in range(8):
                    if dst_out is None:
                        kb.op('dve', lambda v: v.scalar_tensor_tensor(out=dst_xn[:, c, :], in0=h[:, c, :], scalar=nw[:, c:c + 1],
                                                                      in1=rstd[:], op0=ALU.mult, op1=ALU.mult),
                              reads=('h%d' % c, 'nw') + tuple('rstd%d' % t for t in range(4)), writes=('xn%d' % c,))
                    else:
                        s = c % 2
                        kb.op('dve', lambda v: v.scalar_tensor_tensor(out=ob[:, s, :], in0=h[:, c, :], scalar=nw[:, c:c + 1],
                                                                      in1=rstd[:], op0=ALU.mult, op1=ALU.mult),
                              reads=('h%d' % c, 'nw') + tuple('rstd%d' % t for t in range(4)), writes=('ob%d' % s,))
                        kb.dma('sp', dst_out[c], ob[:, s, :], reads=('ob%d' % s,), writes=('out%d' % c,))
                kb.barrier(barscr[:, 0:1])

        def load_w(es, name, src_rows_ap, kc, ncols, q='sp'):
            E = es.enter_context
            st = E(SBT(name + "_st", [128, kc, ncols], F32))
            wb = E(SBT(name + "_bf", [128, kc, ncols], BF16))
            for k in range(kc):
                kb.dma(q if k % 2 == 0 else 'act', st[:, k, :], src_rows_ap[k * 128:(k + 1) * 128, :], writes=(name + '_st%d' % k,))
                kb.op('pool', lambda g: g.tensor_copy(out=wb[:, k, :], in_=st[:, k, :]), reads=(name + '_st%d' % k,),
                      writes=(name + '_bf',))
            return wb

        def prep_s5(l):
            with ExitStack() as es:
                E = es.enter_context
                lam = E(SBT("p_lam", [128, 3, 32], F32))
                Bt = E(SBT("p_B", [128, 2, 32, 16], F32))
                Ct = E(SBT("p_C", [128, 2, 32, 16], F32))
                kvs = E(SBT("p_kv", [128, 2, 32], F32))
                msk = E(SBT("p_msk", [128, 256], F32))
                idt = E(SBT("p_id", [128, 128], F32))
                a_re = E(SBT("p_are", [128, 32], F32))
                a_im = E(SBT("p_aim", [128, 32], F32))
                dtt = E(SBT("p_dt", [128, 32], F32))
                mag = E(SBT("p_mag", [128, 32, 32], F32))
                sn = E(SBT("p_sn", [128, 32, 32], F32))
                cs = E(SBT("p_cs", [128, 32, 32], F32))
                tmp = E(SBT("p_tmp", [128, 32, 32], F32))
                Er = E(SBT("p_Er", [128, 32, 32], F32))
                Ei = E(SBT("p_Ei", [128, 32, 32], F32))
                k4 = E(SBT("p_k4", [128, 8, 32], F32))
                Bb = E(SBT("p_Bb", [128, 2, 32, 16], F32))
                t16 = E(SBT("p_t16", [128, 2, 32, 16], F32))
                RB = E(SBT("p_RB", [128, 2, 32, 128], F32))
                CO = E(SBT("p_CO", [128, 2, 32, 128], F32))
                tb = E(SBT("p_tb", [128, 32, 128], F32))
                binS = E(SBT("p_bin", [128, 32, 2, 2, 64], BF16))
                coutS = E(SBT("p_cout", [128, 32, 2, 128], BF16))
                mgS = E(SBT("p_mg", [128, 32, 128], BF16))
                coefS = E(SBT("p_coef", [128, 3, 32, 8], F32))
                mt = E(SBT("p_mt", [128, 2, 256], F32))
                kb.dma('sp', lam[:], s5lam[l], writes=('lam',))
                kb.dma('sp', Bt[:], s5B[l], writes=('Bt',))
                kb.dma('sp', Ct[:], s5C[l], writes=('Ct',))
                kb.dma('sp', kvs[:], kvt, writes=('kvs',))
                kb.dma('sp', msk[:], masks, writes=('msk',))
                kb.dma('sp', idt[:], ident, writes=('idt',))
                V = lambda fn, r, w: kb.op('dve', fn, reads=r, writes=w)
                A = lambda fn, r, w: kb.op('act', fn, reads=r, writes=w)
                A(lambda a: a.activation(out=dtt[:], in_=lam[:, 2, :], func=AF.Exp), ('lam',), ('dtt',))
                V(lambda v: v.tensor_tensor(out=a_re[:], in0=lam[:, 0, :], in1=dtt[:], op=ALU.mult), ('lam', 'dtt'), ('a_re',))
                V(lambda v: v.tensor_tensor(out=a_im[:], in0=lam[:, 1, :], in1=dtt[:], op=ALU.mult), ('lam', 'dtt'), ('a_im',))
                kvb = kvs[:].rearrange("p d (o k) -> p d o k", o=1).to_broadcast([128, 2, 16, 32])
                are_b = a_re[:].rearrange("p (d g o) -> p d g o", d=2, o=1).to_broadcast([128, 2, 16, 32])
                aim_b = a_im[:].rearrange("p (d g o) -> p d g o", d=2, o=1).to_broadcast([128, 2, 16, 32])
                v4 = lambda t: t[:].rearrange("p (d g) k -> p d g k", d=2)
                V(lambda v: v.tensor_tensor(out=v4(tmp), in0=are_b, in1=kvb, op=ALU.mult), ('a_re', 'kvs'), ('tmp',))
                A(lambda a: a.activation(out=mag[:], in_=tmp[:], func=AF.Exp), ('tmp',), ('mag',))
                V(lambda v: v.tensor_tensor(out=v4(sn), in0=aim_b, in1=kvb, op=ALU.mult), ('a_im', 'kvs'), ('sn',))
                V(lambda v: v.tensor_scalar(out=cs[:], in0=sn[:], scalar1=float(math.pi / 2), scalar2=None, op0=ALU.add), ('sn',), ('cs',))
                range_reduce('dve', sn[:], tmp[:], ('sn',), ('tmp',))
                A(lambda a: a.activation(out=sn[:], in_=sn[:], func=AF.Sin), ('sn',), ('sn',))
                range_reduce('dve', cs[:], tmp[:], ('cs',), ('tmp',))
                A(lambda a: a.activation(out=cs[:], in_=cs[:], func=AF.Sin), ('cs',), ('cs',))
                V(lambda v: v.tensor_tensor(out=Er[:], in0=mag[:], in1=cs[:], op=ALU.mult), ('mag', 'cs'), ('Er',))
                V(lambda v: v.tensor_tensor(out=Ei[:], in0=mag[:], in1=sn[:], op=ALU.mult), ('mag', 'sn'), ('Ei',))
                lr = k4[:, 0, :]; li = k4[:, 1, :]; nr = k4[:, 2, :]; den = k4[:, 3, :]; kr = k4[:, 4, :]; ki = k4[:, 5, :]; t0 = k4[:, 6, :]
                for d in range(2):
                    idx = 8 if d == 0 else 15
                    V(lambda v: v.tensor_copy(out=lr[:, 16 * d:16 * d + 16], in_=Er[:, 16 * d:16 * d + 16, idx]), ('Er',), ('k4',))
                    V(lambda v: v.tensor_copy(out=li[:, 16 * d:16 * d + 16], in_=Ei[:, 16 * d:16 * d + 16, idx]), ('Ei',), ('k4',))
                V(lambda v: v.tensor_scalar(out=nr, in0=lr, scalar1=-1.0, scalar2=None, op0=ALU.add), ('k4',), ('k4',))
                V(lambda v: v.tensor_tensor(out=den, in0=lam[:, 0, :], in1=lam[:, 0, :], op=ALU.mult), ('lam', 'k4'), ('k4',))
                V(lambda v: v.tensor_tensor(out=t0, in0=lam[:, 1, :], in1=lam[:, 1, :], op=ALU.mult), ('lam', 'k4'), ('k4',))
                V(lambda v: v.tensor_tensor(out=den, in0=den, in1=t0, op=ALU.add), ('k4',), ('k4',))
                V(lambda v: v.reciprocal(out=den, in_=den), ('k4',), ('k4',))
                V(lambda v: v.tensor_tensor(out=kr, in0=nr, in1=lam[:, 0, :], op=ALU.mult), ('k4', 'lam'), ('k4',))
                V(lambda v: v.tensor_tensor(out=t0, in0=li, in1=lam[:, 1, :], op=ALU.mult), ('k4', 'lam'), ('k4',))
                V(lambda v: v.tensor_tensor(out=kr, in0=kr, in1=t0, op=ALU.add), ('k4',), ('k4',))
                V(lambda v: v.tensor_tensor(out=kr, in0=kr, in1=den, op=ALU.mult), ('k4',), ('k4',))
                V(lambda v: v.tensor_tensor(out=ki, in0=li, in1=lam[:, 0, :], op=ALU.mult), ('k4', 'lam'), ('k4',))
                V(lambda v: v.tensor_tensor(out=t0, in0=nr, in1=lam[:, 1, :], op=ALU.mult), ('k4', 'lam'), ('k4',))
                V(lambda v: v.tensor_tensor(out=ki, in0=ki, in1=t0, op=ALU.subtract), ('k4',), ('k4',))
                V(lambda v: v.tensor_tensor(out=ki, in0=ki, in1=den, op=ALU.mult), ('k4',), ('k4',))
                krb = kr.rearrange("p (g o) -> p g o", o=1).to_broadcast([128, 32, 16])
                kib = ki.rearrange("p (g o) -> p g o", o=1).to_broadcast([128, 32, 16])
                V(lambda v: v.tensor_tensor(out=Bb[:, 0], in0=Bt[:, 0], in1=krb, op=ALU.mult), ('Bt', 'k4'), ('Bb',))
                V(lambda v: v.tensor_tensor(out=t16[:, 0], in0=Bt[:, 1], in1=kib, op=ALU.mult), ('Bt', 'k4'), ('t16',))
                V(lambda v: v.tensor_tensor(out=Bb[:, 0], in0=Bb[:, 0], in1=t16[:, 0], op=ALU.subtract), ('Bb', 't16'), ('Bb',))
                V(lambda v: v.tensor_tensor(out=Bb[:, 1], in0=Bt[:, 1], in1=krb, op=ALU.mult), ('Bt', 'k4', 'Bb'), ('Bb',))
                V(lambda v: v.tensor_tensor(out=t16[:, 1], in0=Bt[:, 0], in1=kib, op=ALU.mult), ('Bt', 'k4', 't16'), ('t16',))
                V(lambda v: v.tensor_tensor(out=Bb[:, 1], in0=Bb[:, 1], in1=t16[:, 1], op=ALU.add), ('Bb', 't16'), ('Bb',))

                def cprod(dst, k0, X, sign_im, tag):
                    Erb = Er[:, :, k0:k0 + 8].rearrange("p g (k o) -> p g k o", o=1).to_broadcast([128, 32, 8, 16])
                    Eib = Ei[:, :, k0:k0 + 8].rearrange("p g (k o) -> p g k o", o=1).to_broadcast([128, 32, 8, 16])
                    Xr = X[:, 0].rearrange("p g (o h) -> p g o h", o=1).to_broadcast([128, 32, 8, 16])
                    Xi = X[:, 1].rearrange("p g (o h) -> p g o h", o=1).to_broadcast([128, 32, 8, 16])
                    d0 = dst[:, 0].rearrange("p g (k h) -> p g k h", k=8)
                    d1 = dst[:, 1].rearrange("p g (k h) -> p g k h", k=8)
                    tv = tb[:].rearrange("p g (k h) -> p g k h", k=8)
                    rd = ('Er', 'Ei', tag)
                    V(lambda v: v.tensor_tensor(out=d0, in0=Erb, in1=Xr, op=ALU.mult), rd, (tag + 'o',))
                    V(lambda v: v.tensor_tensor(out=tv, in0=Eib, in1=Xi, op=ALU.mult), rd, ('tb',))
                    V(lambda v: v.tensor_tensor(out=d0, in0=d0, in1=tv, op=ALU.subtract), (tag + 'o', 'tb'), (tag + 'o',))
                    V(lambda v: v.tensor_tensor(out=d1, in0=Erb, in1=Xi, op=ALU.mult), rd + (tag + 'o',), (tag + 'o',))
                    V(lambda v: v.tensor_tensor(out=tv, in0=Eib, in1=Xr, op=ALU.mult), rd + ('tb',), ('tb',))
                    V(lambda v: v.tensor_tensor(out=d1, in0=d1, in1=tv, op=ALU.add), (tag + 'o', 'tb'), (tag + 'o',))
                    if sign_im < 0:
                        V(lambda v: v.tensor_scalar(out=dst[:, 1], in0=dst[:, 1], scalar1=-1.0, scalar2=None, op0=ALU.mult),
                          (tag + 'o',), (tag + 'o',))
                cprod(RB, 0, Bb, +1, 'Bb')
                cprod(CO, 8, Ct, -1, 'Ct')
                for ri in range(2):
                    A(lambda a: a.activation(out=coutS[:, :, ri, :], in_=CO[:, ri], func=AF.Identity), ('Cto',), ('coutS',))
                kb.dma('sp', coutD, coutS[:].rearrange("p a b c -> p (a b c)"), reads=('coutS',), writes=('coutD',))
                cprod(CO, 16, Ct, -1, 'Ct')
                RC = CO
                A(lambda a: a.activation(out=coefS[:, 0], in_=Er[:, :, 24:32], func=AF.Identity), ('Er',), ('coefS',))
                A(lambda a: a.activation(out=coefS[:, 1], in_=Ei[:, :, 24:32], func=AF.Identity), ('Ei',), ('coefS',))
                A(lambda a: a.activation(out=coefS[:, 2], in_=Ei[:, :, 24:32], func=AF.Identity, scale=-1.0), ('Ei',), ('coefS',))
                kb.dma('sp', coefD, coefS[:].rearrange("p a b c -> p (a b c)"), reads=('coefS',), writes=('coefD',))
                for dg in range(32):
                    d, gp = dg // 16, dg % 16
                    b = kb.ps()
                    for ri in range(2):
                        kb.op('pe', lambda p: p.transpose(PS(b, 128, off=128 * ri), RB[:, ri, dg, :], idt[:]),
                              reads=('Bbo', 'idt'), writes=(pst(b),), inc=(ri == 1))
                    for ri in range(2):
                        A(lambda a: a.activation(out=binS[:, 2 * gp:2 * gp + 2, d, ri, :],
                                                 in_=PS(b, 128, off=128 * ri).rearrange("p (g q) -> p g q", g=2), func=AF.Identity),
                          (pst(b),), ('binS',))
                kb.dma('sp', binD, binS[:].rearrange("p a b c e -> p (a b c e)"), reads=('binS',), writes=('binD',))
                for g in range(32):
                    gp, gpar = g // 2, g % 2
                    b = kb.ps()
                    p0, p1 = 64 * gpar, 64 * gpar + 64
                    for d in range(2):
                        dg = 16 * d + gp
                        kb.op('pe', lambda p: p.matmul(PS(b, 128, off=128 * d), RB[p0:p1, 0, dg, :], RC[p0:p1, 0, dg, :],
                                                       start=True, stop=False),
                              reads=('Bbo', 'Cto'), writes=(pst(b),), inc=False)
                        kb.op('pe', lambda p: p.matmul(PS(b, 128, off=128 * d), RB[p0:p1, 1, dg, :], RC[p0:p1, 1, dg, :],
                                                       start=False, stop=True),
                              reads=('Bbo', 'Cto'), writes=(pst(b),), inc=(d == 1))
                    s = g % 2
                    V(lambda v: v.tensor_tensor(out=mt[:, s, :], in0=PS(b, 256), in1=msk[:], op=ALU.mult), (pst(b), 'msk'), ('mt%d' % s,))
                    V(lambda v: v.tensor_tensor(out=mgS[:, g, :], in0=mt[:, s, 0:128], in1=mt[:, s, 128:256], op=ALU.add),
                      ('mt%d' % s,), ('mgS',))
                kb.dma('sp', mgD, mgS[:].rearrange("p a b -> p (a b)"), reads=('mgS',), writes=('mgD',))
                kb.barrier(barscr[:, 0:1])

        def phase_s5(l):
            with ExitStack() as es_outer:
                EO = es_outer.enter_context
                gs = EO(SBT("s_gs", [128, 4, L], BF16))
                with ExitStack() as es:
                    E = es.enter_context
                    wb = load_w(es, "s_w", w_in[l][:, 0:1024], 8, 1024)
                    udt = E(SBT("s_ud", [128, 4, 8, 256], BF16))
                    for cc in range(8):
                        for tt in range(4):
                            b = kb.ps()
                            for k in range(8):
                                kb.op('pe', lambda p: p.matmul(PS(b), wb[:, k, cc * 128:(cc + 1) * 128], G['xn'][:, k, tt * 512:(tt + 1) * 512],
                                                               start=(k == 0), stop=(k == 7)),
                                      reads=('s_w_bf', 'xn%d' % k), writes=(pst(b),), inc=(k == 7))
                            if cc < 4:
                                kb.op('act', lambda a: a.activation(out=udt[:, cc, :, tt * 64:(tt + 1) * 64],
                                                                    in_=PS(b).rearrange("p (c j) -> p j c", j=8), func=AF.Identity),
                                      reads=(pst(b),), writes=('udt%d' % cc,))
                            else:
                                kb.op('act', lambda a: a.activation(out=gs[:, cc - 4, tt * 512:(tt + 1) * 512], in_=PS(b), func=AF.Silu),
                                      reads=(pst(b),), writes=('gs',))
                        if cc < 4:
                            kb.dma('sp', ud[cc], udt[:, cc].rearrange("p j c -> p (j c)"), reads=('udt%d' % cc,), writes=('ud',))
                    kb.barrier(barscr[:, 0:1])
                with ExitStack() as es:
                    E = es.enter_context
                    U8 = E(SBT("s_U8", [128, 32, 256], BF16))
                    Mg = E(SBT("s_Mg", [128, 32, 128], BF16))
                    Bin = E(SBT("s_Bin", [128, 32, 2, 2, 64], BF16))
                    Cout = E(SBT("s_Cout", [128, 32, 2, 128], BF16))
                    coef = E(SBT("s_coef", [128, 3, 32, 8], F32))
                    Xs = E(SBT("s_Xs", [128, 32, 2, 256], BF16))
                    Y8 = E(SBT("s_Y8", [128, 32, 256], BF16))
                    NSL = 2
                    XA = E(SBT("s_XA", [128, NSL, 2, 768], F32))
                    XB = E(SBT("s_XB", [128, NSL, 2, 768], F32))
                    T1 = E(SBT("s_T1", [128, NSL, 2, 256], F32))
                    udv = ud.rearrange("cc (g h) (j c) -> h j (cc g) c", h=16, j=8)
                    for j in range(8):
                        kb.dma('sp' if j % 2 == 0 else 'act', U8[16 * j:16 * j + 16, :, :], udv[:, j], reads=('ud',), writes=('U8',))
                    kb.dma('sp', Mg[:].rearrange("p a b -> p (a b)"), mgD, reads=('mgD',), writes=('Mg',))
                    kb.dma('act', Bin[:].rearrange("p a b c e -> p (a b c e)"), binD, reads=('binD',), writes=('Bin',))
                    kb.dma('sp', Cout[:].rearrange("p a b c -> p (a b c)"), coutD, reads=('coutD',), writes=('Cout',))
                    kb.dma('act', coef[:].rearrange("p a b c -> p (a b c)"), coefD, reads=('coefD',), writes=('coef',))
                    kb.op('pool', lambda g: g.memset(XA[:], 0.0), writes=tuple('XA%d' % i for i in range(NSL)))
                    kb.op('pool', lambda g: g.memset(XB[:], 0.0), writes=tuple('XB%d' % i for i in range(NSL)))
                    for s0 in range(0, 32, NSL):
                        for i in range(NSL):
                            dg = s0 + i
                            d, gp = dg // 16, dg % 16
                            b = kb.ps()
                            for ri in range(2):
                                for gpar in range(2):
                                    g = 2 * gp + gpar
                                    kb.op('pe', lambda p: p.matmul(PS(b, 256, 64 * gpar, 64 * gpar + 64, off=256 * ri),
                                                                   Bin[:, g, d, ri, :], U8[:, g, :], start=True, stop=True),
                                          reads=('Bin', 'U8'), writes=(pst(b),), inc=(ri == 1 and gpar == 1))
                            kb.op('act', lambda a: a.activation(out=XA[:, i, :, 256:512], in_=PS(b).rearrange("p (r c) -> p r c", r=2),
                                                                func=AF.Identity),
                                  reads=(pst(b),), writes=('XA%d' % i,))
                        for r in range(8):
                            sft = 2 ** r
                            for i in range(NSL):
                                dg = s0 + i
                                d = dg // 16
                                src, dst = (XA, XB) if r % 2 == 0 else (XB, XA)
                                sn_, dn_ = ('XA%d' % i, 'XB%d' % i) if r % 2 == 0 else ('XB%d' % i, 'XA%d' % i)
                                lo = 256 - sft if d == 0 else 256 + sft
                                e_ = coef[:, 0, dg, r:r + 1]; f_ = coef[:, 1, dg, r:r + 1]; nf_ = coef[:, 2, dg, r:r + 1]
                                Rs = src[:, i, 0, lo:lo + 256]; Is = src[:, i, 1, lo:lo + 256]
                                R0 = src[:, i, 0, 256:512]; I0 = src[:, i, 1, 256:512]
                                tn = 'T1_%d' % i
                                kb.op('dve', lambda v: v.scalar_tensor_tensor(out=T1[:, i, 0, :], in0=Rs, scalar=e_, in1=R0, op0=ALU.mult, op1=ALU.add),
                                      reads=(sn_, 'coef'), writes=(tn + 'a',))
                                kb.op('dve', lambda v: v.scalar_tensor_tensor(out=dst[:, i, 0, 256:512], in0=Is, scalar=nf_, in1=T1[:, i, 0, :],
                                                                              op0=ALU.mult, op1=ALU.add),
                                      reads=(sn_, 'coef', tn + 'a'), writes=(dn_ + 'r',))
                                kb.op('dve', lambda v: v.scalar_tensor_tensor(out=T1[:, i, 1, :], in0=Rs, scalar=f_, in1=I0, op0=ALU.mult, op1=ALU.add),
                                      reads=(sn_, 'coef'), writes=(tn + 'b',))
                                kb.op('dve', lambda v: v.scalar_tensor_tensor(out=dst[:, i, 1, 256:512], in0=Is, scalar=e_, in1=T1[:, i, 1, :],
                                                                              op0=ALU.mult, op1=ALU.add),
                                      reads=(sn_, 'coef', tn + 'b'), writes=(dn_, dn_ + 'r'))
                        for i in range(NSL):
                            dg = s0 + i
                            d = dg // 16
                            lo = 255 if d == 0 else 257
                            kb.op('act', lambda a: a.activation(out=Xs[:, dg, :, :], in_=XA[:, i, :, lo:lo + 256], func=AF.Identity),
                                  reads=('XA%d' % i, 'XA%dr' % i), writes=('Xs',))
                    for g in range(32):
                        gp, gpar = g // 2, g % 2
                        p0, p1 = 64 * gpar, 64 * gpar + 64
                        b = kb.ps()
                        kb.op('pe', lambda p: p.matmul(PS(b, 256), Mg[:, g, :], U8[:, g, :], start=True, stop=False),
                              reads=('Mg', 'U8'), writes=(pst(b),), inc=False)
                        for d in range(2):
                            for ri in range(2):
                                last = (d == 1 and ri == 1)
                                kb.op('pe', lambda p: p.matmul(PS(b, 256), Cout[p0:p1, 16 * d + gp, ri, :], Xs[p0:p1, 16 * d + gp, ri, :],
                                                               start=False, stop=last),
                                      reads=('Cout', 'Xs'), writes=(pst(b),), inc=last)
                        kb.op('act', lambda a: a.activation(out=Y8[:, g, :], in_=PS(b, 256), func=AF.Identity), reads=(pst(b),), writes=('Y8',))
                    ydv = yd.rearrange("cc (g h) (i c) -> h i (cc g) c", h=16, i=8)
                    for i in range(8):
                        kb.dma('sp' if i % 2 == 0 else 'act', ydv[:, i], Y8[16 * i:16 * i + 16, :, :], reads=('Y8',), writes=('yd',))
                    kb.barrier(barscr[:, 0:1])
                with ExitStack() as es:
                    E = es.enter_context
                    wg = E(SBT("c_wg", [128, 4, 512], BF16))
                    wbr = E(SBT("c_wbr", [128, 4, 1024], BF16))
                    wst = E(SBT("c_wst", [128, 2, 1024], F32))
                    yt = E(SBT("c_y", [128, L], BF16))
                    ut = E(SBT("c_u", [128, L], BF16))
                    y1 = E(SBT("c_y1", [128, L], F32))
                    t3 = E(SBT("c_t3", [128, L], F32))
                    yg = E(SBT("c_yg", [128, 4, L], BF16))
                    sg = E(SBT("c_sg", [128, 2, 512], F32))
                    y3 = E(SBT("c_y3", [128, 4, L], BF16))
                    ys = E(SBT("c_ys", [128, L], BF16))
                    dv = E(SBT("c_d", [128, 4], F32))
                    bg = E(SBT("c_bg", [128, 4], F32))
                    kb.dma('sp', dv[:], s5d[l], writes=('dv',))
                    kb.dma('sp', bg[:], b_glu[l], writes=('bg',))
                    for k in range(4):
                        s = k % 2
                        kb.dma('sp', wst[:, s, 0:512], w_glu[l][k * 128:(k + 1) * 128, :], writes=('wst%d' % s,))
                        kb.op('pool', lambda g: g.tensor_copy(out=wg[:, k, :], in_=wst[:, s, 0:512]), reads=('wst%d' % s,), writes=('wg',))
                    for k in range(4):
                        s = k % 2
                        kb.dma('sp', wst[:, s, :], w_bs5[l][k * 128:(k + 1) * 128, :], writes=('wst%d' % s,))
                        kb.op('pool', lambda g: g.tensor_copy(out=wbr[:, k, :], in_=wst[:, s, :]), reads=('wst%d' % s,), writes=('wbr',))
                    for cc in range(4):
                        kb.dma('sp', yt[:], yd[cc], reads=('yd',), writes=('yt',))
                        kb.dma('act', ut[:], ud[cc], reads=('ud',), writes=('ut',))
                        kb.op('dve', lambda v: v.scalar_tensor_tensor(out=y1[:], in0=ut[:], scalar=dv[:, cc:cc + 1], in1=yt[:],
                                                                      op0=ALU.mult, op1=ALU.add),
                              reads=('ut', 'yt', 'dv'), writes=('y1',))
                        kb.op('dve', lambda v: v.tensor_tensor(out=t3[:], in0=y1[:], in1=y1[:], op=ALU.mult), reads=('y1',), writes=('t3',))
                        kb.op('dve', lambda v: v.tensor_scalar(out=t3[:], in0=t3[:], scalar1=0.044715 * 1.5957691216, scalar2=1.5957691216,
                                                               op0=ALU.mult, op1=ALU.add), reads=('t3',), writes=('t3',))
                        kb.op('dve', lambda v: v.tensor_tensor(out=t3[:], in0=t3[:], in1=y1[:], op=ALU.mult), reads=('t3', 'y1'), writes=('t3',))
                        kb.op('act', lambda a: a.activation(out=t3[:], in_=t3[:], func=AF.Sigmoid), reads=('t3',), writes=('t3',))
                        kb.op('dve', lambda v: v.tensor_tensor(out=yg[:, cc, :], in0=t3[:], in1=y1[:], op=ALU.mult), reads=('t3', 'y1'), writes=('yg',))
                    gsp = gs[:].rearrange("p a (c j) -> p a j c", j=8)
                    for cc in range(4):
                        for tt in range(4):
                            b = kb.ps()
                            for k in range(4):
                                kb.op('pe', lambda p: p.matmul(PS(b), wg[:, k, cc * 128:(cc + 1) * 128], yg[:, k, tt * 512:(tt + 1) * 512],
                                                               start=(k == 0), stop=(k == 3)),
                                      reads=('wg', 'yg'), writes=(pst(b),), inc=(k == 3))
                            s = tt % 2
                            kb.op('act', lambda a: a.activation(out=sg[:, s, :], in_=PS(b), func=AF.Sigmoid, bias=bg[:, cc:cc + 1]),
                                  reads=(pst(b), 'bg'), writes=('sg%d' % s,))
                            kb.op('dve', lambda v: v.tensor_tensor(out=sg[:, s, :], in0=sg[:, s, :], in1=yg[:, cc, tt * 512:(tt + 1) * 512], op=ALU.mult),
                                  reads=('sg%d' % s, 'yg'), writes=('sg%d' % s,))
                            kb.op('dve', lambda v: v.tensor_tensor(out=y3[:, cc, tt * 512:(tt + 1) * 512].rearrange("p (j c) -> p j c", j=2),
                                                                   in0=sg[:, s, :].rearrange("p (j c) -> p j c", j=2),
                                                                   in1=gsp[:, cc, 2 * tt:2 * tt + 2, :], op=ALU.mult),
                                  reads=('sg%d' % s, 'gs'), writes=('y3',))
                    for dc in range(8):
                        for tt in range(4):
                            b = kb.ps()
                            for k in range(4):
                                kb.op('pe', lambda p: p.matmul(PS(b), wbr[:, k, dc * 128:(dc + 1) * 128], y3[:, k, tt * 512:(tt + 1) * 512],
                                                               start=(k == 0), stop=(k == 3)),
                                      reads=('wbr', 'y3'), writes=(pst(b),), inc=(k == 3))
                            kb.op('act', lambda a: a.activation(out=ys[:].rearrange("p (c j) -> p j c", j=8)[:, 2 * tt:2 * tt + 2, :],
                                                                in_=PS(b).rearrange("p (j c) -> p j c", j=2), func=AF.Identity),
                                  reads=(pst(b),), writes=('ys',))
                        kb.dma('sp', ys5d[dc], ys[:], reads=('ys',), writes=('ys5d',))
                    kb.barrier(barscr[:, 0:1])

        def fft_fwd(C, zin_list, B, spec_evac):
            for half in range(2):
                g4 = half
                for cpl in range(8):
                    cp = half * 8 + cpl
                    bank = 4 * g4 + cpl // 2
                    off = 256 * (cpl % 2)
                    for zi, (zT, ztok, gk) in enumerate(zin_list):
                        kb.op('pe', lambda p: p.matmul(PS(bank, 256, off=off), zT[:, 2 * cp:2 * cp + 2, :].rearrange("p a b -> p (a b)"),
                                                       C['G1hi'] if gk else C['G1'], start=(zi == 0), stop=(zi == len(zin_list) - 1)),
                              reads=(ztok, 'fc'), writes=(pst(bank),), inc=(zi == len(zin_list) - 1))
                kb.op('act', lambda a: a.activation(
                    out=B[:].rearrange("p k r cp cq -> p cp cq (k r)")[:, half * 8:half * 8 + 8],
                    in_=PS4(g4).rearrange("p (cp cq kr) -> p cp cq kr", cp=8, cq=4), func=AF.Identity),
                    reads=pst4(g4), writes=('B',))
            for c2 in range(2):
                g4 = c2
                for k1 in range(32):
                    bank = 4 * g4 + k1 // 8
                    off = 64 * (k1 % 8)
                    for ri in range(2):
                        kb.op('pe', lambda p: p.matmul(PS(bank, 64, off=off), C['T'][64 * c2:64 * c2 + 64, k1, ri, :],
                                                       B[64 * c2:64 * c2 + 64, k1, ri].rearrange("p a b -> p (a b)"),
                                                       start=(ri == 0), stop=(ri == 1)),
                              reads=('B', 'fc'), writes=(pst(bank),), inc=(ri == 1))
                spec_evac(c2, PS4(g4).rearrange("p (k m) -> p m k", k=32), pst4(g4))

        def fft_inv(C, P1, P2, Dt, conv_out, conv_tok):
            for c2 in range(2):
                g4 = c2
                for cp in range(16):
                    bank = 4 * g4 + cp // 4
                    off = 128 * (cp % 4)
                    kb.op('pe', lambda p: p.matmul(PS(bank, 128, off=off), P1[:, c2, cp * 128:(cp + 1) * 128], C['Ga'], start=True, stop=False),
                          reads=('P1', 'fc'), writes=(pst(bank),), inc=False)
                    kb.op('pe', lambda p: p.matmul(PS(bank, 128, off=off), P2[:, c2, cp * 128:(cp + 1) * 128], C['Gb'], start=False, stop=True),
                          reads=('P2', 'fc'), writes=(pst(bank),), inc=True)
                kb.op('act', lambda a: a.activation(
                    out=Dt[:].rearrange("p n r cp c2 -> p c2 cp (n r)")[:, c2],
                    in_=PS4(g4).rearrange("p (cp nr) -> p cp nr", cp=16), func=AF.Identity),
                    reads=pst4(g4), writes=('Dt',))
            for half in range(2):
                g4 = half
                for n2l in range(32):
                    n2 = half * 32 + n2l
                    bank = 4 * g4 + n2l // 8
                    off = 64 * (n2l % 8)
                    for r in range(2):
                        kb.op('pe', lambda p: p.matmul(PS(bank, 32, off=off), C['H'][:, n2, r, :],
                                                       Dt[:, n2, r].rearrange("p a b -> p (a b)"), start=(r == 0), stop=(r == 1)),
                              reads=('Dt', 'fc'), writes=(pst(bank),), inc=(r == 1))
                kb.op('dve', lambda v: v.transpose(
                    out=conv_out.rearrange("p (n1 n2) -> p n2 n1", n2=64)[:, half * 32:half * 32 + 32, :],
                    in_=PS4(g4).rearrange("p (n s) -> p n s", s=64)[:, :, 0:32]),
                    reads=pst4(g4), writes=(conv_tok,))

        def load_fft_consts(es, with_filter):
            E = es.enter_context
            fs = E(SBT("f_sm", [128, 896], BF16))
            Tb = E(SBT("f_T", [128, 32, 2, 128], BF16))
            st = E(SBT("f_st", [128, 2, 1024], F32))
            cnt = [0]

            def stage(dst_ap, src_ap, n):
                s_ = cnt[0] % 2
                cnt[0] += 1
                kb.dma('sp' if s_ == 0 else 'act', st[:, s_, 0:n], src_ap, writes=('f_st%d' % s_,))
                kb.op('pool', lambda g: g.tensor_copy(out=dst_ap, in_=st[:, s_, 0:n]), reads=('f_st%d' % s_,), writes=('fc',))
            stage(fs[:], fsmall, 896)
            Tv = Tb[:].rearrange("p a b c -> p (a b c)")
            for i in range(8):
                stage(Tv[:, i * 1024:(i + 1) * 1024], fT[:, i * 1024:(i + 1) * 1024], 1024)
            C = {'G1': fs[:, 0:256], 'G1hi': fs[:, 256:512], 'Ga': fs[:, 512:640], 'Gb': fs[:, 640:768], 'Sw': fs[:, 768:896], 'T': Tb}
            if not with_filter:
                Hb = E(SBT("f_H", [128, 64, 2, 128], BF16))
                Hv = Hb[:].rearrange("p a b c -> p (a b c)")
                for i in range(16):
                    stage(Hv[:, i * 1024:(i + 1) * 1024], fH[:, i * 1024:(i + 1) * 1024], 1024)
                C['H'] = Hb
            return C

        def prep_hy(l):
            with ExitStack() as es:
                E = es.enter_context
                C = load_fft_consts(es, True)
                w1 = E(SBT("q_w1", [33, 64], F32))
                w2 = E(SBT("q_w2", [64, 64], F32))
                w3 = E(SBT("q_w3", [64, 4096], F32))
                hv = E(SBT("q_hv", [64, 3], F32))
                bf = E(SBT("q_bf", [64, 2], F32))
                dec = E(SBT("q_dec", [128, 32], F32))
                ft = E(SBT("q_ft", [33, 2, L], F32))
                tt_ = E(SBT("q_tt", [128, 2, L], F32))
                h1 = E(SBT("q_h1", [64, L], F32))
                h2 = E(SBT("q_h2", [64, 2, L], F32))
                tmp = E(SBT("q_tmp", [64, L], F32))
                win = E(SBT("q_win", [128, L], F32))
                hk = E(SBT("q_hk", [128, 2, L], F32))
                junk = E(SBT("q_junk", [128, L], BF16))
                ss = E(SBT("q_ss", [128, 4], F32))
                kbf = E(SBT("q_kbf", [128, 2, L], BF16))
                zT = E(SBT("q_zT", [128, 2, 32, 64], BF16))
                B = E(SBT("q_B", [128, 32, 2, 16, 4], BF16))
                Kh = E(SBT("q_Kh", [128, 2, 2, 2048], BF16))
                kb.dma('sp', w1[:], hw1[l], writes=('w1',))
                kb.dma('sp', w2[:], hw2[l], writes=('w2',))
                kb.dma('sp', w3[:], hw3[l], writes=('w3',))
                kb.dma('sp', hv[:], hvec[l], writes=('hv',))
                kb.dma('sp', dec[:], hdec[l], writes=('dec',))
                for i in range(2):
                    kb.dma('act', ft[:, i, :], feats[i], writes=('ft',))
                    kb.dma('act', tt_[:, i, :], ttab[i], writes=('tt',))
                V = lambda fn, r, w: kb.op('dve', fn, reads=r, writes=w)
                A = lambda fn, r, w: kb.op('act', fn, reads=r, writes=w)
                V(lambda v: v.tensor_tensor(out=bf[:, 0:1], in0=hv[:, 0:1], in1=hv[:, 2:3], op=ALU.mult), ('hv',), ('bf',))
                V(lambda v: v.tensor_tensor(out=bf[:, 1:2], in0=hv[:, 1:2], in1=hv[:, 2:3], op=ALU.mult), ('hv', 'bf'), ('bf',))
                A(lambda a: a.activation(out=dec[:], in_=dec[:], func=AF.Abs), ('dec',), ('dec',))
                V(lambda v: v.tensor_scalar(out=dec[:], in0=dec[:], scalar1=-1.0, scalar2=None, op0=ALU.mult), ('dec',), ('dec',))
                for i in range(2):
                    for stage in range(2):
                        wmat = w1 if stage == 0 else w2
                        dst = h1 if stage == 0 else h2[:, i, :]
                        dtok = 'h1' if stage == 0 else 'h2_%d' % i
                        for t4 in range(4):
                            ts = slice(t4 * 512, (t4 + 1) * 512)
                            b = kb.ps()
                            if stage == 0:
                                kb.op('pe', lambda p: p.matmul(PS(b, 512, 0, 64), w1[:], ft[:, i, ts], start=True, stop=True),
                                      reads=('w1', 'ft'), writes=(pst(b),))
                            else:
                                kb.op('pe', lambda p: p.matmul(PS(b, 512, 0, 64), w2[:], h1[:, ts], start=True, stop=True),
                                      reads=('w2', 'h1'), writes=(pst(b),))
                            dsl = dst[:, ts] if stage == 0 else h2[:, i, ts]
                            A(lambda a: a.activation(out=dsl, in_=PS(b, 512, 0, 64), func=AF.Identity, bias=bf[:, stage:stage + 1], scale=hv[:, 2:3]),
                              (pst(b), 'bf', 'hv'), (dtok,))
                        dfull = h1[:] if stage == 0 else h2[:, i, :]
                        range_reduce('dve', dfull, tmp[:], (dtok,), ('tmp',))
                        A(lambda a: a.activation(out=dfull, in_=dfull, func=AF.Sin), (dtok,), (dtok,))
                for o in range(2):
                    for cc in range(8):
                        for dr in range(2):
                            fc = o * 16 + dr * 8 + cc
                            A(lambda a: a.activation(out=win[:], in_=tt_[:, dr, :], func=AF.Exp, scale=dec[:, fc:fc + 1]), ('tt', 'dec'), ('win',))
                            for t4 in range(4):
                                ts = slice(t4 * 512, (t4 + 1) * 512)
                                b = kb.ps()
                                kb.op('pe', lambda p: p.matmul(PS(b), w3[:, fc * 128:(fc + 1) * 128], h2[:, dr, ts], start=True, stop=True),
                                      reads=('w3', 'h2_%d' % dr), writes=(pst(b),))
                                V(lambda v: v.scalar_tensor_tensor(out=hk[:, dr, ts], in0=win[:, ts], scalar=0.05, in1=PS(b), op0=ALU.add, op1=ALU.mult),
                                  ('win', pst(b)), ('hk%d' % dr,))
                            if dr == 1:
                                V(lambda v: v.memset(hk[:, 1, 0:1], 0.0), ('hk1',), ('hk1',))
                            A(lambda a: a.activation(out=junk[:], in_=hk[:, dr, :], func=AF.Square, accum_out=ss[:, dr:dr + 1]),
                              ('hk%d' % dr,), ('junk', 'ss%d' % dr))
                        V(lambda v: v.tensor_tensor(out=ss[:, 2:3], in0=ss[:, 0:1], in1=ss[:, 1:2], op=ALU.add), ('ss0', 'ss1'), ('ss2',))
                        A(lambda a: a.activation(out=ss[:, 2:3], in_=ss[:, 2:3], func=AF.Sqrt, bias=barscr[:, 1:2]), ('ss2', 'epsq'), ('ss2',))
                        V(lambda v: v.reciprocal(out=ss[:, 3:4], in_=ss[:, 2:3]), ('ss2',), ('ss3',))
                        for dr in range(2):
                            V(lambda v: v.tensor_scalar(out=kbf[:, dr, :], in0=hk[:, dr, :], scalar1=ss[:, 3:4], scalar2=None, op0=ALU.mult),
                              ('hk%d' % dr, 'ss3'), ('kbf%d' % dr,))
                            V(lambda v: v.transpose(out=zT[:, dr].rearrange("p c n -> p n c"),
                                                    in_=kbf[:, dr, :].rearrange("p (n1 n2) -> p n2 n1", n2=64)),
                              ('kbf%d' % dr,), ('zT%d' % dr,))

                        def spec_evac(c2, psv, ptoks):
                            A(lambda a: a.activation(out=Kh[:, 0, c2, :].rearrange("p (m k) -> p m k", k=32), in_=psv, func=AF.Identity),
                              ptoks, ('Kh0',))
                        fft_fwd(C, [(zT[:, 0], 'zT0', 0), (zT[:, 1], 'zT1', 1)], B, spec_evac)
                        for c2 in range(2):
                            for q in range(4):
                                b = kb.ps()
                                kb.op('pe', lambda p: p.matmul(PS(b), C['Sw'], Kh[:, 0, c2, q * 512:(q + 1) * 512], start=True, stop=True),
                                      reads=('fc', 'Kh0'), writes=(pst(b),))
                                A(lambda a: a.activation(out=Kh[:, 1, c2, q * 512:(q + 1) * 512], in_=PS(b), func=AF.Identity), (pst(b),), ('Kh1',))
                        for w_ in range(2):
                            kb.dma('sp', khatD[o, cc, w_], Kh[:, w_].rearrange("p a b -> p (a b)"), reads=('Kh%d' % w_,), writes=('khatD',))
                kb.barrier(barscr[:, 0:1])

        def phase_hy(l):
            with ExitStack() as es:
                E = es.enter_context
                C = load_fft_consts(es, False)
                cw = E(SBT("h_cw", [128, 3, 24], F32))
                cb_ = E(SBT("h_cb", [128, 24], F32))
                dd = E(SBT("h_dd", [128, 2, 8], F32))
                wst = E(SBT("h_wst", [128, 2, 8, 128], F32))
                wbf = E(SBT("h_wbf", [128, 4, 8, 128], BF16))
                pp = E(SBT("h_pp", [128, 1, L + 2], F32))
                sc = E(SBT("h_sc", [128, L], F32))
                vx = E(SBT("h_vx", [128, 4, L], BF16))
                z1 = E(SBT("h_z1", [128, L], BF16))
                zT = E(SBT("h_zT", [128, 32, 64], BF16))
                B = E(SBT("h_B", [128, 32, 2, 16, 4], BF16))
                Dt = E(SBT("h_Dt", [128, 64, 2, 16, 2], BF16))
                P1 = E(SBT("h_P1", [128, 2, 2048], BF16))
                P2 = E(SBT("h_P2", [128, 2, 2048], BF16))
                Kt = E(SBT("h_Kt", [128, 1, 2, 4096], BF16))
                conv = E(SBT("h_conv", [128, L], F32))
                yo = E(SBT("h_yo", [128, 1, L], BF16))
                kb.dma('sp', cw[:], convw[l], writes=('cw',))
                kb.dma('sp', cb_[:], convb[l], writes=('cb',))
                kb.dma('sp', dd[:], hyd[l], writes=('dd',))
                kb.op('pool', lambda g: g.memset(pp[:], 0.0), writes=('pp0',))
                for cb in range(8):
                    colbase = [1024 + cb * 128, 2048 + cb * 128, 3072 + cb * 128, 4096 + cb * 128]
                    for si in range(4):
                        s2 = si % 2
                        for k in range(8):
                            kb.dma('sp' if k % 2 == 0 else 'act', wst[:, s2, k, :], w_in[l][k * 128:(k + 1) * 128, colbase[si]:colbase[si] + 128],
                                   writes=('wst%d' % s2,))
                        kb.op('pool', lambda g: g.tensor_copy(out=wbf[:, si], in_=wst[:, s2]), reads=('wst%d' % s2,), writes=('wbf%d' % si,))
                    for si in range(4):
                        s2 = si % 2
                        for t4 in range(4):
                            ts = slice(t4 * 512, (t4 + 1) * 512)
                            b = kb.ps()
                            for k in range(8):
                                kb.op('pe', lambda p: p.matmul(PS(b), wbf[:, si, k, :], G['xn'][:, k, ts], start=(k == 0), stop=(k == 7)),
                                      reads=('wbf%d' % si, 'xn%d' % k), writes=(pst(b),), inc=(k == 7))
                            if si < 3:
                                kb.op('act', lambda a: a.activation(out=pp[:, 0, 1 + t4 * 512:1 + (t4 + 1) * 512], in_=PS(b), func=AF.Identity),
                                      reads=(pst(b),), writes=('pp0',))
                            else:
                                kb.op('act', lambda a: a.activation(out=vx[:, 3, ts], in_=PS(b), func=AF.Silu), reads=(pst(b),), writes=('vx3',))
                        if si < 3:
                            ch = si * 8 + cb
                            kb.op('dve', lambda v: v.tensor_scalar(out=sc[:], in0=pp[:, 0, 1:L + 1], scalar1=cw[:, 1, ch:ch + 1], scalar2=cb_[:, ch:ch + 1],
                                                                   op0=ALU.mult, op1=ALU.add),
                                  reads=('pp0', 'cw', 'cb'), writes=('sc',))
                            kb.op('dve', lambda v: v.scalar_tensor_tensor(out=sc[:], in0=pp[:, 0, 0:L], scalar=cw[:, 0, ch:ch + 1], in1=sc[:],
                                                                          op0=ALU.mult, op1=ALU.add),
                                  reads=('pp0', 'cw', 'sc'), writes=('sc',))
                            kb.op('dve', lambda v: v.scalar_tensor_tensor(out=vx[:, si, :], in0=pp[:, 0, 2:L + 2], scalar=cw[:, 2, ch:ch + 1], in1=sc[:],
                                                                          op0=ALU.mult, op1=ALU.add),
                                  reads=('pp0', 'cw', 'sc'), writes=('vx%d' % si,))
                    zin, ztok = vx[:, 0, :], 'vx0'
                    for o in range(2):
                        for w_ in range(2):
                            kb.dma('sp' if w_ == 0 else 'act', Kt[:, 0, w_, :], khatD[o, cb, w_], reads=('khatD',), writes=('Kt',))
                        kb.op('dve', lambda v: v.transpose(out=zT[:].rearrange("p c n -> p n c"),
                                                           in_=zin.rearrange("p (n1 n2) -> p n2 n1", n2=64)),
                              reads=(ztok,), writes=('zT',))

                        def spec_evac(c2, psv, ptoks):
                            kb.op('dve', lambda v: v.tensor_tensor(out=P1[:, c2, :].rearrange("p (m k) -> p m k", k=32), in0=psv,
                                                                   in1=Kt[:, 0, 0, c2 * 2048:(c2 + 1) * 2048].rearrange("p (m k) -> p m k", k=32), op=ALU.mult),
                                  reads=ptoks + ('Kt',), writes=('P1',))
                            kb.op('dve', lambda v: v.tensor_tensor(out=P2[:, c2, :].rearrange("p (m k) -> p m k", k=32), in0=psv,
                                                                   in1=Kt[:, 0, 1, c2 * 2048:(c2 + 1) * 2048].rearrange("p (m k) -> p m k", k=32), op=ALU.mult),
                                  reads=ptoks + ('Kt',), writes=('P2',))
                        fft_fwd(C, [(zT[:], 'zT', 0)], B, spec_evac)
                        fft_inv(C, P1, P2, Dt, conv[:], 'conv')
                        kb.op('dve', lambda v: v.scalar_tensor_tensor(out=sc[:], in0=zin, scalar=dd[:, o, cb:cb + 1], in1=conv[:], op0=ALU.mult, op1=ALU.add),
                              reads=(ztok, 'dd', 'conv'), writes=('sc',))
                        if o == 0:
                            kb.op('pool', lambda g: g.tensor_tensor(out=z1[:], in0=sc[:], in1=vx[:, 1, :], op=ALU.mult), reads=('sc', 'vx1'), writes=('z1',))
                            zin, ztok = z1[:], 'z1'
                        else:
                            kb.op('pool', lambda g: g.tensor_tensor(out=sc[:], in0=sc[:], in1=vx[:, 2, :], op=ALU.mult), reads=('sc', 'vx2'), writes=('sc',))
                            kb.op('pool', lambda g: g.tensor_tensor(out=yo[:, 0, :], in0=sc[:], in1=vx[:, 3, :], op=ALU.mult), reads=('sc', 'vx3'), writes=('yo',))
                            kb.dma('sp', yhyd[cb], yo[:, 0, :], reads=('yo',), writes=('yhyd',))
                kb.barrier(barscr[:, 0:1])

        def phase_merge(l, hsrc):
            with ExitStack() as es:
                E = es.enter_context
                wm = E(SBT("m_wm", [128, 8, 2048], BF16))
                wh = E(SBT("m_wh", [128, 8, 1024], BF16))
                wo = E(SBT("m_wo", [128, 8, 1024], BF16))
                st = E(SBT("m_st", [128, 2, 2048], F32))
                mm = E(SBT("m_mm", [128, 16, 512], BF16))
                ys = E(SBT("m_ys", [128, 8, 512], BF16))
                yh = E(SBT("m_yh", [128, 8, 512], BF16))
                mg = E(SBT("m_mg", [128, 8, 512], BF16))
                t1 = E(SBT("m_t1", [128, 2, 512], F32))
                ht = E(SBT("m_ht", [128, 8, 512], F32))
                i = 0
                for (wt, src, ncol, nm) in ((wm, w_in[l][:, 5120:7168], 2048, 'wm'), (wh, w_bhy[l], 1024, 'wh'), (wo, w_out[l], 1024, 'wo')):
                    for k in range(8):
                        s = i % 2; i += 1
                        kb.dma('sp' if s == 0 else 'act', st[:, s, 0:ncol], src[k * 128:(k + 1) * 128, :], writes=('st%d' % s,))
                        kb.op('pool', lambda g: g.tensor_copy(out=wt[:, k, :], in_=st[:, s, 0:ncol]), reads=('st%d' % s,), writes=(nm,))
                for tt in range(4):
                    ts = slice(tt * 512, (tt + 1) * 512)
                    for c in range(8):
                        kb.dma('sp', ys[:, c, :], ys5d[c][:, ts], reads=('ys5d',), writes=('ys',))
                        kb.dma('act', yh[:, c, :], yhyd[c][:, ts], reads=('yhyd',), writes=('yh',))
                        kb.dma('sp', ht[:, c, :], hsrc[c][:, ts], reads=('hsrc',), writes=('ht',))
                    for mc in range(16):
                        b = kb.ps()
                        for k in range(8):
                            kb.op('pe', lambda p: p.matmul(PS(b), wm[:, k, mc * 128:(mc + 1) * 128], G['xn'][:, k, ts], start=(k == 0), stop=(k == 7)),
                                  reads=('wm', 'xn%d' % k), writes=(pst(b),), inc=(k == 7))
                        kb.op('act', lambda a: a.activation(out=mm[:, mc, :], in_=PS(b), func=AF.Sigmoid), reads=(pst(b),), writes=('mm',))
                    for dc in range(8):
                        b = kb.ps()
                        for k in range(8):
                            kb.op('pe', lambda p: p.matmul(PS(b), wh[:, k, dc * 128:(dc + 1) * 128], yh[:, k, :], start=(k == 0), stop=(k == 7)),
                                  reads=('wh', 'yh'), writes=(pst(b),), inc=(k == 7))
                        s = dc % 2
                        kb.op('dve', lambda v: v.tensor_tensor(out=t1[:, s, :], in0=PS(b), in1=mm[:, 8 + dc, :], op=ALU.mult),
                              reads=(pst(b), 'mm'), writes=('t1_%d' % s,))
                        kb.op('pool', lambda g: g.tensor_tensor(out=mg[:, dc, :], in0=mm[:, dc, :], in1=ys[:, dc, :], op=ALU.mult),
                              reads=('mm', 'ys'), writes=('mg%d' % dc,))
                        kb.op('dve', lambda v: v.tensor_tensor(out=mg[:, dc, :], in0=mg[:, dc, :], in1=t1[:, s, :], op=ALU.add),
                              reads=('mg%d' % dc, 't1_%d' % s), writes=('mg%d' % dc, 'mg'))
                    for dc in range(8):
                        b = kb.ps()
                        for k in range(8):
                            kb.op('pe', lambda p: p.matmul(PS(b), wo[:, k, dc * 128:(dc + 1) * 128], mg[:, k, :], start=(k == 0), stop=(k == 7)),
                                  reads=('wo', 'mg'), writes=(pst(b),), inc=(k == 7))
                        kb.op('dve', lambda v: v.tensor_tensor(out=ht[:, dc, :], in0=PS(b), in1=ht[:, dc, :], op=ALU.add),
                              reads=(pst(b), 'ht'), writes=('ht',))
                        kb.dma('sp', hbuf[dc][:, ts], ht[:, dc, :], reads=('ht',), writes=('hbuf',))
                kb.barrier(barscr[:, 0:1])

        kb.op('pool', lambda g: g.memset(barscr[:, 1:2], 1e-6), writes=('epsq',))
        kb.barrier(barscr[:, 0:1])
        order = ['prep_s5', 'prep_hy', 'norm', 's5', 'hy', 'merge']
        nph = len(order) if stop_after is None else order.index(stop_after) + 1
        for l in range(layers):
            hsrc = xT if l == 0 else hbuf
            if nph >= 1:
                prep_s5(l)
            if nph >= 2:
                prep_hy(l)
            with ExitStack() as esl:
                G['xn'] = esl.enter_context(SBT("xn%d" % l, [128, 8, L], BF16))
                if nph >= 3:
                    phase_norm(hsrc, l, G['xn'], None)
                    if 'xnD' in dump:
                        xnD = dscr("xnD", [8, 128, L], BF16)
                        for c in range(8):
                            kb.dma('sp', xnD[c], G['xn'][:, c, :], reads=('xn%d' % c,), writes=('xnD',))
                if nph >= 4:
                    phase_s5(l)
                if nph >= 5:
                    phase_hy(l)
                if nph >= 6:
                    phase_merge(l, hsrc)
        if stop_after is None:
            phase_norm(hbuf, DEPTH, None, outT)
        for i in range(NDS):
            if kb.dcnt[i]:
                kb._wait('sp', ('d', i), kb.dcnt[i])
    return nc, kb


_CACHE = {}


def _host_consts():
    if 'c' not in _CACHE:
        fsmall, fT, fH = fft_consts()
        ft, tt = hy_feats()
        _CACHE['c'] = dict(fsmall=fsmall, fT=fT, fH=fH, feats=ft, ttab=tt, kvt=s5_kv(), masks=s5_masks(),
                           ident=np.eye(128, dtype=np.float32))
    return _CACHE['c']


def _layout_shared(inp):
    f32 = np.float32
    g = lambda k: np.asarray(inp[k], dtype=f32)
    m = dict(_host_consts())
    nw = np.concatenate([g("norm_w"), g("final_norm_w")[None]], 0)
    m["normw"] = np.ascontiguousarray(nw.reshape(3, 8, 128).transpose(0, 2, 1))
    m["w_in"] = g("w_in")
    m["w_glu"] = g("s5_w_glu")
    m["b_glu"] = np.ascontiguousarray(g("s5_b_glu").reshape(DEPTH, 4, 128).transpose(0, 2, 1))
    m["s5d"] = np.ascontiguousarray(g("s5_d").reshape(DEPTH, 4, 128).transpose(0, 2, 1))
    m["w_bs5"] = g("w_branch_s5")
    m["w_bhy"] = g("w_branch_hy")
    m["w_out"] = g("w_out")
    cwv = g("hy_conv_w").reshape(DEPTH, 3, 24, 128)
    m["convw"] = np.ascontiguousarray(cwv.transpose(0, 3, 1, 2))
    m["convb"] = np.ascontiguousarray(g("hy_conv_b").reshape(DEPTH, 24, 128).transpose(0, 2, 1))
    m["hyd"] = np.ascontiguousarray(g("hy_d").reshape(DEPTH, 2, 8, 128).transpose(0, 3, 1, 2))
    def pg(a):
        v = a.reshape(DEPTH, 2, 16, 2, 64)
        return v.transpose(0, 3, 4, 1, 2).reshape(DEPTH, 128, 32)
    ls = np.broadcast_to(g("s5_log_step")[..., None], (DEPTH, 2, 32, 64))
    m["s5lam"] = np.ascontiguousarray(np.stack([pg(g("s5_lam_re")), pg(g("s5_lam_im")), pg(ls)], 2))
    def pgB(a):
        v = a.reshape(DEPTH, 2, 16, 2, 64, 16)
        return v.transpose(0, 3, 4, 1, 2, 5).reshape(DEPTH, 128, 32, 16)
    m["s5B"] = np.ascontiguousarray(np.stack([pgB(g("s5_b_re")), pgB(g("s5_b_im"))], 2))
    ct = lambda a: a.transpose(0, 1, 2, 4, 3)
    m["s5C"] = np.ascontiguousarray(np.stack([pgB(ct(g("s5_c_re"))), pgB(ct(g("s5_c_im")))], 2))
    m["hw1"] = g("hy_w1"); m["hw2"] = g("hy_w2"); m["hw3"] = g("hy_w3")
    m["hvec"] = np.ascontiguousarray(np.stack([g("hy_b1"), g("hy_b2"), g("hy_freq")], -1))
    m["hdec"] = np.ascontiguousarray(g("hy_decay").reshape(DEPTH, 32, 128).transpose(0, 2, 1))
    return m


def kernel(**inputs):
    x = np.asarray(inputs["x"], dtype=np.float32)
    shared = _layout_shared(inputs)
    if 'nc' not in _CACHE:
        _CACHE['nc'] = build()[0]
    nc = _CACHE['nc']
    in_maps = []
    for b in range(NCORES):
        m = dict(shared)
        m["xT"] = np.ascontiguousarray(x[b].T.reshape(8, 128, L))
        in_maps.append(m)
    res = run_bass_kernel_spmd(nc, in_maps, core_ids=list(range(NCORES)))
    out = np.empty((NCORES, L, D), dtype=np.float32)
    for b in range(NCORES):
        out[b] = np.asarray(res.results[b]["outT"]).reshape(D, L).T
    return out
```

```python
import math
from contextlib import ExitStack
import numpy as np
import concourse.bass as bass
import concourse.mybir as mybir
from concourse.bass_utils import run_bass_kernel_spmd

F32 = mybir.dt.float32
BF16 = mybir.dt.bfloat16
U32 = mybir.dt.uint32
ALU = mybir.AluOpType
AF = mybir.ActivationFunctionType

D = 1024; L = 2048; DEPTH = 2; NCORES = 8
INC = 7168
NDS = 48
MAGIC = 12582912.0
TWO_PI = 2.0 * math.pi


class KB:
    def __init__(self, nc, es):
        self.nc = nc
        self.engs = {'pe': nc.tensor, 'act': nc.scalar, 'dve': nc.vector, 'pool': nc.gpsimd, 'sp': nc.sync}
        self.csem = {e: es.enter_context(nc.semaphore('c_' + e)) for e in ('pe', 'act', 'dve', 'pool')}
        self.cnt = {e: 0 for e in self.csem}
        self.dsem = [es.enter_context(nc.semaphore('d%d' % i)) for i in range(NDS)]
        self.dcnt = [0] * NDS
        self.dnext = 0
        self.waited = {e: {} for e in self.engs}
        self.lastw = {}
        self.readers = {}
        self.psn = 0
        self.nins = 0

    def _sem(self, sid):
        return self.csem[sid[1]] if sid[0] == 'c' else self.dsem[sid[1]]

    def _wait(self, e, sid, val):
        if e == 'pe' and sid == ('c', 'pe'):
            return
        w = self.waited[e]
        if w.get(sid, 0) >= val:
            return
        self.engs[e].wait_ge(self._sem(sid), val)
        w[sid] = val

    def _deps(self, e, reads, writes):
        for t in reads:
            if t in self.lastw:
                self._wait(e, *self.lastw[t])
        for t in writes:
            if t in self.lastw:
                self._wait(e, *self.lastw[t])
            for sid, val in self.readers.get(t, {}).items():
                self._wait(e, sid, val)

    def _record(self, sid, val, reads, writes):
        for t in writes:
            self.lastw[t] = (sid, val)
            self.readers[t] = {}
        for t in reads:
            r = self.readers.setdefault(t, {})
            if r.get(sid, 0) < val:
                r[sid] = val

    def op(self, e, fn, reads=(), writes=(), inc=True):
        self._deps(e, reads, writes)
        ins = fn(self.engs[e])
        self.nins += 1
        if e == 'pe' and not inc:
            val = self.cnt['pe'] + 1
        else:
            self.cnt[e] += 1
            val = self.cnt[e]
            ins.then_inc(self.csem[e], 1)
        self._record(('c', e), val, reads, writes)

    def dma(self, q, out, in_, reads=(), writes=()):
        i = self.dnext
        self.dnext = (self.dnext + 1) % NDS
        sid = ('d', i)
        if self.dcnt[i] > 0:
            self._wait(q, sid, self.dcnt[i])
        self._deps(q, reads, writes)
        self.engs[q].dma_start(out=out, in_=in_).then_inc(self.dsem[i], 16)
        self.nins += 1
        self.dcnt[i] += 16
        self._record(sid, self.dcnt[i], reads, writes)

    def barrier(self, scratch):
        for i in range(NDS):
            if self.dcnt[i]:
                self._wait('pool', ('d', i), self.dcnt[i])
        for e in ('pe', 'act', 'dve'):
            if self.cnt[e]:
                self._wait('pool', ('c', e), self.cnt[e])
        self.op('pool', lambda g: g.memset(scratch, 0.0), writes=('__bar',))
        val = self.cnt['pool']
        for e in ('pe', 'act', 'dve', 'sp'):
            self._wait(e, ('c', 'pool'), val)
        self.lastw.clear()
        self.readers.clear()

    def ps(self):
        b = self.psn % 8
        self.psn += 1
        return b

    def ps_hi(self):
        b = 4 + self.psn % 4
        self.psn += 1
        return b


def fft_consts():
    N = 4096
    n1 = np.arange(32); k1 = np.arange(32); n2 = np.arange(64); k2 = np.arange(64)
    ang = -2 * np.pi * np.outer(n1, k1 + 0.5) / 64.0
    g = np.stack([np.cos(ang), np.sin(ang)], -1)
    angh = -2 * np.pi * np.outer(n1 + 32, k1 + 0.5) / 64.0
    gh = -np.stack([np.cos(angh), np.sin(angh)], -1)
    G1 = np.zeros((4, 32, 4, 32, 2)); G1hi = np.zeros((4, 32, 4, 32, 2))
    for cq in range(4):
        G1[cq, :, cq] = g; G1hi[cq, :, cq] = gh
    a = -2 * np.pi * (n2[:, None, None] * (k1[None, :, None] + 0.5) / 4096.0 + n2[:, None, None] * k2[None, None, :] / 64.0)
    twr, twi = np.cos(a), np.sin(a)
    T = np.zeros((64, 32, 2, 64, 2))
    T[:, :, 0, :, 0] = twr; T[:, :, 0, :, 1] = twi
    T[:, :, 1, :, 0] = -twi; T[:, :, 1, :, 1] = twr
    T = np.concatenate([T, T], 0).reshape(128, 32 * 2 * 128)
    e = 2 * np.pi * np.outer(k2, n2) / 64.0
    er, ei = np.cos(e), np.sin(e)
    Ga = np.zeros((64, 2, 64, 2)); Gb = np.zeros((64, 2, 64, 2))
    for rp, s in ((0, 1.0), (1, -1.0)):
        Ga[:, rp, :, 0] = s * er; Ga[:, rp, :, 1] = s * ei
        Gb[:, rp, :, 0] = -ei; Gb[:, rp, :, 1] = er
    phi = 2 * np.pi * (k1[:, None, None] + 0.5) * (64 * n1[None, None, :] + n2[None, :, None]) / 4096.0
    H = np.zeros((4, 32, 64, 2, 4, 32))
    for cq in range(4):
        H[cq, :, :, 0, cq, :] = (2.0 / N) * np.cos(phi)
        H[cq, :, :, 1, cq, :] = -(2.0 / N) * np.sin(phi)
    Sw = np.zeros((64, 2, 64, 2))
    for k in range(64):
        Sw[k, 0, k, 1] = 1; Sw[k, 1, k, 0] = 1
    small = np.concatenate([G1.reshape(128, 256), G1hi.reshape(128, 256), Ga.reshape(128, 128),
                            Gb.reshape(128, 128), Sw.reshape(128, 128)], 1)
    return (small.astype(np.float32), T.astype(np.float32), H.reshape(128, 64 * 2 * 128).astype(np.float32))


def hy_feats():
    f32 = np.float32
    t = np.linspace(0.0, 1.0, L, dtype=f32)
    bands = np.linspace(1e-4, 15, 16, dtype=f32)
    def feats(pos, tt):
        angv = bands[None, :] * pos[:, None].astype(f32) * f32(2.0 * math.pi / L)
        return np.concatenate([tt[:, None], np.cos(angv), -np.sin(angv)], -1).astype(f32)
    posF = np.arange(L)
    posR = (L - np.arange(L)) % L
    fF = feats(posF, t); fR = feats(posR, t[posR])
    ft = np.stack([fF.T, fR.T], 0).astype(f32)
    tt = np.stack([np.broadcast_to(t, (128, L)), np.broadcast_to(t[posR], (128, L))], 0).astype(f32)
    return ft, tt


def s5_kv():
    j = np.arange(8)
    dbl = 8.0 * 2.0 ** np.arange(8)
    f = np.concatenate([7 - j, j + 1, j - 7, dbl])
    b = np.concatenate([j, 8 - j, -j, dbl])
    kv = np.stack([f, b], 0).astype(np.float32)
    return np.broadcast_to(kv[None], (128, 2, 32)).copy()


def s5_masks():
    jj = np.repeat(np.arange(8), 16)
    mf = (jj[None, :] >= jj[:, None]).astype(np.float32)
    mb = (jj[:, None] >= jj[None, :]).astype(np.float32)
    return np.concatenate([mf, mb], 1)


def build(dump=(), layers=DEPTH, stop_after=None):
    nc = bass.Bass("TRN2", target_bir_lowering=False)
    dt_in = {}

    def din(name, shape, dt=F32):
        dt_in[name] = nc.dram_tensor(name, list(shape), dt, kind="ExternalInput").ap()
        return dt_in[name]

    def dscr(name, shape, dt=F32):
        kind = "ExternalOutput" if name in dump else "Internal"
        return nc.dram_tensor(name, list(shape), dt, kind=kind).ap()

    xT = din("xT", [8, 128, L])
    normw = din("normw", [DEPTH + 1, 128, 8])
    w_in = din("w_in", [DEPTH, D, INC])
    w_glu = din("w_glu", [DEPTH, 512, 512])
    b_glu = din("b_glu", [DEPTH, 128, 4])
    s5d = din("s5d", [DEPTH, 128, 4])
    w_bs5 = din("w_bs5", [DEPTH, 512, D])
    w_bhy = din("w_bhy", [DEPTH, D, D])
    w_out = din("w_out", [DEPTH, D, D])
    convw = din("convw", [DEPTH, 128, 3, 24])
    convb = din("convb", [DEPTH, 128, 24])
    hyd = din("hyd", [DEPTH, 128, 2, 8])
    s5lam = din("s5lam", [DEPTH, 128, 3, 32])
    s5B = din("s5B", [DEPTH, 128, 2, 32, 16])
    s5C = din("s5C", [DEPTH, 128, 2, 32, 16])
    kvt = din("kvt", [128, 2, 32])
    masks = din("masks", [128, 256])
    ident = din("ident", [128, 128])
    hw1 = din("hw1", [DEPTH, 33, 64])
    hw2 = din("hw2", [DEPTH, 64, 64])
    hw3 = din("hw3", [DEPTH, 64, 4096])
    hvec = din("hvec", [DEPTH, 64, 3])
    hdec = din("hdec", [DEPTH, 128, 32])
    feats = din("feats", [2, 33, L])
    ttab = din("ttab", [2, 128, L])
    fsmall = din("fsmall", [128, 896], BF16)
    fT = din("fT", [128, 8192], BF16)
    fH = din("fH", [128, 16384], BF16)

    outT = nc.dram_tensor("outT", [8, 128, L], F32, kind="ExternalOutput").ap()

    hbuf = dscr("hbuf", [8, 128, L])
    ud = dscr("ud", [4, 128, L], BF16)
    yd = dscr("yd", [4, 128, L], BF16)
    ys5d = dscr("ys5d", [8, 128, L], BF16)
    yhyd = dscr("yhyd", [8, 128, L], BF16)
    binD = dscr("binD", [128, 32 * 2 * 2 * 64], BF16)
    coutD = dscr("coutD", [128, 2 * 16 * 2 * 128], BF16)
    mgD = dscr("mgD", [128, 32 * 128], BF16)
    coefD = dscr("coefD", [128, 3 * 2 * 16 * 8])
    khatD = dscr("khatD", [2, 8, 2, 128, 4096], BF16)

    _uid = [0]

    def SBT(name, shape, dt):
        _uid[0] += 1
        return nc.sbuf_tensor("%s_%d" % (name, _uid[0]), shape, dt)

    es0 = ExitStack()
    with es0:
        kb = KB(nc, es0)
        E0 = es0.enter_context
        psum = E0(nc.psum_tensor("psum", [128, 8, 512], F32))
        barscr = E0(SBT("barscr", [128, 8], F32))
        G = {}

        def PS(b, n=512, p0=0, p1=128, off=0):
            return psum[p0:p1, b, off:off + n]

        def PS4(g):
            return psum[:, 4 * g:4 * g + 4, :].rearrange("p b n -> p (b n)")

        def pst(b):
            return 'ps%d' % b

        def pst4(g):
            return tuple('ps%d' % (4 * g + i) for i in range(4))

        def range_reduce(eng, x_ap, tmp_ap, rd, wr_tmp):
            kb.op(eng, lambda v: v.tensor_scalar(out=tmp_ap, in0=x_ap, scalar1=float(1.0 / TWO_PI), scalar2=MAGIC,
                                                 op0=ALU.mult, op1=ALU.add), reads=rd, writes=wr_tmp)
            kb.op(eng, lambda v: v.tensor_scalar(out=tmp_ap, in0=tmp_ap, scalar1=-MAGIC, scalar2=None, op0=ALU.add),
                  reads=wr_tmp, writes=wr_tmp)
            kb.op(eng, lambda v: v.scalar_tensor_tensor(out=x_ap, in0=tmp_ap, scalar=float(-TWO_PI), in1=x_ap,
                                                        op0=ALU.mult, op1=ALU.add), reads=wr_tmp + rd, writes=rd)
            kb.op(eng, lambda v: v.tensor_scalar(out=x_ap, in0=x_ap, scalar1=3.14159, scalar2=-3.14159,
                                                 op0=ALU.min, op1=ALU.max), reads=rd, writes=rd)

        def phase_norm(src, wrow, dst_xn, dst_out):
            with ExitStack() as es:
                E = es.enter_context
                h = E(SBT("n_h", [128, 8, L], F32))
                sq = E(SBT("n_sq", [128, 2, 8, 512], BF16))
                ones = E(SBT("n_ones", [128, 128], BF16))
                nw = E(SBT("n_w", [128, 8], F32))
                rt = E(SBT("n_rt", [128, 2, 512], F32))
                rstd = E(SBT("n_rstd", [128, L], F32))
                epsb = E(SBT("n_eps", [128, 1], F32))
                ob = E(SBT("n_ob", [128, 2, L], F32)) if dst_out is not None else None
                kb.op('pool', lambda g: g.memset(ones[:], 1.0), writes=('ones',))
                kb.op('pool', lambda g: g.memset(epsb[:], 1e-6), writes=('epsb',))
                kb.dma('sp', nw[:], normw[wrow], writes=('nw',))
                for c in range(8):
                    kb.dma('sp' if c % 2 == 0 else 'sp', h[:, c, :], src[c], writes=('h%d' % c,))
                for tt in range(4):
                    ts = slice(tt * 512, (tt + 1) * 512)
                    s = tt % 2
                    kb.op('act', lambda a: a.activation(out=sq[:, s], in_=h[:, :, ts], func=AF.Square),
                          reads=tuple('h%d' % c for c in range(8)), writes=('sq%d' % s,))
                    b = kb.ps()
                    for c in range(8):
                        kb.op('pe', lambda p: p.matmul(PS(b), ones[:], sq[:, s, c, :], start=(c == 0), stop=(c == 7)),
                              reads=('ones', 'sq%d' % s), writes=(pst(b),), inc=(c == 7))
                    kb.op('act', lambda a: a.activation(out=rt[:, s, :], in_=PS(b), func=AF.Sqrt, bias=epsb[:, 0:1],
                                                        scale=float(1.0 / D)),
                          reads=(pst(b), 'epsb'), writes=('rt%d' % s,))
                    kb.op('dve', lambda v: v.reciprocal(out=rstd[:, ts], in_=rt[:, s, :]), reads=('rt%d' % s,),
                          writes=('rstd%d' % tt,))
                for c in range(8):
                    if dst_out is None:
                        kb.op('dve', lambda v: v.scalar_tensor_tensor(out=dst_xn[:, c, :], in0=h[:, c, :], scalar=nw[:, c:c + 1],
                                                                      in1=rstd[:], op0=ALU.mult, op1=ALU.mult),
                              reads=('h%d' % c, 'nw') + tuple('rstd%d' % t for t in range(4)), writes=('xn%d' % c,))
                    else:
                        s = c % 2
                        kb.op('dve', lambda v: v.scalar_tensor_tensor(out=ob[:, s, :], in0=h[:, c, :], scalar=nw[:, c:c + 1],
                                                                      in1=rstd[:], op0=ALU.mult, op1=ALU.mult),
                              reads=('h%d' % c, 'nw') + tuple('rstd%d' % t for t in range(4)), writes=('ob%d' % s,))
                        kb.dma('sp', dst_out[c], ob[:, s, :], reads=('ob%d' % s,), writes=('out%d' % c,))
                kb.barrier(barscr[:, 0:1])

        def load_w(es, name, src_rows_ap, kc, ncols, q='sp'):
            E = es.enter_context
            st = E(SBT(name + "_st", [128, kc, ncols], F32))
            wb = E(SBT(name + "_bf", [128, kc, ncols], BF16))
            for k in range(kc):
                kb.dma(q if k % 2 == 0 else 'sp', st[:, k, :], src_rows_ap[k * 128:(k + 1) * 128, :], writes=(name + '_st%d' % k,))
                kb.op('act', lambda a: a.activation(out=wb[:, k, :], in_=st[:, k, :], func=AF.Identity), reads=(name + '_st%d' % k,),
                      writes=(name + '_bf',))
            return wb

        def prep_s5(l):
            with ExitStack() as es:
                E = es.enter_context
                lam = E(SBT("p_lam", [128, 3, 32], F32))
                Bt = E(SBT("p_B", [128, 2, 32, 16], F32))
                Ct = E(SBT("p_C", [128, 2, 32, 16], F32))
                kvs = E(SBT("p_kv", [128, 2, 32], F32))
                msk = E(SBT("p_msk", [128, 256], F32))
                idt = E(SBT("p_id", [128, 128], F32))
                a_re = E(SBT("p_are", [128, 32], F32))
                a_im = E(SBT("p_aim", [128, 32], F32))
                dtt = E(SBT("p_dt", [128, 32], F32))
                mag = E(SBT("p_mag", [128, 32, 32], F32))
                sn = E(SBT("p_sn", [128, 32, 32], F32))
                cs = E(SBT("p_cs", [128, 32, 32], F32))
                tmp = E(SBT("p_tmp", [128, 32, 32], F32))
                Er = E(SBT("p_Er", [128, 32, 32], F32))
                Ei = E(SBT("p_Ei", [128, 32, 32], F32))
                k4 = E(SBT("p_k4", [128, 8, 32], F32))
                Bb = E(SBT("p_Bb", [128, 2, 32, 16], F32))
                t16 = E(SBT("p_t16", [128, 2, 32, 16], F32))
                RB = E(SBT("p_RB", [128, 2, 32, 128], F32))
                CO = E(SBT("p_CO", [128, 2, 32, 128], F32))
                tb = E(SBT("p_tb", [128, 32, 128], F32))
                binS = E(SBT("p_bin", [128, 32, 2, 2, 64], BF16))
                coutS = E(SBT("p_cout", [128, 32, 2, 128], BF16))
                mgS = E(SBT("p_mg", [128, 32, 128], BF16))
                coefS = E(SBT("p_coef", [128, 3, 32, 8], F32))
                mt = E(SBT("p_mt", [128, 2, 256], F32))
                kb.dma('sp', lam[:], s5lam[l], writes=('lam',))
                kb.dma('sp', Bt[:], s5B[l], writes=('Bt',))
                kb.dma('sp', Ct[:], s5C[l], writes=('Ct',))
                kb.dma('sp', kvs[:], kvt, writes=('kvs',))
                kb.dma('sp', msk[:], masks, writes=('msk',))
                kb.dma('sp', idt[:], ident, writes=('idt',))
                V = lambda fn, r, w: kb.op('dve', fn, reads=r, writes=w)
                A = lambda fn, r, w: kb.op('act', fn, reads=r, writes=w)
                A(lambda a: a.activation(out=dtt[:], in_=lam[:, 2, :], func=AF.Exp), ('lam',), ('dtt',))
                V(lambda v: v.tensor_tensor(out=a_re[:], in0=lam[:, 0, :], in1=dtt[:], op=ALU.mult), ('lam', 'dtt'), ('a_re',))
                V(lambda v: v.tensor_tensor(out=a_im[:], in0=lam[:, 1, :], in1=dtt[:], op=ALU.mult), ('lam', 'dtt'), ('a_im',))
                kvb = kvs[:].rearrange("p d (o k) -> p d o k", o=1).to_broadcast([128, 2, 16, 32])
                are_b = a_re[:].rearrange("p (d g o) -> p d g o", d=2, o=1).to_broadcast([128, 2, 16, 32])
                aim_b = a_im[:].rearrange("p (d g o) -> p d g o", d=2, o=1).to_broadcast([128, 2, 16, 32])
                v4 = lambda t: t[:].rearrange("p (d g) k -> p d g k", d=2)
                V(lambda v: v.tensor_tensor(out=v4(tmp), in0=are_b, in1=kvb, op=ALU.mult), ('a_re', 'kvs'), ('tmp',))
                A(lambda a: a.activation(out=mag[:], in_=tmp[:], func=AF.Exp), ('tmp',), ('mag',))
                V(lambda v: v.tensor_tensor(out=v4(sn), in0=aim_b, in1=kvb, op=ALU.mult), ('a_im', 'kvs'), ('sn',))
                V(lambda v: v.tensor_scalar(out=cs[:], in0=sn[:], scalar1=float(math.pi / 2), scalar2=None, op0=ALU.add), ('sn',), ('cs',))
                range_reduce('dve', sn[:], tmp[:], ('sn',), ('tmp',))
                A(lambda a: a.activation(out=sn[:], in_=sn[:], func=AF.Sin), ('sn',), ('sn',))
                range_reduce('dve', cs[:], tmp[:], ('cs',), ('tmp',))
                A(lambda a: a.activation(out=cs[:], in_=cs[:], func=AF.Sin), ('cs',), ('cs',))
                V(lambda v: v.tensor_tensor(out=Er[:], in0=mag[:], in1=cs[:], op=ALU.mult), ('mag', 'cs'), ('Er',))
                V(lambda v: v.tensor_tensor(out=Ei[:], in0=mag[:], in1=sn[:], op=ALU.mult), ('mag', 'sn'), ('Ei',))
                lr = k4[:, 0, :]; li = k4[:, 1, :]; nr = k4[:, 2, :]; den = k4[:, 3, :]; kr = k4[:, 4, :]; ki = k4[:, 5, :]; t0 = k4[:, 6, :]
                for d in range(2):
                    idx = 8 if d == 0 else 15
                    V(lambda v: v.tensor_copy(out=lr[:, 16 * d:16 * d + 16], in_=Er[:, 16 * d:16 * d + 16, idx]), ('Er',), ('k4',))
                    V(lambda v: v.tensor_copy(out=li[:, 16 * d:16 * d + 16], in_=Ei[:, 16 * d:16 * d + 16, idx]), ('Ei',), ('k4',))
                V(lambda v: v.tensor_scalar(out=nr, in0=lr, scalar1=-1.0, scalar2=None, op0=ALU.add), ('k4',), ('k4',))
                V(lambda v: v.tensor_tensor(out=den, in0=lam[:, 0, :], in1=lam[:, 0, :], op=ALU.mult), ('lam', 'k4'), ('k4',))
                V(lambda v: v.tensor_tensor(out=t0, in0=lam[:, 1, :], in1=lam[:, 1, :], op=ALU.mult), ('lam', 'k4'), ('k4',))
                V(lambda v: v.tensor_tensor(out=den, in0=den, in1=t0, op=ALU.add), ('k4',), ('k4',))
                V(lambda v: v.reciprocal(out=den, in_=den), ('k4',), ('k4',))
                V(lambda v: v.tensor_tensor(out=kr, in0=nr, in1=lam[:, 0, :], op=ALU.mult), ('k4', 'lam'), ('k4',))
                V(lambda v: v.tensor_tensor(out=t0, in0=li, in1=lam[:, 1, :], op=ALU.mult), ('k4', 'lam'), ('k4',))
                V(lambda v: v.tensor_tensor(out=kr, in0=kr, in1=t0, op=ALU.add), ('k4',), ('k4',))
                V(lambda v: v.tensor_tensor(out=kr, in0=kr, in1=den, op=ALU.mult), ('k4',), ('k4',))
                V(lambda v: v.tensor_tensor(out=ki, in0=li, in1=lam[:, 0, :], op=ALU.mult), ('k4', 'lam'), ('k4',))
                V(lambda v: v.tensor_tensor(out=t0, in0=nr, in1=lam[:, 1, :], op=ALU.mult), ('k4', 'lam'), ('k4',))
                V(lambda v: v.tensor_tensor(out=ki, in0=ki, in1=t0, op=ALU.subtract), ('k4',), ('k4',))
                V(lambda v: v.tensor_tensor(out=ki, in0=ki, in1=den, op=ALU.mult), ('k4',), ('k4',))
                krb = kr.rearrange("p (g o) -> p g o", o=1).to_broadcast([128, 32, 16])
                kib = ki.rearrange("p (g o) -> p g o", o=1).to_broadcast([128, 32, 16])
                V(lambda v: v.tensor_tensor(out=Bb[:, 0], in0=Bt[:, 0], in1=krb, op=ALU.mult), ('Bt', 'k4'), ('Bb',))
                V(lambda v: v.tensor_tensor(out=t16[:, 0], in0=Bt[:, 1], in1=kib, op=ALU.mult), ('Bt', 'k4'), ('t16',))
                V(lambda v: v.tensor_tensor(out=Bb[:, 0], in0=Bb[:, 0], in1=t16[:, 0], op=ALU.subtract), ('Bb', 't16'), ('Bb',))
                V(lambda v: v.tensor_tensor(out=Bb[:, 1], in0=Bt[:, 1], in1=krb, op=ALU.mult), ('Bt', 'k4', 'Bb'), ('Bb',))
                V(lambda v: v.tensor_tensor(out=t16[:, 1], in0=Bt[:, 0], in1=kib, op=ALU.mult), ('Bt', 'k4', 't16'), ('t16',))
                V(lambda v: v.tensor_tensor(out=Bb[:, 1], in0=Bb[:, 1], in1=t16[:, 1], op=ALU.add), ('Bb', 't16'), ('Bb',))

                def cprod(dst, k0, X, sign_im, tag):
                    Erb = Er[:, :, k0:k0 + 8].rearrange("p g (k o) -> p g k o", o=1).to_broadcast([128, 32, 8, 16])
                    Eib = Ei[:, :, k0:k0 + 8].rearrange("p g (k o) -> p g k o", o=1).to_broadcast([128, 32, 8, 16])
                    Xr = X[:, 0].rearrange("p g (o h) -> p g o h", o=1).to_broadcast([128, 32, 8, 16])
                    Xi = X[:, 1].rearrange("p g (o h) -> p g o h", o=1).to_broadcast([128, 32, 8, 16])
                    d0 = dst[:, 0].rearrange("p g (k h) -> p g k h", k=8)
                    d1 = dst[:, 1].rearrange("p g (k h) -> p g k h", k=8)
                    tv = tb[:].rearrange("p g (k h) -> p g k h", k=8)
                    rd = ('Er', 'Ei', tag)
                    V(lambda v: v.tensor_tensor(out=d0, in0=Erb, in1=Xr, op=ALU.mult), rd, (tag + 'o',))
                    V(lambda v: v.tensor_tensor(out=tv, in0=Eib, in1=Xi, op=ALU.mult), rd, ('tb',))
                    V(lambda v: v.tensor_tensor(out=d0, in0=d0, in1=tv, op=ALU.subtract), (tag + 'o', 'tb'), (tag + 'o',))
                    V(lambda v: v.tensor_tensor(out=d1, in0=Erb, in1=Xi, op=ALU.mult), rd + (tag + 'o',), (tag + 'o',))
                    V(lambda v: v.tensor_tensor(out=tv, in0=Eib, in1=Xr, op=ALU.mult), rd + ('tb',), ('tb',))
                    V(lambda v: v.tensor_tensor(out=d1, in0=d1, in1=tv, op=ALU.add), (tag + 'o', 'tb'), (tag + 'o',))
                    if sign_im < 0:
                        V(lambda v: v.tensor_scalar(out=dst[:, 1], in0=dst[:, 1], scalar1=-1.0, scalar2=None, op0=ALU.mult),
                          (tag + 'o',), (tag + 'o',))
                cprod(RB, 0, Bb, +1, 'Bb')
                cprod(CO, 8, Ct, -1, 'Ct')
                for ri in range(2):
                    A(lambda a: a.activation(out=coutS[:, :, ri, :], in_=CO[:, ri], func=AF.Identity), ('Cto',), ('coutS',))
                kb.dma('sp', coutD, coutS[:].rearrange("p a b c -> p (a b c)"), reads=('coutS',), writes=('coutD',))
                cprod(CO, 16, Ct, -1, 'Ct')
                RC = CO
                A(lambda a: a.activation(out=coefS[:, 0], in_=Er[:, :, 24:32], func=AF.Identity), ('Er',), ('coefS',))
                A(lambda a: a.activation(out=coefS[:, 1], in_=Ei[:, :, 24:32], func=AF.Identity), ('Ei',), ('coefS',))
                A(lambda a: a.activation(out=coefS[:, 2], in_=Ei[:, :, 24:32], func=AF.Identity, scale=-1.0), ('Ei',), ('coefS',))
                kb.dma('sp', coefD, coefS[:].rearrange("p a b c -> p (a b c)"), reads=('coefS',), writes=('coefD',))
                for dg in range(32):
                    d, gp = dg // 16, dg % 16
                    b = kb.ps()
                    for ri in range(2):
                        kb.op('pe', lambda p: p.transpose(PS(b, 128, off=128 * ri), RB[:, ri, dg, :], idt[:]),
                              reads=('Bbo', 'idt'), writes=(pst(b),), inc=(ri == 1))
                    for ri in range(2):
                        A(lambda a: a.activation(out=binS[:, 2 * gp:2 * gp + 2, d, ri, :],
                                                 in_=PS(b, 128, off=128 * ri).rearrange("p (g q) -> p g q", g=2), func=AF.Identity),
                          (pst(b),), ('binS',))
                kb.dma('sp', binD, binS[:].rearrange("p a b c e -> p (a b c e)"), reads=('binS',), writes=('binD',))
                for g in range(32):
                    gp, gpar = g // 2, g % 2
                    b = kb.ps()
                    p0, p1 = 64 * gpar, 64 * gpar + 64
                    for d in range(2):
                        dg = 16 * d + gp
                        kb.op('pe', lambda p: p.matmul(PS(b, 128, off=128 * d), RB[p0:p1, 0, dg, :], RC[p0:p1, 0, dg, :],
                                                       start=True, stop=False),
                              reads=('Bbo', 'Cto'), writes=(pst(b),), inc=False)
                        kb.op('pe', lambda p: p.matmul(PS(b, 128, off=128 * d), RB[p0:p1, 1, dg, :], RC[p0:p1, 1, dg, :],
                                                       start=False, stop=True),
                              reads=('Bbo', 'Cto'), writes=(pst(b),), inc=(d == 1))
                    s = g % 2
                    V(lambda v: v.tensor_tensor(out=mt[:, s, :], in0=PS(b, 256), in1=msk[:], op=ALU.mult), (pst(b), 'msk'), ('mt%d' % s,))
                    V(lambda v: v.tensor_tensor(out=mgS[:, g, :], in0=mt[:, s, 0:128], in1=mt[:, s, 128:256], op=ALU.add),
                      ('mt%d' % s,), ('mgS',))
                kb.dma('sp', mgD, mgS[:].rearrange("p a b -> p (a b)"), reads=('mgS',), writes=('mgD',))
                kb.barrier(barscr[:, 0:1])

        def phase_s5(l):
            with ExitStack() as es_outer:
                EO = es_outer.enter_context
                gs = EO(SBT("s_gs", [128, 4, L], BF16))
                with ExitStack() as es:
                    E = es.enter_context
                    wb = load_w(es, "s_w", w_in[l][:, 0:1024], 8, 1024)
                    udt = E(SBT("s_ud", [128, 4, 8, 256], BF16))
                    for cc in range(8):
                        for tt in range(4):
                            b = kb.ps()
                            for k in range(8):
                                kb.op('pe', lambda p: p.matmul(PS(b), wb[:, k, cc * 128:(cc + 1) * 128], G['xn'][:, k, tt * 512:(tt + 1) * 512],
                                                               start=(k == 0), stop=(k == 7)),
                                      reads=('s_w_bf', 'xn%d' % k), writes=(pst(b),), inc=(k == 7))
                            if cc < 4:
                                kb.op('act', lambda a: a.activation(out=udt[:, cc, :, tt * 64:(tt + 1) * 64],
                                                                    in_=PS(b).rearrange("p (c j) -> p j c", j=8), func=AF.Identity),
                                      reads=(pst(b),), writes=('udt%d' % cc,))
                            else:
                                kb.op('act', lambda a: a.activation(out=gs[:, cc - 4, tt * 512:(tt + 1) * 512], in_=PS(b), func=AF.Silu),
                                      reads=(pst(b),), writes=('gs',))
                        if cc < 4:
                            kb.dma('sp', ud[cc], udt[:, cc].rearrange("p j c -> p (j c)"), reads=('udt%d' % cc,), writes=('ud',))
                    kb.barrier(barscr[:, 0:1])
                with ExitStack() as es:
                    E = es.enter_context
                    U8 = E(SBT("s_U8", [128, 32, 256], BF16))
                    Mg = E(SBT("s_Mg", [128, 32, 128], BF16))
                    Bin = E(SBT("s_Bin", [128, 32, 2, 2, 64], BF16))
                    Cout = E(SBT("s_Cout", [128, 32, 2, 128], BF16))
                    coef = E(SBT("s_coef", [128, 3, 32, 8], F32))
                    Xs = E(SBT("s_Xs", [128, 32, 2, 256], BF16))
                    Y8 = E(SBT("s_Y8", [128, 32, 256], BF16))
                    NSL = 2
                    XA = E(SBT("s_XA", [128, NSL, 2, 768], F32))
                    XB = E(SBT("s_XB", [128, NSL, 2, 768], F32))
                    T1 = E(SBT("s_T1", [128, NSL, 2, 256], F32))
                    udv = ud.rearrange("cc (g h) (j c) -> h j (cc g) c", h=16, j=8)
                    for j in range(8):
                        kb.dma('sp' if j % 2 == 0 else 'sp', U8[16 * j:16 * j + 16, :, :], udv[:, j], reads=('ud',), writes=('U8',))
                    kb.dma('sp', Mg[:].rearrange("p a b -> p (a b)"), mgD, reads=('mgD',), writes=('Mg',))
                    kb.dma('sp', Bin[:].rearrange("p a b c e -> p (a b c e)"), binD, reads=('binD',), writes=('Bin',))
                    kb.dma('sp', Cout[:].rearrange("p a b c -> p (a b c)"), coutD, reads=('coutD',), writes=('Cout',))
                    kb.dma('sp', coef[:].rearrange("p a b c -> p (a b c)"), coefD, reads=('coefD',), writes=('coef',))
                    kb.op('pool', lambda g: g.memset(XA[:], 0.0), writes=tuple('XA%d' % i for i in range(NSL)))
                    kb.op('pool', lambda g: g.memset(XB[:], 0.0), writes=tuple('XB%d' % i for i in range(NSL)))
                    for s0 in range(0, 32, NSL):
                        for i in range(NSL):
                            dg = s0 + i
                            d, gp = dg // 16, dg % 16
                            b = kb.ps()
                            for ri in range(2):
                                for gpar in range(2):
                                    g = 2 * gp + gpar
                                    kb.op('pe', lambda p: p.matmul(PS(b, 256, 64 * gpar, 64 * gpar + 64, off=256 * ri),
                                                                   Bin[:, g, d, ri, :], U8[:, g, :], start=True, stop=True),
                                          reads=('Bin', 'U8'), writes=(pst(b),), inc=(ri == 1 and gpar == 1))
                            kb.op('act', lambda a: a.activation(out=XA[:, i, :, 256:512], in_=PS(b).rearrange("p (r c) -> p r c", r=2),
                                                                func=AF.Identity),
                                  reads=(pst(b),), writes=('XA%d' % i,))
                        for r in range(8):
                            sft = 2 ** r
                            for i in range(NSL):
                                dg = s0 + i
                                d = dg // 16
                                src, dst = (XA, XB) if r % 2 == 0 else (XB, XA)
                                sn_, dn_ = ('XA%d' % i, 'XB%d' % i) if r % 2 == 0 else ('XB%d' % i, 'XA%d' % i)
                                lo = 256 - sft if d == 0 else 256 + sft
                                e_ = coef[:, 0, dg, r:r + 1]; f_ = coef[:, 1, dg, r:r + 1]; nf_ = coef[:, 2, dg, r:r + 1]
                                Rs = src[:, i, 0, lo:lo + 256]; Is = src[:, i, 1, lo:lo + 256]
                                R0 = src[:, i, 0, 256:512]; I0 = src[:, i, 1, 256:512]
                                tn = 'T1_%d' % i
                                kb.op('dve', lambda v: v.scalar_tensor_tensor(out=T1[:, i, 0, :], in0=Rs, scalar=e_, in1=R0, op0=ALU.mult, op1=ALU.add),
                                      reads=(sn_, 'coef'), writes=(tn + 'a',))
                                kb.op('dve', lambda v: v.scalar_tensor_tensor(out=dst[:, i, 0, 256:512], in0=Is, scalar=nf_, in1=T1[:, i, 0, :],
                                                                              op0=ALU.mult, op1=ALU.add),
                                      reads=(sn_, 'coef', tn + 'a'), writes=(dn_ + 'r',))
                                kb.op('dve', lambda v: v.scalar_tensor_tensor(out=T1[:, i, 1, :], in0=Rs, scalar=f_, in1=I0, op0=ALU.mult, op1=ALU.add),
                                      reads=(sn_, 'coef'), writes=(tn + 'b',))
                                kb.op('dve', lambda v: v.scalar_tensor_tensor(out=dst[:, i, 1, 256:512], in0=Is, scalar=e_, in1=T1[:, i, 1, :],
                                                                              op0=ALU.mult, op1=ALU.add),
                                      reads=(sn_, 'coef', tn + 'b'), writes=(dn_, dn_ + 'r'))
                        for i in range(NSL):
                            dg = s0 + i
                            d = dg // 16
                            lo = 255 if d == 0 else 257
                            kb.op('act', lambda a: a.activation(out=Xs[:, dg, :, :], in_=XA[:, i, :, lo:lo + 256], func=AF.Identity),
                                  reads=('XA%d' % i, 'XA%dr' % i), writes=('Xs',))
                    for g in range(32):
                        gp, gpar = g // 2, g % 2
                        p0, p1 = 64 * gpar, 64 * gpar + 64
                        b = kb.ps()
                        kb.op('pe', lambda p: p.matmul(PS(b, 256), Mg[:, g, :], U8[:, g, :], start=True, stop=False),
                              reads=('Mg', 'U8'), writes=(pst(b),), inc=False)
                        for d in range(2):
                            for ri in range(2):
                                last = (d == 1 and ri == 1)
                                kb.op('pe', lambda p: p.matmul(PS(b, 256), Cout[p0:p1, 16 * d + gp, ri, :], Xs[p0:p1, 16 * d + gp, ri, :],
                                                               start=False, stop=last),
                                      reads=('Cout', 'Xs'), writes=(pst(b),), inc=last)
                        kb.op('act', lambda a: a.activation(out=Y8[:, g, :], in_=PS(b, 256), func=AF.Identity), reads=(pst(b),), writes=('Y8',))
                    ydv = yd.rearrange("cc (g h) (i c) -> h i (cc g) c", h=16, i=8)
                    for i in range(8):
                        kb.dma('sp' if i % 2 == 0 else 'sp', ydv[:, i], Y8[16 * i:16 * i + 16, :, :], reads=('Y8',), writes=('yd',))
                    kb.barrier(barscr[:, 0:1])
                with ExitStack() as es:
                    E = es.enter_context
                    wg = E(SBT("c_wg", [128, 4, 512], BF16))
                    wbr = E(SBT("c_wbr", [128, 4, 1024], BF16))
                    wst = E(SBT("c_wst", [128, 2, 1024], F32))
                    yt = E(SBT("c_y", [128, L], BF16))
                    ut = E(SBT("c_u", [128, L], BF16))
                    y1 = E(SBT("c_y1", [128, L], F32))
                    t3 = E(SBT("c_t3", [128, L], F32))
                    yg = E(SBT("c_yg", [128, 4, L], BF16))
                    sg = E(SBT("c_sg", [128, 2, 512], F32))
                    y3 = E(SBT("c_y3", [128, 4, L], BF16))
                    ys = E(SBT("c_ys", [128, L], BF16))
                    dv = E(SBT("c_d", [128, 4], F32))
                    bg = E(SBT("c_bg", [128, 4], F32))
                    kb.dma('sp', dv[:], s5d[l], writes=('dv',))
                    kb.dma('sp', bg[:], b_glu[l], writes=('bg',))
                    for k in range(4):
                        s = k % 2
                        kb.dma('sp', wst[:, s, 0:512], w_glu[l][k * 128:(k + 1) * 128, :], writes=('wst%d' % s,))
                        kb.op('act', lambda a: a.activation(out=wg[:, k, :], in_=wst[:, s, 0:512], func=AF.Identity), reads=('wst%d' % s,), writes=('wg',))
                    for k in range(4):
                        s = k % 2
                        kb.dma('sp', wst[:, s, :], w_bs5[l][k * 128:(k + 1) * 128, :], writes=('wst%d' % s,))
                        kb.op('act', lambda a: a.activation(out=wbr[:, k, :], in_=wst[:, s, :], func=AF.Identity), reads=('wst%d' % s,), writes=('wbr',))
                    for cc in range(4):
                        kb.dma('sp', yt[:], yd[cc], reads=('yd',), writes=('yt',))
                        kb.dma('sp', ut[:], ud[cc], reads=('ud',), writes=('ut',))
                        kb.op('dve', lambda v: v.scalar_tensor_tensor(out=y1[:], in0=ut[:], scalar=dv[:, cc:cc + 1], in1=yt[:],
                                                                      op0=ALU.mult, op1=ALU.add),
                              reads=('ut', 'yt', 'dv'), writes=('y1',))
                        kb.op('dve', lambda v: v.tensor_tensor(out=t3[:], in0=y1[:], in1=y1[:], op=ALU.mult), reads=('y1',), writes=('t3',))
                        kb.op('dve', lambda v: v.tensor_scalar(out=t3[:], in0=t3[:], scalar1=0.044715 * 1.5957691216, scalar2=1.5957691216,
                                                               op0=ALU.mult, op1=ALU.add), reads=('t3',), writes=('t3',))
                        kb.op('dve', lambda v: v.tensor_tensor(out=t3[:], in0=t3[:], in1=y1[:], op=ALU.mult), reads=('t3', 'y1'), writes=('t3',))
                        kb.op('act', lambda a: a.activation(out=t3[:], in_=t3[:], func=AF.Sigmoid), reads=('t3',), writes=('t3',))
                        kb.op('dve', lambda v: v.tensor_tensor(out=yg[:, cc, :], in0=t3[:], in1=y1[:], op=ALU.mult), reads=('t3', 'y1'), writes=('yg',))
                    gsp = gs[:].rearrange("p a (c j) -> p a j c", j=8)
                    for cc in range(4):
                        for tt in range(4):
                            b = kb.ps()
                            for k in range(4):
                                kb.op('pe', lambda p: p.matmul(PS(b), wg[:, k, cc * 128:(cc + 1) * 128], yg[:, k, tt * 512:(tt + 1) * 512],
                                                               start=(k == 0), stop=(k == 3)),
                                      reads=('wg', 'yg'), writes=(pst(b),), inc=(k == 3))
                            s = tt % 2
                            kb.op('act', lambda a: a.activation(out=sg[:, s, :], in_=PS(b), func=AF.Sigmoid, bias=bg[:, cc:cc + 1]),
                                  reads=(pst(b), 'bg'), writes=('sg%d' % s,))
                            kb.op('dve', lambda v: v.tensor_tensor(out=sg[:, s, :], in0=sg[:, s, :], in1=yg[:, cc, tt * 512:(tt + 1) * 512], op=ALU.mult),
                                  reads=('sg%d' % s, 'yg'), writes=('sg%d' % s,))
                            kb.op('dve', lambda v: v.tensor_tensor(out=y3[:, cc, tt * 512:(tt + 1) * 512].rearrange("p (j c) -> p j c", j=2),
                                                                   in0=sg[:, s, :].rearrange("p (j c) -> p j c", j=2),
                                                                   in1=gsp[:, cc, 2 * tt:2 * tt + 2, :], op=ALU.mult),
                                  reads=('sg%d' % s, 'gs'), writes=('y3',))
                    for dc in range(8):
                        for tt in range(4):
                            b = kb.ps()
                            for k in range(4):
                                kb.op('pe', lambda p: p.matmul(PS(b), wbr[:, k, dc * 128:(dc + 1) * 128], y3[:, k, tt * 512:(tt + 1) * 512],
                                                               start=(k == 0), stop=(k == 3)),
                                      reads=('wbr', 'y3'), writes=(pst(b),), inc=(k == 3))
                            kb.op('act', lambda a: a.activation(out=ys[:].rearrange("p (c j) -> p j c", j=8)[:, 2 * tt:2 * tt + 2, :],
                                                                in_=PS(b).rearrange("p (j c) -> p j c", j=2), func=AF.Identity),
                                  reads=(pst(b),), writes=('ys',))
                        kb.dma('sp', ys5d[dc], ys[:], reads=('ys',), writes=('ys5d',))
                    kb.barrier(barscr[:, 0:1])

        def fft_fwd(C, zin_list, B, spec_evac, filler=lambda: None):
            for half in range(2):
                for cpl in range(8):
                    cp = half * 8 + cpl
                    bank = cpl // 2
                    off = 256 * (cpl % 2)
                    for zi, (zT, ztok, gk) in enumerate(zin_list):
                        kb.op('pe', lambda p: p.matmul(PS(bank, 256, off=off), zT[:, 2 * cp:2 * cp + 2, :].rearrange("p a b -> p (a b)"),
                                                       C['G1hi'] if gk else C['G1'], start=(zi == 0), stop=(zi == len(zin_list) - 1)),
                              reads=(ztok,) + C['toks'], writes=(pst(bank),), inc=(zi == len(zin_list) - 1))
                kb.op('act', lambda a: a.activation(
                    out=B[:].rearrange("p k r cp cq -> p (k r) cp cq")[:, :, half * 8:half * 8 + 8, :],
                    in_=PS4(0).rearrange("p (cp cq kr) -> p kr cp cq", cp=8, cq=4), func=AF.Identity),
                    reads=pst4(0), writes=('B',))
                filler()
            for c2 in range(2):
                for k1 in range(32):
                    bank = k1 // 8
                    off = 64 * (k1 % 8)
                    for ri in range(2):
                        kb.op('pe', lambda p: p.matmul(PS(bank, 64, off=off), C['T'][64 * c2:64 * c2 + 64, k1, ri, :],
                                                       B[64 * c2:64 * c2 + 64, k1, ri].rearrange("p a b -> p (a b)"),
                                                       start=(ri == 0), stop=(ri == 1)),
                              reads=('B',) + C['toks'], writes=(pst(bank),), inc=(ri == 1))
                spec_evac(c2, PS4(0).rearrange("p (k m) -> p m k", k=32), pst4(0))
                filler()

        def fft_inv(C, P1, P2, Dt, conv_out, conv_tok, filler=lambda: None):
            for c2 in range(2):
                for cp in range(16):
                    bank = cp // 4
                    off = 128 * (cp % 4)
                    kb.op('pe', lambda p: p.matmul(PS(bank, 128, off=off), P1[:, c2, cp * 128:(cp + 1) * 128], C['Ga'], start=True, stop=False),
                          reads=('P1',) + C['toks'], writes=(pst(bank),), inc=False)
                    kb.op('pe', lambda p: p.matmul(PS(bank, 128, off=off), P2[:, c2, cp * 128:(cp + 1) * 128], C['Gb'], start=False, stop=True),
                          reads=('P2',) + C['toks'], writes=(pst(bank),), inc=True)
                kb.op('act', lambda a: a.activation(
                    out=Dt[:, c2].rearrange("p n r cp -> p (n r) cp"),
                    in_=PS4(0).rearrange("p (cp nr) -> p nr cp", cp=16), func=AF.Identity),
                    reads=pst4(0), writes=('Dt',))
                filler()
            for n2 in range(64):
                bank = n2 // 16
                off = 32 * (n2 % 16)
                for r in range(2):
                    kb.op('pe', lambda p: p.matmul(PS(bank, 32, off=off), C['H'][:, n2, r, :],
                                                   Dt[:, :, n2, r, :].rearrange("p c2 cp -> p cp c2"), start=(r == 0), stop=(r == 1)),
                          reads=('Dt',) + C['toks'], writes=(pst(bank),), inc=(r == 1))
            kb.op('dve', lambda v: v.transpose(
                out=conv_out.rearrange("p (n1 n2) -> p n2 n1", n2=64),
                in_=PS4(0).rearrange("p (n c) -> p n c", c=32)),
                reads=pst4(0), writes=(conv_tok,))
            filler()

        def load_fft_consts(es, with_filter):
            E = es.enter_context
            fs = E(SBT("f_sm", [128, 896], BF16))
            Tb = E(SBT("f_T", [128, 32, 2, 128], BF16))
            kb.dma('sp', fs[:], fsmall, writes=('fc',))
            Tv = Tb[:].rearrange("p a b c -> p (a b c)")
            for i in range(2):
                kb.dma('sp', Tv[:, i * 4096:(i + 1) * 4096], fT[:, i * 4096:(i + 1) * 4096], writes=('fcT%d' % i,))
            C = {'G1': fs[:, 0:256], 'G1hi': fs[:, 256:512], 'Ga': fs[:, 512:640], 'Gb': fs[:, 640:768], 'Sw': fs[:, 768:896], 'T': Tb}
            C['toks'] = ('fc', 'fcT0', 'fcT1')
            if not with_filter:
                Hb = E(SBT("f_H", [128, 64, 2, 128], BF16))
                Hv = Hb[:].rearrange("p a b c -> p (a b c)")
                for i in range(4):
                    kb.dma('sp', Hv[:, i * 4096:(i + 1) * 4096], fH[:, i * 4096:(i + 1) * 4096], writes=('fcH%d' % i,))
                C['H'] = Hb
                C['toks'] = C['toks'] + tuple('fcH%d' % i for i in range(4))
            return C

        def prep_hy(l):
            with ExitStack() as es:
                E = es.enter_context
                C = load_fft_consts(es, True)
                w1 = E(SBT("q_w1", [33, 64], F32))
                w2 = E(SBT("q_w2", [64, 64], F32))
                w3 = E(SBT("q_w3", [64, 4096], F32))
                hv = E(SBT("q_hv", [64, 3], F32))
                bf = E(SBT("q_bf", [64, 2], F32))
                dec = E(SBT("q_dec", [128, 32], F32))
                ft = E(SBT("q_ft", [33, 2, L], F32))
                tt_ = E(SBT("q_tt", [128, 2, L], F32))
                h1 = E(SBT("q_h1", [64, L], F32))
                h2 = E(SBT("q_h2", [64, 2, L], F32))
                tmp = E(SBT("q_tmp", [64, L], F32))
                win = E(SBT("q_win", [128, 2, L], F32))
                hk = E(SBT("q_hk", [128, 2, 2, L], F32))
                junk = E(SBT("q_junk", [128, L], BF16))
                ss = E(SBT("q_ss", [128, 2, 4], F32))
                kbf = E(SBT("q_kbf", [128, 2, L], BF16))
                zT = E(SBT("q_zT", [128, 2, 32, 64], BF16))
                B = E(SBT("q_B", [128, 32, 2, 16, 4], BF16))
                Kh = E(SBT("q_Kh", [128, 2, 2, 2048], BF16))
                kb.dma('sp', w1[:], hw1[l], writes=('w1',))
                kb.dma('sp', w2[:], hw2[l], writes=('w2',))
                kb.dma('sp', w3[:], hw3[l], writes=('w3',))
                kb.dma('sp', hv[:], hvec[l], writes=('hv',))
                kb.dma('sp', dec[:], hdec[l], writes=('dec',))
                for i in range(2):
                    kb.dma('sp', ft[:, i, :], feats[i], writes=('ft',))
                    kb.dma('sp', tt_[:, i, :], ttab[i], writes=('tt',))
                V = lambda fn, r, w: kb.op('dve', fn, reads=r, writes=w)
                A = lambda fn, r, w: kb.op('act', fn, reads=r, writes=w)
                V(lambda v: v.tensor_tensor(out=bf[:, 0:1], in0=hv[:, 0:1], in1=hv[:, 2:3], op=ALU.mult), ('hv',), ('bf',))
                V(lambda v: v.tensor_tensor(out=bf[:, 1:2], in0=hv[:, 1:2], in1=hv[:, 2:3], op=ALU.mult), ('hv', 'bf'), ('bf',))
                A(lambda a: a.activation(out=dec[:], in_=dec[:], func=AF.Abs), ('dec',), ('dec',))
                V(lambda v: v.tensor_scalar(out=dec[:], in0=dec[:], scalar1=-1.0, scalar2=None, op0=ALU.mult), ('dec',), ('dec',))
                for i in range(2):
                    for stage in range(2):
                        wmat = w1 if stage == 0 else w2
                        dst = h1 if stage == 0 else h2[:, i, :]
                        dtok = 'h1' if stage == 0 else 'h2_%d' % i
                        for t4 in range(4):
                            ts = slice(t4 * 512, (t4 + 1) * 512)
                            b = kb.ps()
                            if stage == 0:
                                kb.op('pe', lambda p: p.matmul(PS(b, 512, 0, 64), w1[:], ft[:, i, ts], start=True, stop=True),
                                      reads=('w1', 'ft'), writes=(pst(b),))
                            else:
                                kb.op('pe', lambda p: p.matmul(PS(b, 512, 0, 64), w2[:], h1[:, ts], start=True, stop=True),
                                      reads=('w2', 'h1'), writes=(pst(b),))
                            dsl = dst[:, ts] if stage == 0 else h2[:, i, ts]
                            A(lambda a: a.activation(out=dsl, in_=PS(b, 512, 0, 64), func=AF.Identity, bias=bf[:, stage:stage + 1], scale=hv[:, 2:3]),
                              (pst(b), 'bf', 'hv'), (dtok,))
                        dfull = h1[:] if stage == 0 else h2[:, i, :]
                        range_reduce('dve', dfull, tmp[:], (dtok,), ('tmp',))
                        A(lambda a: a.activation(out=dfull, in_=dfull, func=AF.Sin), (dtok,), (dtok,))
                NIT = 16

                def make_A(n):
                    o, cc = n // 8, n % 8
                    par = n % 2
                    pieces = []
                    for dr in range(2):
                        for t4 in range(4):
                            def piece(dr=dr, t4=t4):
                                fc = o * 16 + dr * 8 + cc
                                if t4 == 0:
                                    A(lambda a: a.activation(out=win[:, dr, :], in_=tt_[:, dr, :], func=AF.Exp, scale=dec[:, fc:fc + 1]),
                                      ('tt', 'dec'), ('win%d' % dr,))
                                ts = slice(t4 * 512, (t4 + 1) * 512)
                                b = kb.ps_hi()
                                kb.op('pe', lambda p: p.matmul(PS(b), w3[:, fc * 128:(fc + 1) * 128], h2[:, dr, ts], start=True, stop=True),
                                      reads=('w3', 'h2_%d' % dr), writes=(pst(b),))
                                V(lambda v: v.scalar_tensor_tensor(out=hk[:, par, dr, ts], in0=win[:, dr, ts], scalar=0.05, in1=PS(b), op0=ALU.add, op1=ALU.mult),
                                  ('win%d' % dr, pst(b)), ('hk%d_%d' % (par, dr),))
                                if t4 == 3:
                                    if dr == 1:
                                        V(lambda v: v.memset(hk[:, par, 1, 0:1], 0.0), ('hk%d_1' % par,), ('hk%d_1' % par,))
                                    A(lambda a: a.activation(out=junk[:], in_=hk[:, par, dr, :], func=AF.Square, accum_out=ss[:, par, dr:dr + 1]),
                                      ('hk%d_%d' % (par, dr),), ('junk', 'ss%d_%d' % (par, dr)))
                            pieces.append(piece)
                    return pieces

                def run_B(n, filler):
                    o, cc = n // 8, n % 8
                    par = n % 2
                    V(lambda v: v.tensor_tensor(out=ss[:, par, 2:3], in0=ss[:, par, 0:1], in1=ss[:, par, 1:2], op=ALU.add),
                      ('ss%d_0' % par, 'ss%d_1' % par), ('ss%d_2' % par,))
                    A(lambda a: a.activation(out=ss[:, par, 2:3], in_=ss[:, par, 2:3], func=AF.Sqrt, bias=barscr[:, 1:2]), ('ss%d_2' % par,), ('ss%d_2' % par,))
                    V(lambda v: v.reciprocal(out=ss[:, par, 3:4], in_=ss[:, par, 2:3]), ('ss%d_2' % par,), ('ss%d_3' % par,))
                    for dr in range(2):
                        V(lambda v: v.tensor_scalar(out=kbf[:, dr, :], in0=hk[:, par, dr, :], scalar1=ss[:, par, 3:4], scalar2=None, op0=ALU.mult),
                          ('hk%d_%d' % (par, dr), 'ss%d_3' % par), ('kbf%d' % dr,))
                        V(lambda v: v.transpose(out=zT[:, dr].bitcast(U32).rearrange("p c m -> p m c"),
                                                in_=kbf[:, dr, :].bitcast(U32).rearrange("p (n1 m) -> p m n1", m=32)),
                          ('kbf%d' % dr,), ('zT%d' % dr,))
                        filler()

                    def spec_evac(c2, psv, ptoks):
                        A(lambda a: a.activation(out=Kh[:, 0, c2, :].rearrange("p (m k) -> p m k", k=32), in_=psv, func=AF.Identity),
                          ptoks, ('Kh0',))
                    fft_fwd(C, [(zT[:, 0], 'zT0', 0), (zT[:, 1], 'zT1', 1)], B, spec_evac, filler)
                    for c2 in range(2):
                        for q in range(4):
                            b = kb.ps_hi()
                            kb.op('pe', lambda p: p.matmul(PS(b), C['Sw'], Kh[:, 0, c2, q * 512:(q + 1) * 512], start=True, stop=True),
                                  reads=C['toks'] + ('Kh0',), writes=(pst(b),))
                            A(lambda a: a.activation(out=Kh[:, 1, c2, q * 512:(q + 1) * 512], in_=PS(b), func=AF.Identity), (pst(b),), ('Kh1',))
                        filler()
                    for w_ in range(2):
                        kb.dma('sp', khatD[o, cc, w_], Kh[:, w_].rearrange("p a b -> p (a b)"), reads=('Kh%d' % w_,), writes=('khatD',))

                cur = make_A(0)
                for pc in cur:
                    pc()
                for n in range(NIT):
                    nxt = make_A(n + 1) if n + 1 < NIT else []

                    def filler():
                        if nxt:
                            nxt.pop(0)()
                    run_B(n, filler)
                    while nxt:
                        nxt.pop(0)()
                kb.barrier(barscr[:, 0:1])

        def phase_hy(l):
            with ExitStack() as es:
                E = es.enter_context
                C = load_fft_consts(es, False)
                cw = E(SBT("h_cw", [128, 3, 24], F32))
                cb_ = E(SBT("h_cb", [128, 24], F32))
                dd = E(SBT("h_dd", [128, 2, 8], F32))
                wst = E(SBT("h_wst", [128, 8, 128], F32))
                wbf = E(SBT("h_wbf", [128, 4, 8, 128], BF16))
                pp = E(SBT("h_pp", [128, L + 2], F32))
                sc = E(SBT("h_sc", [128, L], F32))
                vx = E(SBT("h_vx", [128, 2, 4, L], BF16))
                z1 = E(SBT("h_z1", [128, L], BF16))
                zT = E(SBT("h_zT", [128, 32, 64], BF16))
                B = E(SBT("h_B", [128, 32, 2, 16, 4], BF16))
                Dt = E(SBT("h_Dt", [128, 2, 64, 2, 16], BF16))
                P1 = E(SBT("h_P1", [128, 2, 2048], BF16))
                P2 = E(SBT("h_P2", [128, 2, 2048], BF16))
                Kt = E(SBT("h_Kt", [128, 2, 4096], BF16))
                conv = E(SBT("h_conv", [128, L], F32))
                kb.dma('sp', cw[:], convw[l], writes=('cw',))
                kb.dma('sp', cb_[:], convb[l], writes=('cb',))
                kb.dma('sp', dd[:], hyd[l], writes=('dd',))
                kb.op('pool', lambda g: g.memset(pp[:], 0.0), writes=('pp',))

                def make_A(cb):
                    par = cb % 2
                    colbase = [1024 + cb * 128, 2048 + cb * 128, 3072 + cb * 128, 4096 + cb * 128]
                    pieces = []
                    for si in range(4):
                        for t4 in range(4):
                            def piece(si=si, t4=t4):
                                if t4 == 0:
                                    for k in range(8):
                                        kb.dma('sp' if k % 2 == 0 else 'sp', wst[:, k, :],
                                               w_in[l][k * 128:(k + 1) * 128, colbase[si]:colbase[si] + 128], writes=('wst',))
                                    kb.op('act', lambda a: a.activation(out=wbf[:, si], in_=wst[:], func=AF.Identity), reads=('wst',), writes=('wbf%d' % si,))
                                ts = slice(t4 * 512, (t4 + 1) * 512)
                                b = kb.ps_hi()
                                for k in range(8):
                                    kb.op('pe', lambda p: p.matmul(PS(b), wbf[:, si, k, :], G['xn'][:, k, ts], start=(k == 0), stop=(k == 7)),
                                          reads=('wbf%d' % si, 'xn%d' % k), writes=(pst(b),), inc=(k == 7))
                                if si < 3:
                                    kb.op('act', lambda a: a.activation(out=pp[:, 1 + t4 * 512:1 + (t4 + 1) * 512], in_=PS(b), func=AF.Identity),
                                          reads=(pst(b),), writes=('pp',))
                                    if t4 == 3:
                                        ch = si * 8 + cb
                                        kb.op('dve', lambda v: v.tensor_scalar(out=sc[:], in0=pp[:, 1:L + 1], scalar1=cw[:, 1, ch:ch + 1], scalar2=cb_[:, ch:ch + 1],
                                                                               op0=ALU.mult, op1=ALU.add),
                                              reads=('pp', 'cw', 'cb'), writes=('sc',))
                                        kb.op('dve', lambda v: v.scalar_tensor_tensor(out=sc[:], in0=pp[:, 0:L], scalar=cw[:, 0, ch:ch + 1], in1=sc[:],
                                                                                      op0=ALU.mult, op1=ALU.add),
                                              reads=('pp', 'cw', 'sc'), writes=('sc',))
                                        kb.op('dve', lambda v: v.scalar_tensor_tensor(out=vx[:, par, si, :], in0=pp[:, 2:L + 2], scalar=cw[:, 2, ch:ch + 1], in1=sc[:],
                                                                                      op0=ALU.mult, op1=ALU.add),
                                              reads=('pp', 'cw', 'sc'), writes=('vx%d_%d' % (par, si),))
                                else:
                                    kb.op('act', lambda a: a.activation(out=vx[:, par, 3, ts], in_=PS(b), func=AF.Silu), reads=(pst(b),), writes=('vx%d_3' % par,))
                            pieces.append(piece)
                    return pieces

                def run_B(cb, filler):
                    par = cb % 2
                    zin, ztok = vx[:, par, 0, :], 'vx%d_0' % par
                    for o in range(2):
                        for w_ in range(2):
                            kb.dma('sp' if w_ == 0 else 'sp', Kt[:, w_, :], khatD[o, cb, w_], reads=('khatD',), writes=('Kt',))
                        kb.op('dve', lambda v: v.transpose(out=zT[:].bitcast(U32).rearrange("p c m -> p m c"),
                                                           in_=zin.bitcast(U32).rearrange("p (n1 m) -> p m n1", m=32)),
                              reads=(ztok,), writes=('zT',))
                        filler()

                        def spec_evac(c2, psv, ptoks):
                            kb.op('dve', lambda v: v.tensor_tensor(out=P1[:, c2, :].rearrange("p (m k) -> p m k", k=32), in0=psv,
                                                                   in1=Kt[:, 0, c2 * 2048:(c2 + 1) * 2048].rearrange("p (m k) -> p m k", k=32), op=ALU.mult),
                                  reads=ptoks + ('Kt',), writes=('P1',))
                            kb.op('dve', lambda v: v.tensor_tensor(out=P2[:, c2, :].rearrange("p (m k) -> p m k", k=32), in0=psv,
                                                                   in1=Kt[:, 1, c2 * 2048:(c2 + 1) * 2048].rearrange("p (m k) -> p m k", k=32), op=ALU.mult),
                                  reads=ptoks + ('Kt',), writes=('P2',))
                        fft_fwd(C, [(zT[:], 'zT', 0)], B, spec_evac, filler)
                        fft_inv(C, P1, P2, Dt, conv[:], 'conv', filler)
                        kb.op('dve', lambda v: v.scalar_tensor_tensor(out=conv[:], in0=zin, scalar=dd[:, o, cb:cb + 1], in1=conv[:], op0=ALU.mult, op1=ALU.add),
                              reads=(ztok, 'dd', 'conv'), writes=('conv',))
                        if o == 0:
                            kb.op('pool', lambda g: g.tensor_tensor(out=z1[:], in0=conv[:], in1=vx[:, par, 1, :], op=ALU.mult),
                                  reads=('conv', 'vx%d_1' % par), writes=('z1',))
                            zin, ztok = z1[:], 'z1'
                        else:
                            kb.op('pool', lambda g: g.tensor_tensor(out=conv[:], in0=conv[:], in1=vx[:, par, 2, :], op=ALU.mult),
                                  reads=('conv', 'vx%d_2' % par), writes=('conv',))
                            yo = zT[:].rearrange("p a b -> p (a b)")
                            kb.op('pool', lambda g: g.tensor_tensor(out=yo, in0=conv[:], in1=vx[:, par, 3, :], op=ALU.mult),
                                  reads=('conv', 'vx%d_3' % par), writes=('zT',))
                            kb.dma('sp', yhyd[cb], yo, reads=('zT',), writes=('yhyd',))

                cur = make_A(0)
                for pc in cur:
                    pc()
                for cb in range(8):
                    nxt = make_A(cb + 1) if cb + 1 < 8 else []

                    def filler():
                        if nxt:
                            nxt.pop(0)()
                    run_B(cb, filler)
                    while nxt:
                        nxt.pop(0)()
                kb.barrier(barscr[:, 0:1])

        def phase_merge(l, hsrc):
            with ExitStack() as es:
                E = es.enter_context
                wm = E(SBT("m_wm", [128, 8, 2048], BF16))
                wh = E(SBT("m_wh", [128, 8, 1024], BF16))
                wo = E(SBT("m_wo", [128, 8, 1024], BF16))
                st = E(SBT("m_st", [128, 2, 2048], F32))
                mm = E(SBT("m_mm", [128, 16, 512], BF16))
                ys = E(SBT("m_ys", [128, 8, 512], BF16))
                yh = E(SBT("m_yh", [128, 8, 512], BF16))
                mg = E(SBT("m_mg", [128, 8, 512], BF16))
                t1 = E(SBT("m_t1", [128, 2, 512], F32))
                ht = E(SBT("m_ht", [128, 8, 512], F32))
                i = 0
                for (wt, src, ncol, nm) in ((wm, w_in[l][:, 5120:7168], 2048, 'wm'), (wh, w_bhy[l], 1024, 'wh'), (wo, w_out[l], 1024, 'wo')):
                    for k in range(8):
                        s = i % 2; i += 1
                        kb.dma('sp' if s == 0 else 'sp', st[:, s, 0:ncol], src[k * 128:(k + 1) * 128, :], writes=('st%d' % s,))
                        kb.op('act', lambda a: a.activation(out=wt[:, k, :], in_=st[:, s, 0:ncol], func=AF.Identity), reads=('st%d' % s,), writes=(nm,))
                for tt in range(4):
                    ts = slice(tt * 512, (tt + 1) * 512)
                    for c in range(8):
                        kb.dma('sp', ys[:, c, :], ys5d[c][:, ts], reads=('ys5d',), writes=('ys',))
                        kb.dma('sp', yh[:, c, :], yhyd[c][:, ts], reads=('yhyd',), writes=('yh',))
                        kb.dma('sp', ht[:, c, :], hsrc[c][:, ts], reads=('hsrc',), writes=('ht',))
                    for mc in range(16):
                        b = kb.ps()
                        for k in range(8):
                            kb.op('pe', lambda p: p.matmul(PS(b), wm[:, k, mc * 128:(mc + 1) * 128], G['xn'][:, k, ts], start=(k == 0), stop=(k == 7)),
                                  reads=('wm', 'xn%d' % k), writes=(pst(b),), inc=(k == 7))
                        kb.op('act', lambda a: a.activation(out=mm[:, mc, :], in_=PS(b), func=AF.Sigmoid), reads=(pst(b),), writes=('mm',))
                    for dc in range(8):
                        b = kb.ps()
                        for k in range(8):
                            kb.op('pe', lambda p: p.matmul(PS(b), wh[:, k, dc * 128:(dc + 1) * 128], yh[:, k, :], start=(k == 0), stop=(k == 7)),
                                  reads=('wh', 'yh'), writes=(pst(b),), inc=(k == 7))
                        s = dc % 2
                        kb.op('dve', lambda v: v.tensor_tensor(out=t1[:, s, :], in0=PS(b), in1=mm[:, 8 + dc, :], op=ALU.mult),
                              reads=(pst(b), 'mm'), writes=('t1_%d' % s,))
                        kb.op('pool', lambda g: g.tensor_tensor(out=mg[:, dc, :], in0=mm[:, dc, :], in1=ys[:, dc, :], op=ALU.mult),
                              reads=('mm', 'ys'), writes=('mg%d' % dc,))
                        kb.op('dve', lambda v: v.tensor_tensor(out=mg[:, dc, :], in0=mg[:, dc, :], in1=t1[:, s, :], op=ALU.add),
                              reads=('mg%d' % dc, 't1_%d' % s), writes=('mg%d' % dc, 'mg'))
                    for dc in range(8):
                        b = kb.ps()
                        for k in range(8):
                            kb.op('pe', lambda p: p.matmul(PS(b), wo[:, k, dc * 128:(dc + 1) * 128], mg[:, k, :], start=(k == 0), stop=(k == 7)),
                                  reads=('wo', 'mg'), writes=(pst(b),), inc=(k == 7))
                        kb.op('dve', lambda v: v.tensor_tensor(out=ht[:, dc, :], in0=PS(b), in1=ht[:, dc, :], op=ALU.add),
                              reads=(pst(b), 'ht'), writes=('ht',))
                        kb.dma('sp', hbuf[dc][:, ts], ht[:, dc, :], reads=('ht',), writes=('hbuf',))
                kb.barrier(barscr[:, 0:1])

        kb.op('pool', lambda g: g.memset(barscr[:, 1:2], 1e-6), writes=('epsq',))
        kb.barrier(barscr[:, 0:1])
        order = ['prep_s5', 'prep_hy', 'norm', 's5', 'hy', 'merge']
        nph = len(order) if stop_after is None else order.index(stop_after) + 1
        for l in range(layers):
            hsrc = xT if l == 0 else hbuf
            if nph >= 1:
                prep_s5(l)
            if nph >= 2:
                prep_hy(l)
            with ExitStack() as esl:
                G['xn'] = esl.enter_context(SBT("xn%d" % l, [128, 8, L], BF16))
                if nph >= 3:
                    phase_norm(hsrc, l, G['xn'], None)
                    if 'xnD' in dump:
                        xnD = dscr("xnD", [8, 128, L], BF16)
                        for c in range(8):
                            kb.dma('sp', xnD[c], G['xn'][:, c, :], reads=('xn%d' % c,), writes=('xnD',))
                if nph >= 4:
                    phase_s5(l)
                if nph >= 5:
                    phase_hy(l)
                if nph >= 6:
                    phase_merge(l, hsrc)
        if stop_after is None:
            phase_norm(hbuf, DEPTH, None, outT)
        for i in range(NDS):
            if kb.dcnt[i]:
                kb._wait('sp', ('d', i), kb.dcnt[i])
    return nc, kb


_CACHE = {}


def _host_consts():
    if 'c' not in _CACHE:
        fsmall, fT, fH = fft_consts()
        ft, tt = hy_feats()
        import ml_dtypes
        bfc = lambda a: np.ascontiguousarray(a.astype(ml_dtypes.bfloat16))
        _CACHE['c'] = dict(fsmall=bfc(fsmall), fT=bfc(fT), fH=bfc(fH), feats=ft, ttab=tt, kvt=s5_kv(), masks=s5_masks(),
                           ident=np.eye(128, dtype=np.float32))
    return _CACHE['c']


def _layout_shared(inp):
    f32 = np.float32
    g = lambda k: np.asarray(inp[k], dtype=f32)
    m = dict(_host_consts())
    nw = np.concatenate([g("norm_w"), g("final_norm_w")[None]], 0)
    m["normw"] = np.ascontiguousarray(nw.reshape(3, 8, 128).transpose(0, 2, 1))
    m["w_in"] = g("w_in")
    m["w_glu"] = g("s5_w_glu")
    m["b_glu"] = np.ascontiguousarray(g("s5_b_glu").reshape(DEPTH, 4, 128).transpose(0, 2, 1))
    m["s5d"] = np.ascontiguousarray(g("s5_d").reshape(DEPTH, 4, 128).transpose(0, 2, 1))
    m["w_bs5"] = g("w_branch_s5")
    m["w_bhy"] = g("w_branch_hy")
    m["w_out"] = g("w_out")
    cwv = g("hy_conv_w").reshape(DEPTH, 3, 24, 128)
    m["convw"] = np.ascontiguousarray(cwv.transpose(0, 3, 1, 2))
    m["convb"] = np.ascontiguousarray(g("hy_conv_b").reshape(DEPTH, 24, 128).transpose(0, 2, 1))
    m["hyd"] = np.ascontiguousarray(g("hy_d").reshape(DEPTH, 2, 8, 128).transpose(0, 3, 1, 2))
    def pg(a):
        v = a.reshape(DEPTH, 2, 16, 2, 64)
        return v.transpose(0, 3, 4, 1, 2).reshape(DEPTH, 128, 32)
    ls = np.broadcast_to(g("s5_log_step")[..., None], (DEPTH, 2, 32, 64))
    m["s5lam"] = np.ascontiguousarray(np.stack([pg(g("s5_lam_re")), pg(g("s5_lam_im")), pg(ls)], 2))
    def pgB(a):
        v = a.reshape(DEPTH, 2, 16, 2, 64, 16)
        return v.transpose(0, 3, 4, 1, 2, 5).reshape(DEPTH, 128, 32, 16)
    m["s5B"] = np.ascontiguousarray(np.stack([pgB(g("s5_b_re")), pgB(g("s5_b_im"))], 2))
    ct = lambda a: a.transpose(0, 1, 2, 4, 3)
    m["s5C"] = np.ascontiguousarray(np.stack([pgB(ct(g("s5_c_re"))), pgB(ct(g("s5_c_im")))], 2))
    m["hw1"] = g("hy_w1"); m["hw2"] = g("hy_w2"); m["hw3"] = g("hy_w3")
    m["hvec"] = np.ascontiguousarray(np.stack([g("hy_b1"), g("hy_b2"), g("hy_freq")], -1))
    m["hdec"] = np.ascontiguousarray(g("hy_decay").reshape(DEPTH, 32, 128).transpose(0, 2, 1))
    return m


def kernel(**inputs):
    x = np.asarray(inputs["x"], dtype=np.float32)
    shared = _layout_shared(inputs)
    if 'nc' not in _CACHE:
        _CACHE['nc'] = build()[0]
    nc = _CACHE['nc']
    in_maps = []
    for b in range(NCORES):
        m = dict(shared)
        m["xT"] = np.ascontiguousarray(x[b].T.reshape(8, 128, L))
        in_maps.append(m)
    res = run_bass_kernel_spmd(nc, in_maps, core_ids=list(range(NCORES)))
    out = np.empty((NCORES, L, D), dtype=np.float32)
    for b in range(NCORES):
        out[b] = np.asarray(res.results[b]["outT"]).reshape(D, L).T
    return out
```

```python
import math
from contextlib import ExitStack
import numpy as np
import concourse.bass as bass
import concourse.mybir as mybir
from concourse.bass_utils import run_bass_kernel_spmd

F32 = mybir.dt.float32
BF16 = mybir.dt.bfloat16
U32 = mybir.dt.uint32
ALU = mybir.AluOpType
AF = mybir.ActivationFunctionType

D = 1024; L = 2048; DEPTH = 2; NCORES = 8
INC = 7168
NDS = 48
MAGIC = 12582912.0
TWO_PI = 2.0 * math.pi


class KB:
    def __init__(self, nc, es):
        self.nc = nc
        self.engs = {'pe': nc.tensor, 'act': nc.scalar, 'dve': nc.vector, 'pool': nc.gpsimd, 'sp': nc.sync}
        self.csem = {e: es.enter_context(nc.semaphore('c_' + e)) for e in ('pe', 'act', 'dve', 'pool')}
        self.cnt = {e: 0 for e in self.csem}
        self.dsem = [es.enter_context(nc.semaphore('d%d' % i)) for i in range(NDS)]
        self.dcnt = [0] * NDS
        self.dnext = 0
        self.waited = {e: {} for e in self.engs}
        self.lastw = {}
        self.readers = {}
        self.psn = 0
        self.fp = 0
        self.nins = 0

    def _sem(self, sid):
        return self.csem[sid[1]] if sid[0] == 'c' else self.dsem[sid[1]]

    def _wait(self, e, sid, val):
        if e == 'pe' and sid == ('c', 'pe'):
            return
        w = self.waited[e]
        if w.get(sid, 0) >= val:
            return
        self.engs[e].wait_ge(self._sem(sid), val)
        w[sid] = val

    def _deps(self, e, reads, writes):
        for t in reads:
            if t in self.lastw:
                self._wait(e, *self.lastw[t])
        for t in writes:
            if t in self.lastw:
                self._wait(e, *self.lastw[t])
            for sid, val in self.readers.get(t, {}).items():
                self._wait(e, sid, val)

    def _record(self, sid, val, reads, writes):
        for t in writes:
            self.lastw[t] = (sid, val)
            self.readers[t] = {}
        for t in reads:
            r = self.readers.setdefault(t, {})
            if r.get(sid, 0) < val:
                r[sid] = val

    def op(self, e, fn, reads=(), writes=(), inc=True):
        self._deps(e, reads, writes)
        ins = fn(self.engs[e])
        self.nins += 1
        if e == 'pe' and not inc:
            val = self.cnt['pe'] + 1
        else:
            self.cnt[e] += 1
            val = self.cnt[e]
            ins.then_inc(self.csem[e], 1)
        self._record(('c', e), val, reads, writes)

    def dma(self, q, out, in_, reads=(), writes=()):
        i = self.dnext
        self.dnext = (self.dnext + 1) % NDS
        sid = ('d', i)
        if self.dcnt[i] > 0:
            self._wait(q, sid, self.dcnt[i])
        self._deps(q, reads, writes)
        self.engs[q].dma_start(out=out, in_=in_).then_inc(self.dsem[i], 16)
        self.nins += 1
        self.dcnt[i] += 16
        self._record(sid, self.dcnt[i], reads, writes)

    def barrier(self, scratch):
        for i in range(NDS):
            if self.dcnt[i]:
                self._wait('pool', ('d', i), self.dcnt[i])
        for e in ('pe', 'act', 'dve'):
            if self.cnt[e]:
                self._wait('pool', ('c', e), self.cnt[e])
        self.op('pool', lambda g: g.memset(scratch, 0.0), writes=('__bar',))
        val = self.cnt['pool']
        for e in ('pe', 'act', 'dve', 'sp'):
            self._wait(e, ('c', 'pool'), val)
        self.lastw.clear()
        self.readers.clear()

    def ps(self):
        b = self.psn % 8
        self.psn += 1
        return b

    def ps_hi(self):
        b = 6 + self.psn % 2
        self.psn += 1
        return b


def fft_consts():
    N = 4096
    n1 = np.arange(32); k1 = np.arange(32); n2 = np.arange(64); k2 = np.arange(64)
    ang = -2 * np.pi * np.outer(n1, k1 + 0.5) / 64.0
    g = np.stack([np.cos(ang), np.sin(ang)], -1)
    angh = -2 * np.pi * np.outer(n1 + 32, k1 + 0.5) / 64.0
    gh = -np.stack([np.cos(angh), np.sin(angh)], -1)
    G1 = np.zeros((4, 32, 4, 32, 2)); G1hi = np.zeros((4, 32, 4, 32, 2))
    for cq in range(4):
        G1[cq, :, cq] = g; G1hi[cq, :, cq] = gh
    a = -2 * np.pi * (n2[:, None, None] * (k1[None, :, None] + 0.5) / 4096.0 + n2[:, None, None] * k2[None, None, :] / 64.0)
    twr, twi = np.cos(a), np.sin(a)
    T = np.zeros((64, 32, 2, 64, 2))
    T[:, :, 0, :, 0] = twr; T[:, :, 0, :, 1] = twi
    T[:, :, 1, :, 0] = -twi; T[:, :, 1, :, 1] = twr
    T = np.concatenate([T, T], 0).reshape(128, 32 * 2 * 128)
    e = 2 * np.pi * np.outer(k2, n2) / 64.0
    er, ei = np.cos(e), np.sin(e)
    Ga = np.zeros((64, 2, 64, 2)); Gb = np.zeros((64, 2, 64, 2))
    for rp, s in ((0, 1.0), (1, -1.0)):
        Ga[:, rp, :, 0] = s * er; Ga[:, rp, :, 1] = s * ei
        Gb[:, rp, :, 0] = -ei; Gb[:, rp, :, 1] = er
    phi = 2 * np.pi * (k1[:, None, None] + 0.5) * (64 * n1[None, None, :] + n2[None, :, None]) / 4096.0
    H = np.zeros((4, 32, 64, 2, 4, 32))
    for cq in range(4):
        H[cq, :, :, 0, cq, :] = (2.0 / N) * np.cos(phi)
        H[cq, :, :, 1, cq, :] = -(2.0 / N) * np.sin(phi)
    Sw = np.zeros((64, 2, 64, 2))
    for k in range(64):
        Sw[k, 0, k, 1] = 1; Sw[k, 1, k, 0] = 1
    small = np.concatenate([G1.reshape(128, 256), G1hi.reshape(128, 256), Ga.reshape(128, 128),
                            Gb.reshape(128, 128), Sw.reshape(128, 128)], 1)
    return (small.astype(np.float32), T.astype(np.float32), H.reshape(128, 64 * 2 * 128).astype(np.float32))


def hy_feats():
    f32 = np.float32
    t = np.linspace(0.0, 1.0, L, dtype=f32)
    bands = np.linspace(1e-4, 15, 16, dtype=f32)
    def feats(pos, tt):
        angv = bands[None, :] * pos[:, None].astype(f32) * f32(2.0 * math.pi / L)
        return np.concatenate([tt[:, None], np.cos(angv), -np.sin(angv)], -1).astype(f32)
    posF = np.arange(L)
    posR = (L - np.arange(L)) % L
    fF = feats(posF, t); fR = feats(posR, t[posR])
    ft = np.stack([fF.T, fR.T], 0).astype(f32)
    tt = np.stack([np.broadcast_to(t, (128, L)), np.broadcast_to(t[posR], (128, L))], 0).astype(f32)
    return ft, tt


def s5_kv():
    j = np.arange(8)
    dbl = 8.0 * 2.0 ** np.arange(8)
    f = np.concatenate([7 - j, j + 1, j - 7, dbl])
    b = np.concatenate([j, 8 - j, -j, dbl])
    kv = np.stack([f, b], 0).astype(np.float32)
    return np.broadcast_to(kv[None], (128, 2, 32)).copy()


def s5_masks():
    jj = np.repeat(np.arange(8), 16)
    mf = (jj[None, :] >= jj[:, None]).astype(np.float32)
    mb = (jj[:, None] >= jj[None, :]).astype(np.float32)
    return np.concatenate([mf, mb], 1)


def build(dump=(), layers=DEPTH, stop_after=None):
    nc = bass.Bass("TRN2", target_bir_lowering=False)
    dt_in = {}

    def din(name, shape, dt=F32):
        dt_in[name] = nc.dram_tensor(name, list(shape), dt, kind="ExternalInput").ap()
        return dt_in[name]

    def dscr(name, shape, dt=F32):
        kind = "ExternalOutput" if name in dump else "Internal"
        return nc.dram_tensor(name, list(shape), dt, kind=kind).ap()

    xT = din("xT", [8, 128, L])
    normw = din("normw", [DEPTH + 1, 128, 8])
    w_in = din("w_in", [DEPTH, D, INC])
    w_glu = din("w_glu", [DEPTH, 512, 512])
    b_glu = din("b_glu", [DEPTH, 128, 4])
    s5d = din("s5d", [DEPTH, 128, 4])
    w_bs5 = din("w_bs5", [DEPTH, 512, D])
    w_bhy = din("w_bhy", [DEPTH, D, D])
    w_out = din("w_out", [DEPTH, D, D])
    convw = din("convw", [DEPTH, 128, 3, 24])
    convb = din("convb", [DEPTH, 128, 24])
    hyd = din("hyd", [DEPTH, 128, 2, 8])
    s5lam = din("s5lam", [DEPTH, 128, 3, 32])
    s5B = din("s5B", [DEPTH, 128, 2, 32, 16])
    s5C = din("s5C", [DEPTH, 128, 2, 32, 16])
    kvt = din("kvt", [128, 2, 32])
    masks = din("masks", [128, 256])
    ident = din("ident", [128, 128])
    hw1 = din("hw1", [DEPTH, 33, 64])
    hw2 = din("hw2", [DEPTH, 64, 64])
    hw3 = din("hw3", [DEPTH, 64, 4096])
    hvec = din("hvec", [DEPTH, 64, 3])
    hdec = din("hdec", [DEPTH, 128, 32])
    feats = din("feats", [2, 33, L])
    ttab = din("ttab", [2, 128, L])
    fsmall = din("fsmall", [128, 896], BF16)
    fT = din("fT", [128, 8192], BF16)
    fH = din("fH", [128, 16384], BF16)

    outT = nc.dram_tensor("outT", [8, 128, L], F32, kind="ExternalOutput").ap()

    hbuf = dscr("hbuf", [8, 128, L])
    ud = dscr("ud", [4, 128, L], BF16)
    yd = dscr("yd", [4, 128, L], BF16)
    ys5d = dscr("ys5d", [8, 128, L], BF16)
    yhyd = dscr("yhyd", [8, 128, L], BF16)
    binD = dscr("binD", [128, 32 * 2 * 2 * 64], BF16)
    coutD = dscr("coutD", [128, 2 * 16 * 2 * 128], BF16)
    mgD = dscr("mgD", [128, 32 * 128], BF16)
    coefD = dscr("coefD", [128, 3 * 2 * 16 * 8])
    khatD = dscr("khatD", [2, 8, 2, 128, 4096], BF16)

    _uid = [0]

    def SBT(name, shape, dt):
        _uid[0] += 1
        return nc.sbuf_tensor("%s_%d" % (name, _uid[0]), shape, dt)

    es0 = ExitStack()
    with es0:
        kb = KB(nc, es0)
        E0 = es0.enter_context
        psum = E0(nc.psum_tensor("psum", [128, 8, 512], F32))
        barscr = E0(SBT("barscr", [128, 8], F32))
        G = {}

        def PS(b, n=512, p0=0, p1=128, off=0):
            return psum[p0:p1, b, off:off + n]

        def PS4(g):
            return psum[:, 4 * g:4 * g + 4, :].rearrange("p b n -> p (b n)")

        def pst(b):
            return 'ps%d' % b

        def pst4(g):
            return tuple('ps%d' % (4 * g + i) for i in range(4))

        def range_reduce(eng, x_ap, tmp_ap, rd, wr_tmp):
            kb.op(eng, lambda v: v.tensor_scalar(out=tmp_ap, in0=x_ap, scalar1=float(1.0 / TWO_PI), scalar2=MAGIC,
                                                 op0=ALU.mult, op1=ALU.add), reads=rd, writes=wr_tmp)
            kb.op(eng, lambda v: v.tensor_scalar(out=tmp_ap, in0=tmp_ap, scalar1=-MAGIC, scalar2=None, op0=ALU.add),
                  reads=wr_tmp, writes=wr_tmp)
            kb.op(eng, lambda v: v.scalar_tensor_tensor(out=x_ap, in0=tmp_ap, scalar=float(-TWO_PI), in1=x_ap,
                                                        op0=ALU.mult, op1=ALU.add), reads=wr_tmp + rd, writes=rd)
            kb.op(eng, lambda v: v.tensor_scalar(out=x_ap, in0=x_ap, scalar1=3.14159, scalar2=-3.14159,
                                                 op0=ALU.min, op1=ALU.max), reads=rd, writes=rd)

        def phase_norm(src, wrow, dst_xn, dst_out):
            with ExitStack() as es:
                E = es.enter_context
                h = E(SBT("n_h", [128, 8, L], F32))
                sq = E(SBT("n_sq", [128, 2, 8, 512], BF16))
                ones = E(SBT("n_ones", [128, 128], BF16))
                nw = E(SBT("n_w", [128, 8], F32))
                rt = E(SBT("n_rt", [128, 2, 512], F32))
                rstd = E(SBT("n_rstd", [128, L], F32))
                epsb = E(SBT("n_eps", [128, 1], F32))
                ob = E(SBT("n_ob", [128, 2, L], F32)) if dst_out is not None else None
                kb.op('pool', lambda g: g.memset(ones[:], 1.0), writes=('ones',))
                kb.op('pool', lambda g: g.memset(epsb[:], 1e-6), writes=('epsb',))
                kb.dma('sp', nw[:], normw[wrow], writes=('nw',))
                for c in range(8):
                    kb.dma('sp' if c % 2 == 0 else 'sp', h[:, c, :], src[c], writes=('h%d' % c,))
                for tt in range(4):
                    ts = slice(tt * 512, (tt + 1) * 512)
                    s = tt % 2
                    kb.op('act', lambda a: a.activation(out=sq[:, s], in_=h[:, :, ts], func=AF.Square),
                          reads=tuple('h%d' % c for c in range(8)), writes=('sq%d' % s,))
                    b = kb.ps()
                    for c in range(8):
                        kb.op('pe', lambda p: p.matmul(PS(b), ones[:], sq[:, s, c, :], start=(c == 0), stop=(c == 7)),
                              reads=('ones', 'sq%d' % s), writes=(pst(b),), inc=(c == 7))
                    kb.op('act', lambda a: a.activation(out=rt[:, s, :], in_=PS(b), func=AF.Sqrt, bias=epsb[:, 0:1],
                                                        scale=float(1.0 / D)),
                          reads=(pst(b), 'epsb'), writes=('rt%d' % s,))
                    kb.op('dve', lambda v: v.reciprocal(out=rstd[:, ts], in_=rt[:, s, :]), reads=('rt%d' % s,),
                          writes=('rstd%d' % tt,))
                for c in range(8):
                    if dst_out is None:
                        kb.op('dve', lambda v: v.scalar_tensor_tensor(out=dst_xn[:, c, :], in0=h[:, c, :], scalar=nw[:, c:c + 1],
                                                                      in1=rstd[:], op0=ALU.mult, op1=ALU.mult),
                              reads=('h%d' % c, 'nw') + tuple('rstd%d' % t for t in range(4)), writes=('xn%d' % c,))
                    else:
                        s = c % 2
                        kb.op('dve', lambda v: v.scalar_tensor_tensor(out=ob[:, s, :], in0=h[:, c, :], scalar=nw[:, c:c + 1],
                                                                      in1=rstd[:], op0=ALU.mult, op1=ALU.mult),
                              reads=('h%d' % c, 'nw') + tuple('rstd%d' % t for t in range(4)), writes=('ob%d' % s,))
                        kb.dma('sp', dst_out[c], ob[:, s, :], reads=('ob%d' % s,), writes=('out%d' % c,))
                kb.barrier(barscr[:, 0:1])

        def load_w(es, name, src_rows_ap, kc, ncols, q='sp'):
            E = es.enter_context
            st = E(SBT(name + "_st", [128, kc, ncols], F32))
            wb = E(SBT(name + "_bf", [128, kc, ncols], BF16))
            for k in range(kc):
                kb.dma(q if k % 2 == 0 else 'sp', st[:, k, :], src_rows_ap[k * 128:(k + 1) * 128, :], writes=(name + '_st%d' % k,))
                kb.op('act', lambda a: a.activation(out=wb[:, k, :], in_=st[:, k, :], func=AF.Identity), reads=(name + '_st%d' % k,),
                      writes=(name + '_bf',))
            return wb

        def prep_s5(l):
            with ExitStack() as es:
                E = es.enter_context
                lam = E(SBT("p_lam", [128, 3, 32], F32))
                Bt = E(SBT("p_B", [128, 2, 32, 16], F32))
                Ct = E(SBT("p_C", [128, 2, 32, 16], F32))
                kvs = E(SBT("p_kv", [128, 2, 32], F32))
                msk = E(SBT("p_msk", [128, 256], F32))
                idt = E(SBT("p_id", [128, 128], F32))
                a_re = E(SBT("p_are", [128, 32], F32))
                a_im = E(SBT("p_aim", [128, 32], F32))
                dtt = E(SBT("p_dt", [128, 32], F32))
                mag = E(SBT("p_mag", [128, 32, 32], F32))
                sn = E(SBT("p_sn", [128, 32, 32], F32))
                cs = E(SBT("p_cs", [128, 32, 32], F32))
                tmp = E(SBT("p_tmp", [128, 32, 32], F32))
                Er = E(SBT("p_Er", [128, 32, 32], F32))
                Ei = E(SBT("p_Ei", [128, 32, 32], F32))
                k4 = E(SBT("p_k4", [128, 8, 32], F32))
                Bb = E(SBT("p_Bb", [128, 2, 32, 16], F32))
                t16 = E(SBT("p_t16", [128, 2, 32, 16], F32))
                RB = E(SBT("p_RB", [128, 2, 32, 128], F32))
                CO = E(SBT("p_CO", [128, 2, 32, 128], F32))
                tb = E(SBT("p_tb", [128, 32, 128], F32))
                binS = E(SBT("p_bin", [128, 32, 2, 2, 64], BF16))
                coutS = E(SBT("p_cout", [128, 32, 2, 128], BF16))
                mgS = E(SBT("p_mg", [128, 32, 128], BF16))
                coefS = E(SBT("p_coef", [128, 3, 32, 8], F32))
                mt = E(SBT("p_mt", [128, 2, 256], F32))
                kb.dma('sp', lam[:], s5lam[l], writes=('lam',))
                kb.dma('sp', Bt[:], s5B[l], writes=('Bt',))
                kb.dma('sp', Ct[:], s5C[l], writes=('Ct',))
                kb.dma('sp', kvs[:], kvt, writes=('kvs',))
                kb.dma('sp', msk[:], masks, writes=('msk',))
                kb.dma('sp', idt[:], ident, writes=('idt',))
                V = lambda fn, r, w: kb.op('dve', fn, reads=r, writes=w)
                A = lambda fn, r, w: kb.op('act', fn, reads=r, writes=w)
                A(lambda a: a.activation(out=dtt[:], in_=lam[:, 2, :], func=AF.Exp), ('lam',), ('dtt',))
                V(lambda v: v.tensor_tensor(out=a_re[:], in0=lam[:, 0, :], in1=dtt[:], op=ALU.mult), ('lam', 'dtt'), ('a_re',))
                V(lambda v: v.tensor_tensor(out=a_im[:], in0=lam[:, 1, :], in1=dtt[:], op=ALU.mult), ('lam', 'dtt'), ('a_im',))
                kvb = kvs[:].rearrange("p d (o k) -> p d o k", o=1).to_broadcast([128, 2, 16, 32])
                are_b = a_re[:].rearrange("p (d g o) -> p d g o", d=2, o=1).to_broadcast([128, 2, 16, 32])
                aim_b = a_im[:].rearrange("p (d g o) -> p d g o", d=2, o=1).to_broadcast([128, 2, 16, 32])
                v4 = lambda t: t[:].rearrange("p (d g) k -> p d g k", d=2)
                V(lambda v: v.tensor_tensor(out=v4(tmp), in0=are_b, in1=kvb, op=ALU.mult), ('a_re', 'kvs'), ('tmp',))
                A(lambda a: a.activation(out=mag[:], in_=tmp[:], func=AF.Exp), ('tmp',), ('mag',))
                V(lambda v: v.tensor_tensor(out=v4(sn), in0=aim_b, in1=kvb, op=ALU.mult), ('a_im', 'kvs'), ('sn',))
                V(lambda v: v.tensor_scalar(out=cs[:], in0=sn[:], scalar1=float(math.pi / 2), scalar2=None, op0=ALU.add), ('sn',), ('cs',))
                range_reduce('dve', sn[:], tmp[:], ('sn',), ('tmp',))
                A(lambda a: a.activation(out=sn[:], in_=sn[:], func=AF.Sin), ('sn',), ('sn',))
                range_reduce('dve', cs[:], tmp[:], ('cs',), ('tmp',))
                A(lambda a: a.activation(out=cs[:], in_=cs[:], func=AF.Sin), ('cs',), ('cs',))
                V(lambda v: v.tensor_tensor(out=Er[:], in0=mag[:], in1=cs[:], op=ALU.mult), ('mag', 'cs'), ('Er',))
                V(lambda v: v.tensor_tensor(out=Ei[:], in0=mag[:], in1=sn[:], op=ALU.mult), ('mag', 'sn'), ('Ei',))
                lr = k4[:, 0, :]; li = k4[:, 1, :]; nr = k4[:, 2, :]; den = k4[:, 3, :]; kr = k4[:, 4, :]; ki = k4[:, 5, :]; t0 = k4[:, 6, :]
                for d in range(2):
                    idx = 8 if d == 0 else 15
                    V(lambda v: v.tensor_copy(out=lr[:, 16 * d:16 * d + 16], in_=Er[:, 16 * d:16 * d + 16, idx]), ('Er',), ('k4',))
                    V(lambda v: v.tensor_copy(out=li[:, 16 * d:16 * d + 16], in_=Ei[:, 16 * d:16 * d + 16, idx]), ('Ei',), ('k4',))
                V(lambda v: v.tensor_scalar(out=nr, in0=lr, scalar1=-1.0, scalar2=None, op0=ALU.add), ('k4',), ('k4',))
                V(lambda v: v.tensor_tensor(out=den, in0=lam[:, 0, :], in1=lam[:, 0, :], op=ALU.mult), ('lam', 'k4'), ('k4',))
                V(lambda v: v.tensor_tensor(out=t0, in0=lam[:, 1, :], in1=lam[:, 1, :], op=ALU.mult), ('lam', 'k4'), ('k4',))
                V(lambda v: v.tensor_tensor(out=den, in0=den, in1=t0, op=ALU.add), ('k4',), ('k4',))
                V(lambda v: v.reciprocal(out=den, in_=den), ('k4',), ('k4',))
                V(lambda v: v.tensor_tensor(out=kr, in0=nr, in1=lam[:, 0, :], op=ALU.mult), ('k4', 'lam'), ('k4',))
                V(lambda v: v.tensor_tensor(out=t0, in0=li, in1=lam[:, 1, :], op=ALU.mult), ('k4', 'lam'), ('k4',))
                V(lambda v: v.tensor_tensor(out=kr, in0=kr, in1=t0, op=ALU.add), ('k4',), ('k4',))
                V(lambda v: v.tensor_tensor(out=kr, in0=kr, in1=den, op=ALU.mult), ('k4',), ('k4',))
                V(lambda v: v.tensor_tensor(out=ki, in0=li, in1=lam[:, 0, :], op=ALU.mult), ('k4', 'lam'), ('k4',))
                V(lambda v: v.tensor_tensor(out=t0, in0=nr, in1=lam[:, 1, :], op=ALU.mult), ('k4', 'lam'), ('k4',))
                V(lambda v: v.tensor_tensor(out=ki, in0=ki, in1=t0, op=ALU.subtract), ('k4',), ('k4',))
                V(lambda v: v.tensor_tensor(out=ki, in0=ki, in1=den, op=ALU.mult), ('k4',), ('k4',))
                krb = kr.rearrange("p (g o) -> p g o", o=1).to_broadcast([128, 32, 16])
                kib = ki.rearrange("p (g o) -> p g o", o=1).to_broadcast([128, 32, 16])
                V(lambda v: v.tensor_tensor(out=Bb[:, 0], in0=Bt[:, 0], in1=krb, op=ALU.mult), ('Bt', 'k4'), ('Bb',))
                V(lambda v: v.tensor_tensor(out=t16[:, 0], in0=Bt[:, 1], in1=kib, op=ALU.mult), ('Bt', 'k4'), ('t16',))
                V(lambda v: v.tensor_tensor(out=Bb[:, 0], in0=Bb[:, 0], in1=t16[:, 0], op=ALU.subtract), ('Bb', 't16'), ('Bb',))
                V(lambda v: v.tensor_tensor(out=Bb[:, 1], in0=Bt[:, 1], in1=krb, op=ALU.mult), ('Bt', 'k4', 'Bb'), ('Bb',))
                V(lambda v: v.tensor_tensor(out=t16[:, 1], in0=Bt[:, 0], in1=kib, op=ALU.mult), ('Bt', 'k4', 't16'), ('t16',))
                V(lambda v: v.tensor_tensor(out=Bb[:, 1], in0=Bb[:, 1], in1=t16[:, 1], op=ALU.add), ('Bb', 't16'), ('Bb',))

                def cprod(dst, k0, X, sign_im, tag):
                    Erb = Er[:, :, k0:k0 + 8].rearrange("p g (k o) -> p g k o", o=1).to_broadcast([128, 32, 8, 16])
                    Eib = Ei[:, :, k0:k0 + 8].rearrange("p g (k o) -> p g k o", o=1).to_broadcast([128, 32, 8, 16])
                    Xr = X[:, 0].rearrange("p g (o h) -> p g o h", o=1).to_broadcast([128, 32, 8, 16])
                    Xi = X[:, 1].rearrange("p g (o h) -> p g o h", o=1).to_broadcast([128, 32, 8, 16])
                    d0 = dst[:, 0].rearrange("p g (k h) -> p g k h", k=8)
                    d1 = dst[:, 1].rearrange("p g (k h) -> p g k h", k=8)
                    tv = tb[:].rearrange("p g (k h) -> p g k h", k=8)
                    rd = ('Er', 'Ei', tag)
                    V(lambda v: v.tensor_tensor(out=d0, in0=Erb, in1=Xr, op=ALU.mult), rd, (tag + 'o',))
                    V(lambda v: v.tensor_tensor(out=tv, in0=Eib, in1=Xi, op=ALU.mult), rd, ('tb',))
                    V(lambda v: v.tensor_tensor(out=d0, in0=d0, in1=tv, op=ALU.subtract), (tag + 'o', 'tb'), (tag + 'o',))
                    V(lambda v: v.tensor_tensor(out=d1, in0=Erb, in1=Xi, op=ALU.mult), rd + (tag + 'o',), (tag + 'o',))
                    V(lambda v: v.tensor_tensor(out=tv, in0=Eib, in1=Xr, op=ALU.mult), rd + ('tb',), ('tb',))
                    V(lambda v: v.tensor_tensor(out=d1, in0=d1, in1=tv, op=ALU.add), (tag + 'o', 'tb'), (tag + 'o',))
                    if sign_im < 0:
                        V(lambda v: v.tensor_scalar(out=dst[:, 1], in0=dst[:, 1], scalar1=-1.0, scalar2=None, op0=ALU.mult),
                          (tag + 'o',), (tag + 'o',))
                cprod(RB, 0, Bb, +1, 'Bb')
                cprod(CO, 8, Ct, -1, 'Ct')
                for ri in range(2):
                    A(lambda a: a.activation(out=coutS[:, :, ri, :], in_=CO[:, ri], func=AF.Identity), ('Cto',), ('coutS',))
                kb.dma('sp', coutD, coutS[:].rearrange("p a b c -> p (a b c)"), reads=('coutS',), writes=('coutD',))
                cprod(CO, 16, Ct, -1, 'Ct')
                RC = CO
                A(lambda a: a.activation(out=coefS[:, 0], in_=Er[:, :, 24:32], func=AF.Identity), ('Er',), ('coefS',))
                A(lambda a: a.activation(out=coefS[:, 1], in_=Ei[:, :, 24:32], func=AF.Identity), ('Ei',), ('coefS',))
                A(lambda a: a.activation(out=coefS[:, 2], in_=Ei[:, :, 24:32], func=AF.Identity, scale=-1.0), ('Ei',), ('coefS',))
                kb.dma('sp', coefD, coefS[:].rearrange("p a b c -> p (a b c)"), reads=('coefS',), writes=('coefD',))
                for dg in range(32):
                    d, gp = dg // 16, dg % 16
                    b = kb.ps()
                    for ri in range(2):
                        kb.op('pe', lambda p: p.transpose(PS(b, 128, off=128 * ri), RB[:, ri, dg, :], idt[:]),
                              reads=('Bbo', 'idt'), writes=(pst(b),), inc=(ri == 1))
                    for ri in range(2):
                        A(lambda a: a.activation(out=binS[:, 2 * gp:2 * gp + 2, d, ri, :],
                                                 in_=PS(b, 128, off=128 * ri).rearrange("p (g q) -> p g q", g=2), func=AF.Identity),
                          (pst(b),), ('binS',))
                kb.dma('sp', binD, binS[:].rearrange("p a b c e -> p (a b c e)"), reads=('binS',), writes=('binD',))
                for g in range(32):
                    gp, gpar = g // 2, g % 2
                    b = kb.ps()
                    p0, p1 = 64 * gpar, 64 * gpar + 64
                    for d in range(2):
                        dg = 16 * d + gp
                        kb.op('pe', lambda p: p.matmul(PS(b, 128, off=128 * d), RB[p0:p1, 0, dg, :], RC[p0:p1, 0, dg, :],
                                                       start=True, stop=False),
                              reads=('Bbo', 'Cto'), writes=(pst(b),), inc=False)
                        kb.op('pe', lambda p: p.matmul(PS(b, 128, off=128 * d), RB[p0:p1, 1, dg, :], RC[p0:p1, 1, dg, :],
                                                       start=False, stop=True),
                              reads=('Bbo', 'Cto'), writes=(pst(b),), inc=(d == 1))
                    s = g % 2
                    V(lambda v: v.tensor_tensor(out=mt[:, s, :], in0=PS(b, 256), in1=msk[:], op=ALU.mult), (pst(b), 'msk'), ('mt%d' % s,))
                    V(lambda v: v.tensor_tensor(out=mgS[:, g, :], in0=mt[:, s, 0:128], in1=mt[:, s, 128:256], op=ALU.add),
                      ('mt%d' % s,), ('mgS',))
                kb.dma('sp', mgD, mgS[:].rearrange("p a b -> p (a b)"), reads=('mgS',), writes=('mgD',))
                kb.barrier(barscr[:, 0:1])

        def phase_s5(l):
            with ExitStack() as es_outer:
                EO = es_outer.enter_context
                gs = EO(SBT("s_gs", [128, 4, L], BF16))
                with ExitStack() as es:
                    E = es.enter_context
                    wb = load_w(es, "s_w", w_in[l][:, 0:1024], 8, 1024)
                    udt = E(SBT("s_ud", [128, 4, 8, 256], BF16))
                    for cc in range(8):
                        for tt in range(4):
                            b = kb.ps()
                            for k in range(8):
                                kb.op('pe', lambda p: p.matmul(PS(b), wb[:, k, cc * 128:(cc + 1) * 128], G['xn'][:, k, tt * 512:(tt + 1) * 512],
                                                               start=(k == 0), stop=(k == 7)),
                                      reads=('s_w_bf', 'xn%d' % k), writes=(pst(b),), inc=(k == 7))
                            if cc < 4:
                                kb.op('act', lambda a: a.activation(out=udt[:, cc, :, tt * 64:(tt + 1) * 64],
                                                                    in_=PS(b).rearrange("p (c j) -> p j c", j=8), func=AF.Identity),
                                      reads=(pst(b),), writes=('udt%d' % cc,))
                            else:
                                kb.op('act', lambda a: a.activation(out=gs[:, cc - 4, tt * 512:(tt + 1) * 512], in_=PS(b), func=AF.Silu),
                                      reads=(pst(b),), writes=('gs',))
                        if cc < 4:
                            kb.dma('sp', ud[cc], udt[:, cc].rearrange("p j c -> p (j c)"), reads=('udt%d' % cc,), writes=('ud',))
                    kb.barrier(barscr[:, 0:1])
                with ExitStack() as es:
                    E = es.enter_context
                    U8 = E(SBT("s_U8", [128, 32, 256], BF16))
                    Mg = E(SBT("s_Mg", [128, 32, 128], BF16))
                    Bin = E(SBT("s_Bin", [128, 32, 2, 2, 64], BF16))
                    Cout = E(SBT("s_Cout", [128, 32, 2, 128], BF16))
                    coef = E(SBT("s_coef", [128, 3, 32, 8], F32))
                    Xs = E(SBT("s_Xs", [128, 32, 2, 256], BF16))
                    Y8 = E(SBT("s_Y8", [128, 32, 256], BF16))
                    NSL = 2
                    XA = E(SBT("s_XA", [128, NSL, 2, 768], F32))
                    XB = E(SBT("s_XB", [128, NSL, 2, 768], F32))
                    T1 = E(SBT("s_T1", [128, NSL, 2, 256], F32))
                    udv = ud.rearrange("cc (g h) (j c) -> h j (cc g) c", h=16, j=8)
                    for j in range(8):
                        kb.dma('sp' if j % 2 == 0 else 'sp', U8[16 * j:16 * j + 16, :, :], udv[:, j], reads=('ud',), writes=('U8',))
                    kb.dma('sp', Mg[:].rearrange("p a b -> p (a b)"), mgD, reads=('mgD',), writes=('Mg',))
                    kb.dma('sp', Bin[:].rearrange("p a b c e -> p (a b c e)"), binD, reads=('binD',), writes=('Bin',))
                    kb.dma('sp', Cout[:].rearrange("p a b c -> p (a b c)"), coutD, reads=('coutD',), writes=('Cout',))
                    kb.dma('sp', coef[:].rearrange("p a b c -> p (a b c)"), coefD, reads=('coefD',), writes=('coef',))
                    kb.op('pool', lambda g: g.memset(XA[:], 0.0), writes=tuple('XA%d' % i for i in range(NSL)))
                    kb.op('pool', lambda g: g.memset(XB[:], 0.0), writes=tuple('XB%d' % i for i in range(NSL)))
                    for s0 in range(0, 32, NSL):
                        for i in range(NSL):
                            dg = s0 + i
                            d, gp = dg // 16, dg % 16
                            b = kb.ps()
                            for ri in range(2):
                                for gpar in range(2):
                                    g = 2 * gp + gpar
                                    kb.op('pe', lambda p: p.matmul(PS(b, 256, 64 * gpar, 64 * gpar + 64, off=256 * ri),
                                                                   Bin[:, g, d, ri, :], U8[:, g, :], start=True, stop=True),
                                          reads=('Bin', 'U8'), writes=(pst(b),), inc=(ri == 1 and gpar == 1))
                            kb.op('act', lambda a: a.activation(out=XA[:, i, :, 256:512], in_=PS(b).rearrange("p (r c) -> p r c", r=2),
                                                                func=AF.Identity),
                                  reads=(pst(b),), writes=('XA%d' % i,))
                        for r in range(8):
                            sft = 2 ** r
                            for i in range(NSL):
                                dg = s0 + i
                                d = dg // 16
                                src, dst = (XA, XB) if r % 2 == 0 else (XB, XA)
                                sn_, dn_ = ('XA%d' % i, 'XB%d' % i) if r % 2 == 0 else ('XB%d' % i, 'XA%d' % i)
                                lo = 256 - sft if d == 0 else 256 + sft
                                e_ = coef[:, 0, dg, r:r + 1]; f_ = coef[:, 1, dg, r:r + 1]; nf_ = coef[:, 2, dg, r:r + 1]
                                Rs = src[:, i, 0, lo:lo + 256]; Is = src[:, i, 1, lo:lo + 256]
                                R0 = src[:, i, 0, 256:512]; I0 = src[:, i, 1, 256:512]
                                tn = 'T1_%d' % i
                                kb.op('dve', lambda v: v.scalar_tensor_tensor(out=T1[:, i, 0, :], in0=Rs, scalar=e_, in1=R0, op0=ALU.mult, op1=ALU.add),
                                      reads=(sn_, 'coef'), writes=(tn + 'a',))
                                kb.op('dve', lambda v: v.scalar_tensor_tensor(out=dst[:, i, 0, 256:512], in0=Is, scalar=nf_, in1=T1[:, i, 0, :],
                                                                              op0=ALU.mult, op1=ALU.add),
                                      reads=(sn_, 'coef', tn + 'a'), writes=(dn_ + 'r',))
                                kb.op('dve', lambda v: v.scalar_tensor_tensor(out=T1[:, i, 1, :], in0=Rs, scalar=f_, in1=I0, op0=ALU.mult, op1=ALU.add),
                                      reads=(sn_, 'coef'), writes=(tn + 'b',))
                                kb.op('dve', lambda v: v.scalar_tensor_tensor(out=dst[:, i, 1, 256:512], in0=Is, scalar=e_, in1=T1[:, i, 1, :],
                                                                              op0=ALU.mult, op1=ALU.add),
                                      reads=(sn_, 'coef', tn + 'b'), writes=(dn_, dn_ + 'r'))
                        for i in range(NSL):
                            dg = s0 + i
                            d = dg // 16
                            lo = 255 if d == 0 else 257
                            kb.op('act', lambda a: a.activation(out=Xs[:, dg, :, :], in_=XA[:, i, :, lo:lo + 256], func=AF.Identity),
                                  reads=('XA%d' % i, 'XA%dr' % i), writes=('Xs',))
                    for g in range(32):
                        gp, gpar = g // 2, g % 2
                        p0, p1 = 64 * gpar, 64 * gpar + 64
                        b = kb.ps()
                        kb.op('pe', lambda p: p.matmul(PS(b, 256), Mg[:, g, :], U8[:, g, :], start=True, stop=False),
                              reads=('Mg', 'U8'), writes=(pst(b),), inc=False)
                        for d in range(2):
                            for ri in range(2):
                                last = (d == 1 and ri == 1)
                                kb.op('pe', lambda p: p.matmul(PS(b, 256), Cout[p0:p1, 16 * d + gp, ri, :], Xs[p0:p1, 16 * d + gp, ri, :],
                                                               start=False, stop=last),
                                      reads=('Cout', 'Xs'), writes=(pst(b),), inc=last)
                        kb.op('act', lambda a: a.activation(out=Y8[:, g, :], in_=PS(b, 256), func=AF.Identity), reads=(pst(b),), writes=('Y8',))
                    ydv = yd.rearrange("cc (g h) (i c) -> h i (cc g) c", h=16, i=8)
                    for i in range(8):
                        kb.dma('sp' if i % 2 == 0 else 'sp', ydv[:, i], Y8[16 * i:16 * i + 16, :, :], reads=('Y8',), writes=('yd',))
                    kb.barrier(barscr[:, 0:1])
                with ExitStack() as es:
                    E = es.enter_context
                    wg = E(SBT("c_wg", [128, 4, 512], BF16))
                    wbr = E(SBT("c_wbr", [128, 4, 1024], BF16))
                    wst = E(SBT("c_wst", [128, 2, 1024], F32))
                    yt = E(SBT("c_y", [128, L], BF16))
                    ut = E(SBT("c_u", [128, L], BF16))
                    y1 = E(SBT("c_y1", [128, L], F32))
                    t3 = E(SBT("c_t3", [128, L], F32))
                    yg = E(SBT("c_yg", [128, 4, L], BF16))
                    sg = E(SBT("c_sg", [128, 2, 512], F32))
                    y3 = E(SBT("c_y3", [128, 4, L], BF16))
                    ys = E(SBT("c_ys", [128, L], BF16))
                    dv = E(SBT("c_d", [128, 4], F32))
                    bg = E(SBT("c_bg", [128, 4], F32))
                    kb.dma('sp', dv[:], s5d[l], writes=('dv',))
                    kb.dma('sp', bg[:], b_glu[l], writes=('bg',))
                    for k in range(4):
                        s = k % 2
                        kb.dma('sp', wst[:, s, 0:512], w_glu[l][k * 128:(k + 1) * 128, :], writes=('wst%d' % s,))
                        kb.op('act', lambda a: a.activation(out=wg[:, k, :], in_=wst[:, s, 0:512], func=AF.Identity), reads=('wst%d' % s,), writes=('wg',))
                    for k in range(4):
                        s = k % 2
                        kb.dma('sp', wst[:, s, :], w_bs5[l][k * 128:(k + 1) * 128, :], writes=('wst%d' % s,))
                        kb.op('act', lambda a: a.activation(out=wbr[:, k, :], in_=wst[:, s, :], func=AF.Identity), reads=('wst%d' % s,), writes=('wbr',))
                    for cc in range(4):
                        kb.dma('sp', yt[:], yd[cc], reads=('yd',), writes=('yt',))
                        kb.dma('sp', ut[:], ud[cc], reads=('ud',), writes=('ut',))
                        kb.op('dve', lambda v: v.scalar_tensor_tensor(out=y1[:], in0=ut[:], scalar=dv[:, cc:cc + 1], in1=yt[:],
                                                                      op0=ALU.mult, op1=ALU.add),
                              reads=('ut', 'yt', 'dv'), writes=('y1',))
                        kb.op('dve', lambda v: v.tensor_tensor(out=t3[:], in0=y1[:], in1=y1[:], op=ALU.mult), reads=('y1',), writes=('t3',))
                        kb.op('dve', lambda v: v.tensor_scalar(out=t3[:], in0=t3[:], scalar1=0.044715 * 1.5957691216, scalar2=1.5957691216,
                                                               op0=ALU.mult, op1=ALU.add), reads=('t3',), writes=('t3',))
                        kb.op('dve', lambda v: v.tensor_tensor(out=t3[:], in0=t3[:], in1=y1[:], op=ALU.mult), reads=('t3', 'y1'), writes=('t3',))
                        kb.op('act', lambda a: a.activation(out=t3[:], in_=t3[:], func=AF.Sigmoid), reads=('t3',), writes=('t3',))
                        kb.op('dve', lambda v: v.tensor_tensor(out=yg[:, cc, :], in0=t3[:], in1=y1[:], op=ALU.mult), reads=('t3', 'y1'), writes=('yg',))
                    gsp = gs[:].rearrange("p a (c j) -> p a j c", j=8)
                    for cc in range(4):
                        for tt in range(4):
                            b = kb.ps()
                            for k in range(4):
                                kb.op('pe', lambda p: p.matmul(PS(b), wg[:, k, cc * 128:(cc + 1) * 128], yg[:, k, tt * 512:(tt + 1) * 512],
                                                               start=(k == 0), stop=(k == 3)),
                                      reads=('wg', 'yg'), writes=(pst(b),), inc=(k == 3))
                            s = tt % 2
                            kb.op('act', lambda a: a.activation(out=sg[:, s, :], in_=PS(b), func=AF.Sigmoid, bias=bg[:, cc:cc + 1]),
                                  reads=(pst(b), 'bg'), writes=('sg%d' % s,))
                            kb.op('dve', lambda v: v.tensor_tensor(out=sg[:, s, :], in0=sg[:, s, :], in1=yg[:, cc, tt * 512:(tt + 1) * 512], op=ALU.mult),
                                  reads=('sg%d' % s, 'yg'), writes=('sg%d' % s,))
                            kb.op('dve', lambda v: v.tensor_tensor(out=y3[:, cc, tt * 512:(tt + 1) * 512].rearrange("p (j c) -> p j c", j=2),
                                                                   in0=sg[:, s, :].rearrange("p (j c) -> p j c", j=2),
                                                                   in1=gsp[:, cc, 2 * tt:2 * tt + 2, :], op=ALU.mult),
                                  reads=('sg%d' % s, 'gs'), writes=('y3',))
                    for dc in range(8):
                        for tt in range(4):
                            b = kb.ps()
                            for k in range(4):
                                kb.op('pe', lambda p: p.matmul(PS(b), wbr[:, k, dc * 128:(dc + 1) * 128], y3[:, k, tt * 512:(tt + 1) * 512],
                                                               start=(k == 0), stop=(k == 3)),
                                      reads=('wbr', 'y3'), writes=(pst(b),), inc=(k == 3))
                            kb.op('act', lambda a: a.activation(out=ys[:].rearrange("p (c j) -> p j c", j=8)[:, 2 * tt:2 * tt + 2, :],
                                                                in_=PS(b).rearrange("p (j c) -> p j c", j=2), func=AF.Identity),
                                  reads=(pst(b),), writes=('ys',))
                        kb.dma('sp', ys5d[dc], ys[:], reads=('ys',), writes=('ys5d',))
                    kb.barrier(barscr[:, 0:1])

        def PS2(b0):
            return psum[:, b0:b0 + 2, :].rearrange("p b n -> p (b n)")

        def pst2(b0):
            return ('ps%d' % b0, 'ps%d' % (b0 + 1))

        def next_pair():
            p_ = kb.fp % 3
            kb.fp += 1
            return 2 * p_

        def fft_fwd(C, zin_list, B, spec_evac, filler=lambda: None):
            for q in range(4):
                b0 = next_pair()
                for cpl in range(4):
                    cp = q * 4 + cpl
                    bank = b0 + cpl // 2
                    off = 256 * (cpl % 2)
                    for zi, (zT, ztok, gk) in enumerate(zin_list):
                        kb.op('pe', lambda p: p.matmul(PS(bank, 256, off=off), zT[:, 2 * cp:2 * cp + 2, :].rearrange("p a b -> p (a b)"),
                                                       C['G1hi'] if gk else C['G1'], start=(zi == 0), stop=(zi == len(zin_list) - 1)),
                              reads=(ztok,) + C['toks'], writes=(pst(bank),), inc=(zi == len(zin_list) - 1))
                kb.op('act', lambda a: a.activation(
                    out=B[:].rearrange("p k r cp cq -> p (k r) cp cq")[:, :, q * 4:q * 4 + 4, :],
                    in_=PS2(b0).rearrange("p (cp cq kr) -> p kr cp cq", cp=4, cq=4), func=AF.Identity),
                    reads=pst2(b0), writes=('B',))
                if q % 2 == 1:
                    filler()
            for c2 in range(2):
                for h in range(2):
                    b0 = next_pair()
                    for k1l in range(16):
                        k1 = 16 * h + k1l
                        bank = b0 + k1l // 8
                        off = 64 * (k1l % 8)
                        for ri in range(2):
                            kb.op('pe', lambda p: p.matmul(PS(bank, 64, off=off), C['T'][64 * c2:64 * c2 + 64, k1, ri, :],
                                                           B[64 * c2:64 * c2 + 64, k1, ri].rearrange("p a b -> p (a b)"),
                                                           start=(ri == 0), stop=(ri == 1)),
                                  reads=('B',) + C['toks'], writes=(pst(bank),), inc=(ri == 1))
                    spec_evac(c2, h, PS2(b0).rearrange("p (k m) -> p m k", k=16), pst2(b0))
                filler()

        def fft_inv(C, P1, P2, Dt, conv_out, conv_tok, filler=lambda: None):
            for c2 in range(2):
                for h in range(2):
                    b0 = next_pair()
                    for cpl in range(8):
                        cp = 8 * h + cpl
                        bank = b0 + cpl // 4
                        off = 128 * (cpl % 4)
                        kb.op('pe', lambda p: p.matmul(PS(bank, 128, off=off), P1[:, c2, cp * 128:(cp + 1) * 128], C['Ga'], start=True, stop=False),
                              reads=('P1',) + C['toks'], writes=(pst(bank),), inc=False)
                        kb.op('pe', lambda p: p.matmul(PS(bank, 128, off=off), P2[:, c2, cp * 128:(cp + 1) * 128], C['Gb'], start=False, stop=True),
                              reads=('P2',) + C['toks'], writes=(pst(bank),), inc=True)
                    kb.op('act', lambda a: a.activation(
                        out=Dt[:, c2].rearrange("p n r cp -> p (n r) cp")[:, :, 8 * h:8 * h + 8],
                        in_=PS2(b0).rearrange("p (cp nr) -> p nr cp", cp=8), func=AF.Identity),
                        reads=pst2(b0), writes=('B',))
                filler()
            for h in range(2):
                b0 = next_pair()
                for n2l in range(32):
                    n2 = 32 * h + n2l
                    bank = b0 + n2l // 16
                    off = 32 * (n2l % 16)
                    for r in range(2):
                        kb.op('pe', lambda p: p.matmul(PS(bank, 32, off=off), C['H'][:, n2, r, :],
                                                       Dt[:, :, n2, r, :].rearrange("p c2 cp -> p cp c2"), start=(r == 0), stop=(r == 1)),
                              reads=('B',) + C['toks'], writes=(pst(bank),), inc=(r == 1))
                kb.op('dve', lambda v: v.transpose(
                    out=conv_out.rearrange("p (n1 n2) -> p n2 n1", n2=64)[:, 32 * h:32 * h + 32, :],
                    in_=PS2(b0).rearrange("p (n c) -> p n c", c=32)),
                    reads=pst2(b0), writes=(conv_tok,))
            filler()

        def load_fft_consts(es, with_filter):
            E = es.enter_context
            fs = E(SBT("f_sm", [128, 896], BF16))
            Tb = E(SBT("f_T", [128, 32, 2, 128], BF16))
            kb.dma('sp', fs[:], fsmall, writes=('fc',))
            Tv = Tb[:].rearrange("p a b c -> p (a b c)")
            for i in range(2):
                kb.dma('sp', Tv[:, i * 4096:(i + 1) * 4096], fT[:, i * 4096:(i + 1) * 4096], writes=('fcT%d' % i,))
            C = {'G1': fs[:, 0:256], 'G1hi': fs[:, 256:512], 'Ga': fs[:, 512:640], 'Gb': fs[:, 640:768], 'Sw': fs[:, 768:896], 'T': Tb}
            C['toks'] = ('fc', 'fcT0', 'fcT1')
            if not with_filter:
                Hb = E(SBT("f_H", [128, 64, 2, 128], BF16))
                Hv = Hb[:].rearrange("p a b c -> p (a b c)")
                for i in range(4):
                    kb.dma('sp', Hv[:, i * 4096:(i + 1) * 4096], fH[:, i * 4096:(i + 1) * 4096], writes=('fcH%d' % i,))
                C['H'] = Hb
                C['toks'] = C['toks'] + tuple('fcH%d' % i for i in range(4))
            return C

        def prep_hy(l):
            with ExitStack() as es:
                E = es.enter_context
                C = load_fft_consts(es, True)
                w1 = E(SBT("q_w1", [33, 64], F32))
                w2 = E(SBT("q_w2", [64, 64], F32))
                w3 = E(SBT("q_w3", [64, 4096], F32))
                hv = E(SBT("q_hv", [64, 3], F32))
                bf = E(SBT("q_bf", [64, 2], F32))
                dec = E(SBT("q_dec", [128, 32], F32))
                ft = E(SBT("q_ft", [33, 2, L], F32))
                tt_ = E(SBT("q_tt", [128, 2, L], F32))
                h1 = E(SBT("q_h1", [64, L], F32))
                h2 = E(SBT("q_h2", [64, 2, L], F32))
                tmp = E(SBT("q_tmp", [64, L], F32))
                win = E(SBT("q_win", [128, 2, L], F32))
                hk = E(SBT("q_hk", [128, 2, 2, L], F32))
                junk = E(SBT("q_junk", [128, L], BF16))
                ss = E(SBT("q_ss", [128, 2, 4], F32))
                kbf = E(SBT("q_kbf", [128, 2, L], BF16))
                zT = E(SBT("q_zT", [128, 2, 32, 64], BF16))
                B = E(SBT("q_B", [128, 32, 2, 16, 4], BF16))
                Kh = E(SBT("q_Kh", [128, 2, 2, 2048], BF16))
                kb.dma('sp', w1[:], hw1[l], writes=('w1',))
                kb.dma('sp', w2[:], hw2[l], writes=('w2',))
                kb.dma('sp', w3[:], hw3[l], writes=('w3',))
                kb.dma('sp', hv[:], hvec[l], writes=('hv',))
                kb.dma('sp', dec[:], hdec[l], writes=('dec',))
                for i in range(2):
                    kb.dma('sp', ft[:, i, :], feats[i], writes=('ft',))
                    kb.dma('sp', tt_[:, i, :], ttab[i], writes=('tt',))
                V = lambda fn, r, w: kb.op('dve', fn, reads=r, writes=w)
                A = lambda fn, r, w: kb.op('act', fn, reads=r, writes=w)
                V(lambda v: v.tensor_tensor(out=bf[:, 0:1], in0=hv[:, 0:1], in1=hv[:, 2:3], op=ALU.mult), ('hv',), ('bf',))
                V(lambda v: v.tensor_tensor(out=bf[:, 1:2], in0=hv[:, 1:2], in1=hv[:, 2:3], op=ALU.mult), ('hv', 'bf'), ('bf',))
                A(lambda a: a.activation(out=dec[:], in_=dec[:], func=AF.Abs), ('dec',), ('dec',))
                V(lambda v: v.tensor_scalar(out=dec[:], in0=dec[:], scalar1=-1.0, scalar2=None, op0=ALU.mult), ('dec',), ('dec',))
                for i in range(2):
                    for stage in range(2):
                        wmat = w1 if stage == 0 else w2
                        dst = h1 if stage == 0 else h2[:, i, :]
                        dtok = 'h1' if stage == 0 else 'h2_%d' % i
                        for t4 in range(4):
                            ts = slice(t4 * 512, (t4 + 1) * 512)
                            b = kb.ps()
                            if stage == 0:
                                kb.op('pe', lambda p: p.matmul(PS(b, 512, 0, 64), w1[:], ft[:, i, ts], start=True, stop=True),
                                      reads=('w1', 'ft'), writes=(pst(b),))
                            else:
                                kb.op('pe', lambda p: p.matmul(PS(b, 512, 0, 64), w2[:], h1[:, ts], start=True, stop=True),
                                      reads=('w2', 'h1'), writes=(pst(b),))
                            dsl = dst[:, ts] if stage == 0 else h2[:, i, ts]
                            A(lambda a: a.activation(out=dsl, in_=PS(b, 512, 0, 64), func=AF.Identity, bias=bf[:, stage:stage + 1], scale=hv[:, 2:3]),
                              (pst(b), 'bf', 'hv'), (dtok,))
                        dfull = h1[:] if stage == 0 else h2[:, i, :]
                        range_reduce('dve', dfull, tmp[:], (dtok,), ('tmp',))
                        A(lambda a: a.activation(out=dfull, in_=dfull, func=AF.Sin), (dtok,), (dtok,))
                NIT = 16

                def make_A(n):
                    o, cc = n // 8, n % 8
                    par = n % 2
                    pieces = []
                    for dr in range(2):
                        for t4 in range(4):
                            def piece(dr=dr, t4=t4):
                                fc = o * 16 + dr * 8 + cc
                                if t4 == 0:
                                    A(lambda a: a.activation(out=win[:, dr, :], in_=tt_[:, dr, :], func=AF.Exp, scale=dec[:, fc:fc + 1]),
                                      ('tt', 'dec'), ('win%d' % dr,))
                                ts = slice(t4 * 512, (t4 + 1) * 512)
                                b = kb.ps_hi()
                                kb.op('pe', lambda p: p.matmul(PS(b), w3[:, fc * 128:(fc + 1) * 128], h2[:, dr, ts], start=True, stop=True),
                                      reads=('w3', 'h2_%d' % dr), writes=(pst(b),))
                                V(lambda v: v.scalar_tensor_tensor(out=hk[:, par, dr, ts], in0=win[:, dr, ts], scalar=0.05, in1=PS(b), op0=ALU.add, op1=ALU.mult),
                                  ('win%d' % dr, pst(b)), ('hk%d_%d' % (par, dr),))
                                if t4 == 3:
                                    if dr == 1:
                                        V(lambda v: v.memset(hk[:, par, 1, 0:1], 0.0), ('hk%d_1' % par,), ('hk%d_1' % par,))
                                    A(lambda a: a.activation(out=junk[:], in_=hk[:, par, dr, :], func=AF.Square, accum_out=ss[:, par, dr:dr + 1]),
                                      ('hk%d_%d' % (par, dr),), ('junk', 'ss%d_%d' % (par, dr)))
                            pieces.append(piece)
                    return pieces

                def run_B(n, filler):
                    o, cc = n // 8, n % 8
                    par = n % 2
                    V(lambda v: v.tensor_tensor(out=ss[:, par, 2:3], in0=ss[:, par, 0:1], in1=ss[:, par, 1:2], op=ALU.add),
                      ('ss%d_0' % par, 'ss%d_1' % par), ('ss%d_2' % par,))
                    A(lambda a: a.activation(out=ss[:, par, 2:3], in_=ss[:, par, 2:3], func=AF.Sqrt, bias=barscr[:, 1:2]), ('ss%d_2' % par,), ('ss%d_2' % par,))
                    V(lambda v: v.reciprocal(out=ss[:, par, 3:4], in_=ss[:, par, 2:3]), ('ss%d_2' % par,), ('ss%d_3' % par,))
                    for dr in range(2):
                        V(lambda v: v.tensor_scalar(out=kbf[:, dr, :], in0=hk[:, par, dr, :], scalar1=ss[:, par, 3:4], scalar2=None, op0=ALU.mult),
                          ('hk%d_%d' % (par, dr), 'ss%d_3' % par), ('kbf%d' % dr,))
                        V(lambda v: v.transpose(out=zT[:, dr].bitcast(U32).rearrange("p c m -> p m c"),
                                                in_=kbf[:, dr, :].bitcast(U32).rearrange("p (n1 m) -> p m n1", m=32)),
                          ('kbf%d' % dr,), ('zT%d' % dr,))
                        filler()

                    def spec_evac(c2, h, psv, ptoks):
                        A(lambda a: a.activation(out=Kh[:, 0, c2, :].rearrange("p (m k) -> p m k", k=32)[:, :, 16 * h:16 * h + 16], in_=psv, func=AF.Identity),
                          ptoks, ('Kh0',))
                    fft_fwd(C, [(zT[:, 0], 'zT0', 0), (zT[:, 1], 'zT1', 1)], B, spec_evac, filler)
                    for c2 in range(2):
                        for q in range(4):
                            b = kb.ps_hi()
                            kb.op('pe', lambda p: p.matmul(PS(b), C['Sw'], Kh[:, 0, c2, q * 512:(q + 1) * 512], start=True, stop=True),
                                  reads=C['toks'] + ('Kh0',), writes=(pst(b),))
                            A(lambda a: a.activation(out=Kh[:, 1, c2, q * 512:(q + 1) * 512], in_=PS(b), func=AF.Identity), (pst(b),), ('Kh1',))
                        filler()
                    for w_ in range(2):
                        kb.dma('sp', khatD[o, cc, w_], Kh[:, w_].rearrange("p a b -> p (a b)"), reads=('Kh%d' % w_,), writes=('khatD',))

                cur = make_A(0)
                for pc in cur:
                    pc()
                for n in range(NIT):
                    nxt = make_A(n + 1) if n + 1 < NIT else []

                    def filler():
                        if nxt:
                            nxt.pop(0)()
                    run_B(n, filler)
                    while nxt:
                        nxt.pop(0)()
                kb.barrier(barscr[:, 0:1])

        def phase_hy(l):
            with ExitStack() as es:
                E = es.enter_context
                C = load_fft_consts(es, False)
                cw = E(SBT("h_cw", [128, 3, 24], F32))
                cb_ = E(SBT("h_cb", [128, 24], F32))
                dd = E(SBT("h_dd", [128, 2, 8], F32))
                wst = E(SBT("h_wst", [128, 2, 8, 128], F32))
                wbf = E(SBT("h_wbf", [128, 4, 8, 128], BF16))
                pp = E(SBT("h_pp", [128, 2, L + 2], F32))
                vx = E(SBT("h_vx", [128, 2, 4, L], BF16))
                z1 = E(SBT("h_z1", [128, L], BF16))
                zT = E(SBT("h_zT", [128, 32, 64], BF16))
                BD = E(SBT("h_BD", [128, 4096], BF16))
                B = BD[:].rearrange("p (k r cp cq) -> p k r cp cq", k=32, r=2, cp=16)
                Dt = BD[:].rearrange("p (c2 n r cp) -> p c2 n r cp", c2=2, n=64, r=2)
                P1 = E(SBT("h_P1", [128, 2, 2048], BF16))
                P2 = E(SBT("h_P2", [128, 2, 2048], BF16))
                Kt = E(SBT("h_Kt", [128, 2, 4096], BF16))
                conv = E(SBT("h_conv", [128, L], F32))
                sc = conv
                kb.dma('sp', cw[:], convw[l], writes=('cw',))
                kb.dma('sp', cb_[:], convb[l], writes=('cb',))
                kb.dma('sp', dd[:], hyd[l], writes=('dd',))
                kb.op('pool', lambda g: g.memset(pp[:], 0.0), writes=('pp0', 'pp1'))
                wcnt = [0]

                def load_stream(cb, si):
                    slot = wcnt[0] % 2
                    wcnt[0] += 1
                    col = [1024, 2048, 3072, 4096][si] + cb * 128
                    for k in range(8):
                        kb.dma('sp', wst[:, slot, k, :], w_in[l][k * 128:(k + 1) * 128, col:col + 128], writes=('wst%d' % slot,))
                    kb.op('act', lambda a: a.activation(out=wbf[:, si], in_=wst[:, slot], func=AF.Identity), reads=('wst%d' % slot,), writes=('wbf%d' % si,))

                def short_conv(cb, si):
                    par = cb % 2
                    sp_ = si % 2
                    ch = si * 8 + cb
                    kb.op('dve', lambda v: v.tensor_scalar(out=sc[:], in0=pp[:, sp_, 1:L + 1], scalar1=cw[:, 1, ch:ch + 1], scalar2=cb_[:, ch:ch + 1],
                                                           op0=ALU.mult, op1=ALU.add),
                          reads=('pp%d' % sp_, 'cw', 'cb'), writes=('conv',))
                    kb.op('dve', lambda v: v.scalar_tensor_tensor(out=sc[:], in0=pp[:, sp_, 0:L], scalar=cw[:, 0, ch:ch + 1], in1=sc[:],
                                                                  op0=ALU.mult, op1=ALU.add),
                          reads=('pp%d' % sp_, 'cw', 'conv'), writes=('conv',))
                    kb.op('dve', lambda v: v.scalar_tensor_tensor(out=vx[:, par, si, :], in0=pp[:, sp_, 2:L + 2], scalar=cw[:, 2, ch:ch + 1], in1=sc[:],
                                                                  op0=ALU.mult, op1=ALU.add),
                          reads=('pp%d' % sp_, 'cw', 'conv'), writes=('vx%d_%d' % (par, si),))

                def make_A(cb):
                    par = cb % 2
                    base = []
                    for si in range(4):
                        for t4 in range(4):
                            def piece(si=si, t4=t4):
                                sp_ = si % 2
                                ts = slice(t4 * 512, (t4 + 1) * 512)
                                b = kb.ps_hi()
                                for k in range(8):
                                    kb.op('pe', lambda p: p.matmul(PS(b), wbf[:, si, k, :], G['xn'][:, k, ts], start=(k == 0), stop=(k == 7)),
                                          reads=('wbf%d' % si, 'xn%d' % k), writes=(pst(b),), inc=(k == 7))
                                if si < 3:
                                    kb.op('act', lambda a: a.activation(out=pp[:, sp_, 1 + t4 * 512:1 + (t4 + 1) * 512], in_=PS(b), func=AF.Identity),
                                          reads=(pst(b),), writes=('pp%d' % sp_,))
                                else:
                                    kb.op('act', lambda a: a.activation(out=vx[:, par, 3, ts], in_=PS(b), func=AF.Silu), reads=(pst(b),), writes=('vx%d_3' % par,))
                                if t4 == 0:
                                    if si < 3:
                                        load_stream(cb, si + 1)
                                    elif cb + 1 < 8:
                                        load_stream(cb + 1, 0)
                            base.append([piece])
                    for si in range(3):
                        base[min(4 * si + 5, 15)].append(lambda si=si: short_conv(cb, si))
                    base[15].append(lambda: kb.op('pool', lambda g: g.tensor_tensor(out=vx[:, par, 2, :], in0=vx[:, par, 2, :], in1=vx[:, par, 3, :], op=ALU.mult),
                                                  reads=('vx%d_2' % par, 'vx%d_3' % par), writes=('vx%d_2' % par,)))
                    return [(lambda fs=fs: [f() for f in fs]) for fs in base]

                def run_B(cb, filler):
                    par = cb % 2
                    zin, ztok = vx[:, par, 0, :], 'vx%d_0' % par
                    for o in range(2):
                        for w_ in range(2):
                            kb.dma('sp', Kt[:, w_, :], khatD[o, cb, w_], reads=('khatD',), writes=('Kt',))
                        kb.op('dve', lambda v: v.transpose(out=zT[:].bitcast(U32).rearrange("p c m -> p m c"),
                                                           in_=zin.bitcast(U32).rearrange("p (n1 m) -> p m n1", m=32)),
                              reads=(ztok,), writes=('zT',))
                        filler()

                        def spec_evac(c2, h, psv, ptoks):
                            ksl = slice(16 * h, 16 * h + 16)
                            kb.op('dve', lambda v: v.tensor_tensor(out=P1[:, c2, :].rearrange("p (m k) -> p m k", k=32)[:, :, ksl], in0=psv,
                                                                   in1=Kt[:, 0, c2 * 2048:(c2 + 1) * 2048].rearrange("p (m k) -> p m k", k=32)[:, :, ksl], op=ALU.mult),
                                  reads=ptoks + ('Kt',), writes=('P1',))
                            kb.op('dve', lambda v: v.tensor_tensor(out=P2[:, c2, :].rearrange("p (m k) -> p m k", k=32)[:, :, ksl], in0=psv,
                                                                   in1=Kt[:, 1, c2 * 2048:(c2 + 1) * 2048].rearrange("p (m k) -> p m k", k=32)[:, :, ksl], op=ALU.mult),
                                  reads=ptoks + ('Kt',), writes=('P2',))
                        fft_fwd(C, [(zT[:], 'zT', 0)], B, spec_evac, filler)
                        fft_inv(C, P1, P2, Dt, conv[:], 'conv', filler)
                        kb.op('dve', lambda v: v.scalar_tensor_tensor(out=conv[:], in0=zin, scalar=dd[:, o, cb:cb + 1], in1=conv[:], op0=ALU.mult, op1=ALU.add),
                              reads=(ztok, 'dd', 'conv'), writes=('conv',))
                        if o == 0:
                            kb.op('dve', lambda v: v.tensor_tensor(out=z1[:], in0=conv[:], in1=vx[:, par, 1, :], op=ALU.mult),
                                  reads=('conv', 'vx%d_1' % par), writes=('z1',))
                            zin, ztok = z1[:], 'z1'
                        else:
                            yo = zT[:].rearrange("p a b -> p (a b)")
                            kb.op('dve', lambda v: v.tensor_tensor(out=yo, in0=conv[:], in1=vx[:, par, 2, :], op=ALU.mult),
                                  reads=('conv', 'vx%d_2' % par), writes=('zT',))
                            kb.dma('sp', yhyd[cb], yo, reads=('zT',), writes=('yhyd',))

                load_stream(0, 0)
                cur = make_A(0)
                for pc in cur:
                    pc()
                for cb in range(8):
                    nxt = make_A(cb + 1) if cb + 1 < 8 else []

                    def filler():
                        if nxt:
                            nxt.pop(0)()
                    run_B(cb, filler)
                    while nxt:
                        nxt.pop(0)()
                kb.barrier(barscr[:, 0:1])

        def phase_merge(l, hsrc):
            with ExitStack() as es:
                E = es.enter_context
                wm = E(SBT("m_wm", [128, 8, 2048], BF16))
                wh = E(SBT("m_wh", [128, 8, 1024], BF16))
                wo = E(SBT("m_wo", [128, 8, 1024], BF16))
                st = E(SBT("m_st", [128, 2, 2048], F32))
                mm = E(SBT("m_mm", [128, 16, 512], BF16))
                ys = E(SBT("m_ys", [128, 8, 512], BF16))
                yh = E(SBT("m_yh", [128, 8, 512], BF16))
                mg = E(SBT("m_mg", [128, 8, 512], BF16))
                t1 = E(SBT("m_t1", [128, 2, 512], F32))
                ht = E(SBT("m_ht", [128, 8, 512], F32))
                i = 0
                for (wt, src, ncol, nm) in ((wm, w_in[l][:, 5120:7168], 2048, 'wm'), (wh, w_bhy[l], 1024, 'wh'), (wo, w_out[l], 1024, 'wo')):
                    for k in range(8):
                        s = i % 2; i += 1
                        kb.dma('sp' if s == 0 else 'sp', st[:, s, 0:ncol], src[k * 128:(k + 1) * 128, :], writes=('st%d' % s,))
                        kb.op('act', lambda a: a.activation(out=wt[:, k, :], in_=st[:, s, 0:ncol], func=AF.Identity), reads=('st%d' % s,), writes=(nm,))
                for tt in range(4):
                    ts = slice(tt * 512, (tt + 1) * 512)
                    for c in range(8):
                        kb.dma('sp', ys[:, c, :], ys5d[c][:, ts], reads=('ys5d',), writes=('ys',))
                        kb.dma('sp', yh[:, c, :], yhyd[c][:, ts], reads=('yhyd',), writes=('yh',))
                        kb.dma('sp', ht[:, c, :], hsrc[c][:, ts], reads=('hsrc',), writes=('ht',))
                    for mc in range(16):
                        b = kb.ps()
                        for k in range(8):
                            kb.op('pe', lambda p: p.matmul(PS(b), wm[:, k, mc * 128:(mc + 1) * 128], G['xn'][:, k, ts], start=(k == 0), stop=(k == 7)),
                                  reads=('wm', 'xn%d' % k), writes=(pst(b),), inc=(k == 7))
                        kb.op('act', lambda a: a.activation(out=mm[:, mc, :], in_=PS(b), func=AF.Sigmoid), reads=(pst(b),), writes=('mm',))
                    for dc in range(8):
                        b = kb.ps()
                        for k in range(8):
                            kb.op('pe', lambda p: p.matmul(PS(b), wh[:, k, dc * 128:(dc + 1) * 128], yh[:, k, :], start=(k == 0), stop=(k == 7)),
                                  reads=('wh', 'yh'), writes=(pst(b),), inc=(k == 7))
                        s = dc % 2
                        kb.op('dve', lambda v: v.tensor_tensor(out=t1[:, s, :], in0=PS(b), in1=mm[:, 8 + dc, :], op=ALU.mult),
                              reads=(pst(b), 'mm'), writes=('t1_%d' % s,))
                        kb.op('pool', lambda g: g.tensor_tensor(out=mg[:, dc, :], in0=mm[:, dc, :], in1=ys[:, dc, :], op=ALU.mult),
                              reads=('mm', 'ys'), writes=('mg%d' % dc,))
                        kb.op('dve', lambda v: v.tensor_tensor(out=mg[:, dc, :], in0=mg[:, dc, :], in1=t1[:, s, :], op=ALU.add),
                              reads=('mg%d' % dc, 't1_%d' % s), writes=('mg%d' % dc, 'mg'))
                    for dc in range(8):
                        b = kb.ps()
                        for k in range(8):
                            kb.op('pe', lambda p: p.matmul(PS(b), wo[:, k, dc * 128:(dc + 1) * 128], mg[:, k, :], start=(k == 0), stop=(k == 7)),
                                  reads=('wo', 'mg'), writes=(pst(b),), inc=(k == 7))
                        kb.op('dve', lambda v: v.tensor_tensor(out=ht[:, dc, :], in0=PS(b), in1=ht[:, dc, :], op=ALU.add),
                              reads=(pst(b), 'ht'), writes=('ht',))
                        kb.dma('sp', hbuf[dc][:, ts], ht[:, dc, :], reads=('ht',), writes=('hbuf',))
                kb.barrier(barscr[:, 0:1])

        kb.op('pool', lambda g: g.memset(barscr[:, 1:2], 1e-6), writes=('epsq',))
        kb.barrier(barscr[:, 0:1])
        order = ['prep_s5', 'prep_hy', 'norm', 's5', 'hy', 'merge']
        nph = len(order) if stop_after is None else order.index(stop_after) + 1
        for l in range(layers):
            hsrc = xT if l == 0 else hbuf
            if nph >= 1:
                prep_s5(l)
            if nph >= 2:
                prep_hy(l)
            with ExitStack() as esl:
                G['xn'] = esl.enter_context(SBT("xn%d" % l, [128, 8, L], BF16))
                if nph >= 3:
                    phase_norm(hsrc, l, G['xn'], None)
                    if 'xnD' in dump:
                        xnD = dscr("xnD", [8, 128, L], BF16)
                        for c in range(8):
                            kb.dma('sp', xnD[c], G['xn'][:, c, :], reads=('xn%d' % c,), writes=('xnD',))
                if nph >= 4:
                    phase_s5(l)
                if nph >= 5:
                    phase_hy(l)
                if nph >= 6:
                    phase_merge(l, hsrc)
        if stop_after is None:
            phase_norm(hbuf, DEPTH, None, outT)
        for i in range(NDS):
            if kb.dcnt[i]:
                kb._wait('sp', ('d', i), kb.dcnt[i])
    return nc, kb


_CACHE = {}


def _host_consts():
    if 'c' not in _CACHE:
        fsmall, fT, fH = fft_consts()
        ft, tt = hy_feats()
        import ml_dtypes
        bfc = lambda a: np.ascontiguousarray(a.astype(ml_dtypes.bfloat16))
        _CACHE['c'] = dict(fsmall=bfc(fsmall), fT=bfc(fT), fH=bfc(fH), feats=ft, ttab=tt, kvt=s5_kv(), masks=s5_masks(),
                           ident=np.eye(128, dtype=np.float32))
    return _CACHE['c']


def _layout_shared(inp):
    f32 = np.float32
    g = lambda k: np.asarray(inp[k], dtype=f32)
    m = dict(_host_consts())
    nw = np.concatenate([g("norm_w"), g("final_norm_w")[None]], 0)
    m["normw"] = np.ascontiguousarray(nw.reshape(3, 8, 128).transpose(0, 2, 1))
    m["w_in"] = g("w_in")
    m["w_glu"] = g("s5_w_glu")
    m["b_glu"] = np.ascontiguousarray(g("s5_b_glu").reshape(DEPTH, 4, 128).transpose(0, 2, 1))
    m["s5d"] = np.ascontiguousarray(g("s5_d").reshape(DEPTH, 4, 128).transpose(0, 2, 1))
    m["w_bs5"] = g("w_branch_s5")
    m["w_bhy"] = g("w_branch_hy")
    m["w_out"] = g("w_out")
    cwv = g("hy_conv_w").reshape(DEPTH, 3, 24, 128)
    m["convw"] = np.ascontiguousarray(cwv.transpose(0, 3, 1, 2))
    m["convb"] = np.ascontiguousarray(g("hy_conv_b").reshape(DEPTH, 24, 128).transpose(0, 2, 1))
    m["hyd"] = np.ascontiguousarray(g("hy_d").reshape(DEPTH, 2, 8, 128).transpose(0, 3, 1, 2))
    def pg(a):
        v = a.reshape(DEPTH, 2, 16, 2, 64)
        return v.transpose(0, 3, 4, 1, 2).reshape(DEPTH, 128, 32)
    ls = np.broadcast_to(g("s5_log_step")[..., None], (DEPTH, 2, 32, 64))
    m["s5lam"] = np.ascontiguousarray(np.stack([pg(g("s5_lam_re")), pg(g("s5_lam_im")), pg(ls)], 2))
    def pgB(a):
        v = a.reshape(DEPTH, 2, 16, 2, 64, 16)
        return v.transpose(0, 3, 4, 1, 2, 5).reshape(DEPTH, 128, 32, 16)
    m["s5B"] = np.ascontiguousarray(np.stack([pgB(g("s5_b_re")), pgB(g("s5_b_im"))], 2))
    ct = lambda a: a.transpose(0, 1, 2, 4, 3)
    m["s5C"] = np.ascontiguousarray(np.stack([pgB(ct(g("s5_c_re"))), pgB(ct(g("s5_c_im")))], 2))
    m["hw1"] = g("hy_w1"); m["hw2"] = g("hy_w2"); m["hw3"] = g("hy_w3")
    m["hvec"] = np.ascontiguousarray(np.stack([g("hy_b1"), g("hy_b2"), g("hy_freq")], -1))
    m["hdec"] = np.ascontiguousarray(g("hy_decay").reshape(DEPTH, 32, 128).transpose(0, 2, 1))
    return m


def kernel(**inputs):
    x = np.asarray(inputs["x"], dtype=np.float32)
    shared = _layout_shared(inputs)
    if 'nc' not in _CACHE:
        _CACHE['nc'] = build()[0]
    nc = _CACHE['nc']
    in_maps = []
    for b in range(NCORES):
        m = dict(shared)
        m["xT"] = np.ascontiguousarray(x[b].T.reshape(8, 128, L))
        in_maps.append(m)
    res = run_bass_kernel_spmd(nc, in_maps, core_ids=list(range(NCORES)))
    out = np.empty((NCORES, L, D), dtype=np.float32)
    for b in range(NCORES):
        out[b] = np.asarray(res.results[b]["outT"]).reshape(D, L).T
    return out
```

```python
import math
from contextlib import ExitStack
import numpy as np
import concourse.bass as bass
import concourse.mybir as mybir
from concourse.bass_utils import run_bass_kernel_spmd

F32 = mybir.dt.float32
BF16 = mybir.dt.bfloat16
U32 = mybir.dt.uint32
ALU = mybir.AluOpType
AF = mybir.ActivationFunctionType

D = 1024; L = 2048; DEPTH = 2; NCORES = 8
INC = 7168
NDS = 48
MAGIC = 12582912.0
TWO_PI = 2.0 * math.pi


class KB:
    def __init__(self, nc, es):
        self.nc = nc
        self.engs = {'pe': nc.tensor, 'act': nc.scalar, 'dve': nc.vector, 'pool': nc.gpsimd, 'sp': nc.sync}
        self.csem = {e: es.enter_context(nc.semaphore('c_' + e)) for e in ('pe', 'act', 'dve', 'pool')}
        self.cnt = {e: 0 for e in self.csem}
        self.dsem = [es.enter_context(nc.semaphore('d%d' % i)) for i in range(NDS)]
        self.dcnt = [0] * NDS
        self.dnext = 0
        self.waited = {e: {} for e in self.engs}
        self.lastw = {}
        self.readers = {}
        self.psn = 0
        self.fp = 0
        self.nins = 0

    def _sem(self, sid):
        return self.csem[sid[1]] if sid[0] == 'c' else self.dsem[sid[1]]

    def _wait(self, e, sid, val):
        if e == 'pe' and sid == ('c', 'pe'):
            return
        w = self.waited[e]
        if w.get(sid, 0) >= val:
            return
        self.engs[e].wait_ge(self._sem(sid), val)
        w[sid] = val

    def _deps(self, e, reads, writes):
        for t in reads:
            if t in self.lastw:
                self._wait(e, *self.lastw[t])
        for t in writes:
            if t in self.lastw:
                self._wait(e, *self.lastw[t])
            for sid, val in self.readers.get(t, {}).items():
                self._wait(e, sid, val)

    def _record(self, sid, val, reads, writes):
        for t in writes:
            self.lastw[t] = (sid, val)
            self.readers[t] = {}
        for t in reads:
            r = self.readers.setdefault(t, {})
            if r.get(sid, 0) < val:
                r[sid] = val

    def op(self, e, fn, reads=(), writes=(), inc=True):
        self._deps(e, reads, writes)
        ins = fn(self.engs[e])
        self.nins += 1
        if e == 'pe' and not inc:
            val = self.cnt['pe'] + 1
        else:
            self.cnt[e] += 1
            val = self.cnt[e]
            ins.then_inc(self.csem[e], 1)
        self._record(('c', e), val, reads, writes)

    def dma(self, q, out, in_, reads=(), writes=()):
        i = self.dnext
        self.dnext = (self.dnext + 1) % NDS
        sid = ('d', i)
        if self.dcnt[i] > 0:
            self._wait(q, sid, self.dcnt[i])
        self._deps(q, reads, writes)
        self.engs[q].dma_start(out=out, in_=in_).then_inc(self.dsem[i], 16)
        self.nins += 1
        self.dcnt[i] += 16
        self._record(sid, self.dcnt[i], reads, writes)

    def barrier(self, scratch):
        for i in range(NDS):
            if self.dcnt[i]:
                self._wait('pool', ('d', i), self.dcnt[i])
        for e in ('pe', 'act', 'dve'):
            if self.cnt[e]:
                self._wait('pool', ('c', e), self.cnt[e])
        self.op('pool', lambda g: g.memset(scratch, 0.0), writes=('__bar',))
        val = self.cnt['pool']
        for e in ('pe', 'act', 'dve', 'sp'):
            self._wait(e, ('c', 'pool'), val)
        self.lastw.clear()
        self.readers.clear()

    def ps(self):
        b = self.psn % 8
        self.psn += 1
        return b

    def ps_hi(self):
        b = 4 + self.psn % 4
        self.psn += 1
        return b


def fft_consts():
    N = 4096
    n1 = np.arange(32); k1 = np.arange(32); n2 = np.arange(64); k2 = np.arange(64)
    ang = -2 * np.pi * np.outer(n1, k1 + 0.5) / 64.0
    g = np.stack([np.cos(ang), np.sin(ang)], -1)
    angh = -2 * np.pi * np.outer(n1 + 32, k1 + 0.5) / 64.0
    gh = -np.stack([np.cos(angh), np.sin(angh)], -1)
    G1 = np.zeros((4, 32, 4, 32, 2)); G1hi = np.zeros((4, 32, 4, 32, 2))
    for cq in range(4):
        G1[cq, :, cq] = g; G1hi[cq, :, cq] = gh
    a = -2 * np.pi * (n2[:, None, None] * (k1[None, :, None] + 0.5) / 4096.0 + n2[:, None, None] * k2[None, None, :] / 64.0)
    twr, twi = np.cos(a), np.sin(a)
    T = np.zeros((64, 32, 2, 64, 2))
    T[:, :, 0, :, 0] = twr; T[:, :, 0, :, 1] = twi
    T[:, :, 1, :, 0] = -twi; T[:, :, 1, :, 1] = twr
    T = np.concatenate([T, T], 0).reshape(128, 32 * 2 * 128)
    e = 2 * np.pi * np.outer(k2, n2) / 64.0
    er, ei = np.cos(e), np.sin(e)
    Ga = np.zeros((64, 2, 64, 2)); Gb = np.zeros((64, 2, 64, 2))
    for rp, s in ((0, 1.0), (1, -1.0)):
        Ga[:, rp, :, 0] = s * er; Ga[:, rp, :, 1] = s * ei
        Gb[:, rp, :, 0] = -ei; Gb[:, rp, :, 1] = er
    phi = 2 * np.pi * (k1[:, None, None] + 0.5) * (64 * n1[None, None, :] + n2[None, :, None]) / 4096.0
    H = np.zeros((4, 32, 64, 2, 4, 32))
    for cq in range(4):
        H[cq, :, :, 0, cq, :] = (2.0 / N) * np.cos(phi)
        H[cq, :, :, 1, cq, :] = -(2.0 / N) * np.sin(phi)
    Sw = np.zeros((64, 2, 64, 2))
    for k in range(64):
        Sw[k, 0, k, 1] = 1; Sw[k, 1, k, 0] = 1
    small = np.concatenate([G1.reshape(128, 256), G1hi.reshape(128, 256), Ga.reshape(128, 128),
                            Gb.reshape(128, 128), Sw.reshape(128, 128)], 1)
    return (small.astype(np.float32), T.astype(np.float32), H.reshape(128, 64 * 2 * 128).astype(np.float32))


def hy_feats():
    f32 = np.float32
    t = np.linspace(0.0, 1.0, L, dtype=f32)
    bands = np.linspace(1e-4, 15, 16, dtype=f32)
    def feats(pos, tt):
        angv = bands[None, :] * pos[:, None].astype(f32) * f32(2.0 * math.pi / L)
        return np.concatenate([tt[:, None], np.cos(angv), -np.sin(angv)], -1).astype(f32)
    posF = np.arange(L)
    posR = (L - np.arange(L)) % L
    fF = feats(posF, t); fR = feats(posR, t[posR])
    ft = np.stack([fF.T, fR.T], 0).astype(f32)
    tt = np.stack([np.broadcast_to(t, (128, L)), np.broadcast_to(t[posR], (128, L))], 0).astype(f32)
    return ft, tt


def s5_kv():
    j = np.arange(8)
    dbl = 8.0 * 2.0 ** np.arange(8)
    f = np.concatenate([7 - j, j + 1, j - 7, dbl])
    b = np.concatenate([j, 8 - j, -j, dbl])
    kv = np.stack([f, b], 0).astype(np.float32)
    return np.broadcast_to(kv[None], (128, 2, 32)).copy()


def s5_masks():
    jj = np.repeat(np.arange(8), 16)
    mf = (jj[None, :] >= jj[:, None]).astype(np.float32)
    mb = (jj[:, None] >= jj[None, :]).astype(np.float32)
    return np.concatenate([mf, mb], 1)


def build(dump=(), layers=DEPTH, stop_after=None):
    nc = bass.Bass("TRN2", target_bir_lowering=False)
    dt_in = {}

    def din(name, shape, dt=F32):
        dt_in[name] = nc.dram_tensor(name, list(shape), dt, kind="ExternalInput").ap()
        return dt_in[name]

    def dscr(name, shape, dt=F32):
        kind = "ExternalOutput" if name in dump else "Internal"
        return nc.dram_tensor(name, list(shape), dt, kind=kind).ap()

    xT = din("xT", [8, 128, L])
    normw = din("normw", [DEPTH + 1, 128, 8])
    w_in = din("w_in", [DEPTH, D, INC])
    w_glu = din("w_glu", [DEPTH, 512, 512])
    b_glu = din("b_glu", [DEPTH, 128, 4])
    s5d = din("s5d", [DEPTH, 128, 4])
    w_bs5 = din("w_bs5", [DEPTH, 512, D])
    w_bhy = din("w_bhy", [DEPTH, D, D])
    w_out = din("w_out", [DEPTH, D, D])
    convw = din("convw", [DEPTH, 128, 3, 24])
    convb = din("convb", [DEPTH, 128, 24])
    hyd = din("hyd", [DEPTH, 128, 2, 8])
    s5lam = din("s5lam", [DEPTH, 128, 3, 32])
    s5B = din("s5B", [DEPTH, 128, 2, 32, 16])
    s5C = din("s5C", [DEPTH, 128, 2, 32, 16])
    kvt = din("kvt", [128, 2, 32])
    masks = din("masks", [128, 256])
    ident = din("ident", [128, 128])
    hw1 = din("hw1", [DEPTH, 33, 64])
    hw2 = din("hw2", [DEPTH, 64, 64])
    hw3 = din("hw3", [DEPTH, 64, 4096])
    hvec = din("hvec", [DEPTH, 64, 3])
    hdec = din("hdec", [DEPTH, 128, 32])
    feats = din("feats", [2, 33, L])
    ttab = din("ttab", [2, 128, L])
    fsmall = din("fsmall", [128, 896], BF16)
    fT = din("fT", [128, 8192], BF16)
    fH = din("fH", [128, 16384], BF16)

    outT = nc.dram_tensor("outT", [8, 128, L], F32, kind="ExternalOutput").ap()

    hbuf = dscr("hbuf", [8, 128, L])
    ud = dscr("ud", [4, 128, L], BF16)
    yd = dscr("yd", [4, 128, L], BF16)
    ys5d = dscr("ys5d", [8, 128, L], BF16)
    yhyd = dscr("yhyd", [8, 128, L], BF16)
    binD = dscr("binD", [128, 32 * 2 * 2 * 64], BF16)
    coutD = dscr("coutD", [128, 2 * 16 * 2 * 128], BF16)
    mgD = dscr("mgD", [128, 32 * 128], BF16)
    coefD = dscr("coefD", [128, 3 * 2 * 16 * 8])
    khatD = dscr("khatD", [2, 8, 128, 4096], BF16)
    rsD = dscr("rsD", [128, 16])

    _uid = [0]

    def SBT(name, shape, dt):
        _uid[0] += 1
        return nc.sbuf_tensor("%s_%d" % (name, _uid[0]), shape, dt)

    es0 = ExitStack()
    with es0:
        kb = KB(nc, es0)
        E0 = es0.enter_context
        psum = E0(nc.psum_tensor("psum", [128, 8, 512], F32))
        barscr = E0(SBT("barscr", [128, 8], F32))
        G = {}

        def PS(b, n=512, p0=0, p1=128, off=0):
            return psum[p0:p1, b, off:off + n]

        def PS4(g):
            return psum[:, 4 * g:4 * g + 4, :].rearrange("p b n -> p (b n)")

        def pst(b):
            return 'ps%d' % b

        def pst4(g):
            return tuple('ps%d' % (4 * g + i) for i in range(4))

        def range_reduce(eng, x_ap, tmp_ap, rd, wr_tmp):
            kb.op(eng, lambda v: v.tensor_scalar(out=tmp_ap, in0=x_ap, scalar1=float(1.0 / TWO_PI), scalar2=MAGIC,
                                                 op0=ALU.mult, op1=ALU.add), reads=rd, writes=wr_tmp)
            kb.op(eng, lambda v: v.tensor_scalar(out=tmp_ap, in0=tmp_ap, scalar1=-MAGIC, scalar2=None, op0=ALU.add),
                  reads=wr_tmp, writes=wr_tmp)
            kb.op(eng, lambda v: v.scalar_tensor_tensor(out=x_ap, in0=tmp_ap, scalar=float(-TWO_PI), in1=x_ap,
                                                        op0=ALU.mult, op1=ALU.add), reads=wr_tmp + rd, writes=rd)
            kb.op(eng, lambda v: v.tensor_scalar(out=x_ap, in0=x_ap, scalar1=3.14159, scalar2=-3.14159,
                                                 op0=ALU.min, op1=ALU.max), reads=rd, writes=rd)

        def phase_norm(src, wrow, dst_xn, dst_out):
            with ExitStack() as es:
                E = es.enter_context
                h = E(SBT("n_h", [128, 8, L], F32))
                sq = E(SBT("n_sq", [128, 2, 8, 512], BF16))
                ones = E(SBT("n_ones", [128, 128], BF16))
                nw = E(SBT("n_w", [128, 8], F32))
                rt = E(SBT("n_rt", [128, 2, 512], F32))
                rstd = E(SBT("n_rstd", [128, L], F32))
                epsb = E(SBT("n_eps", [128, 1], F32))
                ob = E(SBT("n_ob", [128, 2, L], F32)) if dst_out is not None else None
                kb.op('pool', lambda g: g.memset(ones[:], 1.0), writes=('ones',))
                kb.op('pool', lambda g: g.memset(epsb[:], 1e-6), writes=('epsb',))
                kb.dma('sp', nw[:], normw[wrow], writes=('nw',))
                for c in range(8):
                    kb.dma('sp' if c % 2 == 0 else 'sp', h[:, c, :], src[c], writes=('h%d' % c,))
                for tt in range(4):
                    ts = slice(tt * 512, (tt + 1) * 512)
                    s = tt % 2
                    kb.op('act', lambda a: a.activation(out=sq[:, s], in_=h[:, :, ts], func=AF.Square),
                          reads=tuple('h%d' % c for c in range(8)), writes=('sq%d' % s,))
                    b = kb.ps()
                    for c in range(8):
                        kb.op('pe', lambda p: p.matmul(PS(b), ones[:], sq[:, s, c, :], start=(c == 0), stop=(c == 7)),
                              reads=('ones', 'sq%d' % s), writes=(pst(b),), inc=(c == 7))
                    kb.op('act', lambda a: a.activation(out=rt[:, s, :], in_=PS(b), func=AF.Sqrt, bias=epsb[:, 0:1],
                                                        scale=float(1.0 / D)),
                          reads=(pst(b), 'epsb'), writes=('rt%d' % s,))
                    kb.op('dve', lambda v: v.reciprocal(out=rstd[:, ts], in_=rt[:, s, :]), reads=('rt%d' % s,),
                          writes=('rstd%d' % tt,))
                for c in range(8):
                    if dst_out is None:
                        kb.op('dve', lambda v: v.scalar_tensor_tensor(out=dst_xn[:, c, :], in0=h[:, c, :], scalar=nw[:, c:c + 1],
                                                                      in1=rstd[:], op0=ALU.mult, op1=ALU.mult),
                              reads=('h%d' % c, 'nw') + tuple('rstd%d' % t for t in range(4)), writes=('xn%d' % c,))
                    else:
                        s = c % 2
                        kb.op('dve', lambda v: v.scalar_tensor_tensor(out=ob[:, s, :], in0=h[:, c, :], scalar=nw[:, c:c + 1],
                                                                      in1=rstd[:], op0=ALU.mult, op1=ALU.mult),
                              reads=('h%d' % c, 'nw') + tuple('rstd%d' % t for t in range(4)), writes=('ob%d' % s,))
                        kb.dma('sp', dst_out[c], ob[:, s, :], reads=('ob%d' % s,), writes=('out%d' % c,))
                kb.barrier(barscr[:, 0:1])

        def load_w(es, name, src_rows_ap, kc, ncols, q='sp'):
            E = es.enter_context
            st = E(SBT(name + "_st", [128, kc, ncols], F32))
            wb = E(SBT(name + "_bf", [128, kc, ncols], BF16))
            for k in range(kc):
                kb.dma(q if k % 2 == 0 else 'sp', st[:, k, :], src_rows_ap[k * 128:(k + 1) * 128, :], writes=(name + '_st%d' % k,))
                kb.op('act', lambda a: a.activation(out=wb[:, k, :], in_=st[:, k, :], func=AF.Identity), reads=(name + '_st%d' % k,),
                      writes=(name + '_bf',))
            return wb

        def prep_s5(l):
            with ExitStack() as es:
                E = es.enter_context
                lam = E(SBT("p_lam", [128, 3, 32], F32))
                Bt = E(SBT("p_B", [128, 2, 32, 16], F32))
                Ct = E(SBT("p_C", [128, 2, 32, 16], F32))
                kvs = E(SBT("p_kv", [128, 2, 32], F32))
                msk = E(SBT("p_msk", [128, 256], F32))
                idt = E(SBT("p_id", [128, 128], F32))
                a_re = E(SBT("p_are", [128, 32], F32))
                a_im = E(SBT("p_aim", [128, 32], F32))
                dtt = E(SBT("p_dt", [128, 32], F32))
                mag = E(SBT("p_mag", [128, 32, 32], F32))
                sn = E(SBT("p_sn", [128, 32, 32], F32))
                cs = E(SBT("p_cs", [128, 32, 32], F32))
                tmp = E(SBT("p_tmp", [128, 32, 32], F32))
                Er = E(SBT("p_Er", [128, 32, 32], F32))
                Ei = E(SBT("p_Ei", [128, 32, 32], F32))
                k4 = E(SBT("p_k4", [128, 8, 32], F32))
                Bb = E(SBT("p_Bb", [128, 2, 32, 16], F32))
                t16 = E(SBT("p_t16", [128, 2, 32, 16], F32))
                RB = E(SBT("p_RB", [128, 2, 32, 128], F32))
                CO = E(SBT("p_CO", [128, 2, 32, 128], F32))
                tb = E(SBT("p_tb", [128, 32, 128], F32))
                binS = E(SBT("p_bin", [128, 32, 2, 2, 64], BF16))
                coutS = E(SBT("p_cout", [128, 32, 2, 128], BF16))
                mgS = E(SBT("p_mg", [128, 32, 128], BF16))
                coefS = E(SBT("p_coef", [128, 3, 32, 8], F32))
                mt = E(SBT("p_mt", [128, 2, 256], F32))
                kb.dma('sp', lam[:], s5lam[l], writes=('lam',))
                kb.dma('sp', Bt[:], s5B[l], writes=('Bt',))
                kb.dma('sp', Ct[:], s5C[l], writes=('Ct',))
                kb.dma('sp', kvs[:], kvt, writes=('kvs',))
                kb.dma('sp', msk[:], masks, writes=('msk',))
                kb.dma('sp', idt[:], ident, writes=('idt',))
                V = lambda fn, r, w: kb.op('dve', fn, reads=r, writes=w)
                A = lambda fn, r, w: kb.op('act', fn, reads=r, writes=w)
                A(lambda a: a.activation(out=dtt[:], in_=lam[:, 2, :], func=AF.Exp), ('lam',), ('dtt',))
                V(lambda v: v.tensor_tensor(out=a_re[:], in0=lam[:, 0, :], in1=dtt[:], op=ALU.mult), ('lam', 'dtt'), ('a_re',))
                V(lambda v: v.tensor_tensor(out=a_im[:], in0=lam[:, 1, :], in1=dtt[:], op=ALU.mult), ('lam', 'dtt'), ('a_im',))
                kvb = kvs[:].rearrange("p d (o k) -> p d o k", o=1).to_broadcast([128, 2, 16, 32])
                are_b = a_re[:].rearrange("p (d g o) -> p d g o", d=2, o=1).to_broadcast([128, 2, 16, 32])
                aim_b = a_im[:].rearrange("p (d g o) -> p d g o", d=2, o=1).to_broadcast([128, 2, 16, 32])
                v4 = lambda t: t[:].rearrange("p (d g) k -> p d g k", d=2)
                V(lambda v: v.tensor_tensor(out=v4(tmp), in0=are_b, in1=kvb, op=ALU.mult), ('a_re', 'kvs'), ('tmp',))
                A(lambda a: a.activation(out=mag[:], in_=tmp[:], func=AF.Exp), ('tmp',), ('mag',))
                V(lambda v: v.tensor_tensor(out=v4(sn), in0=aim_b, in1=kvb, op=ALU.mult), ('a_im', 'kvs'), ('sn',))
                V(lambda v: v.tensor_scalar(out=cs[:], in0=sn[:], scalar1=float(math.pi / 2), scalar2=None, op0=ALU.add), ('sn',), ('cs',))
                range_reduce('dve', sn[:], tmp[:], ('sn',), ('tmp',))
                A(lambda a: a.activation(out=sn[:], in_=sn[:], func=AF.Sin), ('sn',), ('sn',))
                range_reduce('dve', cs[:], tmp[:], ('cs',), ('tmp',))
                A(lambda a: a.activation(out=cs[:], in_=cs[:], func=AF.Sin), ('cs',), ('cs',))
                V(lambda v: v.tensor_tensor(out=Er[:], in0=mag[:], in1=cs[:], op=ALU.mult), ('mag', 'cs'), ('Er',))
                V(lambda v: v.tensor_tensor(out=Ei[:], in0=mag[:], in1=sn[:], op=ALU.mult), ('mag', 'sn'), ('Ei',))
                lr = k4[:, 0, :]; li = k4[:, 1, :]; nr = k4[:, 2, :]; den = k4[:, 3, :]; kr = k4[:, 4, :]; ki = k4[:, 5, :]; t0 = k4[:, 6, :]
                for d in range(2):
                    idx = 8 if d == 0 else 15
                    V(lambda v: v.tensor_copy(out=lr[:, 16 * d:16 * d + 16], in_=Er[:, 16 * d:16 * d + 16, idx]), ('Er',), ('k4',))
                    V(lambda v: v.tensor_copy(out=li[:, 16 * d:16 * d + 16], in_=Ei[:, 16 * d:16 * d + 16, idx]), ('Ei',), ('k4',))
                V(lambda v: v.tensor_scalar(out=nr, in0=lr, scalar1=-1.0, scalar2=None, op0=ALU.add), ('k4',), ('k4',))
                V(lambda v: v.tensor_tensor(out=den, in0=lam[:, 0, :], in1=lam[:, 0, :], op=ALU.mult), ('lam', 'k4'), ('k4',))
                V(lambda v: v.tensor_tensor(out=t0, in0=lam[:, 1, :], in1=lam[:, 1, :], op=ALU.mult), ('lam', 'k4'), ('k4',))
                V(lambda v: v.tensor_tensor(out=den, in0=den, in1=t0, op=ALU.add), ('k4',), ('k4',))
                V(lambda v: v.reciprocal(out=den, in_=den), ('k4',), ('k4',))
                V(lambda v: v.tensor_tensor(out=kr, in0=nr, in1=lam[:, 0, :], op=ALU.mult), ('k4', 'lam'), ('k4',))
                V(lambda v: v.tensor_tensor(out=t0, in0=li, in1=lam[:, 1, :], op=ALU.mult), ('k4', 'lam'), ('k4',))
                V(lambda v: v.tensor_tensor(out=kr, in0=kr, in1=t0, op=ALU.add), ('k4',), ('k4',))
                V(lambda v: v.tensor_tensor(out=kr, in0=kr, in1=den, op=ALU.mult), ('k4',), ('k4',))
                V(lambda v: v.tensor_tensor(out=ki, in0=li, in1=lam[:, 0, :], op=ALU.mult), ('k4', 'lam'), ('k4',))
                V(lambda v: v.tensor_tensor(out=t0, in0=nr, in1=lam[:, 1, :], op=ALU.mult), ('k4', 'lam'), ('k4',))
                V(lambda v: v.tensor_tensor(out=ki, in0=ki, in1=t0, op=ALU.subtract), ('k4',), ('k4',))
                V(lambda v: v.tensor_tensor(out=ki, in0=ki, in1=den, op=ALU.mult), ('k4',), ('k4',))
                krb = kr.rearrange("p (g o) -> p g o", o=1).to_broadcast([128, 32, 16])
                kib = ki.rearrange("p (g o) -> p g o", o=1).to_broadcast([128, 32, 16])
                V(lambda v: v.tensor_tensor(out=Bb[:, 0], in0=Bt[:, 0], in1=krb, op=ALU.mult), ('Bt', 'k4'), ('Bb',))
                V(lambda v: v.tensor_tensor(out=t16[:, 0], in0=Bt[:, 1], in1=kib, op=ALU.mult), ('Bt', 'k4'), ('t16',))
                V(lambda v: v.tensor_tensor(out=Bb[:, 0], in0=Bb[:, 0], in1=t16[:, 0], op=ALU.subtract), ('Bb', 't16'), ('Bb',))
                V(lambda v: v.tensor_tensor(out=Bb[:, 1], in0=Bt[:, 1], in1=krb, op=ALU.mult), ('Bt', 'k4', 'Bb'), ('Bb',))
                V(lambda v: v.tensor_tensor(out=t16[:, 1], in0=Bt[:, 0], in1=kib, op=ALU.mult), ('Bt', 'k4', 't16'), ('t16',))
                V(lambda v: v.tensor_tensor(out=Bb[:, 1], in0=Bb[:, 1], in1=t16[:, 1], op=ALU.add), ('Bb', 't16'), ('Bb',))

                def cprod(dst, k0, X, sign_im, tag):
                    Erb = Er[:, :, k0:k0 + 8].rearrange("p g (k o) -> p g k o", o=1).to_broadcast([128, 32, 8, 16])
                    Eib = Ei[:, :, k0:k0 + 8].rearrange("p g (k o) -> p g k o", o=1).to_broadcast([128, 32, 8, 16])
                    Xr = X[:, 0].rearrange("p g (o h) -> p g o h", o=1).to_broadcast([128, 32, 8, 16])
                    Xi = X[:, 1].rearrange("p g (o h) -> p g o h", o=1).to_broadcast([128, 32, 8, 16])
                    d0 = dst[:, 0].rearrange("p g (k h) -> p g k h", k=8)
                    d1 = dst[:, 1].rearrange("p g (k h) -> p g k h", k=8)
                    tv = tb[:].rearrange("p g (k h) -> p g k h", k=8)
                    rd = ('Er', 'Ei', tag)
                    V(lambda v: v.tensor_tensor(out=d0, in0=Erb, in1=Xr, op=ALU.mult), rd, (tag + 'o',))
                    V(lambda v: v.tensor_tensor(out=tv, in0=Eib, in1=Xi, op=ALU.mult), rd, ('tb',))
                    V(lambda v: v.tensor_tensor(out=d0, in0=d0, in1=tv, op=ALU.subtract), (tag + 'o', 'tb'), (tag + 'o',))
                    V(lambda v: v.tensor_tensor(out=d1, in0=Erb, in1=Xi, op=ALU.mult), rd + (tag + 'o',), (tag + 'o',))
                    V(lambda v: v.tensor_tensor(out=tv, in0=Eib, in1=Xr, op=ALU.mult), rd + ('tb',), ('tb',))
                    V(lambda v: v.tensor_tensor(out=d1, in0=d1, in1=tv, op=ALU.add), (tag + 'o', 'tb'), (tag + 'o',))
                    if sign_im < 0:
                        V(lambda v: v.tensor_scalar(out=dst[:, 1], in0=dst[:, 1], scalar1=-1.0, scalar2=None, op0=ALU.mult),
                          (tag + 'o',), (tag + 'o',))
                cprod(RB, 0, Bb, +1, 'Bb')
                cprod(CO, 8, Ct, -1, 'Ct')
                for ri in range(2):
                    A(lambda a: a.activation(out=coutS[:, :, ri, :], in_=CO[:, ri], func=AF.Identity), ('Cto',), ('coutS',))
                kb.dma('sp', coutD, coutS[:].rearrange("p a b c -> p (a b c)"), reads=('coutS',), writes=('coutD',))
                cprod(CO, 16, Ct, -1, 'Ct')
                RC = CO
                A(lambda a: a.activation(out=coefS[:, 0], in_=Er[:, :, 24:32], func=AF.Identity), ('Er',), ('coefS',))
                A(lambda a: a.activation(out=coefS[:, 1], in_=Ei[:, :, 24:32], func=AF.Identity), ('Ei',), ('coefS',))
                A(lambda a: a.activation(out=coefS[:, 2], in_=Ei[:, :, 24:32], func=AF.Identity, scale=-1.0), ('Ei',), ('coefS',))
                kb.dma('sp', coefD, coefS[:].rearrange("p a b c -> p (a b c)"), reads=('coefS',), writes=('coefD',))
                for dg in range(32):
                    d, gp = dg // 16, dg % 16
                    b = kb.ps()
                    for ri in range(2):
                        kb.op('pe', lambda p: p.transpose(PS(b, 128, off=128 * ri), RB[:, ri, dg, :], idt[:]),
                              reads=('Bbo', 'idt'), writes=(pst(b),), inc=(ri == 1))
                    for ri in range(2):
                        A(lambda a: a.activation(out=binS[:, 2 * gp:2 * gp + 2, d, ri, :],
                                                 in_=PS(b, 128, off=128 * ri).rearrange("p (g q) -> p g q", g=2), func=AF.Identity),
                          (pst(b),), ('binS',))
                kb.dma('sp', binD, binS[:].rearrange("p a b c e -> p (a b c e)"), reads=('binS',), writes=('binD',))
                for g in range(32):
                    gp, gpar = g // 2, g % 2
                    b = kb.ps()
                    p0, p1 = 64 * gpar, 64 * gpar + 64
                    for d in range(2):
                        dg = 16 * d + gp
                        kb.op('pe', lambda p: p.matmul(PS(b, 128, off=128 * d), RB[p0:p1, 0, dg, :], RC[p0:p1, 0, dg, :],
                                                       start=True, stop=False),
                              reads=('Bbo', 'Cto'), writes=(pst(b),), inc=False)
                        kb.op('pe', lambda p: p.matmul(PS(b, 128, off=128 * d), RB[p0:p1, 1, dg, :], RC[p0:p1, 1, dg, :],
                                                       start=False, stop=True),
                              reads=('Bbo', 'Cto'), writes=(pst(b),), inc=(d == 1))
                    s = g % 2
                    V(lambda v: v.tensor_tensor(out=mt[:, s, :], in0=PS(b, 256), in1=msk[:], op=ALU.mult), (pst(b), 'msk'), ('mt%d' % s,))
                    V(lambda v: v.tensor_tensor(out=mgS[:, g, :], in0=mt[:, s, 0:128], in1=mt[:, s, 128:256], op=ALU.add),
                      ('mt%d' % s,), ('mgS',))
                kb.dma('sp', mgD, mgS[:].rearrange("p a b -> p (a b)"), reads=('mgS',), writes=('mgD',))
                kb.barrier(barscr[:, 0:1])

        def phase_s5(l):
            with ExitStack() as es_outer:
                EO = es_outer.enter_context
                gs = EO(SBT("s_gs", [128, 4, L], BF16))
                with ExitStack() as es:
                    E = es.enter_context
                    wb = load_w(es, "s_w", w_in[l][:, 0:1024], 8, 1024)
                    udt = E(SBT("s_ud", [128, 4, 8, 256], BF16))
                    for cc in range(8):
                        for tt in range(4):
                            b = kb.ps()
                            for k in range(8):
                                kb.op('pe', lambda p: p.matmul(PS(b), wb[:, k, cc * 128:(cc + 1) * 128], G['xn'][:, k, tt * 512:(tt + 1) * 512],
                                                               start=(k == 0), stop=(k == 7)),
                                      reads=('s_w_bf', 'xn%d' % k), writes=(pst(b),), inc=(k == 7))
                            if cc < 4:
                                kb.op('act', lambda a: a.activation(out=udt[:, cc, :, tt * 64:(tt + 1) * 64],
                                                                    in_=PS(b).rearrange("p (c j) -> p j c", j=8), func=AF.Identity),
                                      reads=(pst(b),), writes=('udt%d' % cc,))
                            else:
                                kb.op('act', lambda a: a.activation(out=gs[:, cc - 4, tt * 512:(tt + 1) * 512], in_=PS(b), func=AF.Silu),
                                      reads=(pst(b),), writes=('gs',))
                        if cc < 4:
                            kb.dma('sp', ud[cc], udt[:, cc].rearrange("p j c -> p (j c)"), reads=('udt%d' % cc,), writes=('ud',))
                    kb.barrier(barscr[:, 0:1])
                with ExitStack() as es:
                    E = es.enter_context
                    U8 = E(SBT("s_U8", [128, 32, 256], BF16))
                    Mg = E(SBT("s_Mg", [128, 32, 128], BF16))
                    Bin = E(SBT("s_Bin", [128, 32, 2, 2, 64], BF16))
                    Cout = E(SBT("s_Cout", [128, 32, 2, 128], BF16))
                    coef = E(SBT("s_coef", [128, 3, 32, 8], F32))
                    Xs = E(SBT("s_Xs", [128, 32, 2, 256], BF16))
                    Y8 = E(SBT("s_Y8", [128, 32, 256], BF16))
                    NSL = 2
                    XA = E(SBT("s_XA", [128, NSL, 2, 768], F32))
                    XB = E(SBT("s_XB", [128, NSL, 2, 768], F32))
                    T1 = E(SBT("s_T1", [128, NSL, 2, 256], F32))
                    udv = ud.rearrange("cc (g h) (j c) -> h j (cc g) c", h=16, j=8)
                    for j in range(8):
                        kb.dma('sp' if j % 2 == 0 else 'sp', U8[16 * j:16 * j + 16, :, :], udv[:, j], reads=('ud',), writes=('U8',))
                    kb.dma('sp', Mg[:].rearrange("p a b -> p (a b)"), mgD, reads=('mgD',), writes=('Mg',))
                    kb.dma('sp', Bin[:].rearrange("p a b c e -> p (a b c e)"), binD, reads=('binD',), writes=('Bin',))
                    kb.dma('sp', Cout[:].rearrange("p a b c -> p (a b c)"), coutD, reads=('coutD',), writes=('Cout',))
                    kb.dma('sp', coef[:].rearrange("p a b c -> p (a b c)"), coefD, reads=('coefD',), writes=('coef',))
                    kb.op('pool', lambda g: g.memset(XA[:], 0.0), writes=tuple('XA%d' % i for i in range(NSL)))
                    kb.op('pool', lambda g: g.memset(XB[:], 0.0), writes=tuple('XB%d' % i for i in range(NSL)))
                    for s0 in range(0, 32, NSL):
                        for i in range(NSL):
                            dg = s0 + i
                            d, gp = dg // 16, dg % 16
                            b = kb.ps()
                            for ri in range(2):
                                for gpar in range(2):
                                    g = 2 * gp + gpar
                                    kb.op('pe', lambda p: p.matmul(PS(b, 256, 64 * gpar, 64 * gpar + 64, off=256 * ri),
                                                                   Bin[:, g, d, ri, :], U8[:, g, :], start=True, stop=True),
                                          reads=('Bin', 'U8'), writes=(pst(b),), inc=(ri == 1 and gpar == 1))
                            kb.op('act', lambda a: a.activation(out=XA[:, i, :, 256:512], in_=PS(b).rearrange("p (r c) -> p r c", r=2),
                                                                func=AF.Identity),
                                  reads=(pst(b),), writes=('XA%d' % i,))
                        for r in range(8):
                            sft = 2 ** r
                            for i in range(NSL):
                                dg = s0 + i
                                d = dg // 16
                                src, dst = (XA, XB) if r % 2 == 0 else (XB, XA)
                                sn_, dn_ = ('XA%d' % i, 'XB%d' % i) if r % 2 == 0 else ('XB%d' % i, 'XA%d' % i)
                                lo = 256 - sft if d == 0 else 256 + sft
                                e_ = coef[:, 0, dg, r:r + 1]; f_ = coef[:, 1, dg, r:r + 1]; nf_ = coef[:, 2, dg, r:r + 1]
                                Rs = src[:, i, 0, lo:lo + 256]; Is = src[:, i, 1, lo:lo + 256]
                                R0 = src[:, i, 0, 256:512]; I0 = src[:, i, 1, 256:512]
                                tn = 'T1_%d' % i
                                kb.op('dve', lambda v: v.scalar_tensor_tensor(out=T1[:, i, 0, :], in0=Rs, scalar=e_, in1=R0, op0=ALU.mult, op1=ALU.add),
                                      reads=(sn_, 'coef'), writes=(tn + 'a',))
                                kb.op('dve', lambda v: v.scalar_tensor_tensor(out=dst[:, i, 0, 256:512], in0=Is, scalar=nf_, in1=T1[:, i, 0, :],
                                                                              op0=ALU.mult, op1=ALU.add),
                                      reads=(sn_, 'coef', tn + 'a'), writes=(dn_ + 'r',))
                                kb.op('dve', lambda v: v.scalar_tensor_tensor(out=T1[:, i, 1, :], in0=Rs, scalar=f_, in1=I0, op0=ALU.mult, op1=ALU.add),
                                      reads=(sn_, 'coef'), writes=(tn + 'b',))
                                kb.op('dve', lambda v: v.scalar_tensor_tensor(out=dst[:, i, 1, 256:512], in0=Is, scalar=e_, in1=T1[:, i, 1, :],
                                                                              op0=ALU.mult, op1=ALU.add),
                                      reads=(sn_, 'coef', tn + 'b'), writes=(dn_, dn_ + 'r'))
                        for i in range(NSL):
                            dg = s0 + i
                            d = dg // 16
                            lo = 255 if d == 0 else 257
                            kb.op('act', lambda a: a.activation(out=Xs[:, dg, :, :], in_=XA[:, i, :, lo:lo + 256], func=AF.Identity),
                                  reads=('XA%d' % i, 'XA%dr' % i), writes=('Xs',))
                    for g in range(32):
                        gp, gpar = g // 2, g % 2
                        p0, p1 = 64 * gpar, 64 * gpar + 64
                        b = kb.ps()
                        kb.op('pe', lambda p: p.matmul(PS(b, 256), Mg[:, g, :], U8[:, g, :], start=True, stop=False),
                              reads=('Mg', 'U8'), writes=(pst(b),), inc=False)
                        for d in range(2):
                            for ri in range(2):
                                last = (d == 1 and ri == 1)
                                kb.op('pe', lambda p: p.matmul(PS(b, 256), Cout[p0:p1, 16 * d + gp, ri, :], Xs[p0:p1, 16 * d + gp, ri, :],
                                                               start=False, stop=last),
                                      reads=('Cout', 'Xs'), writes=(pst(b),), inc=last)
                        kb.op('act', lambda a: a.activation(out=Y8[:, g, :], in_=PS(b, 256), func=AF.Identity), reads=(pst(b),), writes=('Y8',))
                    ydv = yd.rearrange("cc (g h) (i c) -> h i (cc g) c", h=16, i=8)
                    for i in range(8):
                        kb.dma('sp' if i % 2 == 0 else 'sp', ydv[:, i], Y8[16 * i:16 * i + 16, :, :], reads=('Y8',), writes=('yd',))
                    kb.barrier(barscr[:, 0:1])
                with ExitStack() as es:
                    E = es.enter_context
                    wg = E(SBT("c_wg", [128, 4, 512], BF16))
                    wbr = E(SBT("c_wbr", [128, 4, 1024], BF16))
                    wst = E(SBT("c_wst", [128, 2, 1024], F32))
                    yt = E(SBT("c_y", [128, L], BF16))
                    ut = E(SBT("c_u", [128, L], BF16))
                    y1 = E(SBT("c_y1", [128, L], F32))
                    t3 = E(SBT("c_t3", [128, L], F32))
                    yg = E(SBT("c_yg", [128, 4, L], BF16))
                    sg = E(SBT("c_sg", [128, 2, 512], F32))
                    y3 = E(SBT("c_y3", [128, 4, L], BF16))
                    ys = E(SBT("c_ys", [128, L], BF16))
                    dv = E(SBT("c_d", [128, 4], F32))
                    bg = E(SBT("c_bg", [128, 4], F32))
                    kb.dma('sp', dv[:], s5d[l], writes=('dv',))
                    kb.dma('sp', bg[:], b_glu[l], writes=('bg',))
                    for k in range(4):
                        s = k % 2
                        kb.dma('sp', wst[:, s, 0:512], w_glu[l][k * 128:(k + 1) * 128, :], writes=('wst%d' % s,))
                        kb.op('act', lambda a: a.activation(out=wg[:, k, :], in_=wst[:, s, 0:512], func=AF.Identity), reads=('wst%d' % s,), writes=('wg',))
                    for k in range(4):
                        s = k % 2
                        kb.dma('sp', wst[:, s, :], w_bs5[l][k * 128:(k + 1) * 128, :], writes=('wst%d' % s,))
                        kb.op('act', lambda a: a.activation(out=wbr[:, k, :], in_=wst[:, s, :], func=AF.Identity), reads=('wst%d' % s,), writes=('wbr',))
                    for cc in range(4):
                        kb.dma('sp', yt[:], yd[cc], reads=('yd',), writes=('yt',))
                        kb.dma('sp', ut[:], ud[cc], reads=('ud',), writes=('ut',))
                        kb.op('dve', lambda v: v.scalar_tensor_tensor(out=y1[:], in0=ut[:], scalar=dv[:, cc:cc + 1], in1=yt[:],
                                                                      op0=ALU.mult, op1=ALU.add),
                              reads=('ut', 'yt', 'dv'), writes=('y1',))
                        kb.op('dve', lambda v: v.tensor_tensor(out=t3[:], in0=y1[:], in1=y1[:], op=ALU.mult), reads=('y1',), writes=('t3',))
                        kb.op('dve', lambda v: v.tensor_scalar(out=t3[:], in0=t3[:], scalar1=0.044715 * 1.5957691216, scalar2=1.5957691216,
                                                               op0=ALU.mult, op1=ALU.add), reads=('t3',), writes=('t3',))
                        kb.op('dve', lambda v: v.tensor_tensor(out=t3[:], in0=t3[:], in1=y1[:], op=ALU.mult), reads=('t3', 'y1'), writes=('t3',))
                        kb.op('act', lambda a: a.activation(out=t3[:], in_=t3[:], func=AF.Sigmoid), reads=('t3',), writes=('t3',))
                        kb.op('dve', lambda v: v.tensor_tensor(out=yg[:, cc, :], in0=t3[:], in1=y1[:], op=ALU.mult), reads=('t3', 'y1'), writes=('yg',))
                    gsp = gs[:].rearrange("p a (c j) -> p a j c", j=8)
                    for cc in range(4):
                        for tt in range(4):
                            b = kb.ps()
                            for k in range(4):
                                kb.op('pe', lambda p: p.matmul(PS(b), wg[:, k, cc * 128:(cc + 1) * 128], yg[:, k, tt * 512:(tt + 1) * 512],
                                                               start=(k == 0), stop=(k == 3)),
                                      reads=('wg', 'yg'), writes=(pst(b),), inc=(k == 3))
                            s = tt % 2
                            kb.op('act', lambda a: a.activation(out=sg[:, s, :], in_=PS(b), func=AF.Sigmoid, bias=bg[:, cc:cc + 1]),
                                  reads=(pst(b), 'bg'), writes=('sg%d' % s,))
                            kb.op('dve', lambda v: v.tensor_tensor(out=sg[:, s, :], in0=sg[:, s, :], in1=yg[:, cc, tt * 512:(tt + 1) * 512], op=ALU.mult),
                                  reads=('sg%d' % s, 'yg'), writes=('sg%d' % s,))
                            kb.op('dve', lambda v: v.tensor_tensor(out=y3[:, cc, tt * 512:(tt + 1) * 512].rearrange("p (j c) -> p j c", j=2),
                                                                   in0=sg[:, s, :].rearrange("p (j c) -> p j c", j=2),
                                                                   in1=gsp[:, cc, 2 * tt:2 * tt + 2, :], op=ALU.mult),
                                  reads=('sg%d' % s, 'gs'), writes=('y3',))
                    for dc in range(8):
                        for tt in range(4):
                            b = kb.ps()
                            for k in range(4):
                                kb.op('pe', lambda p: p.matmul(PS(b), wbr[:, k, dc * 128:(dc + 1) * 128], y3[:, k, tt * 512:(tt + 1) * 512],
                                                               start=(k == 0), stop=(k == 3)),
                                      reads=('wbr', 'y3'), writes=(pst(b),), inc=(k == 3))
                            kb.op('act', lambda a: a.activation(out=ys[:].rearrange("p (c j) -> p j c", j=8)[:, 2 * tt:2 * tt + 2, :],
                                                                in_=PS(b).rearrange("p (j c) -> p j c", j=2), func=AF.Identity),
                                  reads=(pst(b),), writes=('ys',))
                        kb.dma('sp', ys5d[dc], ys[:], reads=('ys',), writes=('ys5d',))
                    kb.barrier(barscr[:, 0:1])

        def PS2(b0):
            return psum[:, b0:b0 + 2, :].rearrange("p b n -> p (b n)")

        def pst2(b0):
            return ('ps%d' % b0, 'ps%d' % (b0 + 1))

        def next_pair():
            p_ = kb.fp % 2
            kb.fp += 1
            return 2 * p_

        def fft_fwd(C, zin_list, B, spec_evac, filler=lambda: None):
            for q in range(4):
                b0 = next_pair()
                for cpl in range(4):
                    cp = q * 4 + cpl
                    bank = b0 + cpl // 2
                    off = 256 * (cpl % 2)
                    for zi, (zT, ztok, gk) in enumerate(zin_list):
                        kb.op('pe', lambda p: p.matmul(PS(bank, 256, off=off), zT[:, 2 * cp:2 * cp + 2, :].rearrange("p a b -> p (a b)"),
                                                       C['G1hi'] if gk else C['G1'], start=(zi == 0), stop=(zi == len(zin_list) - 1)),
                              reads=(ztok,) + C['toks'], writes=(pst(bank),), inc=(zi == len(zin_list) - 1))
                kb.op('act', lambda a: a.activation(
                    out=B[:].rearrange("p k r cp cq -> p (k r) cp cq")[:, :, q * 4:q * 4 + 4, :],
                    in_=PS2(b0).rearrange("p (cp cq kr) -> p kr cp cq", cp=4, cq=4), func=AF.Identity),
                    reads=pst2(b0), writes=('B',))
                if q % 2 == 1:
                    filler()
            for c2 in range(2):
                for h in range(2):
                    b0 = next_pair()
                    for k1l in range(16):
                        k1 = 16 * h + k1l
                        bank = b0 + k1l // 8
                        off = 64 * (k1l % 8)
                        for ri in range(2):
                            kb.op('pe', lambda p: p.matmul(PS(bank, 64, off=off), C['T'][64 * c2:64 * c2 + 64, k1, ri, :],
                                                           B[64 * c2:64 * c2 + 64, k1, ri].rearrange("p a b -> p (a b)"),
                                                           start=(ri == 0), stop=(ri == 1)),
                                  reads=('B',) + C['toks'], writes=(pst(bank),), inc=(ri == 1))
                    spec_evac(c2, h, PS2(b0).rearrange("p (k m) -> p m k", k=16), pst2(b0))
                filler()

        def fft_inv(C, P1, P2, Dt, conv_out, conv_tok, filler=lambda: None):
            for c2 in range(2):
                for h in range(2):
                    b0 = next_pair()
                    for cpl in range(8):
                        cp = 8 * h + cpl
                        bank = b0 + cpl // 4
                        off = 128 * (cpl % 4)
                        kb.op('pe', lambda p: p.matmul(PS(bank, 128, off=off), P1[:, c2, cp * 128:(cp + 1) * 128], C['Ga'], start=True, stop=False),
                              reads=('P1',) + C['toks'], writes=(pst(bank),), inc=False)
                        kb.op('pe', lambda p: p.matmul(PS(bank, 128, off=off), P2[:, c2, cp * 128:(cp + 1) * 128], C['Gb'], start=False, stop=True),
                              reads=('P2',) + C['toks'], writes=(pst(bank),), inc=True)
                    kb.op('act', lambda a: a.activation(
                        out=Dt[:, c2].rearrange("p n r cp -> p (n r) cp")[:, :, 8 * h:8 * h + 8],
                        in_=PS2(b0).rearrange("p (cp nr) -> p nr cp", cp=8), func=AF.Identity),
                        reads=pst2(b0), writes=('B',))
                filler()
            for h in range(2):
                b0 = next_pair()
                for n2l in range(32):
                    n2 = 32 * h + n2l
                    bank = b0 + n2l // 16
                    off = 32 * (n2l % 16)
                    for r in range(2):
                        kb.op('pe', lambda p: p.matmul(PS(bank, 32, off=off), C['H'][:, n2, r, :],
                                                       Dt[:, :, n2, r, :].rearrange("p c2 cp -> p cp c2"), start=(r == 0), stop=(r == 1)),
                              reads=('B',) + C['toks'], writes=(pst(bank),), inc=(r == 1))
                kb.op('dve', lambda v: v.transpose(
                    out=conv_out.rearrange("p (n1 n2) -> p n2 n1", n2=64)[:, 32 * h:32 * h + 32, :],
                    in_=PS2(b0).rearrange("p (n c) -> p n c", c=32)),
                    reads=pst2(b0), writes=(conv_tok,))
            filler()

        def load_fft_consts(es, with_filter):
            E = es.enter_context
            fs = E(SBT("f_sm", [128, 896], BF16))
            Tb = E(SBT("f_T", [128, 32, 2, 128], BF16))
            kb.dma('sp', fs[:], fsmall, writes=('fc',))
            Tv = Tb[:].rearrange("p a b c -> p (a b c)")
            for i in range(2):
                kb.dma('sp', Tv[:, i * 4096:(i + 1) * 4096], fT[:, i * 4096:(i + 1) * 4096], writes=('fcT%d' % i,))
            C = {'G1': fs[:, 0:256], 'G1hi': fs[:, 256:512], 'Ga': fs[:, 512:640], 'Gb': fs[:, 640:768], 'Sw': fs[:, 768:896], 'T': Tb}
            C['toks'] = ('fc', 'fcT0', 'fcT1')
            if not with_filter:
                Hb = E(SBT("f_H", [128, 64, 2, 128], BF16))
                Hv = Hb[:].rearrange("p a b c -> p (a b c)")
                for i in range(4):
                    kb.dma('sp', Hv[:, i * 4096:(i + 1) * 4096], fH[:, i * 4096:(i + 1) * 4096], writes=('fcH%d' % i,))
                C['H'] = Hb
                C['toks'] = C['toks'] + tuple('fcH%d' % i for i in range(4))
            return C

        def prep_hy(l):
            with ExitStack() as es:
                E = es.enter_context
                C = load_fft_consts(es, True)
                w1 = E(SBT("q_w1", [33, 64], F32))
                w2 = E(SBT("q_w2", [64, 64], F32))
                w3 = E(SBT("q_w3", [64, 4096], F32))
                hv = E(SBT("q_hv", [64, 3], F32))
                bf = E(SBT("q_bf", [64, 2], F32))
                dec = E(SBT("q_dec", [128, 32], F32))
                ft = E(SBT("q_ft", [33, 2, L], F32))
                tt_ = E(SBT("q_tt", [128, 2, L], F32))
                h1 = E(SBT("q_h1", [64, L], F32))
                h2 = E(SBT("q_h2", [64, 2, L], F32))
                tmp = E(SBT("q_tmp", [64, L], F32))
                win = E(SBT("q_win", [128, 2, L], F32))
                hk = E(SBT("q_hk", [128, 2, 2, L], BF16))
                rsS = E(SBT("q_rs", [128, 16], F32))
                junk = E(SBT("q_junk", [128, L], BF16))
                ss = E(SBT("q_ss", [128, 2, 4], F32))
                zT = E(SBT("q_zT", [128, 2, 32, 64], BF16))
                B = E(SBT("q_B", [128, 32, 2, 16, 4], BF16))
                Kh = E(SBT("q_Kh", [128, 2, 2048], BF16))
                kb.dma('sp', w1[:], hw1[l], writes=('w1',))
                kb.dma('sp', w2[:], hw2[l], writes=('w2',))
                kb.dma('sp', w3[:], hw3[l], writes=('w3',))
                kb.dma('sp', hv[:], hvec[l], writes=('hv',))
                kb.dma('sp', dec[:], hdec[l], writes=('dec',))
                for i in range(2):
                    kb.dma('sp', ft[:, i, :], feats[i], writes=('ft',))
                    kb.dma('sp', tt_[:, i, :], ttab[i], writes=('tt',))
                V = lambda fn, r, w: kb.op('dve', fn, reads=r, writes=w)
                A = lambda fn, r, w: kb.op('act', fn, reads=r, writes=w)
                V(lambda v: v.tensor_tensor(out=bf[:, 0:1], in0=hv[:, 0:1], in1=hv[:, 2:3], op=ALU.mult), ('hv',), ('bf',))
                V(lambda v: v.tensor_tensor(out=bf[:, 1:2], in0=hv[:, 1:2], in1=hv[:, 2:3], op=ALU.mult), ('hv', 'bf'), ('bf',))
                A(lambda a: a.activation(out=dec[:], in_=dec[:], func=AF.Abs), ('dec',), ('dec',))
                V(lambda v: v.tensor_scalar(out=dec[:], in0=dec[:], scalar1=-1.0, scalar2=None, op0=ALU.mult), ('dec',), ('dec',))
                for i in range(2):
                    for stage in range(2):
                        wmat = w1 if stage == 0 else w2
                        dst = h1 if stage == 0 else h2[:, i, :]
                        dtok = 'h1' if stage == 0 else 'h2_%d' % i
                        for t4 in range(4):
                            ts = slice(t4 * 512, (t4 + 1) * 512)
                            b = kb.ps()
                            if stage == 0:
                                kb.op('pe', lambda p: p.matmul(PS(b, 512, 0, 64), w1[:], ft[:, i, ts], start=True, stop=True),
                                      reads=('w1', 'ft'), writes=(pst(b),))
                            else:
                                kb.op('pe', lambda p: p.matmul(PS(b, 512, 0, 64), w2[:], h1[:, ts], start=True, stop=True),
                                      reads=('w2', 'h1'), writes=(pst(b),))
                            dsl = dst[:, ts] if stage == 0 else h2[:, i, ts]
                            A(lambda a: a.activation(out=dsl, in_=PS(b, 512, 0, 64), func=AF.Identity, bias=bf[:, stage:stage + 1], scale=hv[:, 2:3]),
                              (pst(b), 'bf', 'hv'), (dtok,))
                        dfull = h1[:] if stage == 0 else h2[:, i, :]
                        range_reduce('dve', dfull, tmp[:], (dtok,), ('tmp',))
                        A(lambda a: a.activation(out=dfull, in_=dfull, func=AF.Sin), (dtok,), (dtok,))
                NIT = 16
                w3b = E(SBT("q_w3b", [64, 4096], BF16))
                h2b = E(SBT("q_h2b", [64, 2, L], BF16))
                for q_ in range(4):
                    A(lambda a: a.activation(out=w3b[:, q_ * 1024:(q_ + 1) * 1024], in_=w3[:, q_ * 1024:(q_ + 1) * 1024], func=AF.Identity), ('w3',), ('w3b',))
                V(lambda v: v.tensor_copy(out=h2b[:], in_=h2[:]), ('h2_0', 'h2_1'), ('h2b',))

                def make_A(n):
                    o, cc = n // 8, n % 8
                    par = n % 2
                    pieces = []
                    for dr in range(2):
                        for t4 in range(4):
                            def piece(dr=dr, t4=t4):
                                fc = o * 16 + dr * 8 + cc
                                if t4 == 0 and dr == 0:
                                    for d2 in range(2):
                                        fc2 = o * 16 + d2 * 8 + cc
                                        A(lambda a: a.activation(out=win[:, d2, :], in_=tt_[:, d2, :], func=AF.Exp, scale=dec[:, fc2:fc2 + 1]),
                                          ('tt', 'dec'), ('win%d' % d2,))
                                ts = slice(t4 * 512, (t4 + 1) * 512)
                                b = kb.ps_hi()
                                kb.op('pe', lambda p: p.matmul(PS(b), w3b[:, fc * 128:(fc + 1) * 128], h2b[:, dr, ts], start=True, stop=True),
                                      reads=('w3b', 'h2b'), writes=(pst(b),))
                                V(lambda v: v.scalar_tensor_tensor(out=hk[:, par, dr, ts], in0=win[:, dr, ts], scalar=0.05, in1=PS(b), op0=ALU.add, op1=ALU.mult),
                                  ('win%d' % dr, pst(b)), ('hk%d_%d' % (par, dr),))
                                if t4 == 3:
                                    if dr == 1:
                                        V(lambda v: v.memset(hk[:, par, 1, 0:1], 0.0), ('hk%d_1' % par,), ('hk%d_1' % par,))
                                    A(lambda a: a.activation(out=junk[:], in_=hk[:, par, dr, :], func=AF.Square, accum_out=ss[:, par, dr:dr + 1]),
                                      ('hk%d_%d' % (par, dr),), ('junk', 'ss%d_%d' % (par, dr)))
                            pieces.append(piece)
                    return pieces

                def run_B(n, filler):
                    o, cc = n // 8, n % 8
                    par = n % 2
                    V(lambda v: v.tensor_tensor(out=ss[:, par, 2:3], in0=ss[:, par, 0:1], in1=ss[:, par, 1:2], op=ALU.add),
                      ('ss%d_0' % par, 'ss%d_1' % par), ('ss%d_2' % par,))
                    A(lambda a: a.activation(out=ss[:, par, 2:3], in_=ss[:, par, 2:3], func=AF.Sqrt, bias=barscr[:, 1:2]), ('ss%d_2' % par,), ('ss%d_2' % par,))
                    V(lambda v: v.reciprocal(out=rsS[:, n:n + 1], in_=ss[:, par, 2:3]), ('ss%d_2' % par,), ('rsS',))
                    for dr in range(2):
                        V(lambda v: v.transpose(out=zT[:, dr].bitcast(U32).rearrange("p c m -> p m c"),
                                                in_=hk[:, par, dr, :].bitcast(U32).rearrange("p (n1 m) -> p m n1", m=32)),
                          ('hk%d_%d' % (par, dr),), ('zT%d' % dr,))
                        filler()

                    def spec_evac(c2, h, psv, ptoks):
                        V(lambda v: v.tensor_copy(out=Kh[:, c2, :].rearrange("p (m k) -> p m k", k=32)[:, :, 16 * h:16 * h + 16], in_=psv),
                          ptoks, ('Kh',))
                    fft_fwd(C, [(zT[:, 0], 'zT0', 0), (zT[:, 1], 'zT1', 1)], B, spec_evac, filler)
                    kb.dma('sp', khatD[o, cc], Kh[:].rearrange("p a b -> p (a b)"), reads=('Kh',), writes=('khatD',))

                cur = make_A(0)
                for pc in cur:
                    pc()
                for n in range(NIT):
                    nxt = make_A(n + 1) if n + 1 < NIT else []

                    def filler():
                        if nxt:
                            nxt.pop(0)()
                    run_B(n, filler)
                    while nxt:
                        nxt.pop(0)()
                kb.dma('sp', rsD, rsS[:], reads=('rsS',), writes=('rsD',))
                kb.barrier(barscr[:, 0:1])

        def phase_hy(l):
            with ExitStack() as es:
                E = es.enter_context
                C = load_fft_consts(es, False)
                cw = E(SBT("h_cw", [128, 3, 24], F32))
                cb_ = E(SBT("h_cb", [128, 24], F32))
                dd = E(SBT("h_dd", [128, 2, 8], F32))
                wst = E(SBT("h_wst", [128, 2, 8, 128], F32))
                wbf = E(SBT("h_wbf", [128, 4, 8, 128], BF16))
                pp = E(SBT("h_pp", [128, 2, L + 2], F32))
                vx = E(SBT("h_vx", [128, 2, 4, L], BF16))
                z1 = E(SBT("h_z1", [128, L], BF16))
                zT = E(SBT("h_zT", [128, 32, 64], BF16))
                BD = E(SBT("h_BD", [128, 4096], BF16))
                B = BD[:].rearrange("p (k r cp cq) -> p k r cp cq", k=32, r=2, cp=16)
                Dt = BD[:].rearrange("p (c2 n r cp) -> p c2 n r cp", c2=2, n=64, r=2)
                P1 = E(SBT("h_P1", [128, 2, 2048], BF16))
                P2 = E(SBT("h_P2", [128, 2, 2048], BF16))
                Kt = E(SBT("h_Kt", [128, 2, 4096], BF16))
                conv = E(SBT("h_conv", [128, L], F32))
                sc = conv
                kb.dma('sp', cw[:], convw[l], writes=('cw',))
                kb.dma('sp', cb_[:], convb[l], writes=('cb',))
                kb.dma('sp', dd[:], hyd[l], writes=('dd',))
                rs = E(SBT("h_rs", [128, 2, 16], F32))
                kb.dma('sp', rs[:, 0, :], rsD, reads=('rsD',), writes=('rs',))
                kb.op('dve', lambda v: v.reciprocal(out=rs[:, 1, :], in_=rs[:, 0, :]), reads=('rs',), writes=('rs',))
                kb.op('dve', lambda v: v.tensor_tensor(out=cw[:, :, 8:24], in0=cw[:, :, 8:24],
                                                       in1=rs[:, 0:1, :].to_broadcast([128, 3, 16]), op=ALU.mult), reads=('cw', 'rs'), writes=('cw',))
                kb.op('dve', lambda v: v.tensor_tensor(out=cb_[:, 8:24], in0=cb_[:, 8:24], in1=rs[:, 0, :], op=ALU.mult), reads=('cb', 'rs'), writes=('cb',))
                kb.op('dve', lambda v: v.tensor_tensor(out=dd[:].rearrange("p o c -> p (o c)"), in0=dd[:].rearrange("p o c -> p (o c)"), in1=rs[:, 1, :], op=ALU.mult),
                      reads=('dd', 'rs'), writes=('dd',))
                kb.op('pool', lambda g: g.memset(pp[:], 0.0), writes=('pp0', 'pp1'))
                wcnt = [0]

                def load_stream(cb, si):
                    slot = wcnt[0] % 2
                    wcnt[0] += 1
                    col = [1024, 2048, 3072, 4096][si] + cb * 128
                    for k in range(8):
                        kb.dma('sp', wst[:, slot, k, :], w_in[l][k * 128:(k + 1) * 128, col:col + 128], writes=('wst%d' % slot,))
                    kb.op('act', lambda a: a.activation(out=wbf[:, si], in_=wst[:, slot], func=AF.Identity), reads=('wst%d' % slot,), writes=('wbf%d' % si,))

                def short_conv(cb, si):
                    par = cb % 2
                    sp_ = si % 2
                    ch = si * 8 + cb
                    kb.op('dve', lambda v: v.tensor_scalar(out=sc[:], in0=pp[:, sp_, 1:L + 1], scalar1=cw[:, 1, ch:ch + 1], scalar2=cb_[:, ch:ch + 1],
                                                           op0=ALU.mult, op1=ALU.add),
                          reads=('pp%d' % sp_, 'cw', 'cb'), writes=('conv',))
                    kb.op('dve', lambda v: v.scalar_tensor_tensor(out=sc[:], in0=pp[:, sp_, 0:L], scalar=cw[:, 0, ch:ch + 1], in1=sc[:],
                                                                  op0=ALU.mult, op1=ALU.add),
                          reads=('pp%d' % sp_, 'cw', 'conv'), writes=('conv',))
                    kb.op('dve', lambda v: v.scalar_tensor_tensor(out=vx[:, par, si, :], in0=pp[:, sp_, 2:L + 2], scalar=cw[:, 2, ch:ch + 1], in1=sc[:],
                                                                  op0=ALU.mult, op1=ALU.add),
                          reads=('pp%d' % sp_, 'cw', 'conv'), writes=('vx%d_%d' % (par, si),))

                def make_A(cb):
                    par = cb % 2
                    base = []
                    for si in range(4):
                        for t4 in range(4):
                            def piece(si=si, t4=t4):
                                sp_ = si % 2
                                ts = slice(t4 * 512, (t4 + 1) * 512)
                                b = kb.ps_hi()
                                for k in range(8):
                                    kb.op('pe', lambda p: p.matmul(PS(b), wbf[:, si, k, :], G['xn'][:, k, ts], start=(k == 0), stop=(k == 7)),
                                          reads=('wbf%d' % si, 'xn%d' % k), writes=(pst(b),), inc=(k == 7))
                                if si < 3:
                                    kb.op('act', lambda a: a.activation(out=pp[:, sp_, 1 + t4 * 512:1 + (t4 + 1) * 512], in_=PS(b), func=AF.Identity),
                                          reads=(pst(b),), writes=('pp%d' % sp_,))
                                else:
                                    kb.op('act', lambda a: a.activation(out=vx[:, par, 3, ts], in_=PS(b), func=AF.Silu), reads=(pst(b),), writes=('vx%d_3' % par,))
                                if t4 == 0:
                                    if si < 3:
                                        load_stream(cb, si + 1)
                                    elif cb + 1 < 8:
                                        load_stream(cb + 1, 0)
                            base.append([piece])
                    for si in range(3):
                        base[min(4 * si + 5, 15)].append(lambda si=si: short_conv(cb, si))
                    base[15].append(lambda: kb.op('pool', lambda g: g.tensor_tensor(out=vx[:, par, 2, :], in0=vx[:, par, 2, :], in1=vx[:, par, 3, :], op=ALU.mult),
                                                  reads=('vx%d_2' % par, 'vx%d_3' % par), writes=('vx%d_2' % par,)))
                    return [(lambda fs=fs: [f() for f in fs]) for fs in base]

                def run_B(cb, filler):
                    par = cb % 2
                    zin, ztok = vx[:, par, 0, :], 'vx%d_0' % par
                    for o in range(2):
                        kb.dma('sp', Kt[:, 0, :], khatD[o, cb], reads=('khatD',), writes=('Kt',))
                        for r_ in range(2):
                            kb.dma('sp', Kt[r_:128:2, 1, :], khatD[o, cb][1 - r_:128:2, :], reads=('khatD',), writes=('Kt',))
                        kb.op('dve', lambda v: v.transpose(out=zT[:].bitcast(U32).rearrange("p c m -> p m c"),
                                                           in_=zin.bitcast(U32).rearrange("p (n1 m) -> p m n1", m=32)),
                              reads=(ztok,), writes=('zT',))
                        filler()

                        def spec_evac(c2, h, psv, ptoks):
                            ksl = slice(16 * h, 16 * h + 16)
                            kb.op('dve', lambda v: v.tensor_tensor(out=P1[:, c2, :].rearrange("p (m k) -> p m k", k=32)[:, :, ksl], in0=psv,
                                                                   in1=Kt[:, 0, c2 * 2048:(c2 + 1) * 2048].rearrange("p (m k) -> p m k", k=32)[:, :, ksl], op=ALU.mult),
                                  reads=ptoks + ('Kt',), writes=('P1',))
                            kb.op('dve', lambda v: v.tensor_tensor(out=P2[:, c2, :].rearrange("p (m k) -> p m k", k=32)[:, :, ksl], in0=psv,
                                                                   in1=Kt[:, 1, c2 * 2048:(c2 + 1) * 2048].rearrange("p (m k) -> p m k", k=32)[:, :, ksl], op=ALU.mult),
                                  reads=ptoks + ('Kt',), writes=('P2',))
                        fft_fwd(C, [(zT[:], 'zT', 0)], B, spec_evac, filler)
                        fft_inv(C, P1, P2, Dt, conv[:], 'conv', filler)
                        kb.op('dve', lambda v: v.scalar_tensor_tensor(out=conv[:], in0=zin, scalar=dd[:, o, cb:cb + 1], in1=conv[:], op0=ALU.mult, op1=ALU.add),
                              reads=(ztok, 'dd', 'conv'), writes=('conv',))
                        if o == 0:
                            kb.op('dve', lambda v: v.tensor_tensor(out=z1[:], in0=conv[:], in1=vx[:, par, 1, :], op=ALU.mult),
                                  reads=('conv', 'vx%d_1' % par), writes=('z1',))
                            zin, ztok = z1[:], 'z1'
                        else:
                            yo = zT[:].rearrange("p a b -> p (a b)")
                            kb.op('dve', lambda v: v.tensor_tensor(out=yo, in0=conv[:], in1=vx[:, par, 2, :], op=ALU.mult),
                                  reads=('conv', 'vx%d_2' % par), writes=('zT',))
                            kb.dma('sp', yhyd[cb], yo, reads=('zT',), writes=('yhyd',))

                load_stream(0, 0)
                cur = make_A(0)
                for pc in cur:
                    pc()
                for cb in range(8):
                    nxt = make_A(cb + 1) if cb + 1 < 8 else []

                    def filler():
                        if nxt:
                            nxt.pop(0)()
                    run_B(cb, filler)
                    while nxt:
                        nxt.pop(0)()
                kb.barrier(barscr[:, 0:1])

        def phase_merge(l, hsrc):
            with ExitStack() as es:
                E = es.enter_context
                wm = E(SBT("m_wm", [128, 8, 2048], BF16))
                wh = E(SBT("m_wh", [128, 8, 1024], BF16))
                wo = E(SBT("m_wo", [128, 8, 1024], BF16))
                st = E(SBT("m_st", [128, 2, 2048], F32))
                mm = E(SBT("m_mm", [128, 16, 512], BF16))
                ys = E(SBT("m_ys", [128, 8, 512], BF16))
                yh = E(SBT("m_yh", [128, 8, 512], BF16))
                mg = E(SBT("m_mg", [128, 8, 512], BF16))
                t1 = E(SBT("m_t1", [128, 2, 512], F32))
                ht = E(SBT("m_ht", [128, 8, 512], F32))
                i = 0
                for (wt, src, ncol, nm) in ((wm, w_in[l][:, 5120:7168], 2048, 'wm'), (wh, w_bhy[l], 1024, 'wh'), (wo, w_out[l], 1024, 'wo')):
                    for k in range(8):
                        s = i % 2; i += 1
                        kb.dma('sp' if s == 0 else 'sp', st[:, s, 0:ncol], src[k * 128:(k + 1) * 128, :], writes=('st%d' % s,))
                        kb.op('act', lambda a: a.activation(out=wt[:, k, :], in_=st[:, s, 0:ncol], func=AF.Identity), reads=('st%d' % s,), writes=(nm,))
                for tt in range(4):
                    ts = slice(tt * 512, (tt + 1) * 512)
                    for c in range(8):
                        kb.dma('sp', ys[:, c, :], ys5d[c][:, ts], reads=('ys5d',), writes=('ys',))
                        kb.dma('sp', yh[:, c, :], yhyd[c][:, ts], reads=('yhyd',), writes=('yh',))
                        kb.dma('sp', ht[:, c, :], hsrc[c][:, ts], reads=('hsrc',), writes=('ht',))
                    for mc in range(16):
                        b = kb.ps()
                        for k in range(8):
                            kb.op('pe', lambda p: p.matmul(PS(b), wm[:, k, mc * 128:(mc + 1) * 128], G['xn'][:, k, ts], start=(k == 0), stop=(k == 7)),
                                  reads=('wm', 'xn%d' % k), writes=(pst(b),), inc=(k == 7))
                        kb.op('act', lambda a: a.activation(out=mm[:, mc, :], in_=PS(b), func=AF.Sigmoid), reads=(pst(b),), writes=('mm',))
                    for dc in range(8):
                        b = kb.ps()
                        for k in range(8):
                            kb.op('pe', lambda p: p.matmul(PS(b), wh[:, k, dc * 128:(dc + 1) * 128], yh[:, k, :], start=(k == 0), stop=(k == 7)),
                                  reads=('wh', 'yh'), writes=(pst(b),), inc=(k == 7))
                        s = dc % 2
                        kb.op('dve', lambda v: v.tensor_tensor(out=t1[:, s, :], in0=PS(b), in1=mm[:, 8 + dc, :], op=ALU.mult),
                              reads=(pst(b), 'mm'), writes=('t1_%d' % s,))
                        kb.op('pool', lambda g: g.tensor_tensor(out=mg[:, dc, :], in0=mm[:, dc, :], in1=ys[:, dc, :], op=ALU.mult),
                              reads=('mm', 'ys'), writes=('mg%d' % dc,))
                        kb.op('dve', lambda v: v.tensor_tensor(out=mg[:, dc, :], in0=mg[:, dc, :], in1=t1[:, s, :], op=ALU.add),
                              reads=('mg%d' % dc, 't1_%d' % s), writes=('mg%d' % dc, 'mg'))
                    for dc in range(8):
                        b = kb.ps()
                        for k in range(8):
                            kb.op('pe', lambda p: p.matmul(PS(b), wo[:, k, dc * 128:(dc + 1) * 128], mg[:, k, :], start=(k == 0), stop=(k == 7)),
                                  reads=('wo', 'mg'), writes=(pst(b),), inc=(k == 7))
                        kb.op('dve', lambda v: v.tensor_tensor(out=ht[:, dc, :], in0=PS(b), in1=ht[:, dc, :], op=ALU.add),
                              reads=(pst(b), 'ht'), writes=('ht',))
                        kb.dma('sp', hbuf[dc][:, ts], ht[:, dc, :], reads=('ht',), writes=('hbuf',))
                kb.barrier(barscr[:, 0:1])

        kb.op('pool', lambda g: g.memset(barscr[:, 1:2], 1e-6), writes=('epsq',))
        kb.barrier(barscr[:, 0:1])
        order = ['prep_s5', 'prep_hy', 'norm', 's5', 'hy', 'merge']
        nph = len(order) if stop_after is None else order.index(stop_after) + 1
        for l in range(layers):
            hsrc = xT if l == 0 else hbuf
            if nph >= 1:
                prep_s5(l)
            if nph >= 2:
                prep_hy(l)
            with ExitStack() as esl:
                G['xn'] = esl.enter_context(SBT("xn%d" % l, [128, 8, L], BF16))
                if nph >= 3:
                    phase_norm(hsrc, l, G['xn'], None)
                    if 'xnD' in dump:
                        xnD = dscr("xnD", [8, 128, L], BF16)
                        for c in range(8):
                            kb.dma('sp', xnD[c], G['xn'][:, c, :], reads=('xn%d' % c,), writes=('xnD',))
                if nph >= 4:
                    phase_s5(l)
                if nph >= 5:
                    phase_hy(l)
                if nph >= 6:
                    phase_merge(l, hsrc)
        if stop_after is None:
            phase_norm(hbuf, DEPTH, None, outT)
        for i in range(NDS):
            if kb.dcnt[i]:
                kb._wait('sp', ('d', i), kb.dcnt[i])
    return nc, kb


_CACHE = {}


def _host_consts():
    if 'c' not in _CACHE:
        fsmall, fT, fH = fft_consts()
        ft, tt = hy_feats()
        import ml_dtypes
        bfc = lambda a: np.ascontiguousarray(a.astype(ml_dtypes.bfloat16))
        _CACHE['c'] = dict(fsmall=bfc(fsmall), fT=bfc(fT), fH=bfc(fH), feats=ft, ttab=tt, kvt=s5_kv(), masks=s5_masks(),
                           ident=np.eye(128, dtype=np.float32))
    return _CACHE['c']


def _layout_shared(inp):
    f32 = np.float32
    g = lambda k: np.asarray(inp[k], dtype=f32)
    m = dict(_host_consts())
    nw = np.concatenate([g("norm_w"), g("final_norm_w")[None]], 0)
    m["normw"] = np.ascontiguousarray(nw.reshape(3, 8, 128).transpose(0, 2, 1))
    m["w_in"] = g("w_in")
    m["w_glu"] = g("s5_w_glu")
    m["b_glu"] = np.ascontiguousarray(g("s5_b_glu").reshape(DEPTH, 4, 128).transpose(0, 2, 1))
    m["s5d"] = np.ascontiguousarray(g("s5_d").reshape(DEPTH, 4, 128).transpose(0, 2, 1))
    m["w_bs5"] = g("w_branch_s5")
    m["w_bhy"] = g("w_branch_hy")
    m["w_out"] = g("w_out")
    cwv = g("hy_conv_w").reshape(DEPTH, 3, 24, 128)
    m["convw"] = np.ascontiguousarray(cwv.transpose(0, 3, 1, 2))
    m["convb"] = np.ascontiguousarray(g("hy_conv_b").reshape(DEPTH, 24, 128).transpose(0, 2, 1))
    m["hyd"] = np.ascontiguousarray(g("hy_d").reshape(DEPTH, 2, 8, 128).transpose(0, 3, 1, 2))
    def pg(a):
        v = a.reshape(DEPTH, 2, 16, 2, 64)
        return v.transpose(0, 3, 4, 1, 2).reshape(DEPTH, 128, 32)
    ls = np.broadcast_to(g("s5_log_step")[..., None], (DEPTH, 2, 32, 64))
    m["s5lam"] = np.ascontiguousarray(np.stack([pg(g("s5_lam_re")), pg(g("s5_lam_im")), pg(ls)], 2))
    def pgB(a):
        v = a.reshape(DEPTH, 2, 16, 2, 64, 16)
        return v.transpose(0, 3, 4, 1, 2, 5).reshape(DEPTH, 128, 32, 16)
    m["s5B"] = np.ascontiguousarray(np.stack([pgB(g("s5_b_re")), pgB(g("s5_b_im"))], 2))
    ct = lambda a: a.transpose(0, 1, 2, 4, 3)
    m["s5C"] = np.ascontiguousarray(np.stack([pgB(ct(g("s5_c_re"))), pgB(ct(g("s5_c_im")))], 2))
    m["hw1"] = g("hy_w1"); m["hw2"] = g("hy_w2"); m["hw3"] = g("hy_w3")
    m["hvec"] = np.ascontiguousarray(np.stack([g("hy_b1"), g("hy_b2"), g("hy_freq")], -1))
    m["hdec"] = np.ascontiguousarray(g("hy_decay").reshape(DEPTH, 32, 128).transpose(0, 2, 1))
    return m


def kernel(**inputs):
    x = np.asarray(inputs["x"], dtype=np.float32)
    shared = _layout_shared(inputs)
    if 'nc' not in _CACHE:
        _CACHE['nc'] = build()[0]
    nc = _CACHE['nc']
    in_maps = []
    for b in range(NCORES):
        m = dict(shared)
        m["xT"] = np.ascontiguousarray(x[b].T.reshape(8, 128, L))
        in_maps.append(m)
    res = run_bass_kernel_spmd(nc, in_maps, core_ids=list(range(NCORES)))
    out = np.empty((NCORES, L, D), dtype=np.float32)
    for b in range(NCORES):
        out[b] = np.asarray(res.results[b]["outT"]).reshape(D, L).T
    return out
```

```python
import math
from contextlib import ExitStack
import numpy as np
import concourse.bass as bass
import concourse.mybir as mybir
from concourse.bass_utils import run_bass_kernel_spmd

F32 = mybir.dt.float32
BF16 = mybir.dt.bfloat16
U32 = mybir.dt.uint32
ALU = mybir.AluOpType
AF = mybir.ActivationFunctionType

D = 1024; L = 2048; DEPTH = 2; NCORES = 8
INC = 7168
NDS = 48
MAGIC = 12582912.0
TWO_PI = 2.0 * math.pi


class KB:
    def __init__(self, nc, es):
        self.nc = nc
        self.engs = {'pe': nc.tensor, 'act': nc.scalar, 'dve': nc.vector, 'pool': nc.gpsimd, 'sp': nc.sync}
        self.csem = {e: es.enter_context(nc.semaphore('c_' + e)) for e in ('pe', 'act', 'dve', 'pool')}
        self.cnt = {e: 0 for e in self.csem}
        self.dsem = [es.enter_context(nc.semaphore('d%d' % i)) for i in range(NDS)]
        self.dcnt = [0] * NDS
        self.dnext = 0
        self.waited = {e: {} for e in self.engs}
        self.lastw = {}
        self.readers = {}
        self.psn = 0
        self.fp = 0
        self.nins = 0

    def _sem(self, sid):
        return self.csem[sid[1]] if sid[0] == 'c' else self.dsem[sid[1]]

    def _wait(self, e, sid, val):
        if e == 'pe' and sid == ('c', 'pe'):
            return
        w = self.waited[e]
        if w.get(sid, 0) >= val:
            return
        self.engs[e].wait_ge(self._sem(sid), val)
        w[sid] = val

    def _deps(self, e, reads, writes):
        for t in reads:
            if t in self.lastw:
                self._wait(e, *self.lastw[t])
        for t in writes:
            if t in self.lastw:
                self._wait(e, *self.lastw[t])
            for sid, val in self.readers.get(t, {}).items():
                self._wait(e, sid, val)

    def _record(self, sid, val, reads, writes):
        for t in writes:
            self.lastw[t] = (sid, val)
            self.readers[t] = {}
        for t in reads:
            r = self.readers.setdefault(t, {})
            if r.get(sid, 0) < val:
                r[sid] = val

    def op(self, e, fn, reads=(), writes=(), inc=True):
        self._deps(e, reads, writes)
        ins = fn(self.engs[e])
        self.nins += 1
        if e == 'pe' and not inc:
            val = self.cnt['pe'] + 1
        else:
            self.cnt[e] += 1
            val = self.cnt[e]
            ins.then_inc(self.csem[e], 1)
        self._record(('c', e), val, reads, writes)

    def dma(self, q, out, in_, reads=(), writes=()):
        i = self.dnext
        self.dnext = (self.dnext + 1) % NDS
        sid = ('d', i)
        if self.dcnt[i] > 0:
            self._wait(q, sid, self.dcnt[i])
        self._deps(q, reads, writes)
        self.engs[q].dma_start(out=out, in_=in_).then_inc(self.dsem[i], 16)
        self.nins += 1
        self.dcnt[i] += 16
        self._record(sid, self.dcnt[i], reads, writes)

    def barrier(self, scratch):
        for i in range(NDS):
            if self.dcnt[i]:
                self._wait('pool', ('d', i), self.dcnt[i])
        for e in ('pe', 'act', 'dve'):
            if self.cnt[e]:
                self._wait('pool', ('c', e), self.cnt[e])
        self.op('pool', lambda g: g.memset(scratch, 0.0), writes=('__bar',))
        val = self.cnt['pool']
        for e in ('pe', 'act', 'dve', 'sp'):
            self._wait(e, ('c', 'pool'), val)
        self.lastw.clear()
        self.readers.clear()

    def ps(self):
        b = self.psn % 8
        self.psn += 1
        return b

    def ps_hi(self):
        b = 4 + self.psn % 4
        self.psn += 1
        return b


def fft_consts():
    N = 4096
    n1 = np.arange(32); k1 = np.arange(32); n2 = np.arange(64); k2 = np.arange(64)
    ang = -2 * np.pi * np.outer(n1, k1 + 0.5) / 64.0
    g = np.stack([np.cos(ang), np.sin(ang)], -1)
    angh = -2 * np.pi * np.outer(n1 + 32, k1 + 0.5) / 64.0
    gh = -np.stack([np.cos(angh), np.sin(angh)], -1)
    G1 = np.zeros((4, 32, 4, 32, 2)); G1hi = np.zeros((4, 32, 4, 32, 2))
    for cq in range(4):
        G1[cq, :, cq] = g; G1hi[cq, :, cq] = gh
    a = -2 * np.pi * (n2[:, None, None] * (k1[None, :, None] + 0.5) / 4096.0 + n2[:, None, None] * k2[None, None, :] / 64.0)
    twr, twi = np.cos(a), np.sin(a)
    T = np.zeros((64, 32, 2, 64, 2))
    T[:, :, 0, :, 0] = twr; T[:, :, 0, :, 1] = twi
    T[:, :, 1, :, 0] = -twi; T[:, :, 1, :, 1] = twr
    T = np.concatenate([T, T], 0).reshape(128, 32 * 2 * 128)
    e = 2 * np.pi * np.outer(k2, n2) / 64.0
    er, ei = np.cos(e), np.sin(e)
    Ga = np.zeros((64, 2, 64, 2)); Gb = np.zeros((64, 2, 64, 2))
    for rp, s in ((0, 1.0), (1, -1.0)):
        Ga[:, rp, :, 0] = s * er; Ga[:, rp, :, 1] = s * ei
        Gb[:, rp, :, 0] = -ei; Gb[:, rp, :, 1] = er
    phi = 2 * np.pi * (k1[:, None, None] + 0.5) * (64 * n1[None, None, :] + n2[None, :, None]) / 4096.0
    H = np.zeros((4, 32, 64, 2, 4, 32))
    for cq in range(4):
        H[cq, :, :, 0, cq, :] = (2.0 / N) * np.cos(phi)
        H[cq, :, :, 1, cq, :] = -(2.0 / N) * np.sin(phi)
    Sw = np.zeros((64, 2, 64, 2))
    for k in range(64):
        Sw[k, 0, k, 1] = 1; Sw[k, 1, k, 0] = 1
    small = np.concatenate([G1.reshape(128, 256), G1hi.reshape(128, 256), Ga.reshape(128, 128),
                            Gb.reshape(128, 128), Sw.reshape(128, 128)], 1)
    return (small.astype(np.float32), T.astype(np.float32), H.reshape(128, 64 * 2 * 128).astype(np.float32))


def hy_feats():
    f32 = np.float32
    t = np.linspace(0.0, 1.0, L, dtype=f32)
    bands = np.linspace(1e-4, 15, 16, dtype=f32)
    def feats(pos, tt):
        angv = bands[None, :] * pos[:, None].astype(f32) * f32(2.0 * math.pi / L)
        return np.concatenate([tt[:, None], np.cos(angv), -np.sin(angv)], -1).astype(f32)
    posF = np.arange(L)
    posR = (L - np.arange(L)) % L
    fF = feats(posF, t); fR = feats(posR, t[posR])
    ft = np.stack([fF.T, fR.T], 0).astype(f32)
    tt = np.stack([np.broadcast_to(t, (128, L)), np.broadcast_to(t[posR], (128, L))], 0).astype(f32)
    return ft, tt


def s5_kv():
    j = np.arange(8)
    dbl = 8.0 * 2.0 ** np.arange(8)
    f = np.concatenate([7 - j, j + 1, j - 7, dbl])
    b = np.concatenate([j, 8 - j, -j, dbl])
    kv = np.stack([f, b], 0).astype(np.float32)
    return np.broadcast_to(kv[None], (128, 2, 32)).copy()


def s5_masks():
    jj = np.repeat(np.arange(8), 16)
    mf = (jj[None, :] >= jj[:, None]).astype(np.float32)
    mb = (jj[:, None] >= jj[None, :]).astype(np.float32)
    return np.concatenate([mf, mb], 1)


def build(dump=(), layers=DEPTH, stop_after=None):
    nc = bass.Bass("TRN2", target_bir_lowering=False)
    dt_in = {}

    def din(name, shape, dt=F32):
        dt_in[name] = nc.dram_tensor(name, list(shape), dt, kind="ExternalInput").ap()
        return dt_in[name]

    def dscr(name, shape, dt=F32):
        kind = "ExternalOutput" if name in dump else "Internal"
        return nc.dram_tensor(name, list(shape), dt, kind=kind).ap()

    xT = din("xT", [8, 128, L])
    normw = din("normw", [DEPTH + 1, 128, 8])
    w_in = din("w_in", [DEPTH, D, INC])
    w_glu = din("w_glu", [DEPTH, 512, 512])
    b_glu = din("b_glu", [DEPTH, 128, 4])
    s5d = din("s5d", [DEPTH, 128, 4])
    w_bs5 = din("w_bs5", [DEPTH, 512, D])
    w_bhy = din("w_bhy", [DEPTH, D, D])
    w_out = din("w_out", [DEPTH, D, D])
    convw = din("convw", [DEPTH, 128, 3, 24])
    convb = din("convb", [DEPTH, 128, 24])
    hyd = din("hyd", [DEPTH, 128, 2, 8])
    s5lam = din("s5lam", [DEPTH, 128, 3, 32])
    s5B = din("s5B", [DEPTH, 128, 2, 32, 16])
    s5C = din("s5C", [DEPTH, 128, 2, 32, 16])
    kvt = din("kvt", [128, 2, 32])
    masks = din("masks", [128, 256])
    ident = din("ident", [128, 128])
    hw1 = din("hw1", [DEPTH, 33, 64])
    hw2 = din("hw2", [DEPTH, 64, 64])
    hw3 = din("hw3", [DEPTH, 64, 4096])
    hvec = din("hvec", [DEPTH, 64, 3])
    hdec = din("hdec", [DEPTH, 128, 32])
    feats = din("feats", [2, 33, L])
    ttab = din("ttab", [2, 128, L])
    fsmall = din("fsmall", [128, 896], BF16)
    fT = din("fT", [128, 8192], BF16)
    fH = din("fH", [128, 16384], BF16)

    outT = nc.dram_tensor("outT", [8, 128, L], F32, kind="ExternalOutput").ap()

    hbuf = dscr("hbuf", [8, 128, L])
    ud = dscr("ud", [4, 128, L], BF16)
    yd = dscr("yd", [4, 128, L], BF16)
    ys5d = dscr("ys5d", [8, 128, L], BF16)
    yhyd = dscr("yhyd", [8, 128, L], BF16)
    binD = dscr("binD", [128, 32 * 2 * 2 * 64], BF16)
    coutD = dscr("coutD", [128, 2 * 16 * 2 * 128], BF16)
    mgD = dscr("mgD", [128, 32 * 128], BF16)
    coefD = dscr("coefD", [128, 3 * 2 * 16 * 8])
    khatD = dscr("khatD", [2, 8, 128, 4096], BF16)
    rsD = dscr("rsD", [128, 16])
    pD = dscr("pD", [4, 8, 128, L], BF16)
    mmD = dscr("mmD", [16, 128, L], BF16)

    _uid = [0]

    def SBT(name, shape, dt):
        _uid[0] += 1
        return nc.sbuf_tensor("%s_%d" % (name, _uid[0]), shape, dt)

    es0 = ExitStack()
    with es0:
        kb = KB(nc, es0)
        E0 = es0.enter_context
        psum = E0(nc.psum_tensor("psum", [128, 8, 512], F32))
        barscr = E0(SBT("barscr", [128, 8], F32))
        G = {}

        def PS(b, n=512, p0=0, p1=128, off=0):
            return psum[p0:p1, b, off:off + n]

        def PS4(g):
            return psum[:, 4 * g:4 * g + 4, :].rearrange("p b n -> p (b n)")

        def pst(b):
            return 'ps%d' % b

        def pst4(g):
            return tuple('ps%d' % (4 * g + i) for i in range(4))

        def range_reduce(eng, x_ap, tmp_ap, rd, wr_tmp):
            kb.op(eng, lambda v: v.tensor_scalar(out=tmp_ap, in0=x_ap, scalar1=float(1.0 / TWO_PI), scalar2=MAGIC,
                                                 op0=ALU.mult, op1=ALU.add), reads=rd, writes=wr_tmp)
            kb.op(eng, lambda v: v.tensor_scalar(out=tmp_ap, in0=tmp_ap, scalar1=-MAGIC, scalar2=None, op0=ALU.add),
                  reads=wr_tmp, writes=wr_tmp)
            kb.op(eng, lambda v: v.scalar_tensor_tensor(out=x_ap, in0=tmp_ap, scalar=float(-TWO_PI), in1=x_ap,
                                                        op0=ALU.mult, op1=ALU.add), reads=wr_tmp + rd, writes=rd)
            kb.op(eng, lambda v: v.tensor_scalar(out=x_ap, in0=x_ap, scalar1=3.14159, scalar2=-3.14159,
                                                 op0=ALU.min, op1=ALU.max), reads=rd, writes=rd)

        def phase_norm(src, wrow, dst_xn, dst_out):
            with ExitStack() as es:
                E = es.enter_context
                h = E(SBT("n_h", [128, 8, L], F32))
                sq = E(SBT("n_sq", [128, 2, 8, 512], BF16))
                ones = E(SBT("n_ones", [128, 128], BF16))
                nw = E(SBT("n_w", [128, 8], F32))
                rt = E(SBT("n_rt", [128, 2, 512], F32))
                rstd = E(SBT("n_rstd", [128, L], F32))
                epsb = E(SBT("n_eps", [128, 1], F32))
                ob = E(SBT("n_ob", [128, 2, L], F32)) if dst_out is not None else None
                kb.op('pool', lambda g: g.memset(ones[:], 1.0), writes=('ones',))
                kb.op('pool', lambda g: g.memset(epsb[:], 1e-6), writes=('epsb',))
                kb.dma('sp', nw[:], normw[wrow], writes=('nw',))
                for c in range(8):
                    kb.dma('sp' if c % 2 == 0 else 'sp', h[:, c, :], src[c], writes=('h%d' % c,))
                for tt in range(4):
                    ts = slice(tt * 512, (tt + 1) * 512)
                    s = tt % 2
                    kb.op('act', lambda a: a.activation(out=sq[:, s], in_=h[:, :, ts], func=AF.Square),
                          reads=tuple('h%d' % c for c in range(8)), writes=('sq%d' % s,))
                    b = kb.ps()
                    for c in range(8):
                        kb.op('pe', lambda p: p.matmul(PS(b), ones[:], sq[:, s, c, :], start=(c == 0), stop=(c == 7)),
                              reads=('ones', 'sq%d' % s), writes=(pst(b),), inc=(c == 7))
                    kb.op('act', lambda a: a.activation(out=rt[:, s, :], in_=PS(b), func=AF.Sqrt, bias=epsb[:, 0:1],
                                                        scale=float(1.0 / D)),
                          reads=(pst(b), 'epsb'), writes=('rt%d' % s,))
                    kb.op('dve', lambda v: v.reciprocal(out=rstd[:, ts], in_=rt[:, s, :]), reads=('rt%d' % s,),
                          writes=('rstd%d' % tt,))
                for c in range(8):
                    if dst_out is None:
                        kb.op('dve', lambda v: v.scalar_tensor_tensor(out=dst_xn[:, c, :], in0=h[:, c, :], scalar=nw[:, c:c + 1],
                                                                      in1=rstd[:], op0=ALU.mult, op1=ALU.mult),
                              reads=('h%d' % c, 'nw') + tuple('rstd%d' % t for t in range(4)), writes=('xn%d' % c,))
                    else:
                        s = c % 2
                        kb.op('dve', lambda v: v.scalar_tensor_tensor(out=ob[:, s, :], in0=h[:, c, :], scalar=nw[:, c:c + 1],
                                                                      in1=rstd[:], op0=ALU.mult, op1=ALU.mult),
                              reads=('h%d' % c, 'nw') + tuple('rstd%d' % t for t in range(4)), writes=('ob%d' % s,))
                        kb.dma('sp', dst_out[c], ob[:, s, :], reads=('ob%d' % s,), writes=('out%d' % c,))
                kb.barrier(barscr[:, 0:1])

        def load_w(es, name, src_rows_ap, kc, ncols, q='sp'):
            E = es.enter_context
            st = E(SBT(name + "_st", [128, kc, ncols], F32))
            wb = E(SBT(name + "_bf", [128, kc, ncols], BF16))
            for k in range(kc):
                kb.dma(q if k % 2 == 0 else 'sp', st[:, k, :], src_rows_ap[k * 128:(k + 1) * 128, :], writes=(name + '_st%d' % k,))
                kb.op('act', lambda a: a.activation(out=wb[:, k, :], in_=st[:, k, :], func=AF.Identity), reads=(name + '_st%d' % k,),
                      writes=(name + '_bf',))
            return wb

        def prep_s5(l):
            with ExitStack() as es:
                E = es.enter_context
                lam = E(SBT("p_lam", [128, 3, 32], F32))
                Bt = E(SBT("p_B", [128, 2, 32, 16], F32))
                Ct = E(SBT("p_C", [128, 2, 32, 16], F32))
                kvs = E(SBT("p_kv", [128, 2, 32], F32))
                msk = E(SBT("p_msk", [128, 256], F32))
                idt = E(SBT("p_id", [128, 128], F32))
                a_re = E(SBT("p_are", [128, 32], F32))
                a_im = E(SBT("p_aim", [128, 32], F32))
                dtt = E(SBT("p_dt", [128, 32], F32))
                mag = E(SBT("p_mag", [128, 32, 32], F32))
                sn = E(SBT("p_sn", [128, 32, 32], F32))
                cs = E(SBT("p_cs", [128, 32, 32], F32))
                tmp = E(SBT("p_tmp", [128, 32, 32], F32))
                Er = E(SBT("p_Er", [128, 32, 32], F32))
                Ei = E(SBT("p_Ei", [128, 32, 32], F32))
                k4 = E(SBT("p_k4", [128, 8, 32], F32))
                Bb = E(SBT("p_Bb", [128, 2, 32, 16], F32))
                t16 = E(SBT("p_t16", [128, 2, 32, 16], F32))
                RB = E(SBT("p_RB", [128, 2, 32, 128], F32))
                CO = E(SBT("p_CO", [128, 2, 32, 128], F32))
                tb = E(SBT("p_tb", [128, 32, 128], F32))
                binS = E(SBT("p_bin", [128, 32, 2, 2, 64], BF16))
                coutS = E(SBT("p_cout", [128, 32, 2, 128], BF16))
                mgS = E(SBT("p_mg", [128, 32, 128], BF16))
                coefS = E(SBT("p_coef", [128, 3, 32, 8], F32))
                mt = E(SBT("p_mt", [128, 2, 256], F32))
                kb.dma('sp', lam[:], s5lam[l], writes=('lam',))
                kb.dma('sp', Bt[:], s5B[l], writes=('Bt',))
                kb.dma('sp', Ct[:], s5C[l], writes=('Ct',))
                kb.dma('sp', kvs[:], kvt, writes=('kvs',))
                kb.dma('sp', msk[:], masks, writes=('msk',))
                kb.dma('sp', idt[:], ident, writes=('idt',))
                V = lambda fn, r, w: kb.op('dve', fn, reads=r, writes=w)
                A = lambda fn, r, w: kb.op('act', fn, reads=r, writes=w)
                A(lambda a: a.activation(out=dtt[:], in_=lam[:, 2, :], func=AF.Exp), ('lam',), ('dtt',))
                V(lambda v: v.tensor_tensor(out=a_re[:], in0=lam[:, 0, :], in1=dtt[:], op=ALU.mult), ('lam', 'dtt'), ('a_re',))
                V(lambda v: v.tensor_tensor(out=a_im[:], in0=lam[:, 1, :], in1=dtt[:], op=ALU.mult), ('lam', 'dtt'), ('a_im',))
                kvb = kvs[:].rearrange("p d (o k) -> p d o k", o=1).to_broadcast([128, 2, 16, 32])
                are_b = a_re[:].rearrange("p (d g o) -> p d g o", d=2, o=1).to_broadcast([128, 2, 16, 32])
                aim_b = a_im[:].rearrange("p (d g o) -> p d g o", d=2, o=1).to_broadcast([128, 2, 16, 32])
                v4 = lambda t: t[:].rearrange("p (d g) k -> p d g k", d=2)
                V(lambda v: v.tensor_tensor(out=v4(tmp), in0=are_b, in1=kvb, op=ALU.mult), ('a_re', 'kvs'), ('tmp',))
                A(lambda a: a.activation(out=mag[:], in_=tmp[:], func=AF.Exp), ('tmp',), ('mag',))
                V(lambda v: v.tensor_tensor(out=v4(sn), in0=aim_b, in1=kvb, op=ALU.mult), ('a_im', 'kvs'), ('sn',))
                V(lambda v: v.tensor_scalar(out=cs[:], in0=sn[:], scalar1=float(math.pi / 2), scalar2=None, op0=ALU.add), ('sn',), ('cs',))
                range_reduce('dve', sn[:], tmp[:], ('sn',), ('tmp',))
                A(lambda a: a.activation(out=sn[:], in_=sn[:], func=AF.Sin), ('sn',), ('sn',))
                range_reduce('dve', cs[:], tmp[:], ('cs',), ('tmp',))
                A(lambda a: a.activation(out=cs[:], in_=cs[:], func=AF.Sin), ('cs',), ('cs',))
                V(lambda v: v.tensor_tensor(out=Er[:], in0=mag[:], in1=cs[:], op=ALU.mult), ('mag', 'cs'), ('Er',))
                V(lambda v: v.tensor_tensor(out=Ei[:], in0=mag[:], in1=sn[:], op=ALU.mult), ('mag', 'sn'), ('Ei',))
                lr = k4[:, 0, :]; li = k4[:, 1, :]; nr = k4[:, 2, :]; den = k4[:, 3, :]; kr = k4[:, 4, :]; ki = k4[:, 5, :]; t0 = k4[:, 6, :]
                for d in range(2):
                    idx = 8 if d == 0 else 15
                    V(lambda v: v.tensor_copy(out=lr[:, 16 * d:16 * d + 16], in_=Er[:, 16 * d:16 * d + 16, idx]), ('Er',), ('k4',))
                    V(lambda v: v.tensor_copy(out=li[:, 16 * d:16 * d + 16], in_=Ei[:, 16 * d:16 * d + 16, idx]), ('Ei',), ('k4',))
                V(lambda v: v.tensor_scalar(out=nr, in0=lr, scalar1=-1.0, scalar2=None, op0=ALU.add), ('k4',), ('k4',))
                V(lambda v: v.tensor_tensor(out=den, in0=lam[:, 0, :], in1=lam[:, 0, :], op=ALU.mult), ('lam', 'k4'), ('k4',))
                V(lambda v: v.tensor_tensor(out=t0, in0=lam[:, 1, :], in1=lam[:, 1, :], op=ALU.mult), ('lam', 'k4'), ('k4',))
                V(lambda v: v.tensor_tensor(out=den, in0=den, in1=t0, op=ALU.add), ('k4',), ('k4',))
                V(lambda v: v.reciprocal(out=den, in_=den), ('k4',), ('k4',))
                V(lambda v: v.tensor_tensor(out=kr, in0=nr, in1=lam[:, 0, :], op=ALU.mult), ('k4', 'lam'), ('k4',))
                V(lambda v: v.tensor_tensor(out=t0, in0=li, in1=lam[:, 1, :], op=ALU.mult), ('k4', 'lam'), ('k4',))
                V(lambda v: v.tensor_tensor(out=kr, in0=kr, in1=t0, op=ALU.add), ('k4',), ('k4',))
                V(lambda v: v.tensor_tensor(out=kr, in0=kr, in1=den, op=ALU.mult), ('k4',), ('k4',))
                V(lambda v: v.tensor_tensor(out=ki, in0=li, in1=lam[:, 0, :], op=ALU.mult), ('k4', 'lam'), ('k4',))
                V(lambda v: v.tensor_tensor(out=t0, in0=nr, in1=lam[:, 1, :], op=ALU.mult), ('k4', 'lam'), ('k4',))
                V(lambda v: v.tensor_tensor(out=ki, in0=ki, in1=t0, op=ALU.subtract), ('k4',), ('k4',))
                V(lambda v: v.tensor_tensor(out=ki, in0=ki, in1=den, op=ALU.mult), ('k4',), ('k4',))
                krb = kr.rearrange("p (g o) -> p g o", o=1).to_broadcast([128, 32, 16])
                kib = ki.rearrange("p (g o) -> p g o", o=1).to_broadcast([128, 32, 16])
                V(lambda v: v.tensor_tensor(out=Bb[:, 0], in0=Bt[:, 0], in1=krb, op=ALU.mult), ('Bt', 'k4'), ('Bb',))
                V(lambda v: v.tensor_tensor(out=t16[:, 0], in0=Bt[:, 1], in1=kib, op=ALU.mult), ('Bt', 'k4'), ('t16',))
                V(lambda v: v.tensor_tensor(out=Bb[:, 0], in0=Bb[:, 0], in1=t16[:, 0], op=ALU.subtract), ('Bb', 't16'), ('Bb',))
                V(lambda v: v.tensor_tensor(out=Bb[:, 1], in0=Bt[:, 1], in1=krb, op=ALU.mult), ('Bt', 'k4', 'Bb'), ('Bb',))
                V(lambda v: v.tensor_tensor(out=t16[:, 1], in0=Bt[:, 0], in1=kib, op=ALU.mult), ('Bt', 'k4', 't16'), ('t16',))
                V(lambda v: v.tensor_tensor(out=Bb[:, 1], in0=Bb[:, 1], in1=t16[:, 1], op=ALU.add), ('Bb', 't16'), ('Bb',))

                def cprod(dst, k0, X, sign_im, tag):
                    Erb = Er[:, :, k0:k0 + 8].rearrange("p g (k o) -> p g k o", o=1).to_broadcast([128, 32, 8, 16])
                    Eib = Ei[:, :, k0:k0 + 8].rearrange("p g (k o) -> p g k o", o=1).to_broadcast([128, 32, 8, 16])
                    Xr = X[:, 0].rearrange("p g (o h) -> p g o h", o=1).to_broadcast([128, 32, 8, 16])
                    Xi = X[:, 1].rearrange("p g (o h) -> p g o h", o=1).to_broadcast([128, 32, 8, 16])
                    d0 = dst[:, 0].rearrange("p g (k h) -> p g k h", k=8)
                    d1 = dst[:, 1].rearrange("p g (k h) -> p g k h", k=8)
                    tv = tb[:].rearrange("p g (k h) -> p g k h", k=8)
                    rd = ('Er', 'Ei', tag)
                    V(lambda v: v.tensor_tensor(out=d0, in0=Erb, in1=Xr, op=ALU.mult), rd, (tag + 'o',))
                    V(lambda v: v.tensor_tensor(out=tv, in0=Eib, in1=Xi, op=ALU.mult), rd, ('tb',))
                    V(lambda v: v.tensor_tensor(out=d0, in0=d0, in1=tv, op=ALU.subtract), (tag + 'o', 'tb'), (tag + 'o',))
                    V(lambda v: v.tensor_tensor(out=d1, in0=Erb, in1=Xi, op=ALU.mult), rd + (tag + 'o',), (tag + 'o',))
                    V(lambda v: v.tensor_tensor(out=tv, in0=Eib, in1=Xr, op=ALU.mult), rd + ('tb',), ('tb',))
                    V(lambda v: v.tensor_tensor(out=d1, in0=d1, in1=tv, op=ALU.add), (tag + 'o', 'tb'), (tag + 'o',))
                    if sign_im < 0:
                        V(lambda v: v.tensor_scalar(out=dst[:, 1], in0=dst[:, 1], scalar1=-1.0, scalar2=None, op0=ALU.mult),
                          (tag + 'o',), (tag + 'o',))
                cprod(RB, 0, Bb, +1, 'Bb')
                cprod(CO, 8, Ct, -1, 'Ct')
                for ri in range(2):
                    A(lambda a: a.activation(out=coutS[:, :, ri, :], in_=CO[:, ri], func=AF.Identity), ('Cto',), ('coutS',))
                kb.dma('sp', coutD, coutS[:].rearrange("p a b c -> p (a b c)"), reads=('coutS',), writes=('coutD',))
                cprod(CO, 16, Ct, -1, 'Ct')
                RC = CO
                A(lambda a: a.activation(out=coefS[:, 0], in_=Er[:, :, 24:32], func=AF.Identity), ('Er',), ('coefS',))
                A(lambda a: a.activation(out=coefS[:, 1], in_=Ei[:, :, 24:32], func=AF.Identity), ('Ei',), ('coefS',))
                A(lambda a: a.activation(out=coefS[:, 2], in_=Ei[:, :, 24:32], func=AF.Identity, scale=-1.0), ('Ei',), ('coefS',))
                kb.dma('sp', coefD, coefS[:].rearrange("p a b c -> p (a b c)"), reads=('coefS',), writes=('coefD',))
                for dg in range(32):
                    d, gp = dg // 16, dg % 16
                    b = kb.ps()
                    for ri in range(2):
                        kb.op('pe', lambda p: p.transpose(PS(b, 128, off=128 * ri), RB[:, ri, dg, :], idt[:]),
                              reads=('Bbo', 'idt'), writes=(pst(b),), inc=(ri == 1))
                    for ri in range(2):
                        A(lambda a: a.activation(out=binS[:, 2 * gp:2 * gp + 2, d, ri, :],
                                                 in_=PS(b, 128, off=128 * ri).rearrange("p (g q) -> p g q", g=2), func=AF.Identity),
                          (pst(b),), ('binS',))
                kb.dma('sp', binD, binS[:].rearrange("p a b c e -> p (a b c e)"), reads=('binS',), writes=('binD',))
                for g in range(32):
                    gp, gpar = g // 2, g % 2
                    b = kb.ps()
                    p0, p1 = 64 * gpar, 64 * gpar + 64
                    for d in range(2):
                        dg = 16 * d + gp
                        kb.op('pe', lambda p: p.matmul(PS(b, 128, off=128 * d), RB[p0:p1, 0, dg, :], RC[p0:p1, 0, dg, :],
                                                       start=True, stop=False),
                              reads=('Bbo', 'Cto'), writes=(pst(b),), inc=False)
                        kb.op('pe', lambda p: p.matmul(PS(b, 128, off=128 * d), RB[p0:p1, 1, dg, :], RC[p0:p1, 1, dg, :],
                                                       start=False, stop=True),
                              reads=('Bbo', 'Cto'), writes=(pst(b),), inc=(d == 1))
                    s = g % 2
                    V(lambda v: v.tensor_tensor(out=mt[:, s, :], in0=PS(b, 256), in1=msk[:], op=ALU.mult), (pst(b), 'msk'), ('mt%d' % s,))
                    V(lambda v: v.tensor_tensor(out=mgS[:, g, :], in0=mt[:, s, 0:128], in1=mt[:, s, 128:256], op=ALU.add),
                      ('mt%d' % s,), ('mgS',))
                kb.dma('sp', mgD, mgS[:].rearrange("p a b -> p (a b)"), reads=('mgS',), writes=('mgD',))
                kb.barrier(barscr[:, 0:1])

        def phase_s5(l):
            with ExitStack() as es_outer:
                EO = es_outer.enter_context
                gs = EO(SBT("s_gs", [128, 4, L], BF16))
                with ExitStack() as es:
                    E = es.enter_context
                    wb = load_w(es, "s_w", w_in[l][:, 0:1024], 8, 1024)
                    udt = E(SBT("s_ud", [128, 4, 8, 256], BF16))
                    for cc in range(8):
                        for tt in range(4):
                            b = kb.ps()
                            for k in range(8):
                                kb.op('pe', lambda p: p.matmul(PS(b), wb[:, k, cc * 128:(cc + 1) * 128], G['xn'][:, k, tt * 512:(tt + 1) * 512],
                                                               start=(k == 0), stop=(k == 7)),
                                      reads=('s_w_bf', 'xn%d' % k), writes=(pst(b),), inc=(k == 7))
                            if cc < 4:
                                kb.op('act', lambda a: a.activation(out=udt[:, cc, :, tt * 64:(tt + 1) * 64],
                                                                    in_=PS(b).rearrange("p (c j) -> p j c", j=8), func=AF.Identity),
                                      reads=(pst(b),), writes=('udt%d' % cc,))
                            else:
                                kb.op('act', lambda a: a.activation(out=gs[:, cc - 4, tt * 512:(tt + 1) * 512], in_=PS(b), func=AF.Silu),
                                      reads=(pst(b),), writes=('gs',))
                        if cc < 4:
                            kb.dma('sp', ud[cc], udt[:, cc].rearrange("p j c -> p (j c)"), reads=('udt%d' % cc,), writes=('ud',))
                    kb.barrier(barscr[:, 0:1])
                with ExitStack() as es:
                    E = es.enter_context
                    U8 = E(SBT("s_U8", [128, 32, 256], BF16))
                    Mg = E(SBT("s_Mg", [128, 32, 128], BF16))
                    Bin = E(SBT("s_Bin", [128, 32, 2, 2, 64], BF16))
                    Cout = E(SBT("s_Cout", [128, 32, 2, 128], BF16))
                    coef = E(SBT("s_coef", [128, 3, 32, 8], F32))
                    Xs = E(SBT("s_Xs", [128, 32, 2, 256], BF16))
                    Y8 = U8
                    NSL = 2
                    XA = E(SBT("s_XA", [128, NSL, 2, 768], F32))
                    XB = E(SBT("s_XB", [128, NSL, 2, 768], F32))
                    T1 = E(SBT("s_T1", [128, NSL, 2, 256], F32))
                    udv = ud.rearrange("cc (g h) (j c) -> h j (cc g) c", h=16, j=8)
                    for j in range(8):
                        kb.dma('sp' if j % 2 == 0 else 'sp', U8[16 * j:16 * j + 16, :, :], udv[:, j], reads=('ud',), writes=('U8',))
                    kb.dma('sp', Mg[:].rearrange("p a b -> p (a b)"), mgD, reads=('mgD',), writes=('Mg',))
                    kb.dma('sp', Bin[:].rearrange("p a b c e -> p (a b c e)"), binD, reads=('binD',), writes=('Bin',))
                    kb.dma('sp', Cout[:].rearrange("p a b c -> p (a b c)"), coutD, reads=('coutD',), writes=('Cout',))
                    kb.dma('sp', coef[:].rearrange("p a b c -> p (a b c)"), coefD, reads=('coefD',), writes=('coef',))
                    kb.op('pool', lambda g: g.memset(XA[:], 0.0), writes=tuple('XA%d' % i for i in range(NSL)))
                    kb.op('pool', lambda g: g.memset(XB[:], 0.0), writes=tuple('XB%d' % i for i in range(NSL)))
                    fwst = E(SBT("s_fwst", [128, 8, 512], F32))
                    fwbf = E(SBT("s_fwbf", [128, 2, 8, 512], BF16))
                    fost = E(SBT("s_fost", [128, 4, 512], BF16))
                    groups = []
                    for si in range(4):
                        for q in range(2):
                            groups.append(([1024, 2048, 3072, 4096][si] + q * 512, [pD[si, 4 * q + j] for j in range(4)],
                                           AF.Silu if si == 3 else AF.Identity))
                    for q in range(4):
                        groups.append((5120 + q * 512, [mmD[4 * q + j] for j in range(4)], AF.Sigmoid))
                    fcnt = [0]

                    def load_group(gi):
                        slot = gi % 2
                        col = groups[gi][0]
                        for k in range(8):
                            kb.dma('sp', fwst[:, k, :], w_in[l][k * 128:(k + 1) * 128, col:col + 512], writes=('fwst',))
                        kb.op('act', lambda a: a.activation(out=fwbf[:, slot], in_=fwst[:], func=AF.Identity), reads=('fwst',), writes=('fwbf%d' % slot,))

                    fpieces = []
                    for gi in range(len(groups)):
                        for j in range(4):
                            for t4 in range(4):
                                def fpiece(gi=gi, j=j, t4=t4):
                                    slot = gi % 2
                                    ts = slice(t4 * 512, (t4 + 1) * 512)
                                    b = kb.ps()
                                    for k in range(8):
                                        kb.op('pe', lambda p: p.matmul(PS(b), fwbf[:, slot, k, j * 128:(j + 1) * 128], G['xn'][:, k, ts], start=(k == 0), stop=(k == 7)),
                                              reads=('fwbf%d' % slot, 'xn%d' % k), writes=(pst(b),), inc=(k == 7))
                                    os_ = fcnt[0] % 4
                                    fcnt[0] += 1
                                    kb.op('act', lambda a: a.activation(out=fost[:, os_, :], in_=PS(b), func=groups[gi][2]), reads=(pst(b),), writes=('fost%d' % os_,))
                                    kb.dma('sp', groups[gi][1][j][:, ts], fost[:, os_, :], reads=('fost%d' % os_,), writes=('fdst',))
                                    if j == 0 and t4 == 0 and gi + 1 < len(groups):
                                        load_group(gi + 1)
                                fpieces.append(fpiece)
                    load_group(0)

                    def sfill():
                        if fpieces:
                            fpieces.pop(0)()

                    for s0 in range(0, 32, NSL):
                        for i in range(NSL):
                            dg = s0 + i
                            d, gp = dg // 16, dg % 16
                            b = kb.ps()
                            for ri in range(2):
                                for gpar in range(2):
                                    g = 2 * gp + gpar
                                    kb.op('pe', lambda p: p.matmul(PS(b, 256, 64 * gpar, 64 * gpar + 64, off=256 * ri),
                                                                   Bin[:, g, d, ri, :], U8[:, g, :], start=True, stop=True),
                                          reads=('Bin', 'U8'), writes=(pst(b),), inc=(ri == 1 and gpar == 1))
                            kb.op('act', lambda a: a.activation(out=XA[:, i, :, 256:512], in_=PS(b).rearrange("p (r c) -> p r c", r=2),
                                                                func=AF.Identity),
                                  reads=(pst(b),), writes=('XA%d' % i,))
                        for r in range(8):
                            sft = 2 ** r
                            stage = [[], []]
                            for i in range(NSL):
                                dg = s0 + i
                                d = dg // 16
                                src, dst = (XA, XB) if r % 2 == 0 else (XB, XA)
                                sn_, dn_ = ('XA%d' % i, 'XB%d' % i) if r % 2 == 0 else ('XB%d' % i, 'XA%d' % i)
                                lo = 256 - sft if d == 0 else 256 + sft
                                e_ = coef[:, 0, dg, r:r + 1]; f_ = coef[:, 1, dg, r:r + 1]; nf_ = coef[:, 2, dg, r:r + 1]
                                Rs = src[:, i, 0, lo:lo + 256]; Is = src[:, i, 1, lo:lo + 256]
                                R0 = src[:, i, 0, 256:512]; I0 = src[:, i, 1, 256:512]
                                tn = 'T1_%d' % i

                                def st1(i=i, Rs=Rs, Is=Is, R0=R0, I0=I0, e_=e_, f_=f_, sn_=sn_, tn=tn):
                                    kb.op('dve', lambda v: v.scalar_tensor_tensor(out=T1[:, i, 0, :], in0=Rs, scalar=e_, in1=R0, op0=ALU.mult, op1=ALU.add),
                                          reads=(sn_, 'coef'), writes=(tn + 'a',))
                                    kb.op('dve', lambda v: v.scalar_tensor_tensor(out=T1[:, i, 1, :], in0=Rs, scalar=f_, in1=I0, op0=ALU.mult, op1=ALU.add),
                                          reads=(sn_, 'coef'), writes=(tn + 'b',))

                                def st2(i=i, Is=Is, e_=e_, nf_=nf_, sn_=sn_, dn_=dn_, tn=tn, dst=dst):
                                    kb.op('dve', lambda v: v.scalar_tensor_tensor(out=dst[:, i, 0, 256:512], in0=Is, scalar=nf_, in1=T1[:, i, 0, :],
                                                                                  op0=ALU.mult, op1=ALU.add),
                                          reads=(sn_, 'coef', tn + 'a'), writes=(dn_ + 'r',))
                                    kb.op('dve', lambda v: v.scalar_tensor_tensor(out=dst[:, i, 1, 256:512], in0=Is, scalar=e_, in1=T1[:, i, 1, :],
                                                                                  op0=ALU.mult, op1=ALU.add),
                                          reads=(sn_, 'coef', tn + 'b'), writes=(dn_, dn_ + 'r'))
                                stage[0].append(st1)
                                stage[1].append(st2)
                            for st in stage:
                                for f in st:
                                    f()
                                sfill()
                        for i in range(NSL):
                            dg = s0 + i
                            d = dg // 16
                            lo = 255 if d == 0 else 257
                            kb.op('act', lambda a: a.activation(out=Xs[:, dg, :, :], in_=XA[:, i, :, lo:lo + 256], func=AF.Identity),
                                  reads=('XA%d' % i, 'XA%dr' % i), writes=('Xs',))
                    while fpieces:
                        fpieces.pop(0)()
                    for g in range(32):
                        gp, gpar = g // 2, g % 2
                        p0, p1 = 64 * gpar, 64 * gpar + 64
                        b = kb.ps()
                        kb.op('pe', lambda p: p.matmul(PS(b, 256), Mg[:, g, :], U8[:, g, :], start=True, stop=False),
                              reads=('Mg', 'U8'), writes=(pst(b),), inc=False)
                        for d in range(2):
                            for ri in range(2):
                                last = (d == 1 and ri == 1)
                                kb.op('pe', lambda p: p.matmul(PS(b, 256), Cout[p0:p1, 16 * d + gp, ri, :], Xs[p0:p1, 16 * d + gp, ri, :],
                                                               start=False, stop=last),
                                      reads=('Cout', 'Xs'), writes=(pst(b),), inc=last)
                        kb.op('act', lambda a: a.activation(out=Y8[:, g, :], in_=PS(b, 256), func=AF.Identity), reads=(pst(b),), writes=('Y8',))
                    ydv = yd.rearrange("cc (g h) (i c) -> h i (cc g) c", h=16, i=8)
                    for i in range(8):
                        kb.dma('sp' if i % 2 == 0 else 'sp', ydv[:, i], Y8[16 * i:16 * i + 16, :, :], reads=('Y8',), writes=('yd',))
                    kb.barrier(barscr[:, 0:1])
                with ExitStack() as es:
                    E = es.enter_context
                    wg = E(SBT("c_wg", [128, 4, 512], BF16))
                    wbr = E(SBT("c_wbr", [128, 4, 1024], BF16))
                    wst = E(SBT("c_wst", [128, 2, 1024], F32))
                    yt = E(SBT("c_y", [128, L], BF16))
                    ut = E(SBT("c_u", [128, L], BF16))
                    y1 = E(SBT("c_y1", [128, L], F32))
                    t3 = E(SBT("c_t3", [128, L], F32))
                    yg = E(SBT("c_yg", [128, 4, L], BF16))
                    sg = E(SBT("c_sg", [128, 2, 512], F32))
                    y3 = E(SBT("c_y3", [128, 4, L], BF16))
                    ys = E(SBT("c_ys", [128, L], BF16))
                    dv = E(SBT("c_d", [128, 4], F32))
                    bg = E(SBT("c_bg", [128, 4], F32))
                    kb.dma('sp', dv[:], s5d[l], writes=('dv',))
                    kb.dma('sp', bg[:], b_glu[l], writes=('bg',))
                    for k in range(4):
                        s = k % 2
                        kb.dma('sp', wst[:, s, 0:512], w_glu[l][k * 128:(k + 1) * 128, :], writes=('wst%d' % s,))
                        kb.op('act', lambda a: a.activation(out=wg[:, k, :], in_=wst[:, s, 0:512], func=AF.Identity), reads=('wst%d' % s,), writes=('wg',))
                    for k in range(4):
                        s = k % 2
                        kb.dma('sp', wst[:, s, :], w_bs5[l][k * 128:(k + 1) * 128, :], writes=('wst%d' % s,))
                        kb.op('act', lambda a: a.activation(out=wbr[:, k, :], in_=wst[:, s, :], func=AF.Identity), reads=('wst%d' % s,), writes=('wbr',))
                    for cc in range(4):
                        kb.dma('sp', yt[:], yd[cc], reads=('yd',), writes=('yt',))
                        kb.dma('sp', ut[:], ud[cc], reads=('ud',), writes=('ut',))
                        kb.op('dve', lambda v: v.scalar_tensor_tensor(out=y1[:], in0=ut[:], scalar=dv[:, cc:cc + 1], in1=yt[:],
                                                                      op0=ALU.mult, op1=ALU.add),
                              reads=('ut', 'yt', 'dv'), writes=('y1',))
                        kb.op('dve', lambda v: v.tensor_tensor(out=t3[:], in0=y1[:], in1=y1[:], op=ALU.mult), reads=('y1',), writes=('t3',))
                        kb.op('dve', lambda v: v.tensor_scalar(out=t3[:], in0=t3[:], scalar1=0.044715 * 1.5957691216, scalar2=1.5957691216,
                                                               op0=ALU.mult, op1=ALU.add), reads=('t3',), writes=('t3',))
                        kb.op('dve', lambda v: v.tensor_tensor(out=t3[:], in0=t3[:], in1=y1[:], op=ALU.mult), reads=('t3', 'y1'), writes=('t3',))
                        kb.op('act', lambda a: a.activation(out=t3[:], in_=t3[:], func=AF.Sigmoid), reads=('t3',), writes=('t3',))
                        kb.op('dve', lambda v: v.tensor_tensor(out=yg[:, cc, :], in0=t3[:], in1=y1[:], op=ALU.mult), reads=('t3', 'y1'), writes=('yg',))
                    gsp = gs[:].rearrange("p a (c j) -> p a j c", j=8)
                    for cc in range(4):
                        for tt in range(4):
                            b = kb.ps()
                            for k in range(4):
                                kb.op('pe', lambda p: p.matmul(PS(b), wg[:, k, cc * 128:(cc + 1) * 128], yg[:, k, tt * 512:(tt + 1) * 512],
                                                               start=(k == 0), stop=(k == 3)),
                                      reads=('wg', 'yg'), writes=(pst(b),), inc=(k == 3))
                            s = tt % 2
                            kb.op('act', lambda a: a.activation(out=sg[:, s, :], in_=PS(b), func=AF.Sigmoid, bias=bg[:, cc:cc + 1]),
                                  reads=(pst(b), 'bg'), writes=('sg%d' % s,))
                            kb.op('dve', lambda v: v.tensor_tensor(out=sg[:, s, :], in0=sg[:, s, :], in1=yg[:, cc, tt * 512:(tt + 1) * 512], op=ALU.mult),
                                  reads=('sg%d' % s, 'yg'), writes=('sg%d' % s,))
                            kb.op('dve', lambda v: v.tensor_tensor(out=y3[:, cc, tt * 512:(tt + 1) * 512].rearrange("p (j c) -> p j c", j=2),
                                                                   in0=sg[:, s, :].rearrange("p (j c) -> p j c", j=2),
                                                                   in1=gsp[:, cc, 2 * tt:2 * tt + 2, :], op=ALU.mult),
                                  reads=('sg%d' % s, 'gs'), writes=('y3',))
                    for dc in range(8):
                        for tt in range(4):
                            b = kb.ps()
                            for k in range(4):
                                kb.op('pe', lambda p: p.matmul(PS(b), wbr[:, k, dc * 128:(dc + 1) * 128], y3[:, k, tt * 512:(tt + 1) * 512],
                                                               start=(k == 0), stop=(k == 3)),
                                      reads=('wbr', 'y3'), writes=(pst(b),), inc=(k == 3))
                            kb.op('act', lambda a: a.activation(out=ys[:].rearrange("p (c j) -> p j c", j=8)[:, 2 * tt:2 * tt + 2, :],
                                                                in_=PS(b).rearrange("p (j c) -> p j c", j=2), func=AF.Identity),
                                  reads=(pst(b),), writes=('ys',))
                        kb.dma('sp', ys5d[dc], ys[:], reads=('ys',), writes=('ys5d',))
                    kb.barrier(barscr[:, 0:1])

        def PS2(b0):
            return psum[:, b0:b0 + 2, :].rearrange("p b n -> p (b n)")

        def pst2(b0):
            return ('ps%d' % b0, 'ps%d' % (b0 + 1))

        def next_pair():
            p_ = kb.fp % 2
            kb.fp += 1
            return 2 * p_

        def fft_fwd(C, zin_list, B, spec_evac, filler=lambda: None):
            for q in range(4):
                b0 = next_pair()
                for cpl in range(4):
                    cp = q * 4 + cpl
                    bank = b0 + cpl // 2
                    off = 256 * (cpl % 2)
                    for zi, (zT, ztok, gk) in enumerate(zin_list):
                        kb.op('pe', lambda p: p.matmul(PS(bank, 256, off=off), zT[:, 2 * cp:2 * cp + 2, :].rearrange("p a b -> p (a b)"),
                                                       C['G1hi'] if gk else C['G1'], start=(zi == 0), stop=(zi == len(zin_list) - 1)),
                              reads=(ztok,) + C['toks'], writes=(pst(bank),), inc=(zi == len(zin_list) - 1))
                kb.op('act', lambda a: a.activation(
                    out=B[:].rearrange("p k r cp cq -> p (k r) cp cq")[:, :, q * 4:q * 4 + 4, :],
                    in_=PS2(b0).rearrange("p (cp cq kr) -> p kr cp cq", cp=4, cq=4), func=AF.Identity),
                    reads=pst2(b0), writes=('B',))
                if q % 2 == 1:
                    filler()
            for c2 in range(2):
                for h in range(2):
                    b0 = next_pair()
                    for k1l in range(16):
                        k1 = 16 * h + k1l
                        bank = b0 + k1l // 8
                        off = 64 * (k1l % 8)
                        for ri in range(2):
                            kb.op('pe', lambda p: p.matmul(PS(bank, 64, off=off), C['T'][64 * c2:64 * c2 + 64, k1, ri, :],
                                                           B[64 * c2:64 * c2 + 64, k1, ri].rearrange("p a b -> p (a b)"),
                                                           start=(ri == 0), stop=(ri == 1)),
                                  reads=('B',) + C['toks'], writes=(pst(bank),), inc=(ri == 1))
                    spec_evac(c2, h, PS2(b0).rearrange("p (k m) -> p m k", k=16), pst2(b0))
                filler()

        def fft_inv(C, P1, P2, Dt, conv_out, conv_tok, filler=lambda: None):
            for c2 in range(2):
                for h in range(2):
                    b0 = next_pair()
                    for cpl in range(8):
                        cp = 8 * h + cpl
                        bank = b0 + cpl // 4
                        off = 128 * (cpl % 4)
                        kb.op('pe', lambda p: p.matmul(PS(bank, 128, off=off), P1[:, c2, cp * 128:(cp + 1) * 128], C['Ga'], start=True, stop=False),
                              reads=('P1',) + C['toks'], writes=(pst(bank),), inc=False)
                        kb.op('pe', lambda p: p.matmul(PS(bank, 128, off=off), P2[:, c2, cp * 128:(cp + 1) * 128], C['Gb'], start=False, stop=True),
                              reads=('P2',) + C['toks'], writes=(pst(bank),), inc=True)
                    kb.op('act', lambda a: a.activation(
                        out=Dt[:, c2].rearrange("p n r cp -> p (n r) cp")[:, :, 8 * h:8 * h + 8],
                        in_=PS2(b0).rearrange("p (cp nr) -> p nr cp", cp=8), func=AF.Identity),
                        reads=pst2(b0), writes=('B',))
                filler()
            for h in range(2):
                b0 = next_pair()
                for n2l in range(32):
                    n2 = 32 * h + n2l
                    bank = b0 + n2l // 16
                    off = 32 * (n2l % 16)
                    for r in range(2):
                        kb.op('pe', lambda p: p.matmul(PS(bank, 32, off=off), C['H'][:, n2, r, :],
                                                       Dt[:, :, n2, r, :].rearrange("p c2 cp -> p cp c2"), start=(r == 0), stop=(r == 1)),
                              reads=('B',) + C['toks'], writes=(pst(bank),), inc=(r == 1))
                kb.op('dve', lambda v: v.transpose(
                    out=conv_out.rearrange("p (n1 n2) -> p n2 n1", n2=64)[:, 32 * h:32 * h + 32, :],
                    in_=PS2(b0).rearrange("p (n c) -> p n c", c=32)),
                    reads=pst2(b0), writes=(conv_tok,))
            filler()

        def load_fft_consts(es, with_filter):
            E = es.enter_context
            fs = E(SBT("f_sm", [128, 896], BF16))
            Tb = E(SBT("f_T", [128, 32, 2, 128], BF16))
            kb.dma('sp', fs[:], fsmall, writes=('fc',))
            Tv = Tb[:].rearrange("p a b c -> p (a b c)")
            for i in range(2):
                kb.dma('sp', Tv[:, i * 4096:(i + 1) * 4096], fT[:, i * 4096:(i + 1) * 4096], writes=('fcT%d' % i,))
            C = {'G1': fs[:, 0:256], 'G1hi': fs[:, 256:512], 'Ga': fs[:, 512:640], 'Gb': fs[:, 640:768], 'Sw': fs[:, 768:896], 'T': Tb}
            C['toks'] = ('fc', 'fcT0', 'fcT1')
            if not with_filter:
                Hb = E(SBT("f_H", [128, 64, 2, 128], BF16))
                Hv = Hb[:].rearrange("p a b c -> p (a b c)")
                for i in range(4):
                    kb.dma('sp', Hv[:, i * 4096:(i + 1) * 4096], fH[:, i * 4096:(i + 1) * 4096], writes=('fcH%d' % i,))
                C['H'] = Hb
                C['toks'] = C['toks'] + tuple('fcH%d' % i for i in range(4))
            return C

        def prep_hy(l):
            with ExitStack() as es:
                E = es.enter_context
                C = load_fft_consts(es, True)
                w1 = E(SBT("q_w1", [33, 64], F32))
                w2 = E(SBT("q_w2", [64, 64], F32))
                w3 = E(SBT("q_w3", [64, 4096], F32))
                hv = E(SBT("q_hv", [64, 3], F32))
                bf = E(SBT("q_bf", [64, 2], F32))
                dec = E(SBT("q_dec", [128, 32], F32))
                ft = E(SBT("q_ft", [33, 2, L], F32))
                tt_ = E(SBT("q_tt", [128, 2, L], F32))
                h1 = E(SBT("q_h1", [64, L], F32))
                h2 = E(SBT("q_h2", [64, 2, L], F32))
                tmp = E(SBT("q_tmp", [64, L], F32))
                win = E(SBT("q_win", [128, 2, L], F32))
                hk = E(SBT("q_hk", [128, 2, 2, L], BF16))
                rsS = E(SBT("q_rs", [128, 16], F32))
                junk = E(SBT("q_junk", [128, L], BF16))
                ss = E(SBT("q_ss", [128, 2, 4], F32))
                zT = E(SBT("q_zT", [128, 2, 32, 64], BF16))
                B = E(SBT("q_B", [128, 32, 2, 16, 4], BF16))
                Kh = E(SBT("q_Kh", [128, 2, 2048], BF16))
                kb.dma('sp', w1[:], hw1[l], writes=('w1',))
                kb.dma('sp', w2[:], hw2[l], writes=('w2',))
                kb.dma('sp', w3[:], hw3[l], writes=('w3',))
                kb.dma('sp', hv[:], hvec[l], writes=('hv',))
                kb.dma('sp', dec[:], hdec[l], writes=('dec',))
                for i in range(2):
                    kb.dma('sp', ft[:, i, :], feats[i], writes=('ft',))
                    kb.dma('sp', tt_[:, i, :], ttab[i], writes=('tt',))
                V = lambda fn, r, w: kb.op('dve', fn, reads=r, writes=w)
                A = lambda fn, r, w: kb.op('act', fn, reads=r, writes=w)
                V(lambda v: v.tensor_tensor(out=bf[:, 0:1], in0=hv[:, 0:1], in1=hv[:, 2:3], op=ALU.mult), ('hv',), ('bf',))
                V(lambda v: v.tensor_tensor(out=bf[:, 1:2], in0=hv[:, 1:2], in1=hv[:, 2:3], op=ALU.mult), ('hv', 'bf'), ('bf',))
                A(lambda a: a.activation(out=dec[:], in_=dec[:], func=AF.Abs), ('dec',), ('dec',))
                V(lambda v: v.tensor_scalar(out=dec[:], in0=dec[:], scalar1=-1.0, scalar2=None, op0=ALU.mult), ('dec',), ('dec',))
                for i in range(2):
                    for stage in range(2):
                        wmat = w1 if stage == 0 else w2
                        dst = h1 if stage == 0 else h2[:, i, :]
                        dtok = 'h1' if stage == 0 else 'h2_%d' % i
                        for t4 in range(4):
                            ts = slice(t4 * 512, (t4 + 1) * 512)
                            b = kb.ps()
                            if stage == 0:
                                kb.op('pe', lambda p: p.matmul(PS(b, 512, 0, 64), w1[:], ft[:, i, ts], start=True, stop=True),
                                      reads=('w1', 'ft'), writes=(pst(b),))
                            else:
                                kb.op('pe', lambda p: p.matmul(PS(b, 512, 0, 64), w2[:], h1[:, ts], start=True, stop=True),
                                      reads=('w2', 'h1'), writes=(pst(b),))
                            dsl = dst[:, ts] if stage == 0 else h2[:, i, ts]
                            A(lambda a: a.activation(out=dsl, in_=PS(b, 512, 0, 64), func=AF.Identity, bias=bf[:, stage:stage + 1], scale=hv[:, 2:3]),
                              (pst(b), 'bf', 'hv'), (dtok,))
                        dfull = h1[:] if stage == 0 else h2[:, i, :]
                        range_reduce('dve', dfull, tmp[:], (dtok,), ('tmp',))
                        A(lambda a: a.activation(out=dfull, in_=dfull, func=AF.Sin), (dtok,), (dtok,))
                NIT = 16
                w3b = E(SBT("q_w3b", [64, 4096], BF16))
                h2b = E(SBT("q_h2b", [64, 2, L], BF16))
                for q_ in range(4):
                    A(lambda a: a.activation(out=w3b[:, q_ * 1024:(q_ + 1) * 1024], in_=w3[:, q_ * 1024:(q_ + 1) * 1024], func=AF.Identity), ('w3',), ('w3b',))
                V(lambda v: v.tensor_copy(out=h2b[:], in_=h2[:]), ('h2_0', 'h2_1'), ('h2b',))

                def make_A(n):
                    o, cc = n // 8, n % 8
                    par = n % 2
                    pieces = []
                    for dr in range(2):
                        for t4 in range(4):
                            def piece(dr=dr, t4=t4):
                                fc = o * 16 + dr * 8 + cc
                                if t4 == 0 and dr == 0:
                                    for d2 in range(2):
                                        fc2 = o * 16 + d2 * 8 + cc
                                        A(lambda a: a.activation(out=win[:, d2, :], in_=tt_[:, d2, :], func=AF.Exp, scale=dec[:, fc2:fc2 + 1]),
                                          ('tt', 'dec'), ('win%d' % d2,))
                                ts = slice(t4 * 512, (t4 + 1) * 512)
                                b = kb.ps_hi()
                                kb.op('pe', lambda p: p.matmul(PS(b), w3b[:, fc * 128:(fc + 1) * 128], h2b[:, dr, ts], start=True, stop=True),
                                      reads=('w3b', 'h2b'), writes=(pst(b),))
                                V(lambda v: v.scalar_tensor_tensor(out=hk[:, par, dr, ts], in0=win[:, dr, ts], scalar=0.05, in1=PS(b), op0=ALU.add, op1=ALU.mult),
                                  ('win%d' % dr, pst(b)), ('hk%d_%d' % (par, dr),))
                                if t4 == 3:
                                    if dr == 1:
                                        V(lambda v: v.memset(hk[:, par, 1, 0:1], 0.0), ('hk%d_1' % par,), ('hk%d_1' % par,))
                                    A(lambda a: a.activation(out=junk[:], in_=hk[:, par, dr, :], func=AF.Square, accum_out=ss[:, par, dr:dr + 1]),
                                      ('hk%d_%d' % (par, dr),), ('junk', 'ss%d_%d' % (par, dr)))
                            pieces.append(piece)
                    return pieces

                def run_B(n, filler):
                    o, cc = n // 8, n % 8
                    par = n % 2
                    V(lambda v: v.tensor_tensor(out=ss[:, par, 2:3], in0=ss[:, par, 0:1], in1=ss[:, par, 1:2], op=ALU.add),
                      ('ss%d_0' % par, 'ss%d_1' % par), ('ss%d_2' % par,))
                    A(lambda a: a.activation(out=ss[:, par, 2:3], in_=ss[:, par, 2:3], func=AF.Sqrt, bias=barscr[:, 1:2]), ('ss%d_2' % par,), ('ss%d_2' % par,))
                    V(lambda v: v.reciprocal(out=rsS[:, n:n + 1], in_=ss[:, par, 2:3]), ('ss%d_2' % par,), ('rsS',))
                    for dr in range(2):
                        V(lambda v: v.transpose(out=zT[:, dr].bitcast(U32).rearrange("p c m -> p m c"),
                                                in_=hk[:, par, dr, :].bitcast(U32).rearrange("p (n1 m) -> p m n1", m=32)),
                          ('hk%d_%d' % (par, dr),), ('zT%d' % dr,))
                        filler()

                    def spec_evac(c2, h, psv, ptoks):
                        V(lambda v: v.tensor_copy(out=Kh[:, c2, :].rearrange("p (m k) -> p m k", k=32)[:, :, 16 * h:16 * h + 16], in_=psv),
                          ptoks, ('Kh',))
                    fft_fwd(C, [(zT[:, 0], 'zT0', 0), (zT[:, 1], 'zT1', 1)], B, spec_evac, filler)
                    kb.dma('sp', khatD[o, cc], Kh[:].rearrange("p a b -> p (a b)"), reads=('Kh',), writes=('khatD',))

                cur = make_A(0)
                for pc in cur:
                    pc()
                for n in range(NIT):
                    nxt = make_A(n + 1) if n + 1 < NIT else []

                    def filler():
                        if nxt:
                            nxt.pop(0)()
                    run_B(n, filler)
                    while nxt:
                        nxt.pop(0)()
                kb.dma('sp', rsD, rsS[:], reads=('rsS',), writes=('rsD',))
                kb.barrier(barscr[:, 0:1])

        def phase_hy(l):
            with ExitStack() as es:
                E = es.enter_context
                C = load_fft_consts(es, False)
                cw = E(SBT("h_cw", [128, 3, 24], F32))
                cb_ = E(SBT("h_cb", [128, 24], F32))
                dd = E(SBT("h_dd", [128, 2, 8], F32))
                pp = E(SBT("h_ppb", [128, 3, L + 2], BF16))
                vx = E(SBT("h_vx", [128, 2, 4, L], BF16))
                z1 = E(SBT("h_z1", [128, L], BF16))
                zT = E(SBT("h_zT", [128, 32, 64], BF16))
                BD = E(SBT("h_BD", [128, 4096], BF16))
                B = BD[:].rearrange("p (k r cp cq) -> p k r cp cq", k=32, r=2, cp=16)
                Dt = BD[:].rearrange("p (c2 n r cp) -> p c2 n r cp", c2=2, n=64, r=2)
                P1 = E(SBT("h_P1", [128, 2, 2048], BF16))
                P2 = E(SBT("h_P2", [128, 2, 2048], BF16))
                Kt = E(SBT("h_Kt", [128, 2, 4096], BF16))
                conv = E(SBT("h_conv", [128, L], F32))
                sc = conv
                kb.dma('sp', cw[:], convw[l], writes=('cw',))
                kb.dma('sp', cb_[:], convb[l], writes=('cb',))
                kb.dma('sp', dd[:], hyd[l], writes=('dd',))
                rs = E(SBT("h_rs", [128, 2, 16], F32))
                kb.dma('sp', rs[:, 0, :], rsD, reads=('rsD',), writes=('rs',))
                kb.op('dve', lambda v: v.reciprocal(out=rs[:, 1, :], in_=rs[:, 0, :]), reads=('rs',), writes=('rs',))
                kb.op('dve', lambda v: v.tensor_tensor(out=cw[:, :, 8:24], in0=cw[:, :, 8:24],
                                                       in1=rs[:, 0:1, :].to_broadcast([128, 3, 16]), op=ALU.mult), reads=('cw', 'rs'), writes=('cw',))
                kb.op('dve', lambda v: v.tensor_tensor(out=cb_[:, 8:24], in0=cb_[:, 8:24], in1=rs[:, 0, :], op=ALU.mult), reads=('cb', 'rs'), writes=('cb',))
                kb.op('dve', lambda v: v.tensor_tensor(out=dd[:].rearrange("p o c -> p (o c)"), in0=dd[:].rearrange("p o c -> p (o c)"), in1=rs[:, 1, :], op=ALU.mult),
                      reads=('dd', 'rs'), writes=('dd',))
                kb.op('pool', lambda g: g.memset(pp[:], 0.0), writes=('pp0', 'pp1', 'pp2'))

                def short_conv(cb, si):
                    par = cb % 2
                    sp_ = si
                    ch = si * 8 + cb
                    kb.op('dve', lambda v: v.tensor_scalar(out=sc[:], in0=pp[:, sp_, 1:L + 1], scalar1=cw[:, 1, ch:ch + 1], scalar2=cb_[:, ch:ch + 1],
                                                           op0=ALU.mult, op1=ALU.add),
                          reads=('pp%d' % sp_, 'cw', 'cb'), writes=('conv',))
                    kb.op('dve', lambda v: v.scalar_tensor_tensor(out=sc[:], in0=pp[:, sp_, 0:L], scalar=cw[:, 0, ch:ch + 1], in1=sc[:],
                                                                  op0=ALU.mult, op1=ALU.add),
                          reads=('pp%d' % sp_, 'cw', 'conv'), writes=('conv',))
                    kb.op('dve', lambda v: v.scalar_tensor_tensor(out=vx[:, par, si, :], in0=pp[:, sp_, 2:L + 2], scalar=cw[:, 2, ch:ch + 1], in1=sc[:],
                                                                  op0=ALU.mult, op1=ALU.add),
                          reads=('pp%d' % sp_, 'cw', 'conv'), writes=('vx%d_%d' % (par, si),))

                def make_A(cb):
                    par = cb % 2
                    base = [[] for _ in range(16)]
                    for si in range(3):
                        base[si].append(lambda si=si: kb.dma('sp', pp[:, si, 1:L + 1], pD[si, cb], writes=('pp%d' % si,)))
                    base[3].append(lambda: kb.dma('sp', vx[:, par, 3, :], pD[3, cb], writes=('vx%d_3' % par,)))
                    for si in range(3):
                        base[4 * si + 5].append(lambda si=si: short_conv(cb, si))
                    base[15].append(lambda: kb.op('pool', lambda g: g.tensor_tensor(out=vx[:, par, 2, :], in0=vx[:, par, 2, :], in1=vx[:, par, 3, :], op=ALU.mult),
                                                  reads=('vx%d_2' % par, 'vx%d_3' % par), writes=('vx%d_2' % par,)))
                    return [(lambda fs=fs: [f() for f in fs]) for fs in base]

                def run_B(cb, filler):
                    par = cb % 2
                    zin, ztok = vx[:, par, 0, :], 'vx%d_0' % par
                    for o in range(2):
                        kb.dma('sp', Kt[:, 0, :], khatD[o, cb], reads=('khatD',), writes=('Kt',))
                        for r_ in range(2):
                            kb.dma('sp', Kt[r_:128:2, 1, :], khatD[o, cb][1 - r_:128:2, :], reads=('khatD',), writes=('Kt',))
                        kb.op('dve', lambda v: v.transpose(out=zT[:].bitcast(U32).rearrange("p c m -> p m c"),
                                                           in_=zin.bitcast(U32).rearrange("p (n1 m) -> p m n1", m=32)),
                              reads=(ztok,), writes=('zT',))
                        filler()

                        def spec_evac(c2, h, psv, ptoks):
                            ksl = slice(16 * h, 16 * h + 16)
                            kb.op('dve', lambda v: v.tensor_tensor(out=P1[:, c2, :].rearrange("p (m k) -> p m k", k=32)[:, :, ksl], in0=psv,
                                                                   in1=Kt[:, 0, c2 * 2048:(c2 + 1) * 2048].rearrange("p (m k) -> p m k", k=32)[:, :, ksl], op=ALU.mult),
                                  reads=ptoks + ('Kt',), writes=('P1',))
                            kb.op('dve', lambda v: v.tensor_tensor(out=P2[:, c2, :].rearrange("p (m k) -> p m k", k=32)[:, :, ksl], in0=psv,
                                                                   in1=Kt[:, 1, c2 * 2048:(c2 + 1) * 2048].rearrange("p (m k) -> p m k", k=32)[:, :, ksl], op=ALU.mult),
                                  reads=ptoks + ('Kt',), writes=('P2',))
                        fft_fwd(C, [(zT[:], 'zT', 0)], B, spec_evac, filler)
                        fft_inv(C, P1, P2, Dt, conv[:], 'conv', filler)
                        kb.op('dve', lambda v: v.scalar_tensor_tensor(out=conv[:], in0=zin, scalar=dd[:, o, cb:cb + 1], in1=conv[:], op0=ALU.mult, op1=ALU.add),
                              reads=(ztok, 'dd', 'conv'), writes=('conv',))
                        if o == 0:
                            kb.op('dve', lambda v: v.tensor_tensor(out=z1[:], in0=conv[:], in1=vx[:, par, 1, :], op=ALU.mult),
                                  reads=('conv', 'vx%d_1' % par), writes=('z1',))
                            zin, ztok = z1[:], 'z1'
                        else:
                            yo = zT[:].rearrange("p a b -> p (a b)")
                            kb.op('dve', lambda v: v.tensor_tensor(out=yo, in0=conv[:], in1=vx[:, par, 2, :], op=ALU.mult),
                                  reads=('conv', 'vx%d_2' % par), writes=('zT',))
                            kb.dma('sp', yhyd[cb], yo, reads=('zT',), writes=('yhyd',))

                cur = make_A(0)
                for pc in cur:
                    pc()
                for cb in range(8):
                    nxt = make_A(cb + 1) if cb + 1 < 8 else []

                    def filler():
                        if nxt:
                            nxt.pop(0)()
                    run_B(cb, filler)
                    while nxt:
                        nxt.pop(0)()
                kb.barrier(barscr[:, 0:1])

        def phase_merge(l, hsrc):
            with ExitStack() as es:
                E = es.enter_context
                wh = E(SBT("m_wh", [128, 8, 1024], BF16))
                wo = E(SBT("m_wo", [128, 8, 1024], BF16))
                st = E(SBT("m_st", [128, 2, 1024], F32))
                mm = E(SBT("m_mm", [128, 16, 512], BF16))
                ys = E(SBT("m_ys", [128, 8, 512], BF16))
                yh = E(SBT("m_yh", [128, 8, 512], BF16))
                mg = E(SBT("m_mg", [128, 8, 512], BF16))
                t1 = E(SBT("m_t1", [128, 2, 512], F32))
                ht = E(SBT("m_ht", [128, 8, 512], F32))
                i = 0
                for (wt, src, ncol, nm) in ((wh, w_bhy[l], 1024, 'wh'), (wo, w_out[l], 1024, 'wo')):
                    for k in range(8):
                        s = i % 2; i += 1
                        kb.dma('sp' if s == 0 else 'sp', st[:, s, 0:ncol], src[k * 128:(k + 1) * 128, :], writes=('st%d' % s,))
                        kb.op('act', lambda a: a.activation(out=wt[:, k, :], in_=st[:, s, 0:ncol], func=AF.Identity), reads=('st%d' % s,), writes=(nm,))
                for tt in range(4):
                    ts = slice(tt * 512, (tt + 1) * 512)
                    kb.dma('sp', mm[:], mmD.rearrange("c p t -> p c t")[:, :, ts], writes=('mm',))
                    kb.dma('sp', yh[:], yhyd.rearrange("c p t -> p c t")[:, :, ts], reads=('yhyd',), writes=('yh',))
                    kb.dma('sp', ys[:], ys5d.rearrange("c p t -> p c t")[:, :, ts], reads=('ys5d',), writes=('ys',))
                    kb.dma('sp', ht[:], hsrc.rearrange("c p t -> p c t")[:, :, ts], reads=('hsrc',), writes=('ht',))
                    for dc in range(8):
                        b = kb.ps()
                        for k in range(8):
                            kb.op('pe', lambda p: p.matmul(PS(b), wh[:, k, dc * 128:(dc + 1) * 128], yh[:, k, :], start=(k == 0), stop=(k == 7)),
                                  reads=('wh', 'yh'), writes=(pst(b),), inc=(k == 7))
                        s = dc % 2
                        kb.op('dve', lambda v: v.tensor_tensor(out=t1[:, s, :], in0=PS(b), in1=mm[:, 8 + dc, :], op=ALU.mult),
                              reads=(pst(b), 'mm'), writes=('t1_%d' % s,))
                        kb.op('pool', lambda g: g.tensor_tensor(out=mg[:, dc, :], in0=mm[:, dc, :], in1=ys[:, dc, :], op=ALU.mult),
                              reads=('mm', 'ys'), writes=('mg%d' % dc,))
                        kb.op('dve', lambda v: v.tensor_tensor(out=mg[:, dc, :], in0=mg[:, dc, :], in1=t1[:, s, :], op=ALU.add),
                              reads=('mg%d' % dc, 't1_%d' % s), writes=('mg%d' % dc, 'mg'))
                    for dc in range(8):
                        b = kb.ps()
                        for k in range(8):
                            kb.op('pe', lambda p: p.matmul(PS(b), wo[:, k, dc * 128:(dc + 1) * 128], mg[:, k, :], start=(k == 0), stop=(k == 7)),
                                  reads=('wo', 'mg'), writes=(pst(b),), inc=(k == 7))
                        kb.op('dve', lambda v: v.tensor_tensor(out=ht[:, dc, :], in0=PS(b), in1=ht[:, dc, :], op=ALU.add),
                              reads=(pst(b), 'ht'), writes=('ht',))
                        kb.dma('sp', hbuf[dc][:, ts], ht[:, dc, :], reads=('ht',), writes=('hbuf',))
                kb.barrier(barscr[:, 0:1])

        kb.op('pool', lambda g: g.memset(barscr[:, 1:2], 1e-6), writes=('epsq',))
        kb.barrier(barscr[:, 0:1])
        order = ['prep_s5', 'prep_hy', 'norm', 's5', 'hy', 'merge']
        nph = len(order) if stop_after is None else order.index(stop_after) + 1
        for l in range(layers):
            hsrc = xT if l == 0 else hbuf
            if nph >= 1:
                prep_s5(l)
            if nph >= 2:
                prep_hy(l)
            with ExitStack() as esl:
                G['xn'] = esl.enter_context(SBT("xn%d" % l, [128, 8, L], BF16))
                if nph >= 3:
                    phase_norm(hsrc, l, G['xn'], None)
                    if 'xnD' in dump:
                        xnD = dscr("xnD", [8, 128, L], BF16)
                        for c in range(8):
                            kb.dma('sp', xnD[c], G['xn'][:, c, :], reads=('xn%d' % c,), writes=('xnD',))
                if nph >= 4:
                    phase_s5(l)
            if nph >= 5:
                phase_hy(l)
            if nph >= 6:
                phase_merge(l, hsrc)
        if stop_after is None:
            phase_norm(hbuf, DEPTH, None, outT)
        for i in range(NDS):
            if kb.dcnt[i]:
                kb._wait('sp', ('d', i), kb.dcnt[i])
    return nc, kb


_CACHE = {}


def _host_consts():
    if 'c' not in _CACHE:
        fsmall, fT, fH = fft_consts()
        ft, tt = hy_feats()
        import ml_dtypes
        bfc = lambda a: np.ascontiguousarray(a.astype(ml_dtypes.bfloat16))
        _CACHE['c'] = dict(fsmall=bfc(fsmall), fT=bfc(fT), fH=bfc(fH), feats=ft, ttab=tt, kvt=s5_kv(), masks=s5_masks(),
                           ident=np.eye(128, dtype=np.float32))
    return _CACHE['c']


def _layout_shared(inp):
    f32 = np.float32
    g = lambda k: np.asarray(inp[k], dtype=f32)
    m = dict(_host_consts())
    nw = np.concatenate([g("norm_w"), g("final_norm_w")[None]], 0)
    m["normw"] = np.ascontiguousarray(nw.reshape(3, 8, 128).transpose(0, 2, 1))
    m["w_in"] = g("w_in")
    m["w_glu"] = g("s5_w_glu")
    m["b_glu"] = np.ascontiguousarray(g("s5_b_glu").reshape(DEPTH, 4, 128).transpose(0, 2, 1))
    m["s5d"] = np.ascontiguousarray(g("s5_d").reshape(DEPTH, 4, 128).transpose(0, 2, 1))
    m["w_bs5"] = g("w_branch_s5")
    m["w_bhy"] = g("w_branch_hy")
    m["w_out"] = g("w_out")
    cwv = g("hy_conv_w").reshape(DEPTH, 3, 24, 128)
    m["convw"] = np.ascontiguousarray(cwv.transpose(0, 3, 1, 2))
    m["convb"] = np.ascontiguousarray(g("hy_conv_b").reshape(DEPTH, 24, 128).transpose(0, 2, 1))
    m["hyd"] = np.ascontiguousarray(g("hy_d").reshape(DEPTH, 2, 8, 128).transpose(0, 3, 1, 2))
    def pg(a):
        v = a.reshape(DEPTH, 2, 16, 2, 64)
        return v.transpose(0, 3, 4, 1, 2).reshape(DEPTH, 128, 32)
    ls = np.broadcast_to(g("s5_log_step")[..., None], (DEPTH, 2, 32, 64))
    m["s5lam"] = np.ascontiguousarray(np.stack([pg(g("s5_lam_re")), pg(g("s5_lam_im")), pg(ls)], 2))
    def pgB(a):
        v = a.reshape(DEPTH, 2, 16, 2, 64, 16)
        return v.transpose(0, 3, 4, 1, 2, 5).reshape(DEPTH, 128, 32, 16)
    m["s5B"] = np.ascontiguousarray(np.stack([pgB(g("s5_b_re")), pgB(g("s5_b_im"))], 2))
    ct = lambda a: a.transpose(0, 1, 2, 4, 3)
    m["s5C"] = np.ascontiguousarray(np.stack([pgB(ct(g("s5_c_re"))), pgB(ct(g("s5_c_im")))], 2))
    m["hw1"] = g("hy_w1"); m["hw2"] = g("hy_w2"); m["hw3"] = g("hy_w3")
    m["hvec"] = np.ascontiguousarray(np.stack([g("hy_b1"), g("hy_b2"), g("hy_freq")], -1))
    m["hdec"] = np.ascontiguousarray(g("hy_decay").reshape(DEPTH, 32, 128).transpose(0, 2, 1))
    return m


def kernel(**inputs):
    x = np.asarray(inputs["x"], dtype=np.float32)
    shared = _layout_shared(inputs)
    if 'nc' not in _CACHE:
        _CACHE['nc'] = build()[0]
    nc = _CACHE['nc']
    in_maps = []
    for b in range(NCORES):
        m = dict(shared)
        m["xT"] = np.ascontiguousarray(x[b].T.reshape(8, 128, L))
        in_maps.append(m)
    res = run_bass_kernel_spmd(nc, in_maps, core_ids=list(range(NCORES)))
    out = np.empty((NCORES, L, D), dtype=np.float32)
    for b in range(NCORES):
        out[b] = np.asarray(res.results[b]["outT"]).reshape(D, L).T
    return out
```

```python
import math
from contextlib import ExitStack
import numpy as np
import concourse.bass as bass
import concourse.mybir as mybir
from concourse.bass_utils import run_bass_kernel_spmd

F32 = mybir.dt.float32
BF16 = mybir.dt.bfloat16
U32 = mybir.dt.uint32
ALU = mybir.AluOpType
AF = mybir.ActivationFunctionType

D = 1024; L = 2048; DEPTH = 2; NCORES = 8
INC = 7168
NDS = 48
MAGIC = 12582912.0
TWO_PI = 2.0 * math.pi


class KB:
    def __init__(self, nc, es):
        self.nc = nc
        self.engs = {'pe': nc.tensor, 'act': nc.scalar, 'dve': nc.vector, 'pool': nc.gpsimd, 'sp': nc.sync}
        self.csem = {e: es.enter_context(nc.semaphore('c_' + e)) for e in ('pe', 'act', 'dve', 'pool')}
        self.cnt = {e: 0 for e in self.csem}
        self.dsem = [es.enter_context(nc.semaphore('d%d' % i)) for i in range(NDS)]
        self.dcnt = [0] * NDS
        self.dnext = 0
        self.waited = {e: {} for e in self.engs}
        self.lastw = {}
        self.readers = {}
        self.psn = 0
        self.fp = 0
        self.nins = 0

    def _sem(self, sid):
        return self.csem[sid[1]] if sid[0] == 'c' else self.dsem[sid[1]]

    def _wait(self, e, sid, val):
        if e == 'pe' and sid == ('c', 'pe'):
            return
        w = self.waited[e]
        if w.get(sid, 0) >= val:
            return
        self.engs[e].wait_ge(self._sem(sid), val)
        w[sid] = val

    def _deps(self, e, reads, writes):
        for t in reads:
            if t in self.lastw:
                self._wait(e, *self.lastw[t])
        for t in writes:
            if t in self.lastw:
                self._wait(e, *self.lastw[t])
            for sid, val in self.readers.get(t, {}).items():
                self._wait(e, sid, val)

    def _record(self, sid, val, reads, writes):
        for t in writes:
            self.lastw[t] = (sid, val)
            self.readers[t] = {}
        for t in reads:
            r = self.readers.setdefault(t, {})
            if r.get(sid, 0) < val:
                r[sid] = val

    def op(self, e, fn, reads=(), writes=(), inc=True):
        self._deps(e, reads, writes)
        ins = fn(self.engs[e])
        self.nins += 1
        if e == 'pe' and not inc:
            val = self.cnt['pe'] + 1
        else:
            self.cnt[e] += 1
            val = self.cnt[e]
            ins.then_inc(self.csem[e], 1)
        self._record(('c', e), val, reads, writes)

    def dma(self, q, out, in_, reads=(), writes=()):
        i = self.dnext
        self.dnext = (self.dnext + 1) % NDS
        sid = ('d', i)
        if self.dcnt[i] > 0:
            self._wait(q, sid, self.dcnt[i])
        self._deps(q, reads, writes)
        self.engs[q].dma_start(out=out, in_=in_).then_inc(self.dsem[i], 16)
        self.nins += 1
        self.dcnt[i] += 16
        self._record(sid, self.dcnt[i], reads, writes)

    def barrier(self, scratch):
        for i in range(NDS):
            if self.dcnt[i]:
                self._wait('pool', ('d', i), self.dcnt[i])
        for e in ('pe', 'act', 'dve'):
            if self.cnt[e]:
                self._wait('pool', ('c', e), self.cnt[e])
        self.op('pool', lambda g: g.memset(scratch, 0.0), writes=('__bar',))
        val = self.cnt['pool']
        for e in ('pe', 'act', 'dve', 'sp'):
            self._wait(e, ('c', 'pool'), val)
        self.lastw.clear()
        self.readers.clear()

    def ps(self):
        b = self.psn % 8
        self.psn += 1
        return b

    def ps_hi(self):
        b = 4 + self.psn % 4
        self.psn += 1
        return b


def fft_consts():
    N = 4096
    n1 = np.arange(32); k1 = np.arange(32); n2 = np.arange(64); k2 = np.arange(64)
    ang = -2 * np.pi * np.outer(n1, k1 + 0.5) / 64.0
    g = np.stack([np.cos(ang), np.sin(ang)], -1)
    angh = -2 * np.pi * np.outer(n1 + 32, k1 + 0.5) / 64.0
    gh = -np.stack([np.cos(angh), np.sin(angh)], -1)
    G1 = np.zeros((4, 32, 4, 32, 2)); G1hi = np.zeros((4, 32, 4, 32, 2))
    for cq in range(4):
        G1[cq, :, cq] = g; G1hi[cq, :, cq] = gh
    a = -2 * np.pi * (n2[:, None, None] * (k1[None, :, None] + 0.5) / 4096.0 + n2[:, None, None] * k2[None, None, :] / 64.0)
    twr, twi = np.cos(a), np.sin(a)
    T = np.zeros((64, 32, 2, 64, 2))
    T[:, :, 0, :, 0] = twr; T[:, :, 0, :, 1] = twi
    T[:, :, 1, :, 0] = -twi; T[:, :, 1, :, 1] = twr
    T = np.concatenate([T, T], 0).reshape(128, 32 * 2 * 128)
    e = 2 * np.pi * np.outer(k2, n2) / 64.0
    er, ei = np.cos(e), np.sin(e)
    Ga = np.zeros((64, 2, 64, 2)); Gb = np.zeros((64, 2, 64, 2))
    for rp, s in ((0, 1.0), (1, -1.0)):
        Ga[:, rp, :, 0] = s * er; Ga[:, rp, :, 1] = s * ei
        Gb[:, rp, :, 0] = -ei; Gb[:, rp, :, 1] = er
    phi = 2 * np.pi * (k1[:, None, None] + 0.5) * (64 * n1[None, None, :] + n2[None, :, None]) / 4096.0
    H = np.zeros((4, 32, 64, 2, 4, 32))
    for cq in range(4):
        H[cq, :, :, 0, cq, :] = (2.0 / N) * np.cos(phi)
        H[cq, :, :, 1, cq, :] = -(2.0 / N) * np.sin(phi)
    Sw = np.zeros((64, 2, 64, 2))
    for k in range(64):
        Sw[k, 0, k, 1] = 1; Sw[k, 1, k, 0] = 1
    small = np.concatenate([G1.reshape(128, 256), G1hi.reshape(128, 256), Ga.reshape(128, 128),
                            Gb.reshape(128, 128), Sw.reshape(128, 128)], 1)
    return (small.astype(np.float32), T.astype(np.float32), H.reshape(128, 64 * 2 * 128).astype(np.float32))


def hy_feats():
    f32 = np.float32
    t = np.linspace(0.0, 1.0, L, dtype=f32)
    bands = np.linspace(1e-4, 15, 16, dtype=f32)
    def feats(pos, tt):
        angv = bands[None, :] * pos[:, None].astype(f32) * f32(2.0 * math.pi / L)
        return np.concatenate([tt[:, None], np.cos(angv), -np.sin(angv)], -1).astype(f32)
    posF = np.arange(L)
    posR = (L - np.arange(L)) % L
    fF = feats(posF, t); fR = feats(posR, t[posR])
    ft = np.stack([fF.T, fR.T], 0).astype(f32)
    tt = np.stack([np.broadcast_to(t, (128, L)), np.broadcast_to(t[posR], (128, L))], 0).astype(f32)
    return ft, tt


def s5_kv():
    j = np.arange(8)
    dbl = 8.0 * 2.0 ** np.arange(8)
    f = np.concatenate([7 - j, j + 1, j - 7, dbl])
    b = np.concatenate([j, 8 - j, -j, dbl])
    kv = np.stack([f, b], 0).astype(np.float32)
    return np.broadcast_to(kv[None], (128, 2, 32)).copy()


def s5_masks():
    jj = np.repeat(np.arange(8), 16)
    mf = (jj[None, :] >= jj[:, None]).astype(np.float32)
    mb = (jj[:, None] >= jj[None, :]).astype(np.float32)
    return np.concatenate([mf, mb], 1)


def build(dump=(), layers=DEPTH, stop_after=None):
    nc = bass.Bass("TRN2", target_bir_lowering=False)
    dt_in = {}

    def din(name, shape, dt=F32):
        dt_in[name] = nc.dram_tensor(name, list(shape), dt, kind="ExternalInput").ap()
        return dt_in[name]

    def dscr(name, shape, dt=F32):
        kind = "ExternalOutput" if name in dump else "Internal"
        return nc.dram_tensor(name, list(shape), dt, kind=kind).ap()

    xT = din("xT", [8, 128, L])
    normw = din("normw", [DEPTH + 1, 128, 8])
    w_in = din("w_in", [DEPTH, D, INC])
    w_glu = din("w_glu", [DEPTH, 512, 512])
    b_glu = din("b_glu", [DEPTH, 128, 4])
    s5d = din("s5d", [DEPTH, 128, 4])
    w_bs5 = din("w_bs5", [DEPTH, 512, D])
    w_bhy = din("w_bhy", [DEPTH, D, D])
    w_out = din("w_out", [DEPTH, D, D])
    convw = din("convw", [DEPTH, 128, 3, 24])
    convb = din("convb", [DEPTH, 128, 24])
    hyd = din("hyd", [DEPTH, 128, 2, 8])
    s5lam = din("s5lam", [DEPTH, 128, 3, 32])
    s5B = din("s5B", [DEPTH, 128, 2, 32, 16])
    s5C = din("s5C", [DEPTH, 128, 2, 32, 16])
    kvt = din("kvt", [128, 2, 32])
    masks = din("masks", [128, 256])
    ident = din("ident", [128, 128])
    hw1 = din("hw1", [DEPTH, 33, 64])
    hw2 = din("hw2", [DEPTH, 64, 64])
    hw3 = din("hw3", [DEPTH, 64, 4096])
    hvec = din("hvec", [DEPTH, 64, 3])
    hdec = din("hdec", [DEPTH, 128, 32])
    feats = din("feats", [2, 33, L])
    ttab = din("ttab", [2, 128, L])
    fsmall = din("fsmall", [128, 896], BF16)
    fT = din("fT", [128, 8192], BF16)
    fH = din("fH", [128, 16384], BF16)

    outT = nc.dram_tensor("outT", [8, 128, L], F32, kind="ExternalOutput").ap()

    hbuf = dscr("hbuf", [8, 128, L])
    ud = dscr("ud", [4, 128, L], BF16)
    yd = dscr("yd", [4, 128, L], BF16)
    ys5d = dscr("ys5d", [8, 128, L], BF16)
    yhyd = dscr("yhyd", [8, 128, L], BF16)
    binD = dscr("binD", [128, 32 * 2 * 2 * 64], BF16)
    coutD = dscr("coutD", [128, 2 * 16 * 2 * 128], BF16)
    mgD = dscr("mgD", [128, 32 * 128], BF16)
    coefD = dscr("coefD", [128, 3 * 2 * 16 * 8])
    khatD = dscr("khatD", [2, 8, 128, 4096], BF16)
    rsD = dscr("rsD", [128, 16])
    pD = dscr("pD", [4, 8, 128, L], BF16)
    mmD = dscr("mmD", [16, 128, L], BF16)

    _uid = [0]

    def SBT(name, shape, dt):
        _uid[0] += 1
        return nc.sbuf_tensor("%s_%d" % (name, _uid[0]), shape, dt)

    es0 = ExitStack()
    with es0:
        kb = KB(nc, es0)
        E0 = es0.enter_context
        psum = E0(nc.psum_tensor("psum", [128, 8, 512], F32))
        barscr = E0(SBT("barscr", [128, 8], F32))
        G = {}

        def PS(b, n=512, p0=0, p1=128, off=0):
            return psum[p0:p1, b, off:off + n]

        def PS4(g):
            return psum[:, 4 * g:4 * g + 4, :].rearrange("p b n -> p (b n)")

        def pst(b):
            return 'ps%d' % b

        def pst4(g):
            return tuple('ps%d' % (4 * g + i) for i in range(4))

        def range_reduce(eng, x_ap, tmp_ap, rd, wr_tmp):
            kb.op(eng, lambda v: v.tensor_scalar(out=tmp_ap, in0=x_ap, scalar1=float(1.0 / TWO_PI), scalar2=MAGIC,
                                                 op0=ALU.mult, op1=ALU.add), reads=rd, writes=wr_tmp)
            kb.op(eng, lambda v: v.tensor_scalar(out=tmp_ap, in0=tmp_ap, scalar1=-MAGIC, scalar2=None, op0=ALU.add),
                  reads=wr_tmp, writes=wr_tmp)
            kb.op(eng, lambda v: v.scalar_tensor_tensor(out=x_ap, in0=tmp_ap, scalar=float(-TWO_PI), in1=x_ap,
                                                        op0=ALU.mult, op1=ALU.add), reads=wr_tmp + rd, writes=rd)
            kb.op(eng, lambda v: v.tensor_scalar(out=x_ap, in0=x_ap, scalar1=3.14159, scalar2=-3.14159,
                                                 op0=ALU.min, op1=ALU.max), reads=rd, writes=rd)

        def phase_norm(src, wrow, dst_xn, dst_out):
            with ExitStack() as es:
                E = es.enter_context
                h = E(SBT("n_h", [128, 8, L], F32))
                sq = E(SBT("n_sq", [128, 2, 8, 512], BF16))
                ones = E(SBT("n_ones", [128, 128], BF16))
                nw = E(SBT("n_w", [128, 8], F32))
                rt = E(SBT("n_rt", [128, 2, 512], F32))
                rstd = E(SBT("n_rstd", [128, L], F32))
                epsb = E(SBT("n_eps", [128, 1], F32))
                ob = E(SBT("n_ob", [128, 2, L], F32)) if dst_out is not None else None
                kb.op('pool', lambda g: g.memset(ones[:], 1.0), writes=('ones',))
                kb.op('pool', lambda g: g.memset(epsb[:], 1e-6), writes=('epsb',))
                kb.dma('sp', nw[:], normw[wrow], writes=('nw',))
                for c in range(8):
                    kb.dma('sp' if c % 2 == 0 else 'sp', h[:, c, :], src[c], writes=('h%d' % c,))
                for tt in range(4):
                    ts = slice(tt * 512, (tt + 1) * 512)
                    s = tt % 2
                    kb.op('act', lambda a: a.activation(out=sq[:, s], in_=h[:, :, ts], func=AF.Square),
                          reads=tuple('h%d' % c for c in range(8)), writes=('sq%d' % s,))
                    b = kb.ps()
                    for c in range(8):
                        kb.op('pe', lambda p: p.matmul(PS(b), ones[:], sq[:, s, c, :], start=(c == 0), stop=(c == 7)),
                              reads=('ones', 'sq%d' % s), writes=(pst(b),), inc=(c == 7))
                    kb.op('act', lambda a: a.activation(out=rt[:, s, :], in_=PS(b), func=AF.Sqrt, bias=epsb[:, 0:1],
                                                        scale=float(1.0 / D)),
                          reads=(pst(b), 'epsb'), writes=('rt%d' % s,))
                    kb.op('dve', lambda v: v.reciprocal(out=rstd[:, ts], in_=rt[:, s, :]), reads=('rt%d' % s,),
                          writes=('rstd%d' % tt,))
                for c in range(8):
                    if dst_out is None:
                        kb.op('dve', lambda v: v.scalar_tensor_tensor(out=dst_xn[:, c, :], in0=h[:, c, :], scalar=nw[:, c:c + 1],
                                                                      in1=rstd[:], op0=ALU.mult, op1=ALU.mult),
                              reads=('h%d' % c, 'nw') + tuple('rstd%d' % t for t in range(4)), writes=('xn%d' % c,))
                    else:
                        s = c % 2
                        kb.op('dve', lambda v: v.scalar_tensor_tensor(out=ob[:, s, :], in0=h[:, c, :], scalar=nw[:, c:c + 1],
                                                                      in1=rstd[:], op0=ALU.mult, op1=ALU.mult),
                              reads=('h%d' % c, 'nw') + tuple('rstd%d' % t for t in range(4)), writes=('ob%d' % s,))
                        kb.dma('sp', dst_out[c], ob[:, s, :], reads=('ob%d' % s,), writes=('out%d' % c,))
                kb.barrier(barscr[:, 0:1])

        def load_w(es, name, src_rows_ap, kc, ncols, q='sp'):
            E = es.enter_context
            st = E(SBT(name + "_st", [128, kc, ncols], F32))
            wb = E(SBT(name + "_bf", [128, kc, ncols], BF16))
            for k in range(kc):
                kb.dma(q if k % 2 == 0 else 'sp', st[:, k, :], src_rows_ap[k * 128:(k + 1) * 128, :], writes=(name + '_st%d' % k,))
                kb.op('act', lambda a: a.activation(out=wb[:, k, :], in_=st[:, k, :], func=AF.Identity), reads=(name + '_st%d' % k,),
                      writes=(name + '_bf',))
            return wb

        def prep_s5(l):
            with ExitStack() as es:
                E = es.enter_context
                lam = E(SBT("p_lam", [128, 3, 32], F32))
                Bt = E(SBT("p_B", [128, 2, 32, 16], F32))
                Ct = E(SBT("p_C", [128, 2, 32, 16], F32))
                kvs = E(SBT("p_kv", [128, 2, 32], F32))
                msk = E(SBT("p_msk", [128, 256], F32))
                idt = E(SBT("p_id", [128, 128], F32))
                a_re = E(SBT("p_are", [128, 32], F32))
                a_im = E(SBT("p_aim", [128, 32], F32))
                dtt = E(SBT("p_dt", [128, 32], F32))
                mag = E(SBT("p_mag", [128, 32, 32], F32))
                sn = E(SBT("p_sn", [128, 32, 32], F32))
                cs = E(SBT("p_cs", [128, 32, 32], F32))
                tmp = E(SBT("p_tmp", [128, 32, 32], F32))
                Er = E(SBT("p_Er", [128, 32, 32], F32))
                Ei = E(SBT("p_Ei", [128, 32, 32], F32))
                k4 = E(SBT("p_k4", [128, 8, 32], F32))
                Bb = E(SBT("p_Bb", [128, 2, 32, 16], F32))
                t16 = E(SBT("p_t16", [128, 2, 32, 16], F32))
                RB = E(SBT("p_RB", [128, 2, 32, 128], F32))
                CO = E(SBT("p_CO", [128, 2, 32, 128], F32))
                tb = E(SBT("p_tb", [128, 32, 128], F32))
                binS = E(SBT("p_bin", [128, 32, 2, 2, 64], BF16))
                coutS = E(SBT("p_cout", [128, 32, 2, 128], BF16))
                mgS = E(SBT("p_mg", [128, 32, 128], BF16))
                coefS = E(SBT("p_coef", [128, 3, 32, 8], F32))
                mt = E(SBT("p_mt", [128, 2, 256], F32))
                kb.dma('sp', lam[:], s5lam[l], writes=('lam',))
                kb.dma('sp', Bt[:], s5B[l], writes=('Bt',))
                kb.dma('sp', Ct[:], s5C[l], writes=('Ct',))
                kb.dma('sp', kvs[:], kvt, writes=('kvs',))
                kb.dma('sp', msk[:], masks, writes=('msk',))
                kb.dma('sp', idt[:], ident, writes=('idt',))
                V = lambda fn, r, w: kb.op('dve', fn, reads=r, writes=w)
                A = lambda fn, r, w: kb.op('act', fn, reads=r, writes=w)
                A(lambda a: a.activation(out=dtt[:], in_=lam[:, 2, :], func=AF.Exp), ('lam',), ('dtt',))
                V(lambda v: v.tensor_tensor(out=a_re[:], in0=lam[:, 0, :], in1=dtt[:], op=ALU.mult), ('lam', 'dtt'), ('a_re',))
                V(lambda v: v.tensor_tensor(out=a_im[:], in0=lam[:, 1, :], in1=dtt[:], op=ALU.mult), ('lam', 'dtt'), ('a_im',))
                kvb = kvs[:].rearrange("p d (o k) -> p d o k", o=1).to_broadcast([128, 2, 16, 32])
                are_b = a_re[:].rearrange("p (d g o) -> p d g o", d=2, o=1).to_broadcast([128, 2, 16, 32])
                aim_b = a_im[:].rearrange("p (d g o) -> p d g o", d=2, o=1).to_broadcast([128, 2, 16, 32])
                v4 = lambda t: t[:].rearrange("p (d g) k -> p d g k", d=2)
                V(lambda v: v.tensor_tensor(out=v4(tmp), in0=are_b, in1=kvb, op=ALU.mult), ('a_re', 'kvs'), ('tmp',))
                A(lambda a: a.activation(out=mag[:], in_=tmp[:], func=AF.Exp), ('tmp',), ('mag',))
                V(lambda v: v.tensor_tensor(out=v4(sn), in0=aim_b, in1=kvb, op=ALU.mult), ('a_im', 'kvs'), ('sn',))
                V(lambda v: v.tensor_scalar(out=cs[:], in0=sn[:], scalar1=float(math.pi / 2), scalar2=None, op0=ALU.add), ('sn',), ('cs',))
                range_reduce('dve', sn[:], tmp[:], ('sn',), ('tmp',))
                A(lambda a: a.activation(out=sn[:], in_=sn[:], func=AF.Sin), ('sn',), ('sn',))
                range_reduce('dve', cs[:], tmp[:], ('cs',), ('tmp',))
                A(lambda a: a.activation(out=cs[:], in_=cs[:], func=AF.Sin), ('cs',), ('cs',))
                V(lambda v: v.tensor_tensor(out=Er[:], in0=mag[:], in1=cs[:], op=ALU.mult), ('mag', 'cs'), ('Er',))
                V(lambda v: v.tensor_tensor(out=Ei[:], in0=mag[:], in1=sn[:], op=ALU.mult), ('mag', 'sn'), ('Ei',))
                lr = k4[:, 0, :]; li = k4[:, 1, :]; nr = k4[:, 2, :]; den = k4[:, 3, :]; kr = k4[:, 4, :]; ki = k4[:, 5, :]; t0 = k4[:, 6, :]
                for d in range(2):
                    idx = 8 if d == 0 else 15
                    V(lambda v: v.tensor_copy(out=lr[:, 16 * d:16 * d + 16], in_=Er[:, 16 * d:16 * d + 16, idx]), ('Er',), ('k4',))
                    V(lambda v: v.tensor_copy(out=li[:, 16 * d:16 * d + 16], in_=Ei[:, 16 * d:16 * d + 16, idx]), ('Ei',), ('k4',))
                V(lambda v: v.tensor_scalar(out=nr, in0=lr, scalar1=-1.0, scalar2=None, op0=ALU.add), ('k4',), ('k4',))
                V(lambda v: v.tensor_tensor(out=den, in0=lam[:, 0, :], in1=lam[:, 0, :], op=ALU.mult), ('lam', 'k4'), ('k4',))
                V(lambda v: v.tensor_tensor(out=t0, in0=lam[:, 1, :], in1=lam[:, 1, :], op=ALU.mult), ('lam', 'k4'), ('k4',))
                V(lambda v: v.tensor_tensor(out=den, in0=den, in1=t0, op=ALU.add), ('k4',), ('k4',))
                V(lambda v: v.reciprocal(out=den, in_=den), ('k4',), ('k4',))
                V(lambda v: v.tensor_tensor(out=kr, in0=nr, in1=lam[:, 0, :], op=ALU.mult), ('k4', 'lam'), ('k4',))
                V(lambda v: v.tensor_tensor(out=t0, in0=li, in1=lam[:, 1, :], op=ALU.mult), ('k4', 'lam'), ('k4',))
                V(lambda v: v.tensor_tensor(out=kr, in0=kr, in1=t0, op=ALU.add), ('k4',), ('k4',))
                V(lambda v: v.tensor_tensor(out=kr, in0=kr, in1=den, op=ALU.mult), ('k4',), ('k4',))
                V(lambda v: v.tensor_tensor(out=ki, in0=li, in1=lam[:, 0, :], op=ALU.mult), ('k4', 'lam'), ('k4',))
                V(lambda v: v.tensor_tensor(out=t0, in0=nr, in1=lam[:, 1, :], op=ALU.mult), ('k4', 'lam'), ('k4',))
                V(lambda v: v.tensor_tensor(out=ki, in0=ki, in1=t0, op=ALU.subtract), ('k4',), ('k4',))
                V(lambda v: v.tensor_tensor(out=ki, in0=ki, in1=den, op=ALU.mult), ('k4',), ('k4',))
                krb = kr.rearrange("p (g o) -> p g o", o=1).to_broadcast([128, 32, 16])
                kib = ki.rearrange("p (g o) -> p g o", o=1).to_broadcast([128, 32, 16])
                V(lambda v: v.tensor_tensor(out=Bb[:, 0], in0=Bt[:, 0], in1=krb, op=ALU.mult), ('Bt', 'k4'), ('Bb',))
                V(lambda v: v.tensor_tensor(out=t16[:, 0], in0=Bt[:, 1], in1=kib, op=ALU.mult), ('Bt', 'k4'), ('t16',))
                V(lambda v: v.tensor_tensor(out=Bb[:, 0], in0=Bb[:, 0], in1=t16[:, 0], op=ALU.subtract), ('Bb', 't16'), ('Bb',))
                V(lambda v: v.tensor_tensor(out=Bb[:, 1], in0=Bt[:, 1], in1=krb, op=ALU.mult), ('Bt', 'k4', 'Bb'), ('Bb',))
                V(lambda v: v.tensor_tensor(out=t16[:, 1], in0=Bt[:, 0], in1=kib, op=ALU.mult), ('Bt', 'k4', 't16'), ('t16',))
                V(lambda v: v.tensor_tensor(out=Bb[:, 1], in0=Bb[:, 1], in1=t16[:, 1], op=ALU.add), ('Bb', 't16'), ('Bb',))

                def cprod(dst, k0, X, sign_im, tag):
                    Erb = Er[:, :, k0:k0 + 8].rearrange("p g (k o) -> p g k o", o=1).to_broadcast([128, 32, 8, 16])
                    Eib = Ei[:, :, k0:k0 + 8].rearrange("p g (k o) -> p g k o", o=1).to_broadcast([128, 32, 8, 16])
                    Xr = X[:, 0].rearrange("p g (o h) -> p g o h", o=1).to_broadcast([128, 32, 8, 16])
                    Xi = X[:, 1].rearrange("p g (o h) -> p g o h", o=1).to_broadcast([128, 32, 8, 16])
                    d0 = dst[:, 0].rearrange("p g (k h) -> p g k h", k=8)
                    d1 = dst[:, 1].rearrange("p g (k h) -> p g k h", k=8)
                    tv = tb[:].rearrange("p g (k h) -> p g k h", k=8)
                    rd = ('Er', 'Ei', tag)
                    V(lambda v: v.tensor_tensor(out=d0, in0=Erb, in1=Xr, op=ALU.mult), rd, (tag + 'o',))
                    V(lambda v: v.tensor_tensor(out=tv, in0=Eib, in1=Xi, op=ALU.mult), rd, ('tb',))
                    V(lambda v: v.tensor_tensor(out=d0, in0=d0, in1=tv, op=ALU.subtract), (tag + 'o', 'tb'), (tag + 'o',))
                    V(lambda v: v.tensor_tensor(out=d1, in0=Erb, in1=Xi, op=ALU.mult), rd + (tag + 'o',), (tag + 'o',))
                    V(lambda v: v.tensor_tensor(out=tv, in0=Eib, in1=Xr, op=ALU.mult), rd + ('tb',), ('tb',))
                    V(lambda v: v.tensor_tensor(out=d1, in0=d1, in1=tv, op=ALU.add), (tag + 'o', 'tb'), (tag + 'o',))
                    if sign_im < 0:
                        V(lambda v: v.tensor_scalar(out=dst[:, 1], in0=dst[:, 1], scalar1=-1.0, scalar2=None, op0=ALU.mult),
                          (tag + 'o',), (tag + 'o',))
                cprod(RB, 0, Bb, +1, 'Bb')
                cprod(CO, 8, Ct, -1, 'Ct')
                for ri in range(2):
                    A(lambda a: a.activation(out=coutS[:, :, ri, :], in_=CO[:, ri], func=AF.Identity), ('Cto',), ('coutS',))
                kb.dma('sp', coutD, coutS[:].rearrange("p a b c -> p (a b c)"), reads=('coutS',), writes=('coutD',))
                cprod(CO, 16, Ct, -1, 'Ct')
                RC = CO
                A(lambda a: a.activation(out=coefS[:, 0], in_=Er[:, :, 24:32], func=AF.Identity), ('Er',), ('coefS',))
                A(lambda a: a.activation(out=coefS[:, 1], in_=Ei[:, :, 24:32], func=AF.Identity), ('Ei',), ('coefS',))
                A(lambda a: a.activation(out=coefS[:, 2], in_=Ei[:, :, 24:32], func=AF.Identity, scale=-1.0), ('Ei',), ('coefS',))
                kb.dma('sp', coefD, coefS[:].rearrange("p a b c -> p (a b c)"), reads=('coefS',), writes=('coefD',))
                for dg in range(32):
                    d, gp = dg // 16, dg % 16
                    b = kb.ps()
                    for ri in range(2):
                        kb.op('pe', lambda p: p.transpose(PS(b, 128, off=128 * ri), RB[:, ri, dg, :], idt[:]),
                              reads=('Bbo', 'idt'), writes=(pst(b),), inc=(ri == 1))
                    for ri in range(2):
                        A(lambda a: a.activation(out=binS[:, 2 * gp:2 * gp + 2, d, ri, :],
                                                 in_=PS(b, 128, off=128 * ri).rearrange("p (g q) -> p g q", g=2), func=AF.Identity),
                          (pst(b),), ('binS',))
                kb.dma('sp', binD, binS[:].rearrange("p a b c e -> p (a b c e)"), reads=('binS',), writes=('binD',))
                for g in range(32):
                    gp, gpar = g // 2, g % 2
                    b = kb.ps()
                    p0, p1 = 64 * gpar, 64 * gpar + 64
                    for d in range(2):
                        dg = 16 * d + gp
                        kb.op('pe', lambda p: p.matmul(PS(b, 128, off=128 * d), RB[p0:p1, 0, dg, :], RC[p0:p1, 0, dg, :],
                                                       start=True, stop=False),
                              reads=('Bbo', 'Cto'), writes=(pst(b),), inc=False)
                        kb.op('pe', lambda p: p.matmul(PS(b, 128, off=128 * d), RB[p0:p1, 1, dg, :], RC[p0:p1, 1, dg, :],
                                                       start=False, stop=True),
                              reads=('Bbo', 'Cto'), writes=(pst(b),), inc=(d == 1))
                    s = g % 2
                    V(lambda v: v.tensor_tensor(out=mt[:, s, :], in0=PS(b, 256), in1=msk[:], op=ALU.mult), (pst(b), 'msk'), ('mt%d' % s,))
                    V(lambda v: v.tensor_tensor(out=mgS[:, g, :], in0=mt[:, s, 0:128], in1=mt[:, s, 128:256], op=ALU.add),
                      ('mt%d' % s,), ('mgS',))
                kb.dma('sp', mgD, mgS[:].rearrange("p a b -> p (a b)"), reads=('mgS',), writes=('mgD',))
                kb.barrier(barscr[:, 0:1])

        def phase_s5(l):
            with ExitStack() as es_outer:
                EO = es_outer.enter_context
                gs = EO(SBT("s_gs", [128, 4, L], BF16))
                with ExitStack() as es:
                    E = es.enter_context
                    wb = load_w(es, "s_w", w_in[l][:, 0:1024], 8, 1024)
                    udt = E(SBT("s_ud", [128, 4, 8, 256], BF16))
                    for cc in range(8):
                        for tt in range(4):
                            b = kb.ps()
                            for k in range(8):
                                kb.op('pe', lambda p: p.matmul(PS(b), wb[:, k, cc * 128:(cc + 1) * 128], G['xn'][:, k, tt * 512:(tt + 1) * 512],
                                                               start=(k == 0), stop=(k == 7)),
                                      reads=('s_w_bf', 'xn%d' % k), writes=(pst(b),), inc=(k == 7))
                            if cc < 4:
                                kb.op('act', lambda a: a.activation(out=udt[:, cc, :, tt * 64:(tt + 1) * 64],
                                                                    in_=PS(b).rearrange("p (c j) -> p j c", j=8), func=AF.Identity),
                                      reads=(pst(b),), writes=('udt%d' % cc,))
                            else:
                                kb.op('act', lambda a: a.activation(out=gs[:, cc - 4, :].rearrange("p (j c) -> p j c", j=8)[:, :, tt * 64:(tt + 1) * 64],
                                                                    in_=PS(b).rearrange("p (c j) -> p j c", j=8), func=AF.Silu),
                                      reads=(pst(b),), writes=('gs',))
                        if cc < 4:
                            kb.dma('sp', ud[cc], udt[:, cc].rearrange("p j c -> p (j c)"), reads=('udt%d' % cc,), writes=('ud',))
                    kb.barrier(barscr[:, 0:1])
                with ExitStack() as es:
                    E = es.enter_context
                    U8 = E(SBT("s_U8", [128, 32, 256], BF16))
                    Mg = E(SBT("s_Mg", [128, 32, 128], BF16))
                    Bin = E(SBT("s_Bin", [128, 32, 2, 2, 64], BF16))
                    Cout = E(SBT("s_Cout", [128, 32, 2, 128], BF16))
                    coef = E(SBT("s_coef", [128, 3, 32, 8], F32))
                    Xs = E(SBT("s_Xs", [128, 32, 2, 256], BF16))
                    Y8 = U8
                    NSL = 2
                    XA = E(SBT("s_XA", [128, NSL, 2, 768], F32))
                    XB = E(SBT("s_XB", [128, NSL, 2, 768], F32))
                    T1 = E(SBT("s_T1", [128, NSL, 2, 256], F32))
                    udv = ud.rearrange("cc (g h) (j c) -> h j (cc g) c", h=16, j=8)
                    for j in range(8):
                        kb.dma('sp' if j % 2 == 0 else 'sp', U8[16 * j:16 * j + 16, :, :], udv[:, j], reads=('ud',), writes=('U8',))
                    kb.dma('sp', Mg[:].rearrange("p a b -> p (a b)"), mgD, reads=('mgD',), writes=('Mg',))
                    kb.dma('sp', Bin[:].rearrange("p a b c e -> p (a b c e)"), binD, reads=('binD',), writes=('Bin',))
                    kb.dma('sp', Cout[:].rearrange("p a b c -> p (a b c)"), coutD, reads=('coutD',), writes=('Cout',))
                    kb.dma('sp', coef[:].rearrange("p a b c -> p (a b c)"), coefD, reads=('coefD',), writes=('coef',))
                    kb.op('pool', lambda g: g.memset(XA[:], 0.0), writes=tuple('XA%d' % i for i in range(NSL)))
                    kb.op('pool', lambda g: g.memset(XB[:], 0.0), writes=tuple('XB%d' % i for i in range(NSL)))
                    fwst = E(SBT("s_fwst", [128, 8, 512], F32))
                    fwbf = E(SBT("s_fwbf", [128, 2, 8, 512], BF16))
                    fost = E(SBT("s_fost", [128, 8, 512], BF16))
                    groups = []
                    for si in range(4):
                        for q in range(2):
                            groups.append(([1024, 2048, 3072, 4096][si] + q * 512, [pD[si, 4 * q + j] for j in range(4)],
                                           AF.Silu if si == 3 else AF.Identity))
                    for q in range(4):
                        groups.append((5120 + q * 512, [mmD[4 * q + j] for j in range(4)], AF.Sigmoid))
                    fcnt = [0]

                    def dma_group(gi, ks=range(8)):
                        col = groups[gi][0]
                        for k in ks:
                            kb.dma('sp', fwst[:, k, :], w_in[l][k * 128:(k + 1) * 128, col:col + 512], writes=('fwst',))

                    def cast_group(gi):
                        slot = gi % 2
                        for k in range(8):
                            kb.op('act', lambda a: a.activation(out=fwbf[:, slot, k, :], in_=fwst[:, k, :], func=AF.Identity), reads=('fwst',), writes=('fwbf%d' % slot,))

                    fpieces = []
                    for gi in range(len(groups)):
                        for j in range(4):
                            for t4 in range(4):
                                def fpiece(gi=gi, j=j, t4=t4):
                                    slot = gi % 2
                                    pi_ = j * 4 + t4
                                    if pi_ == 0:
                                        cast_group(gi)
                                    if pi_ < 8 and gi + 1 < len(groups):
                                        dma_group(gi + 1, [pi_])
                                    ts = slice(t4 * 512, (t4 + 1) * 512)
                                    b = kb.ps()
                                    for k in range(8):
                                        kb.op('pe', lambda p: p.matmul(PS(b), fwbf[:, slot, k, j * 128:(j + 1) * 128], G['xn'][:, k, ts], start=(k == 0), stop=(k == 7)),
                                              reads=('fwbf%d' % slot, 'xn%d' % k), writes=(pst(b),), inc=(k == 7))
                                    os_ = fcnt[0] % 8
                                    fcnt[0] += 1
                                    kb.op('act', lambda a: a.activation(out=fost[:, os_, :], in_=PS(b), func=groups[gi][2]), reads=(pst(b),), writes=('fost%d' % os_,))
                                    kb.dma('sp', groups[gi][1][j][:, ts], fost[:, os_, :], reads=('fost%d' % os_,), writes=('fdst',))
                                fpieces.append(fpiece)
                    dma_group(0)

                    def sfill():
                        if fpieces:
                            fpieces.pop(0)()

                    for s0 in range(0, 32, NSL):
                        for i in range(NSL):
                            dg = s0 + i
                            d, gp = dg // 16, dg % 16
                            b = kb.ps()
                            for ri in range(2):
                                for gpar in range(2):
                                    g = 2 * gp + gpar
                                    kb.op('pe', lambda p: p.matmul(PS(b, 256, 64 * gpar, 64 * gpar + 64, off=256 * ri),
                                                                   Bin[:, g, d, ri, :], U8[:, g, :], start=True, stop=True),
                                          reads=('Bin', 'U8'), writes=(pst(b),), inc=(ri == 1 and gpar == 1))
                            kb.op('act', lambda a: a.activation(out=XA[:, i, :, 256:512], in_=PS(b).rearrange("p (r c) -> p r c", r=2),
                                                                func=AF.Identity),
                                  reads=(pst(b),), writes=('XA%d' % i,))
                        for r in range(8):
                            sft = 2 ** r
                            stage = [[], []]
                            for i in range(NSL):
                                dg = s0 + i
                                d = dg // 16
                                src, dst = (XA, XB) if r % 2 == 0 else (XB, XA)
                                sn_, dn_ = ('XA%d' % i, 'XB%d' % i) if r % 2 == 0 else ('XB%d' % i, 'XA%d' % i)
                                lo = 256 - sft if d == 0 else 256 + sft
                                e_ = coef[:, 0, dg, r:r + 1]; f_ = coef[:, 1, dg, r:r + 1]; nf_ = coef[:, 2, dg, r:r + 1]
                                Rs = src[:, i, 0, lo:lo + 256]; Is = src[:, i, 1, lo:lo + 256]
                                R0 = src[:, i, 0, 256:512]; I0 = src[:, i, 1, 256:512]
                                tn = 'T1_%d' % i

                                def st1(i=i, Rs=Rs, Is=Is, R0=R0, I0=I0, e_=e_, f_=f_, sn_=sn_, tn=tn):
                                    kb.op('dve', lambda v: v.scalar_tensor_tensor(out=T1[:, i, 0, :], in0=Rs, scalar=e_, in1=R0, op0=ALU.mult, op1=ALU.add),
                                          reads=(sn_, 'coef'), writes=(tn + 'a',))
                                    kb.op('dve', lambda v: v.scalar_tensor_tensor(out=T1[:, i, 1, :], in0=Rs, scalar=f_, in1=I0, op0=ALU.mult, op1=ALU.add),
                                          reads=(sn_, 'coef'), writes=(tn + 'b',))

                                def st2(i=i, Is=Is, e_=e_, nf_=nf_, sn_=sn_, dn_=dn_, tn=tn, dst=dst):
                                    kb.op('dve', lambda v: v.scalar_tensor_tensor(out=dst[:, i, 0, 256:512], in0=Is, scalar=nf_, in1=T1[:, i, 0, :],
                                                                                  op0=ALU.mult, op1=ALU.add),
                                          reads=(sn_, 'coef', tn + 'a'), writes=(dn_ + 'r',))
                                    kb.op('dve', lambda v: v.scalar_tensor_tensor(out=dst[:, i, 1, 256:512], in0=Is, scalar=e_, in1=T1[:, i, 1, :],
                                                                                  op0=ALU.mult, op1=ALU.add),
                                          reads=(sn_, 'coef', tn + 'b'), writes=(dn_, dn_ + 'r'))
                                stage[0].append(st1)
                                stage[1].append(st2)
                            for st in stage:
                                for f in st:
                                    f()
                                sfill()
                        for i in range(NSL):
                            dg = s0 + i
                            d = dg // 16
                            lo = 255 if d == 0 else 257
                            kb.op('act', lambda a: a.activation(out=Xs[:, dg, :, :], in_=XA[:, i, :, lo:lo + 256], func=AF.Identity),
                                  reads=('XA%d' % i, 'XA%dr' % i), writes=('Xs',))
                    while fpieces:
                        fpieces.pop(0)()
                    for g in range(32):
                        gp, gpar = g // 2, g % 2
                        p0, p1 = 64 * gpar, 64 * gpar + 64
                        b = kb.ps()
                        kb.op('pe', lambda p: p.matmul(PS(b, 256), Mg[:, g, :], U8[:, g, :], start=True, stop=False),
                              reads=('Mg', 'U8'), writes=(pst(b),), inc=False)
                        for d in range(2):
                            for ri in range(2):
                                last = (d == 1 and ri == 1)
                                kb.op('pe', lambda p: p.matmul(PS(b, 256), Cout[p0:p1, 16 * d + gp, ri, :], Xs[p0:p1, 16 * d + gp, ri, :],
                                                               start=False, stop=last),
                                      reads=('Cout', 'Xs'), writes=(pst(b),), inc=last)
                        kb.op('act', lambda a: a.activation(out=Y8[:, g, :], in_=PS(b, 256), func=AF.Identity), reads=(pst(b),), writes=('Y8',))
                    ydv = yd.rearrange("cc (g h) (i c) -> h i (cc g) c", h=16, i=8)
                    for i in range(8):
                        kb.dma('sp' if i % 2 == 0 else 'sp', ydv[:, i], Y8[16 * i:16 * i + 16, :, :], reads=('Y8',), writes=('yd',))
                    kb.barrier(barscr[:, 0:1])
                with ExitStack() as es:
                    E = es.enter_context
                    wg = E(SBT("c_wg", [128, 4, 512], BF16))
                    wbr = E(SBT("c_wbr", [128, 4, 1024], BF16))
                    wst = E(SBT("c_wst", [128, 2, 1024], F32))
                    yt = E(SBT("c_y", [128, L], BF16))
                    ut = E(SBT("c_u", [128, L], BF16))
                    y1 = E(SBT("c_y1", [128, L], F32))
                    t3 = E(SBT("c_t3", [128, L], F32))
                    yg = E(SBT("c_yg", [128, 4, L], BF16))
                    sg = E(SBT("c_sg", [128, 2, 512], F32))
                    y3 = E(SBT("c_y3", [128, 4, L], BF16))
                    ys = E(SBT("c_ys", [128, L], BF16))
                    dv = E(SBT("c_d", [128, 4], F32))
                    bg = E(SBT("c_bg", [128, 4], F32))
                    kb.dma('sp', dv[:], s5d[l], writes=('dv',))
                    kb.dma('sp', bg[:], b_glu[l], writes=('bg',))
                    for k in range(4):
                        s = k % 2
                        kb.dma('sp', wst[:, s, 0:512], w_glu[l][k * 128:(k + 1) * 128, :], writes=('wst%d' % s,))
                        kb.op('act', lambda a: a.activation(out=wg[:, k, :], in_=wst[:, s, 0:512], func=AF.Identity), reads=('wst%d' % s,), writes=('wg',))
                    for k in range(4):
                        s = k % 2
                        kb.dma('sp', wst[:, s, :], w_bs5[l][k * 128:(k + 1) * 128, :], writes=('wst%d' % s,))
                        kb.op('act', lambda a: a.activation(out=wbr[:, k, :], in_=wst[:, s, :], func=AF.Identity), reads=('wst%d' % s,), writes=('wbr',))
                    for cc in range(4):
                        kb.dma('sp', yt[:], yd[cc], reads=('yd',), writes=('yt',))
                        kb.dma('sp', ut[:], ud[cc], reads=('ud',), writes=('ut',))
                        kb.op('dve', lambda v: v.scalar_tensor_tensor(out=y1[:], in0=ut[:], scalar=dv[:, cc:cc + 1], in1=yt[:],
                                                                      op0=ALU.mult, op1=ALU.add),
                              reads=('ut', 'yt', 'dv'), writes=('y1',))
                        kb.op('dve', lambda v: v.tensor_tensor(out=t3[:], in0=y1[:], in1=y1[:], op=ALU.mult), reads=('y1',), writes=('t3',))
                        kb.op('dve', lambda v: v.tensor_scalar(out=t3[:], in0=t3[:], scalar1=0.044715 * 1.5957691216, scalar2=1.5957691216,
                                                               op0=ALU.mult, op1=ALU.add), reads=('t3',), writes=('t3',))
                        kb.op('dve', lambda v: v.tensor_tensor(out=t3[:], in0=t3[:], in1=y1[:], op=ALU.mult), reads=('t3', 'y1'), writes=('t3',))
                        kb.op('act', lambda a: a.activation(out=t3[:], in_=t3[:], func=AF.Sigmoid), reads=('t3',), writes=('t3',))
                        kb.op('dve', lambda v: v.tensor_tensor(out=yg[:, cc, :], in0=t3[:], in1=y1[:], op=ALU.mult), reads=('t3', 'y1'), writes=('yg',))
                    gsp = gs[:].rearrange("p a (j c) -> p a j c", j=8)
                    for cc in range(4):
                        for tt in range(4):
                            b = kb.ps()
                            for k in range(4):
                                kb.op('pe', lambda p: p.matmul(PS(b), wg[:, k, cc * 128:(cc + 1) * 128], yg[:, k, tt * 512:(tt + 1) * 512],
                                                               start=(k == 0), stop=(k == 3)),
                                      reads=('wg', 'yg'), writes=(pst(b),), inc=(k == 3))
                            s = tt % 2
                            kb.op('act', lambda a: a.activation(out=sg[:, s, :], in_=PS(b), func=AF.Sigmoid, bias=bg[:, cc:cc + 1]),
                                  reads=(pst(b), 'bg'), writes=('sg%d' % s,))
                            kb.op('dve', lambda v: v.tensor_tensor(out=sg[:, s, :], in0=sg[:, s, :], in1=yg[:, cc, tt * 512:(tt + 1) * 512], op=ALU.mult),
                                  reads=('sg%d' % s, 'yg'), writes=('sg%d' % s,))
                            kb.op('dve', lambda v: v.tensor_tensor(out=y3[:, cc, tt * 512:(tt + 1) * 512].rearrange("p (j c) -> p j c", j=2),
                                                                   in0=sg[:, s, :].rearrange("p (j c) -> p j c", j=2),
                                                                   in1=gsp[:, cc, 2 * tt:2 * tt + 2, :], op=ALU.mult),
                                  reads=('sg%d' % s, 'gs'), writes=('y3',))
                    for dc in range(8):
                        for tt in range(4):
                            b = kb.ps()
                            for k in range(4):
                                kb.op('pe', lambda p: p.matmul(PS(b), wbr[:, k, dc * 128:(dc + 1) * 128],
                                                               y3[:, k, :].rearrange("p (j c) -> p c j", j=8)[:, tt * 64:(tt + 1) * 64, :],
                                                               start=(k == 0), stop=(k == 3)),
                                      reads=('wbr', 'y3'), writes=(pst(b),), inc=(k == 3))
                            kb.op('act', lambda a: a.activation(out=ys[:, tt * 512:(tt + 1) * 512], in_=PS(b), func=AF.Identity),
                                  reads=(pst(b),), writes=('ys',))
                        kb.dma('sp', ys5d[dc], ys[:], reads=('ys',), writes=('ys5d',))
                    kb.barrier(barscr[:, 0:1])

        def PS2(b0):
            return psum[:, b0:b0 + 2, :].rearrange("p b n -> p (b n)")

        def pst2(b0):
            return ('ps%d' % b0, 'ps%d' % (b0 + 1))

        def next_pair():
            p_ = kb.fp % 2
            kb.fp += 1
            return 2 * p_

        def fft_fwd(C, zin_list, B, spec_evac, filler=lambda: None):
            for q in range(4):
                b0 = next_pair()
                for cpl in range(4):
                    cp = q * 4 + cpl
                    bank = b0 + cpl // 2
                    off = 256 * (cpl % 2)
                    for zi, (zT, ztok, gk) in enumerate(zin_list):
                        kb.op('pe', lambda p: p.matmul(PS(bank, 256, off=off), zT[:, 2 * cp:2 * cp + 2, :].rearrange("p a b -> p (a b)"),
                                                       C['G1hi'] if gk else C['G1'], start=(zi == 0), stop=(zi == len(zin_list) - 1)),
                              reads=(ztok,) + C['toks'], writes=(pst(bank),), inc=(zi == len(zin_list) - 1))
                kb.op('act', lambda a: a.activation(
                    out=B[:].rearrange("p k r cp cq -> p (k r) cp cq")[:, :, q * 4:q * 4 + 4, :],
                    in_=PS2(b0).rearrange("p (cp cq kr) -> p kr cp cq", cp=4, cq=4), func=AF.Identity),
                    reads=pst2(b0), writes=('B',))
                if q % 2 == 1:
                    filler()
            for c2 in range(2):
                for h in range(2):
                    b0 = next_pair()
                    for k1l in range(16):
                        k1 = 16 * h + k1l
                        bank = b0 + k1l // 8
                        off = 64 * (k1l % 8)
                        for ri in range(2):
                            kb.op('pe', lambda p: p.matmul(PS(bank, 64, off=off), C['T'][64 * c2:64 * c2 + 64, k1, ri, :],
                                                           B[64 * c2:64 * c2 + 64, k1, ri].rearrange("p a b -> p (a b)"),
                                                           start=(ri == 0), stop=(ri == 1)),
                                  reads=('B',) + C['toks'], writes=(pst(bank),), inc=(ri == 1))
                    spec_evac(c2, h, PS2(b0).rearrange("p (k m) -> p m k", k=16), pst2(b0))
                filler()

        def fft_inv(C, P1, P2, Dt, conv_out, conv_tok, filler=lambda: None):
            for c2 in range(2):
                for h in range(2):
                    b0 = next_pair()
                    for cpl in range(8):
                        cp = 8 * h + cpl
                        bank = b0 + cpl // 4
                        off = 128 * (cpl % 4)
                        kb.op('pe', lambda p: p.matmul(PS(bank, 128, off=off), P1[:, c2, cp * 128:(cp + 1) * 128], C['Ga'], start=True, stop=False),
                              reads=('P1',) + C['toks'], writes=(pst(bank),), inc=False)
                        kb.op('pe', lambda p: p.matmul(PS(bank, 128, off=off), P2[:, c2, cp * 128:(cp + 1) * 128], C['Gb'], start=False, stop=True),
                              reads=('P2',) + C['toks'], writes=(pst(bank),), inc=True)
                    kb.op('act', lambda a: a.activation(
                        out=Dt[:, c2].rearrange("p n r cp -> p (n r) cp")[:, :, 8 * h:8 * h + 8],
                        in_=PS2(b0).rearrange("p (cp nr) -> p nr cp", cp=8), func=AF.Identity),
                        reads=pst2(b0), writes=('B',))
                filler()
            for h in range(2):
                b0 = next_pair()
                for n2l in range(32):
                    n2 = 32 * h + n2l
                    bank = b0 + n2l // 16
                    off = 32 * (n2l % 16)
                    for r in range(2):
                        kb.op('pe', lambda p: p.matmul(PS(bank, 32, off=off), C['H'][:, n2, r, :],
                                                       Dt[:, :, n2, r, :].rearrange("p c2 cp -> p cp c2"), start=(r == 0), stop=(r == 1)),
                              reads=('B',) + C['toks'], writes=(pst(bank),), inc=(r == 1))
                kb.op('dve', lambda v: v.transpose(
                    out=conv_out.rearrange("p (n1 n2) -> p n2 n1", n2=64)[:, 32 * h:32 * h + 32, :],
                    in_=PS2(b0).rearrange("p (n c) -> p n c", c=32)),
                    reads=pst2(b0), writes=(conv_tok,))
            filler()

        def load_fft_consts(es, with_filter):
            E = es.enter_context
            fs = E(SBT("f_sm", [128, 896], BF16))
            Tb = E(SBT("f_T", [128, 32, 2, 128], BF16))
            kb.dma('sp', fs[:], fsmall, writes=('fc',))
            Tv = Tb[:].rearrange("p a b c -> p (a b c)")
            for i in range(2):
                kb.dma('sp', Tv[:, i * 4096:(i + 1) * 4096], fT[:, i * 4096:(i + 1) * 4096], writes=('fcT%d' % i,))
            C = {'G1': fs[:, 0:256], 'G1hi': fs[:, 256:512], 'Ga': fs[:, 512:640], 'Gb': fs[:, 640:768], 'Sw': fs[:, 768:896], 'T': Tb}
            C['toks'] = ('fc', 'fcT0', 'fcT1')
            if not with_filter:
                Hb = E(SBT("f_H", [128, 64, 2, 128], BF16))
                Hv = Hb[:].rearrange("p a b c -> p (a b c)")
                for i in range(4):
                    kb.dma('sp', Hv[:, i * 4096:(i + 1) * 4096], fH[:, i * 4096:(i + 1) * 4096], writes=('fcH%d' % i,))
                C['H'] = Hb
                C['toks'] = C['toks'] + tuple('fcH%d' % i for i in range(4))
            return C

        def prep_hy(l):
            with ExitStack() as es:
                E = es.enter_context
                C = load_fft_consts(es, True)
                w1 = E(SBT("q_w1", [33, 64], F32))
                w2 = E(SBT("q_w2", [64, 64], F32))
                w3 = E(SBT("q_w3", [64, 4096], F32))
                hv = E(SBT("q_hv", [64, 3], F32))
                bf = E(SBT("q_bf", [64, 2], F32))
                dec = E(SBT("q_dec", [128, 32], F32))
                ft = E(SBT("q_ft", [33, 2, L], F32))
                tt_ = E(SBT("q_tt", [128, 2, L], F32))
                h1 = E(SBT("q_h1", [64, L], F32))
                h2 = E(SBT("q_h2", [64, 2, L], F32))
                tmp = E(SBT("q_tmp", [64, L], F32))
                win = E(SBT("q_win", [128, 2, L], F32))
                hk = E(SBT("q_hk", [128, 2, 2, L], BF16))
                rsS = E(SBT("q_rs", [128, 16], F32))
                junk = E(SBT("q_junk", [128, L], BF16))
                ss = E(SBT("q_ss", [128, 2, 4], F32))
                zT = E(SBT("q_zT", [128, 2, 32, 64], BF16))
                B = E(SBT("q_B", [128, 32, 2, 16, 4], BF16))
                Kh = E(SBT("q_Kh", [128, 2, 2048], BF16))
                kb.dma('sp', w1[:], hw1[l], writes=('w1',))
                kb.dma('sp', w2[:], hw2[l], writes=('w2',))
                kb.dma('sp', w3[:], hw3[l], writes=('w3',))
                kb.dma('sp', hv[:], hvec[l], writes=('hv',))
                kb.dma('sp', dec[:], hdec[l], writes=('dec',))
                for i in range(2):
                    kb.dma('sp', ft[:, i, :], feats[i], writes=('ft',))
                    kb.dma('sp', tt_[:, i, :], ttab[i], writes=('tt',))
                V = lambda fn, r, w: kb.op('dve', fn, reads=r, writes=w)
                A = lambda fn, r, w: kb.op('act', fn, reads=r, writes=w)
                V(lambda v: v.tensor_tensor(out=bf[:, 0:1], in0=hv[:, 0:1], in1=hv[:, 2:3], op=ALU.mult), ('hv',), ('bf',))
                V(lambda v: v.tensor_tensor(out=bf[:, 1:2], in0=hv[:, 1:2], in1=hv[:, 2:3], op=ALU.mult), ('hv', 'bf'), ('bf',))
                A(lambda a: a.activation(out=dec[:], in_=dec[:], func=AF.Abs), ('dec',), ('dec',))
                V(lambda v: v.tensor_scalar(out=dec[:], in0=dec[:], scalar1=-1.0, scalar2=None, op0=ALU.mult), ('dec',), ('dec',))
                for i in range(2):
                    for stage in range(2):
                        wmat = w1 if stage == 0 else w2
                        dst = h1 if stage == 0 else h2[:, i, :]
                        dtok = 'h1' if stage == 0 else 'h2_%d' % i
                        for t4 in range(4):
                            ts = slice(t4 * 512, (t4 + 1) * 512)
                            b = kb.ps()
                            if stage == 0:
                                kb.op('pe', lambda p: p.matmul(PS(b, 512, 0, 64), w1[:], ft[:, i, ts], start=True, stop=True),
                                      reads=('w1', 'ft'), writes=(pst(b),))
                            else:
                                kb.op('pe', lambda p: p.matmul(PS(b, 512, 0, 64), w2[:], h1[:, ts], start=True, stop=True),
                                      reads=('w2', 'h1'), writes=(pst(b),))
                            dsl = dst[:, ts] if stage == 0 else h2[:, i, ts]
                            A(lambda a: a.activation(out=dsl, in_=PS(b, 512, 0, 64), func=AF.Identity, bias=bf[:, stage:stage + 1], scale=hv[:, 2:3]),
                              (pst(b), 'bf', 'hv'), (dtok,))
                        dfull = h1[:] if stage == 0 else h2[:, i, :]
                        range_reduce('dve', dfull, tmp[:], (dtok,), ('tmp',))
                        A(lambda a: a.activation(out=dfull, in_=dfull, func=AF.Sin), (dtok,), (dtok,))
                NIT = 16
                w3b = E(SBT("q_w3b", [64, 4096], BF16))
                h2b = E(SBT("q_h2b", [64, 2, L], BF16))
                for q_ in range(4):
                    A(lambda a: a.activation(out=w3b[:, q_ * 1024:(q_ + 1) * 1024], in_=w3[:, q_ * 1024:(q_ + 1) * 1024], func=AF.Identity), ('w3',), ('w3b',))
                V(lambda v: v.tensor_copy(out=h2b[:], in_=h2[:]), ('h2_0', 'h2_1'), ('h2b',))

                def make_A(n):
                    o, cc = n // 8, n % 8
                    par = n % 2
                    pieces = []
                    for dr in range(2):
                        for t4 in range(4):
                            def piece(dr=dr, t4=t4):
                                fc = o * 16 + dr * 8 + cc
                                if t4 == 0 and dr == 0:
                                    for d2 in range(2):
                                        fc2 = o * 16 + d2 * 8 + cc
                                        A(lambda a: a.activation(out=win[:, d2, :], in_=tt_[:, d2, :], func=AF.Exp, scale=dec[:, fc2:fc2 + 1]),
                                          ('tt', 'dec'), ('win%d' % d2,))
                                ts = slice(t4 * 512, (t4 + 1) * 512)
                                b = kb.ps_hi()
                                kb.op('pe', lambda p: p.matmul(PS(b), w3b[:, fc * 128:(fc + 1) * 128], h2b[:, dr, ts], start=True, stop=True),
                                      reads=('w3b', 'h2b'), writes=(pst(b),))
                                V(lambda v: v.scalar_tensor_tensor(out=hk[:, par, dr, ts], in0=win[:, dr, ts], scalar=0.05, in1=PS(b), op0=ALU.add, op1=ALU.mult),
                                  ('win%d' % dr, pst(b)), ('hk%d_%d' % (par, dr),))
                                if t4 == 3:
                                    if dr == 1:
                                        V(lambda v: v.memset(hk[:, par, 1, 0:1], 0.0), ('hk%d_1' % par,), ('hk%d_1' % par,))
                                    A(lambda a: a.activation(out=junk[:], in_=hk[:, par, dr, :], func=AF.Square, accum_out=ss[:, par, dr:dr + 1]),
                                      ('hk%d_%d' % (par, dr),), ('junk', 'ss%d_%d' % (par, dr)))
                            pieces.append(piece)
                    return pieces

                def run_B(n, filler):
                    o, cc = n // 8, n % 8
                    par = n % 2
                    V(lambda v: v.tensor_tensor(out=ss[:, par, 2:3], in0=ss[:, par, 0:1], in1=ss[:, par, 1:2], op=ALU.add),
                      ('ss%d_0' % par, 'ss%d_1' % par), ('ss%d_2' % par,))
                    A(lambda a: a.activation(out=ss[:, par, 2:3], in_=ss[:, par, 2:3], func=AF.Sqrt, bias=barscr[:, 1:2]), ('ss%d_2' % par,), ('ss%d_2' % par,))
                    V(lambda v: v.reciprocal(out=rsS[:, n:n + 1], in_=ss[:, par, 2:3]), ('ss%d_2' % par,), ('rsS',))
                    for dr in range(2):
                        V(lambda v: v.transpose(out=zT[:, dr].bitcast(U32).rearrange("p c m -> p m c"),
                                                in_=hk[:, par, dr, :].bitcast(U32).rearrange("p (n1 m) -> p m n1", m=32)),
                          ('hk%d_%d' % (par, dr),), ('zT%d' % dr,))
                        filler()

                    def spec_evac(c2, h, psv, ptoks):
                        V(lambda v: v.tensor_copy(out=Kh[:, c2, :].rearrange("p (m k) -> p m k", k=32)[:, :, 16 * h:16 * h + 16], in_=psv),
                          ptoks, ('Kh',))
                    fft_fwd(C, [(zT[:, 0], 'zT0', 0), (zT[:, 1], 'zT1', 1)], B, spec_evac, filler)
                    kb.dma('sp', khatD[o, cc], Kh[:].rearrange("p a b -> p (a b)"), reads=('Kh',), writes=('khatD',))

                cur = make_A(0)
                for pc in cur:
                    pc()
                for n in range(NIT):
                    nxt = make_A(n + 1) if n + 1 < NIT else []

                    def filler():
                        if nxt:
                            nxt.pop(0)()
                    run_B(n, filler)
                    while nxt:
                        nxt.pop(0)()
                kb.dma('sp', rsD, rsS[:], reads=('rsS',), writes=('rsD',))
                kb.barrier(barscr[:, 0:1])

        def phase_hy(l):
            with ExitStack() as es:
                E = es.enter_context
                C = load_fft_consts(es, False)
                cw = E(SBT("h_cw", [128, 3, 24], F32))
                cb_ = E(SBT("h_cb", [128, 24], F32))
                dd = E(SBT("h_dd", [128, 2, 8], F32))
                pp = E(SBT("h_ppb", [128, 3, L + 2], BF16))
                vx = E(SBT("h_vx", [128, 2, 4, L], BF16))
                z1 = E(SBT("h_z1", [128, L], BF16))
                zT = E(SBT("h_zT", [128, 32, 64], BF16))
                BD = E(SBT("h_BD", [128, 4096], BF16))
                B = BD[:].rearrange("p (k r cp cq) -> p k r cp cq", k=32, r=2, cp=16)
                Dt = BD[:].rearrange("p (c2 n r cp) -> p c2 n r cp", c2=2, n=64, r=2)
                P1 = E(SBT("h_P1", [128, 2, 2048], BF16))
                P2 = E(SBT("h_P2", [128, 2, 2048], BF16))
                Kt = E(SBT("h_Kt", [128, 2, 2, 4096], BF16))
                conv = E(SBT("h_conv", [128, L], F32))
                sc = conv
                kb.dma('sp', cw[:], convw[l], writes=('cw',))
                kb.dma('sp', cb_[:], convb[l], writes=('cb',))
                kb.dma('sp', dd[:], hyd[l], writes=('dd',))
                rs = E(SBT("h_rs", [128, 2, 16], F32))
                kb.dma('sp', rs[:, 0, :], rsD, reads=('rsD',), writes=('rs',))
                kb.op('dve', lambda v: v.reciprocal(out=rs[:, 1, :], in_=rs[:, 0, :]), reads=('rs',), writes=('rs',))
                kb.op('dve', lambda v: v.tensor_tensor(out=cw[:, :, 8:24], in0=cw[:, :, 8:24],
                                                       in1=rs[:, 0:1, :].to_broadcast([128, 3, 16]), op=ALU.mult), reads=('cw', 'rs'), writes=('cw',))
                kb.op('dve', lambda v: v.tensor_tensor(out=cb_[:, 8:24], in0=cb_[:, 8:24], in1=rs[:, 0, :], op=ALU.mult), reads=('cb', 'rs'), writes=('cb',))
                kb.op('dve', lambda v: v.tensor_tensor(out=dd[:].rearrange("p o c -> p (o c)"), in0=dd[:].rearrange("p o c -> p (o c)"), in1=rs[:, 1, :], op=ALU.mult),
                      reads=('dd', 'rs'), writes=('dd',))
                kb.op('pool', lambda g: g.memset(pp[:], 0.0), writes=('pp0', 'pp1', 'pp2'))

                def short_conv(cb, si):
                    par = cb % 2
                    sp_ = si
                    ch = si * 8 + cb
                    kb.op('dve', lambda v: v.tensor_scalar(out=sc[:], in0=pp[:, sp_, 1:L + 1], scalar1=cw[:, 1, ch:ch + 1], scalar2=cb_[:, ch:ch + 1],
                                                           op0=ALU.mult, op1=ALU.add),
                          reads=('pp%d' % sp_, 'cw', 'cb'), writes=('conv',))
                    kb.op('dve', lambda v: v.scalar_tensor_tensor(out=sc[:], in0=pp[:, sp_, 0:L], scalar=cw[:, 0, ch:ch + 1], in1=sc[:],
                                                                  op0=ALU.mult, op1=ALU.add),
                          reads=('pp%d' % sp_, 'cw', 'conv'), writes=('conv',))
                    kb.op('dve', lambda v: v.scalar_tensor_tensor(out=vx[:, par, si, :], in0=pp[:, sp_, 2:L + 2], scalar=cw[:, 2, ch:ch + 1], in1=sc[:],
                                                                  op0=ALU.mult, op1=ALU.add),
                          reads=('pp%d' % sp_, 'cw', 'conv'), writes=('vx%d_%d' % (par, si),))

                def make_A(cb):
                    par = cb % 2
                    base = [[] for _ in range(16)]
                    for si in range(3):
                        base[si].append(lambda si=si: kb.dma('sp', pp[:, si, 1:L + 1], pD[si, cb], writes=('pp%d' % si,)))
                    base[3].append(lambda: kb.dma('sp', vx[:, par, 3, :], pD[3, cb], writes=('vx%d_3' % par,)))
                    for si in range(3):
                        base[4 * si + 5].append(lambda si=si: short_conv(cb, si))
                    base[15].append(lambda: kb.op('pool', lambda g: g.tensor_tensor(out=vx[:, par, 2, :], in0=vx[:, par, 2, :], in1=vx[:, par, 3, :], op=ALU.mult),
                                                  reads=('vx%d_2' % par, 'vx%d_3' % par), writes=('vx%d_2' % par,)))
                    return [(lambda fs=fs: [f() for f in fs]) for fs in base]

                def load_kt(ci):
                    cb_i, o_i, kb_ = ci // 2, ci % 2, ci % 2
                    kb.dma('sp', Kt[:, kb_, 0, :], khatD[o_i, cb_i], writes=('Kt%d' % kb_,))
                    for r_ in range(2):
                        kb.dma('sp', Kt[r_:128:2, kb_, 1, :], khatD[o_i, cb_i][1 - r_:128:2, :], writes=('Kt%d' % kb_,))

                def run_B(cb, filler):
                    par = cb % 2
                    zin, ztok = vx[:, par, 0, :], 'vx%d_0' % par
                    for o in range(2):
                        ci = 2 * cb + o
                        kbuf = ci % 2
                        if ci + 1 < 16:
                            load_kt(ci + 1)
                        kb.op('dve', lambda v: v.transpose(out=zT[:].bitcast(U32).rearrange("p c m -> p m c"),
                                                           in_=zin.bitcast(U32).rearrange("p (n1 m) -> p m n1", m=32)),
                              reads=(ztok,), writes=('zT',))
                        filler()

                        def spec_evac(c2, h, psv, ptoks):
                            ksl = slice(16 * h, 16 * h + 16)
                            kb.op('dve', lambda v: v.tensor_tensor(out=P1[:, c2, :].rearrange("p (m k) -> p m k", k=32)[:, :, ksl], in0=psv,
                                                                   in1=Kt[:, kbuf, 0, c2 * 2048:(c2 + 1) * 2048].rearrange("p (m k) -> p m k", k=32)[:, :, ksl], op=ALU.mult),
                                  reads=ptoks + ('Kt%d' % kbuf,), writes=('P1',))
                            kb.op('dve', lambda v: v.tensor_tensor(out=P2[:, c2, :].rearrange("p (m k) -> p m k", k=32)[:, :, ksl], in0=psv,
                                                                   in1=Kt[:, kbuf, 1, c2 * 2048:(c2 + 1) * 2048].rearrange("p (m k) -> p m k", k=32)[:, :, ksl], op=ALU.mult),
                                  reads=ptoks + ('Kt%d' % kbuf,), writes=('P2',))
                        fft_fwd(C, [(zT[:], 'zT', 0)], B, spec_evac, filler)
                        fft_inv(C, P1, P2, Dt, conv[:], 'conv', filler)
                        kb.op('dve', lambda v: v.scalar_tensor_tensor(out=conv[:], in0=zin, scalar=dd[:, o, cb:cb + 1], in1=conv[:], op0=ALU.mult, op1=ALU.add),
                              reads=(ztok, 'dd', 'conv'), writes=('conv',))
                        if o == 0:
                            kb.op('dve', lambda v: v.tensor_tensor(out=z1[:], in0=conv[:], in1=vx[:, par, 1, :], op=ALU.mult),
                                  reads=('conv', 'vx%d_1' % par), writes=('z1',))
                            zin, ztok = z1[:], 'z1'
                        else:
                            yo = zT[:].rearrange("p a b -> p (a b)")
                            kb.op('dve', lambda v: v.tensor_tensor(out=yo, in0=conv[:], in1=vx[:, par, 2, :], op=ALU.mult),
                                  reads=('conv', 'vx%d_2' % par), writes=('zT',))
                            kb.dma('sp', yhyd[cb], yo, reads=('zT',), writes=('yhyd',))

                load_kt(0)
                cur = make_A(0)
                for pc in cur:
                    pc()
                for cb in range(8):
                    nxt = make_A(cb + 1) if cb + 1 < 8 else []

                    def filler():
                        if nxt:
                            nxt.pop(0)()
                    run_B(cb, filler)
                    while nxt:
                        nxt.pop(0)()
                kb.barrier(barscr[:, 0:1])

        def phase_merge(l, hsrc):
            with ExitStack() as es:
                E = es.enter_context
                wh = E(SBT("m_wh", [128, 8, 1024], BF16))
                wo = E(SBT("m_wo", [128, 8, 1024], BF16))
                st = E(SBT("m_st", [128, 2, 1024], F32))
                mm = E(SBT("m_mm", [128, 16, 512], BF16))
                ys = E(SBT("m_ys", [128, 8, 512], BF16))
                yh = E(SBT("m_yh", [128, 8, 512], BF16))
                mg = E(SBT("m_mg", [128, 8, 512], BF16))
                t1 = E(SBT("m_t1", [128, 2, 512], F32))
                ht = E(SBT("m_ht", [128, 8, 512], F32))
                i = 0
                for (wt, src, ncol, nm) in ((wh, w_bhy[l], 1024, 'wh'), (wo, w_out[l], 1024, 'wo')):
                    for k in range(8):
                        s = i % 2; i += 1
                        kb.dma('sp' if s == 0 else 'sp', st[:, s, 0:ncol], src[k * 128:(k + 1) * 128, :], writes=('st%d' % s,))
                        kb.op('act', lambda a: a.activation(out=wt[:, k, :], in_=st[:, s, 0:ncol], func=AF.Identity), reads=('st%d' % s,), writes=(nm,))
                for tt in range(4):
                    ts = slice(tt * 512, (tt + 1) * 512)
                    kb.dma('sp', mm[:], mmD.rearrange("c p t -> p c t")[:, :, ts], writes=('mm',))
                    kb.dma('sp', yh[:], yhyd.rearrange("c p t -> p c t")[:, :, ts], reads=('yhyd',), writes=('yh',))
                    kb.dma('sp', ys[:], ys5d.rearrange("c p t -> p c t")[:, :, ts], reads=('ys5d',), writes=('ys',))
                    kb.dma('sp', ht[:], hsrc.rearrange("c p t -> p c t")[:, :, ts], reads=('hsrc',), writes=('ht',))
                    for dc in range(8):
                        b = kb.ps()
                        for k in range(8):
                            kb.op('pe', lambda p: p.matmul(PS(b), wh[:, k, dc * 128:(dc + 1) * 128], yh[:, k, :], start=(k == 0), stop=(k == 7)),
                                  reads=('wh', 'yh'), writes=(pst(b),), inc=(k == 7))
                        s = dc % 2
                        kb.op('dve', lambda v: v.tensor_tensor(out=t1[:, s, :], in0=PS(b), in1=mm[:, 8 + dc, :], op=ALU.mult),
                              reads=(pst(b), 'mm'), writes=('t1_%d' % s,))
                        kb.op('pool', lambda g: g.tensor_tensor(out=mg[:, dc, :], in0=mm[:, dc, :], in1=ys[:, dc, :], op=ALU.mult),
                              reads=('mm', 'ys'), writes=('mg%d' % dc,))
                        kb.op('dve', lambda v: v.tensor_tensor(out=mg[:, dc, :], in0=mg[:, dc, :], in1=t1[:, s, :], op=ALU.add),
                              reads=('mg%d' % dc, 't1_%d' % s), writes=('mg%d' % dc, 'mg'))
                    for dc in range(8):
                        b = kb.ps()
                        for k in range(8):
                            kb.op('pe', lambda p: p.matmul(PS(b), wo[:, k, dc * 128:(dc + 1) * 128], mg[:, k, :], start=(k == 0), stop=(k == 7)),
                                  reads=('wo', 'mg'), writes=(pst(b),), inc=(k == 7))
                        kb.op('dve', lambda v: v.tensor_tensor(out=ht[:, dc, :], in0=PS(b), in1=ht[:, dc, :], op=ALU.add),
                              reads=(pst(b), 'ht'), writes=('ht',))
                        kb.dma('sp', hbuf[dc][:, ts], ht[:, dc, :], reads=('ht',), writes=('hbuf',))
                kb.barrier(barscr[:, 0:1])

        kb.op('pool', lambda g: g.memset(barscr[:, 1:2], 1e-6), writes=('epsq',))
        kb.barrier(barscr[:, 0:1])
        order = ['prep_s5', 'prep_hy', 'norm', 's5', 'hy', 'merge']
        nph = len(order) if stop_after is None else order.index(stop_after) + 1
        for l in range(layers):
            hsrc = xT if l == 0 else hbuf
            if nph >= 1:
                prep_s5(l)
            if nph >= 2:
                prep_hy(l)
            with ExitStack() as esl:
                G['xn'] = esl.enter_context(SBT("xn%d" % l, [128, 8, L], BF16))
                if nph >= 3:
                    phase_norm(hsrc, l, G['xn'], None)
                    if 'xnD' in dump:
                        xnD = dscr("xnD", [8, 128, L], BF16)
                        for c in range(8):
                            kb.dma('sp', xnD[c], G['xn'][:, c, :], reads=('xn%d' % c,), writes=('xnD',))
                if nph >= 4:
                    phase_s5(l)
            if nph >= 5:
                phase_hy(l)
            if nph >= 6:
                phase_merge(l, hsrc)
        if stop_after is None:
            phase_norm(hbuf, DEPTH, None, outT)
        for i in range(NDS):
            if kb.dcnt[i]:
                kb._wait('sp', ('d', i), kb.dcnt[i])
    return nc, kb


_CACHE = {}


def _host_consts():
    if 'c' not in _CACHE:
        fsmall, fT, fH = fft_consts()
        ft, tt = hy_feats()
        import ml_dtypes
        bfc = lambda a: np.ascontiguousarray(a.astype(ml_dtypes.bfloat16))
        _CACHE['c'] = dict(fsmall=bfc(fsmall), fT=bfc(fT), fH=bfc(fH), feats=ft, ttab=tt, kvt=s5_kv(), masks=s5_masks(),
                           ident=np.eye(128, dtype=np.float32))
    return _CACHE['c']


def _layout_shared(inp):
    f32 = np.float32
    g = lambda k: np.asarray(inp[k], dtype=f32)
    m = dict(_host_consts())
    nw = np.concatenate([g("norm_w"), g("final_norm_w")[None]], 0)
    m["normw"] = np.ascontiguousarray(nw.reshape(3, 8, 128).transpose(0, 2, 1))
    m["w_in"] = g("w_in")
    m["w_glu"] = g("s5_w_glu")
    m["b_glu"] = np.ascontiguousarray(g("s5_b_glu").reshape(DEPTH, 4, 128).transpose(0, 2, 1))
    m["s5d"] = np.ascontiguousarray(g("s5_d").reshape(DEPTH, 4, 128).transpose(0, 2, 1))
    m["w_bs5"] = g("w_branch_s5")
    m["w_bhy"] = g("w_branch_hy")
    m["w_out"] = g("w_out")
    cwv = g("hy_conv_w").reshape(DEPTH, 3, 24, 128)
    m["convw"] = np.ascontiguousarray(cwv.transpose(0, 3, 1, 2))
    m["convb"] = np.ascontiguousarray(g("hy_conv_b").reshape(DEPTH, 24, 128).transpose(0, 2, 1))
    m["hyd"] = np.ascontiguousarray(g("hy_d").reshape(DEPTH, 2, 8, 128).transpose(0, 3, 1, 2))
    def pg(a):
        v = a.reshape(DEPTH, 2, 16, 2, 64)
        return v.transpose(0, 3, 4, 1, 2).reshape(DEPTH, 128, 32)
    ls = np.broadcast_to(g("s5_log_step")[..., None], (DEPTH, 2, 32, 64))
    m["s5lam"] = np.ascontiguousarray(np.stack([pg(g("s5_lam_re")), pg(g("s5_lam_im")), pg(ls)], 2))
    def pgB(a):
        v = a.reshape(DEPTH, 2, 16, 2, 64, 16)
        return v.transpose(0, 3, 4, 1, 2, 5).reshape(DEPTH, 128, 32, 16)
    m["s5B"] = np.ascontiguousarray(np.stack([pgB(g("s5_b_re")), pgB(g("s5_b_im"))], 2))
    ct = lambda a: a.transpose(0, 1, 2, 4, 3)
    m["s5C"] = np.ascontiguousarray(np.stack([pgB(ct(g("s5_c_re"))), pgB(ct(g("s5_c_im")))], 2))
    m["hw1"] = g("hy_w1"); m["hw2"] = g("hy_w2"); m["hw3"] = g("hy_w3")
    m["hvec"] = np.ascontiguousarray(np.stack([g("hy_b1"), g("hy_b2"), g("hy_freq")], -1))
    m["hdec"] = np.ascontiguousarray(g("hy_decay").reshape(DEPTH, 32, 128).transpose(0, 2, 1))
    return m


def kernel(**inputs):
    x = np.asarray(inputs["x"], dtype=np.float32)
    shared = _layout_shared(inputs)
    if 'nc' not in _CACHE:
        _CACHE['nc'] = build()[0]
    nc = _CACHE['nc']
    in_maps = []
    for b in range(NCORES):
        m = dict(shared)
        m["xT"] = np.ascontiguousarray(x[b].T.reshape(8, 128, L))
        in_maps.append(m)
    res = run_bass_kernel_spmd(nc, in_maps, core_ids=list(range(NCORES)))
    out = np.empty((NCORES, L, D), dtype=np.float32)
    for b in range(NCORES):
        out[b] = np.asarray(res.results[b]["outT"]).reshape(D, L).T
    return out
```

```python
import math
from contextlib import ExitStack
import numpy as np
import concourse.bass as bass
import concourse.mybir as mybir
from concourse.bass_utils import run_bass_kernel_spmd

F32 = mybir.dt.float32
BF16 = mybir.dt.bfloat16
U32 = mybir.dt.uint32
ALU = mybir.AluOpType
AF = mybir.ActivationFunctionType

D = 1024; L = 2048; DEPTH = 2; NCORES = 8
INC = 7168
NDS = 48
MAGIC = 12582912.0
TWO_PI = 2.0 * math.pi


class KB:
    def __init__(self, nc, es):
        self.nc = nc
        self.engs = {'pe': nc.tensor, 'act': nc.scalar, 'dve': nc.vector, 'pool': nc.gpsimd, 'sp': nc.sync}
        self.csem = {e: es.enter_context(nc.semaphore('c_' + e)) for e in ('pe', 'act', 'dve', 'pool')}
        self.cnt = {e: 0 for e in self.csem}
        self.dsem = [es.enter_context(nc.semaphore('d%d' % i)) for i in range(NDS)]
        self.dcnt = [0] * NDS
        self.dnext = 0
        self.waited = {e: {} for e in self.engs}
        self.lastw = {}
        self.readers = {}
        self.psn = 0
        self.fp = 0
        self.nins = 0

    def _sem(self, sid):
        return self.csem[sid[1]] if sid[0] == 'c' else self.dsem[sid[1]]

    def _wait(self, e, sid, val):
        if e == 'pe' and sid == ('c', 'pe'):
            return
        w = self.waited[e]
        if w.get(sid, 0) >= val:
            return
        self.engs[e].wait_ge(self._sem(sid), val)
        w[sid] = val

    def _deps(self, e, reads, writes):
        for t in reads:
            if t in self.lastw:
                self._wait(e, *self.lastw[t])
        for t in writes:
            if t in self.lastw:
                self._wait(e, *self.lastw[t])
            for sid, val in self.readers.get(t, {}).items():
                self._wait(e, sid, val)

    def _record(self, sid, val, reads, writes):
        for t in writes:
            self.lastw[t] = (sid, val)
            self.readers[t] = {}
        for t in reads:
            r = self.readers.setdefault(t, {})
            if r.get(sid, 0) < val:
                r[sid] = val

    def op(self, e, fn, reads=(), writes=(), inc=True):
        self._deps(e, reads, writes)
        ins = fn(self.engs[e])
        self.nins += 1
        if e == 'pe' and not inc:
            val = self.cnt['pe'] + 1
        else:
            self.cnt[e] += 1
            val = self.cnt[e]
            ins.then_inc(self.csem[e], 1)
        self._record(('c', e), val, reads, writes)

    def dma(self, q, out, in_, reads=(), writes=()):
        i = self.dnext
        self.dnext = (self.dnext + 1) % NDS
        sid = ('d', i)
        if self.dcnt[i] > 0:
            self._wait(q, sid, self.dcnt[i])
        self._deps(q, reads, writes)
        self.engs[q].dma_start(out=out, in_=in_).then_inc(self.dsem[i], 16)
        self.nins += 1
        self.dcnt[i] += 16
        self._record(sid, self.dcnt[i], reads, writes)

    def barrier(self, scratch):
        for i in range(NDS):
            if self.dcnt[i]:
                self._wait('pool', ('d', i), self.dcnt[i])
        for e in ('pe', 'act', 'dve'):
            if self.cnt[e]:
                self._wait('pool', ('c', e), self.cnt[e])
        self.op('pool', lambda g: g.memset(scratch, 0.0), writes=('__bar',))
        val = self.cnt['pool']
        for e in ('pe', 'act', 'dve', 'sp'):
            self._wait(e, ('c', 'pool'), val)
        self.lastw.clear()
        self.readers.clear()

    def ps(self):
        b = self.psn % 8
        self.psn += 1
        return b

    def ps_hi(self):
        b = 4 + self.psn % 4
        self.psn += 1
        return b


def fft_consts():
    N = 4096
    n1 = np.arange(32); k1 = np.arange(32); n2 = np.arange(64); k2 = np.arange(64)
    ang = -2 * np.pi * np.outer(n1, k1 + 0.5) / 64.0
    g = np.stack([np.cos(ang), np.sin(ang)], -1)
    angh = -2 * np.pi * np.outer(n1 + 32, k1 + 0.5) / 64.0
    gh = -np.stack([np.cos(angh), np.sin(angh)], -1)
    G1 = np.zeros((4, 32, 4, 32, 2)); G1hi = np.zeros((4, 32, 4, 32, 2))
    for cq in range(4):
        G1[cq, :, cq] = g; G1hi[cq, :, cq] = gh
    a = -2 * np.pi * (n2[:, None, None] * (k1[None, :, None] + 0.5) / 4096.0 + n2[:, None, None] * k2[None, None, :] / 64.0)
    twr, twi = np.cos(a), np.sin(a)
    T = np.zeros((64, 32, 2, 64, 2))
    T[:, :, 0, :, 0] = twr; T[:, :, 0, :, 1] = twi
    T[:, :, 1, :, 0] = -twi; T[:, :, 1, :, 1] = twr
    T = np.concatenate([T, T], 0).reshape(128, 32 * 2 * 128)
    e = 2 * np.pi * np.outer(k2, n2) / 64.0
    er, ei = np.cos(e), np.sin(e)
    Ga = np.zeros((64, 2, 64, 2)); Gb = np.zeros((64, 2, 64, 2))
    for rp, s in ((0, 1.0), (1, -1.0)):
        Ga[:, rp, :, 0] = s * er; Ga[:, rp, :, 1] = s * ei
        Gb[:, rp, :, 0] = -ei; Gb[:, rp, :, 1] = er
    phi = 2 * np.pi * (k1[:, None, None] + 0.5) * (64 * n1[None, None, :] + n2[None, :, None]) / 4096.0
    H = np.zeros((4, 32, 64, 2, 4, 32))
    for cq in range(4):
        H[cq, :, :, 0, cq, :] = (2.0 / N) * np.cos(phi)
        H[cq, :, :, 1, cq, :] = -(2.0 / N) * np.sin(phi)
    Sw = np.zeros((64, 2, 64, 2))
    for k in range(64):
        Sw[k, 0, k, 1] = 1; Sw[k, 1, k, 0] = 1
    small = np.concatenate([G1.reshape(128, 256), G1hi.reshape(128, 256), Ga.reshape(128, 128),
                            Gb.reshape(128, 128), Sw.reshape(128, 128)], 1)
    return (small.astype(np.float32), T.astype(np.float32), H.reshape(128, 64 * 2 * 128).astype(np.float32))


def hy_feats():
    f32 = np.float32
    t = np.linspace(0.0, 1.0, L, dtype=f32)
    bands = np.linspace(1e-4, 15, 16, dtype=f32)
    def feats(pos, tt):
        angv = bands[None, :] * pos[:, None].astype(f32) * f32(2.0 * math.pi / L)
        return np.concatenate([tt[:, None], np.cos(angv), -np.sin(angv)], -1).astype(f32)
    posF = np.arange(L)
    posR = (L - np.arange(L)) % L
    fF = feats(posF, t); fR = feats(posR, t[posR])
    ft = np.stack([fF.T, fR.T], 0).astype(f32)
    tt = np.stack([np.broadcast_to(t, (128, L)), np.broadcast_to(t[posR], (128, L))], 0).astype(f32)
    return ft, tt


def s5_kv():
    j = np.arange(8)
    dbl = 8.0 * 2.0 ** np.arange(8)
    f = np.concatenate([7 - j, j + 1, j - 7, dbl])
    b = np.concatenate([j, 8 - j, -j, dbl])
    kv = np.stack([f, b], 0).astype(np.float32)
    return np.broadcast_to(kv[None], (128, 2, 32)).copy()


def s5_masks():
    jj = np.repeat(np.arange(8), 16)
    mf = (jj[None, :] >= jj[:, None]).astype(np.float32)
    mb = (jj[:, None] >= jj[None, :]).astype(np.float32)
    return np.concatenate([mf, mb], 1)


def build(dump=(), layers=DEPTH, stop_after=None):
    nc = bass.Bass("TRN2", target_bir_lowering=False)
    dt_in = {}

    def din(name, shape, dt=F32):
        dt_in[name] = nc.dram_tensor(name, list(shape), dt, kind="ExternalInput").ap()
        return dt_in[name]

    def dscr(name, shape, dt=F32):
        kind = "ExternalOutput" if name in dump else "Internal"
        return nc.dram_tensor(name, list(shape), dt, kind=kind).ap()

    xT = din("xT", [8, 128, L])
    normw = din("normw", [DEPTH + 1, 128, 8])
    w_in = din("w_in", [DEPTH, D, INC])
    w_glu = din("w_glu", [DEPTH, 512, 512])
    b_glu = din("b_glu", [DEPTH, 128, 4])
    s5d = din("s5d", [DEPTH, 128, 4])
    w_bs5 = din("w_bs5", [DEPTH, 512, D])
    w_bhy = din("w_bhy", [DEPTH, D, D])
    w_out = din("w_out", [DEPTH, D, D])
    convw = din("convw", [DEPTH, 128, 3, 24])
    convb = din("convb", [DEPTH, 128, 24])
    hyd = din("hyd", [DEPTH, 128, 2, 8])
    s5lam = din("s5lam", [DEPTH, 128, 3, 32])
    s5B = din("s5B", [DEPTH, 128, 2, 32, 16])
    s5C = din("s5C", [DEPTH, 128, 2, 32, 16])
    kvt = din("kvt", [128, 2, 32])
    masks = din("masks", [128, 256])
    ident = din("ident", [128, 128])
    hw1 = din("hw1", [DEPTH, 33, 64])
    hw2 = din("hw2", [DEPTH, 64, 64])
    hw3 = din("hw3", [DEPTH, 64, 4096])
    hvec = din("hvec", [DEPTH, 64, 3])
    hdec = din("hdec", [DEPTH, 128, 32])
    feats = din("feats", [2, 33, L])
    ttab = din("ttab", [2, 128, L])
    fsmall = din("fsmall", [128, 896], BF16)
    fT = din("fT", [128, 8192], BF16)
    fH = din("fH", [128, 16384], BF16)

    outT = nc.dram_tensor("outT", [8, 128, L], F32, kind="ExternalOutput").ap()

    hbuf = dscr("hbuf", [8, 128, L])
    ud = dscr("ud", [4, 128, L], BF16)
    yd = dscr("yd", [4, 128, L], BF16)
    ys5d = dscr("ys5d", [8, 128, L], BF16)
    yhyd = dscr("yhyd", [8, 128, L], BF16)
    binD = dscr("binD", [128, 32 * 2 * 2 * 64], BF16)
    coutD = dscr("coutD", [128, 2 * 16 * 2 * 128], BF16)
    mgD = dscr("mgD", [128, 32 * 128], BF16)
    coefD = dscr("coefD", [128, 3 * 2 * 16 * 8])
    khatD = dscr("khatD", [2, 8, 128, 4096], BF16)
    rsD = dscr("rsD", [128, 16])
    pD = dscr("pD", [4, 8, 128, L], BF16)
    mmD = dscr("mmD", [16, 128, L], BF16)

    _uid = [0]

    def SBT(name, shape, dt):
        _uid[0] += 1
        return nc.sbuf_tensor("%s_%d" % (name, _uid[0]), shape, dt)

    es0 = ExitStack()
    with es0:
        kb = KB(nc, es0)
        E0 = es0.enter_context
        psum = E0(nc.psum_tensor("psum", [128, 8, 512], F32))
        barscr = E0(SBT("barscr", [128, 8], F32))
        G = {}

        def PS(b, n=512, p0=0, p1=128, off=0):
            return psum[p0:p1, b, off:off + n]

        def PS4(g):
            return psum[:, 4 * g:4 * g + 4, :].rearrange("p b n -> p (b n)")

        def pst(b):
            return 'ps%d' % b

        def pst4(g):
            return tuple('ps%d' % (4 * g + i) for i in range(4))

        def range_reduce(eng, x_ap, tmp_ap, rd, wr_tmp):
            kb.op(eng, lambda v: v.tensor_scalar(out=tmp_ap, in0=x_ap, scalar1=float(1.0 / TWO_PI), scalar2=MAGIC,
                                                 op0=ALU.mult, op1=ALU.add), reads=rd, writes=wr_tmp)
            kb.op(eng, lambda v: v.tensor_scalar(out=tmp_ap, in0=tmp_ap, scalar1=-MAGIC, scalar2=None, op0=ALU.add),
                  reads=wr_tmp, writes=wr_tmp)
            kb.op(eng, lambda v: v.scalar_tensor_tensor(out=x_ap, in0=tmp_ap, scalar=float(-TWO_PI), in1=x_ap,
                                                        op0=ALU.mult, op1=ALU.add), reads=wr_tmp + rd, writes=rd)
            kb.op(eng, lambda v: v.tensor_scalar(out=x_ap, in0=x_ap, scalar1=3.14159, scalar2=-3.14159,
                                                 op0=ALU.min, op1=ALU.max), reads=rd, writes=rd)

        def phase_norm(src, wrow, dst_xn, dst_out):
            with ExitStack() as es:
                E = es.enter_context
                h = E(SBT("n_h", [128, 8, L], F32))
                sq = E(SBT("n_sq", [128, 2, 8, 512], BF16))
                ones = E(SBT("n_ones", [128, 128], BF16))
                nw = E(SBT("n_w", [128, 8], F32))
                rt = E(SBT("n_rt", [128, 2, 512], F32))
                rstd = E(SBT("n_rstd", [128, L], F32))
                epsb = E(SBT("n_eps", [128, 1], F32))
                ob = E(SBT("n_ob", [128, 2, L], F32)) if dst_out is not None else None
                kb.op('pool', lambda g: g.memset(ones[:], 1.0), writes=('ones',))
                kb.op('pool', lambda g: g.memset(epsb[:], 1e-6), writes=('epsb',))
                kb.dma('sp', nw[:], normw[wrow], writes=('nw',))
                for c in range(8):
                    kb.dma('sp' if c % 2 == 0 else 'sp', h[:, c, :], src[c], writes=('h%d' % c,))
                for tt in range(4):
                    ts = slice(tt * 512, (tt + 1) * 512)
                    s = tt % 2
                    kb.op('act', lambda a: a.activation(out=sq[:, s], in_=h[:, :, ts], func=AF.Square),
                          reads=tuple('h%d' % c for c in range(8)), writes=('sq%d' % s,))
                    b = kb.ps()
                    for c in range(8):
                        kb.op('pe', lambda p: p.matmul(PS(b), ones[:], sq[:, s, c, :], start=(c == 0), stop=(c == 7)),
                              reads=('ones', 'sq%d' % s), writes=(pst(b),), inc=(c == 7))
                    kb.op('act', lambda a: a.activation(out=rt[:, s, :], in_=PS(b), func=AF.Sqrt, bias=epsb[:, 0:1],
                                                        scale=float(1.0 / D)),
                          reads=(pst(b), 'epsb'), writes=('rt%d' % s,))
                    kb.op('dve', lambda v: v.reciprocal(out=rstd[:, ts], in_=rt[:, s, :]), reads=('rt%d' % s,),
                          writes=('rstd%d' % tt,))
                for c in range(8):
                    if dst_out is None:
                        kb.op('dve', lambda v: v.scalar_tensor_tensor(out=dst_xn[:, c, :], in0=h[:, c, :], scalar=nw[:, c:c + 1],
                                                                      in1=rstd[:], op0=ALU.mult, op1=ALU.mult),
                              reads=('h%d' % c, 'nw') + tuple('rstd%d' % t for t in range(4)), writes=('xn%d' % c,))
                    else:
                        s = c % 2
                        kb.op('dve', lambda v: v.scalar_tensor_tensor(out=ob[:, s, :], in0=h[:, c, :], scalar=nw[:, c:c + 1],
                                                                      in1=rstd[:], op0=ALU.mult, op1=ALU.mult),
                              reads=('h%d' % c, 'nw') + tuple('rstd%d' % t for t in range(4)), writes=('ob%d' % s,))
                        kb.dma('sp', dst_out[c], ob[:, s, :], reads=('ob%d' % s,), writes=('out%d' % c,))
                kb.barrier(barscr[:, 0:1])

        def load_w(es, name, src_rows_ap, kc, ncols, q='sp'):
            E = es.enter_context
            st = E(SBT(name + "_st", [128, kc, ncols], F32))
            wb = E(SBT(name + "_bf", [128, kc, ncols], BF16))
            for k in range(kc):
                kb.dma(q if k % 2 == 0 else 'sp', st[:, k, :], src_rows_ap[k * 128:(k + 1) * 128, :], writes=(name + '_st%d' % k,))
                kb.op('act', lambda a: a.activation(out=wb[:, k, :], in_=st[:, k, :], func=AF.Identity), reads=(name + '_st%d' % k,),
                      writes=(name + '_bf',))
            return wb

        def prep_s5(l):
            with ExitStack() as es:
                E = es.enter_context
                lam = E(SBT("p_lam", [128, 3, 32], F32))
                Bt = E(SBT("p_B", [128, 2, 32, 16], F32))
                Ct = E(SBT("p_C", [128, 2, 32, 16], F32))
                kvs = E(SBT("p_kv", [128, 2, 32], F32))
                msk = E(SBT("p_msk", [128, 256], F32))
                idt = E(SBT("p_id", [128, 128], F32))
                a_re = E(SBT("p_are", [128, 32], F32))
                a_im = E(SBT("p_aim", [128, 32], F32))
                dtt = E(SBT("p_dt", [128, 32], F32))
                mag = E(SBT("p_mag", [128, 32, 32], F32))
                sn = E(SBT("p_sn", [128, 32, 32], F32))
                cs = E(SBT("p_cs", [128, 32, 32], F32))
                tmp = E(SBT("p_tmp", [128, 32, 32], F32))
                Er = E(SBT("p_Er", [128, 32, 32], F32))
                Ei = E(SBT("p_Ei", [128, 32, 32], F32))
                k4 = E(SBT("p_k4", [128, 8, 32], F32))
                Bb = E(SBT("p_Bb", [128, 2, 32, 16], F32))
                t16 = E(SBT("p_t16", [128, 2, 32, 16], F32))
                RB = E(SBT("p_RB", [128, 2, 32, 128], F32))
                CO = E(SBT("p_CO", [128, 2, 32, 128], F32))
                tb = E(SBT("p_tb", [128, 32, 128], F32))
                binS = E(SBT("p_bin", [128, 32, 2, 2, 64], BF16))
                coutS = E(SBT("p_cout", [128, 32, 2, 128], BF16))
                mgS = E(SBT("p_mg", [128, 32, 128], BF16))
                coefS = E(SBT("p_coef", [128, 3, 32, 8], F32))
                mt = E(SBT("p_mt", [128, 2, 256], F32))
                kb.dma('sp', lam[:], s5lam[l], writes=('lam',))
                kb.dma('sp', Bt[:], s5B[l], writes=('Bt',))
                kb.dma('sp', Ct[:], s5C[l], writes=('Ct',))
                kb.dma('sp', kvs[:], kvt, writes=('kvs',))
                kb.dma('sp', msk[:], masks, writes=('msk',))
                kb.dma('sp', idt[:], ident, writes=('idt',))
                V = lambda fn, r, w: kb.op('dve', fn, reads=r, writes=w)
                A = lambda fn, r, w: kb.op('act', fn, reads=r, writes=w)
                A(lambda a: a.activation(out=dtt[:], in_=lam[:, 2, :], func=AF.Exp), ('lam',), ('dtt',))
                V(lambda v: v.tensor_tensor(out=a_re[:], in0=lam[:, 0, :], in1=dtt[:], op=ALU.mult), ('lam', 'dtt'), ('a_re',))
                V(lambda v: v.tensor_tensor(out=a_im[:], in0=lam[:, 1, :], in1=dtt[:], op=ALU.mult), ('lam', 'dtt'), ('a_im',))
                kvb = kvs[:].rearrange("p d (o k) -> p d o k", o=1).to_broadcast([128, 2, 16, 32])
                are_b = a_re[:].rearrange("p (d g o) -> p d g o", d=2, o=1).to_broadcast([128, 2, 16, 32])
                aim_b = a_im[:].rearrange("p (d g o) -> p d g o", d=2, o=1).to_broadcast([128, 2, 16, 32])
                v4 = lambda t: t[:].rearrange("p (d g) k -> p d g k", d=2)
                V(lambda v: v.tensor_tensor(out=v4(tmp), in0=are_b, in1=kvb, op=ALU.mult), ('a_re', 'kvs'), ('tmp',))
                A(lambda a: a.activation(out=mag[:], in_=tmp[:], func=AF.Exp), ('tmp',), ('mag',))
                V(lambda v: v.tensor_tensor(out=v4(sn), in0=aim_b, in1=kvb, op=ALU.mult), ('a_im', 'kvs'), ('sn',))
                V(lambda v: v.tensor_scalar(out=cs[:], in0=sn[:], scalar1=float(math.pi / 2), scalar2=None, op0=ALU.add), ('sn',), ('cs',))
                range_reduce('dve', sn[:], tmp[:], ('sn',), ('tmp',))
                A(lambda a: a.activation(out=sn[:], in_=sn[:], func=AF.Sin), ('sn',), ('sn',))
                range_reduce('dve', cs[:], tmp[:], ('cs',), ('tmp',))
                A(lambda a: a.activation(out=cs[:], in_=cs[:], func=AF.Sin), ('cs',), ('cs',))
                V(lambda v: v.tensor_tensor(out=Er[:], in0=mag[:], in1=cs[:], op=ALU.mult), ('mag', 'cs'), ('Er',))
                V(lambda v: v.tensor_tensor(out=Ei[:], in0=mag[:], in1=sn[:], op=ALU.mult), ('mag', 'sn'), ('Ei',))
                lr = k4[:, 0, :]; li = k4[:, 1, :]; nr = k4[:, 2, :]; den = k4[:, 3, :]; kr = k4[:, 4, :]; ki = k4[:, 5, :]; t0 = k4[:, 6, :]
                for d in range(2):
                    idx = 8 if d == 0 else 15
                    V(lambda v: v.tensor_copy(out=lr[:, 16 * d:16 * d + 16], in_=Er[:, 16 * d:16 * d + 16, idx]), ('Er',), ('k4',))
                    V(lambda v: v.tensor_copy(out=li[:, 16 * d:16 * d + 16], in_=Ei[:, 16 * d:16 * d + 16, idx]), ('Ei',), ('k4',))
                V(lambda v: v.tensor_scalar(out=nr, in0=lr, scalar1=-1.0, scalar2=None, op0=ALU.add), ('k4',), ('k4',))
                V(lambda v: v.tensor_tensor(out=den, in0=lam[:, 0, :], in1=lam[:, 0, :], op=ALU.mult), ('lam', 'k4'), ('k4',))
                V(lambda v: v.tensor_tensor(out=t0, in0=lam[:, 1, :], in1=lam[:, 1, :], op=ALU.mult), ('lam', 'k4'), ('k4',))
                V(lambda v: v.tensor_tensor(out=den, in0=den, in1=t0, op=ALU.add), ('k4',), ('k4',))
                V(lambda v: v.reciprocal(out=den, in_=den), ('k4',), ('k4',))
                V(lambda v: v.tensor_tensor(out=kr, in0=nr, in1=lam[:, 0, :], op=ALU.mult), ('k4', 'lam'), ('k4',))
                V(lambda v: v.tensor_tensor(out=t0, in0=li, in1=lam[:, 1, :], op=ALU.mult), ('k4', 'lam'), ('k4',))
                V(lambda v: v.tensor_tensor(out=kr, in0=kr, in1=t0, op=ALU.add), ('k4',), ('k4',))
                V(lambda v: v.tensor_tensor(out=kr, in0=kr, in1=den, op=ALU.mult), ('k4',), ('k4',))
                V(lambda v: v.tensor_tensor(out=ki, in0=li, in1=lam[:, 0, :], op=ALU.mult), ('k4', 'lam'), ('k4',))
                V(lambda v: v.tensor_tensor(out=t0, in0=nr, in1=lam[:, 1, :], op=ALU.mult), ('k4', 'lam'), ('k4',))
                V(lambda v: v.tensor_tensor(out=ki, in0=ki, in1=t0, op=ALU.subtract), ('k4',), ('k4',))
                V(lambda v: v.tensor_tensor(out=ki, in0=ki, in1=den, op=ALU.mult), ('k4',), ('k4',))
                krb = kr.rearrange("p (g o) -> p g o", o=1).to_broadcast([128, 32, 16])
                kib = ki.rearrange("p (g o) -> p g o", o=1).to_broadcast([128, 32, 16])
                V(lambda v: v.tensor_tensor(out=Bb[:, 0], in0=Bt[:, 0], in1=krb, op=ALU.mult), ('Bt', 'k4'), ('Bb',))
                V(lambda v: v.tensor_tensor(out=t16[:, 0], in0=Bt[:, 1], in1=kib, op=ALU.mult), ('Bt', 'k4'), ('t16',))
                V(lambda v: v.tensor_tensor(out=Bb[:, 0], in0=Bb[:, 0], in1=t16[:, 0], op=ALU.subtract), ('Bb', 't16'), ('Bb',))
                V(lambda v: v.tensor_tensor(out=Bb[:, 1], in0=Bt[:, 1], in1=krb, op=ALU.mult), ('Bt', 'k4', 'Bb'), ('Bb',))
                V(lambda v: v.tensor_tensor(out=t16[:, 1], in0=Bt[:, 0], in1=kib, op=ALU.mult), ('Bt', 'k4', 't16'), ('t16',))
                V(lambda v: v.tensor_tensor(out=Bb[:, 1], in0=Bb[:, 1], in1=t16[:, 1], op=ALU.add), ('Bb', 't16'), ('Bb',))

                def cprod(dst, k0, X, sign_im, tag):
                    Erb = Er[:, :, k0:k0 + 8].rearrange("p g (k o) -> p g k o", o=1).to_broadcast([128, 32, 8, 16])
                    Eib = Ei[:, :, k0:k0 + 8].rearrange("p g (k o) -> p g k o", o=1).to_broadcast([128, 32, 8, 16])
                    Xr = X[:, 0].rearrange("p g (o h) -> p g o h", o=1).to_broadcast([128, 32, 8, 16])
                    Xi = X[:, 1].rearrange("p g (o h) -> p g o h", o=1).to_broadcast([128, 32, 8, 16])
                    d0 = dst[:, 0].rearrange("p g (k h) -> p g k h", k=8)
                    d1 = dst[:, 1].rearrange("p g (k h) -> p g k h", k=8)
                    tv = tb[:].rearrange("p g (k h) -> p g k h", k=8)
                    rd = ('Er', 'Ei', tag)
                    V(lambda v: v.tensor_tensor(out=d0, in0=Erb, in1=Xr, op=ALU.mult), rd, (tag + 'o',))
                    V(lambda v: v.tensor_tensor(out=tv, in0=Eib, in1=Xi, op=ALU.mult), rd, ('tb',))
                    V(lambda v: v.tensor_tensor(out=d0, in0=d0, in1=tv, op=ALU.subtract), (tag + 'o', 'tb'), (tag + 'o',))
                    V(lambda v: v.tensor_tensor(out=d1, in0=Erb, in1=Xi, op=ALU.mult), rd + (tag + 'o',), (tag + 'o',))
                    V(lambda v: v.tensor_tensor(out=tv, in0=Eib, in1=Xr, op=ALU.mult), rd + ('tb',), ('tb',))
                    V(lambda v: v.tensor_tensor(out=d1, in0=d1, in1=tv, op=ALU.add), (tag + 'o', 'tb'), (tag + 'o',))
                    if sign_im < 0:
                        V(lambda v: v.tensor_scalar(out=dst[:, 1], in0=dst[:, 1], scalar1=-1.0, scalar2=None, op0=ALU.mult),
                          (tag + 'o',), (tag + 'o',))
                cprod(RB, 0, Bb, +1, 'Bb')
                cprod(CO, 8, Ct, -1, 'Ct')
                for ri in range(2):
                    A(lambda a: a.activation(out=coutS[:, :, ri, :], in_=CO[:, ri], func=AF.Identity), ('Cto',), ('coutS',))
                kb.dma('sp', coutD, coutS[:].rearrange("p a b c -> p (a b c)"), reads=('coutS',), writes=('coutD',))
                cprod(CO, 16, Ct, -1, 'Ct')
                RC = CO
                A(lambda a: a.activation(out=coefS[:, 0], in_=Er[:, :, 24:32], func=AF.Identity), ('Er',), ('coefS',))
                A(lambda a: a.activation(out=coefS[:, 1], in_=Ei[:, :, 24:32], func=AF.Identity), ('Ei',), ('coefS',))
                A(lambda a: a.activation(out=coefS[:, 2], in_=Ei[:, :, 24:32], func=AF.Identity, scale=-1.0), ('Ei',), ('coefS',))
                kb.dma('sp', coefD, coefS[:].rearrange("p a b c -> p (a b c)"), reads=('coefS',), writes=('coefD',))
                for dg in range(32):
                    d, gp = dg // 16, dg % 16
                    b = kb.ps()
                    for ri in range(2):
                        kb.op('pe', lambda p: p.transpose(PS(b, 128, off=128 * ri), RB[:, ri, dg, :], idt[:]),
                              reads=('Bbo', 'idt'), writes=(pst(b),), inc=(ri == 1))
                    for ri in range(2):
                        A(lambda a: a.activation(out=binS[:, 2 * gp:2 * gp + 2, d, ri, :],
                                                 in_=PS(b, 128, off=128 * ri).rearrange("p (g q) -> p g q", g=2), func=AF.Identity),
                          (pst(b),), ('binS',))
                kb.dma('sp', binD, binS[:].rearrange("p a b c e -> p (a b c e)"), reads=('binS',), writes=('binD',))
                for g in range(32):
                    gp, gpar = g // 2, g % 2
                    b = kb.ps()
                    p0, p1 = 64 * gpar, 64 * gpar + 64
                    for d in range(2):
                        dg = 16 * d + gp
                        kb.op('pe', lambda p: p.matmul(PS(b, 128, off=128 * d), RB[p0:p1, 0, dg, :], RC[p0:p1, 0, dg, :],
                                                       start=True, stop=False),
                              reads=('Bbo', 'Cto'), writes=(pst(b),), inc=False)
                        kb.op('pe', lambda p: p.matmul(PS(b, 128, off=128 * d), RB[p0:p1, 1, dg, :], RC[p0:p1, 1, dg, :],
                                                       start=False, stop=True),
                              reads=('Bbo', 'Cto'), writes=(pst(b),), inc=(d == 1))
                    s = g % 2
                    V(lambda v: v.tensor_tensor(out=mt[:, s, :], in0=PS(b, 256), in1=msk[:], op=ALU.mult), (pst(b), 'msk'), ('mt%d' % s,))
                    V(lambda v: v.tensor_tensor(out=mgS[:, g, :], in0=mt[:, s, 0:128], in1=mt[:, s, 128:256], op=ALU.add),
                      ('mt%d' % s,), ('mgS',))
                kb.dma('sp', mgD, mgS[:].rearrange("p a b -> p (a b)"), reads=('mgS',), writes=('mgD',))
                kb.barrier(barscr[:, 0:1])

        def phase_s5(l):
            with ExitStack() as es_outer:
                EO = es_outer.enter_context
                gs = EO(SBT("s_gs", [128, 4, L], BF16))
                with ExitStack() as es:
                    E = es.enter_context
                    wb = load_w(es, "s_w", w_in[l][:, 0:1024], 8, 1024)
                    udt = E(SBT("s_ud", [128, 4, 8, 256], BF16))
                    for cc in range(8):
                        for tt in range(4):
                            b = kb.ps()
                            for k in range(8):
                                kb.op('pe', lambda p: p.matmul(PS(b), wb[:, k, cc * 128:(cc + 1) * 128], G['xn'][:, k, tt * 512:(tt + 1) * 512],
                                                               start=(k == 0), stop=(k == 7)),
                                      reads=('s_w_bf', 'xn%d' % k), writes=(pst(b),), inc=(k == 7))
                            if cc < 4:
                                kb.op('act', lambda a: a.activation(out=udt[:, cc, :, tt * 64:(tt + 1) * 64],
                                                                    in_=PS(b).rearrange("p (c j) -> p j c", j=8), func=AF.Identity),
                                      reads=(pst(b),), writes=('udt%d' % cc,))
                            else:
                                kb.op('act', lambda a: a.activation(out=gs[:, cc - 4, :].rearrange("p (j c) -> p j c", j=8)[:, :, tt * 64:(tt + 1) * 64],
                                                                    in_=PS(b).rearrange("p (c j) -> p j c", j=8), func=AF.Silu),
                                      reads=(pst(b),), writes=('gs',))
                        if cc < 4:
                            kb.dma('sp', ud[cc], udt[:, cc].rearrange("p j c -> p (j c)"), reads=('udt%d' % cc,), writes=('ud',))
                    kb.barrier(barscr[:, 0:1])
                with ExitStack() as es:
                    E = es.enter_context
                    U8 = E(SBT("s_U8", [128, 32, 256], BF16))
                    Mg = E(SBT("s_Mg", [128, 32, 128], BF16))
                    Bin = E(SBT("s_Bin", [128, 32, 2, 2, 64], BF16))
                    Cout = E(SBT("s_Cout", [128, 32, 2, 128], BF16))
                    coef = E(SBT("s_coef", [128, 3, 32, 8], F32))
                    Xs = E(SBT("s_Xs", [128, 32, 2, 256], BF16))
                    Y8 = U8
                    NSL = 2
                    XA = E(SBT("s_XA", [128, NSL, 2, 768], F32))
                    XB = E(SBT("s_XB", [128, NSL, 2, 768], F32))
                    T1 = E(SBT("s_T1", [128, NSL, 2, 256], F32))
                    udv = ud.rearrange("cc (g h) (j c) -> h j (cc g) c", h=16, j=8)
                    for j in range(8):
                        kb.dma('sp' if j % 2 == 0 else 'sp', U8[16 * j:16 * j + 16, :, :], udv[:, j], reads=('ud',), writes=('U8',))
                    kb.dma('sp', Mg[:].rearrange("p a b -> p (a b)"), mgD, reads=('mgD',), writes=('Mg',))
                    kb.dma('sp', Bin[:].rearrange("p a b c e -> p (a b c e)"), binD, reads=('binD',), writes=('Bin',))
                    kb.dma('sp', Cout[:].rearrange("p a b c -> p (a b c)"), coutD, reads=('coutD',), writes=('Cout',))
                    kb.dma('sp', coef[:].rearrange("p a b c -> p (a b c)"), coefD, reads=('coefD',), writes=('coef',))
                    kb.op('pool', lambda g: g.memset(XA[:], 0.0), writes=tuple('XA%d' % i for i in range(NSL)))
                    kb.op('pool', lambda g: g.memset(XB[:], 0.0), writes=tuple('XB%d' % i for i in range(NSL)))
                    fwst = E(SBT("s_fwst", [128, 8, 512], F32))
                    fwbf = E(SBT("s_fwbf", [128, 2, 8, 512], BF16))
                    fost = E(SBT("s_fost", [128, 8, 512], BF16))
                    groups = []
                    for si in range(4):
                        for q in range(2):
                            groups.append(([1024, 2048, 3072, 4096][si] + q * 512, [pD[si, 4 * q + j] for j in range(4)],
                                           AF.Silu if si == 3 else AF.Identity))
                    for q in range(4):
                        groups.append((5120 + q * 512, [mmD[4 * q + j] for j in range(4)], AF.Sigmoid))
                    fcnt = [0]

                    def dma_group(gi, ks=range(8)):
                        col = groups[gi][0]
                        for k in ks:
                            kb.dma('sp', fwst[:, k, :], w_in[l][k * 128:(k + 1) * 128, col:col + 512], writes=('fwst',))

                    def cast_group(gi):
                        slot = gi % 2
                        for k in range(8):
                            kb.op('act', lambda a: a.activation(out=fwbf[:, slot, k, :], in_=fwst[:, k, :], func=AF.Identity), reads=('fwst',), writes=('fwbf%d' % slot,))

                    fpieces = []
                    for gi in range(len(groups)):
                        for j in range(4):
                            for t4 in range(4):
                                def fpiece(gi=gi, j=j, t4=t4):
                                    slot = gi % 2
                                    pi_ = j * 4 + t4
                                    if pi_ == 0:
                                        cast_group(gi)
                                    if pi_ < 8 and gi + 1 < len(groups):
                                        dma_group(gi + 1, [pi_])
                                    ts = slice(t4 * 512, (t4 + 1) * 512)
                                    b = kb.ps()
                                    for k in range(8):
                                        kb.op('pe', lambda p: p.matmul(PS(b), fwbf[:, slot, k, j * 128:(j + 1) * 128], G['xn'][:, k, ts], start=(k == 0), stop=(k == 7)),
                                              reads=('fwbf%d' % slot, 'xn%d' % k), writes=(pst(b),), inc=(k == 7))
                                    os_ = fcnt[0] % 8
                                    fcnt[0] += 1
                                    kb.op('act', lambda a: a.activation(out=fost[:, os_, :], in_=PS(b), func=groups[gi][2]), reads=(pst(b),), writes=('fost%d' % os_,))
                                    kb.dma('sp', groups[gi][1][j][:, ts], fost[:, os_, :], reads=('fost%d' % os_,), writes=('fdst',))
                                fpieces.append(fpiece)
                    dma_group(0)

                    def sfill():
                        if fpieces:
                            fpieces.pop(0)()

                    for s0 in range(0, 32, NSL):
                        for i in range(NSL):
                            dg = s0 + i
                            d, gp = dg // 16, dg % 16
                            b = kb.ps()
                            for ri in range(2):
                                for gpar in range(2):
                                    g = 2 * gp + gpar
                                    kb.op('pe', lambda p: p.matmul(PS(b, 256, 64 * gpar, 64 * gpar + 64, off=256 * ri),
                                                                   Bin[:, g, d, ri, :], U8[:, g, :], start=True, stop=True),
                                          reads=('Bin', 'U8'), writes=(pst(b),), inc=(ri == 1 and gpar == 1))
                            kb.op('act', lambda a: a.activation(out=XA[:, i, :, 256:512], in_=PS(b).rearrange("p (r c) -> p r c", r=2),
                                                                func=AF.Identity),
                                  reads=(pst(b),), writes=('XA%d' % i,))
                        for r in range(8):
                            sft = 2 ** r
                            stage = [[], []]
                            for i in range(NSL):
                                dg = s0 + i
                                d = dg // 16
                                src, dst = (XA, XB) if r % 2 == 0 else (XB, XA)
                                sn_, dn_ = ('XA%d' % i, 'XB%d' % i) if r % 2 == 0 else ('XB%d' % i, 'XA%d' % i)
                                lo = 256 - sft if d == 0 else 256 + sft
                                e_ = coef[:, 0, dg, r:r + 1]; f_ = coef[:, 1, dg, r:r + 1]; nf_ = coef[:, 2, dg, r:r + 1]
                                Rs = src[:, i, 0, lo:lo + 256]; Is = src[:, i, 1, lo:lo + 256]
                                R0 = src[:, i, 0, 256:512]; I0 = src[:, i, 1, 256:512]
                                tn = 'T1_%d' % i

                                def st1(i=i, Rs=Rs, Is=Is, R0=R0, I0=I0, e_=e_, f_=f_, sn_=sn_, tn=tn):
                                    kb.op('dve', lambda v: v.scalar_tensor_tensor(out=T1[:, i, 0, :], in0=Rs, scalar=e_, in1=R0, op0=ALU.mult, op1=ALU.add),
                                          reads=(sn_, 'coef'), writes=(tn + 'a',))
                                    kb.op('dve', lambda v: v.scalar_tensor_tensor(out=T1[:, i, 1, :], in0=Rs, scalar=f_, in1=I0, op0=ALU.mult, op1=ALU.add),
                                          reads=(sn_, 'coef'), writes=(tn + 'b',))

                                def st2(i=i, Is=Is, e_=e_, nf_=nf_, sn_=sn_, dn_=dn_, tn=tn, dst=dst):
                                    kb.op('dve', lambda v: v.scalar_tensor_tensor(out=dst[:, i, 0, 256:512], in0=Is, scalar=nf_, in1=T1[:, i, 0, :],
                                                                                  op0=ALU.mult, op1=ALU.add),
                                          reads=(sn_, 'coef', tn + 'a'), writes=(dn_ + 'r',))
                                    kb.op('dve', lambda v: v.scalar_tensor_tensor(out=dst[:, i, 1, 256:512], in0=Is, scalar=e_, in1=T1[:, i, 1, :],
                                                                                  op0=ALU.mult, op1=ALU.add),
                                          reads=(sn_, 'coef', tn + 'b'), writes=(dn_, dn_ + 'r'))
                                stage[0].append(st1)
                                stage[1].append(st2)
                            for st in stage:
                                for f in st:
                                    f()
                                sfill()
                        for i in range(NSL):
                            dg = s0 + i
                            d = dg // 16
                            lo = 255 if d == 0 else 257
                            kb.op('act', lambda a: a.activation(out=Xs[:, dg, :, :], in_=XA[:, i, :, lo:lo + 256], func=AF.Identity),
                                  reads=('XA%d' % i, 'XA%dr' % i), writes=('Xs',))
                    while fpieces:
                        fpieces.pop(0)()
                    for g in range(32):
                        gp, gpar = g // 2, g % 2
                        p0, p1 = 64 * gpar, 64 * gpar + 64
                        b = kb.ps()
                        kb.op('pe', lambda p: p.matmul(PS(b, 256), Mg[:, g, :], U8[:, g, :], start=True, stop=False),
                              reads=('Mg', 'U8'), writes=(pst(b),), inc=False)
                        for d in range(2):
                            for ri in range(2):
                                last = (d == 1 and ri == 1)
                                kb.op('pe', lambda p: p.matmul(PS(b, 256), Cout[p0:p1, 16 * d + gp, ri, :], Xs[p0:p1, 16 * d + gp, ri, :],
                                                               start=False, stop=last),
                                      reads=('Cout', 'Xs'), writes=(pst(b),), inc=last)
                        kb.op('act', lambda a: a.activation(out=Y8[:, g, :], in_=PS(b, 256), func=AF.Identity), reads=(pst(b),), writes=('Y8',))
                    ydv = yd.rearrange("cc (g h) (i c) -> h i (cc g) c", h=16, i=8)
                    for i in range(8):
                        kb.dma('sp' if i % 2 == 0 else 'sp', ydv[:, i], Y8[16 * i:16 * i + 16, :, :], reads=('Y8',), writes=('yd',))
                    kb.barrier(barscr[:, 0:1])
                with ExitStack() as es:
                    E = es.enter_context
                    wg = E(SBT("c_wg", [128, 4, 512], BF16))
                    wbr = E(SBT("c_wbr", [128, 4, 1024], BF16))
                    wst = E(SBT("c_wst", [128, 2, 1024], F32))
                    yt = E(SBT("c_y", [128, L], BF16))
                    ut = E(SBT("c_u", [128, L], BF16))
                    y1 = E(SBT("c_y1", [128, L], F32))
                    t3 = E(SBT("c_t3", [128, L], F32))
                    yg = E(SBT("c_yg", [128, 4, L], BF16))
                    sg = E(SBT("c_sg", [128, 2, 512], F32))
                    y3 = E(SBT("c_y3", [128, 4, L], BF16))
                    ys = E(SBT("c_ys", [128, L], BF16))
                    dv = E(SBT("c_d", [128, 4], F32))
                    bg = E(SBT("c_bg", [128, 4], F32))
                    kb.dma('sp', dv[:], s5d[l], writes=('dv',))
                    kb.dma('sp', bg[:], b_glu[l], writes=('bg',))
                    for k in range(4):
                        s = k % 2
                        kb.dma('sp', wst[:, s, 0:512], w_glu[l][k * 128:(k + 1) * 128, :], writes=('wst%d' % s,))
                        kb.op('act', lambda a: a.activation(out=wg[:, k, :], in_=wst[:, s, 0:512], func=AF.Identity), reads=('wst%d' % s,), writes=('wg',))
                    for k in range(4):
                        s = k % 2
                        kb.dma('sp', wst[:, s, :], w_bs5[l][k * 128:(k + 1) * 128, :], writes=('wst%d' % s,))
                        kb.op('act', lambda a: a.activation(out=wbr[:, k, :], in_=wst[:, s, :], func=AF.Identity), reads=('wst%d' % s,), writes=('wbr',))
                    for cc in range(4):
                        kb.dma('sp', yt[:], yd[cc], reads=('yd',), writes=('yt',))
                        kb.dma('sp', ut[:], ud[cc], reads=('ud',), writes=('ut',))
                        kb.op('dve', lambda v: v.scalar_tensor_tensor(out=y1[:], in0=ut[:], scalar=dv[:, cc:cc + 1], in1=yt[:],
                                                                      op0=ALU.mult, op1=ALU.add),
                              reads=('ut', 'yt', 'dv'), writes=('y1',))
                        kb.op('dve', lambda v: v.tensor_tensor(out=t3[:], in0=y1[:], in1=y1[:], op=ALU.mult), reads=('y1',), writes=('t3',))
                        kb.op('dve', lambda v: v.tensor_scalar(out=t3[:], in0=t3[:], scalar1=0.044715 * 1.5957691216, scalar2=1.5957691216,
                                                               op0=ALU.mult, op1=ALU.add), reads=('t3',), writes=('t3',))
                        kb.op('dve', lambda v: v.tensor_tensor(out=t3[:], in0=t3[:], in1=y1[:], op=ALU.mult), reads=('t3', 'y1'), writes=('t3',))
                        kb.op('act', lambda a: a.activation(out=t3[:], in_=t3[:], func=AF.Sigmoid), reads=('t3',), writes=('t3',))
                        kb.op('dve', lambda v: v.tensor_tensor(out=yg[:, cc, :], in0=t3[:], in1=y1[:], op=ALU.mult), reads=('t3', 'y1'), writes=('yg',))
                    gsp = gs[:].rearrange("p a (j c) -> p a j c", j=8)
                    for cc in range(4):
                        for tt in range(4):
                            b = kb.ps()
                            for k in range(4):
                                kb.op('pe', lambda p: p.matmul(PS(b), wg[:, k, cc * 128:(cc + 1) * 128], yg[:, k, tt * 512:(tt + 1) * 512],
                                                               start=(k == 0), stop=(k == 3)),
                                      reads=('wg', 'yg'), writes=(pst(b),), inc=(k == 3))
                            s = tt % 2
                            kb.op('act', lambda a: a.activation(out=sg[:, s, :], in_=PS(b), func=AF.Sigmoid, bias=bg[:, cc:cc + 1]),
                                  reads=(pst(b), 'bg'), writes=('sg%d' % s,))
                            kb.op('dve', lambda v: v.tensor_tensor(out=sg[:, s, :], in0=sg[:, s, :], in1=yg[:, cc, tt * 512:(tt + 1) * 512], op=ALU.mult),
                                  reads=('sg%d' % s, 'yg'), writes=('sg%d' % s,))
                            kb.op('dve', lambda v: v.tensor_tensor(out=y3[:, cc, tt * 512:(tt + 1) * 512].rearrange("p (j c) -> p j c", j=2),
                                                                   in0=sg[:, s, :].rearrange("p (j c) -> p j c", j=2),
                                                                   in1=gsp[:, cc, 2 * tt:2 * tt + 2, :], op=ALU.mult),
                                  reads=('sg%d' % s, 'gs'), writes=('y3',))
                    for dc in range(8):
                        for tt in range(4):
                            b = kb.ps()
                            for k in range(4):
                                kb.op('pe', lambda p: p.matmul(PS(b), wbr[:, k, dc * 128:(dc + 1) * 128],
                                                               y3[:, k, :].rearrange("p (j c) -> p c j", j=8)[:, tt * 64:(tt + 1) * 64, :],
                                                               start=(k == 0), stop=(k == 3)),
                                      reads=('wbr', 'y3'), writes=(pst(b),), inc=(k == 3))
                            kb.op('act', lambda a: a.activation(out=ys[:, tt * 512:(tt + 1) * 512], in_=PS(b), func=AF.Identity),
                                  reads=(pst(b),), writes=('ys',))
                        kb.dma('sp', ys5d[dc], ys[:], reads=('ys',), writes=('ys5d',))
                    kb.barrier(barscr[:, 0:1])

        def PS2(b0):
            return psum[:, b0:b0 + 2, :].rearrange("p b n -> p (b n)")

        def pst2(b0):
            return ('ps%d' % b0, 'ps%d' % (b0 + 1))

        def next_pair():
            p_ = kb.fp % 2
            kb.fp += 1
            return 2 * p_

        def fft_fwd(C, zin_list, B, spec_evac, filler=lambda: None):
            for q in range(4):
                b0 = next_pair()
                for cpl in range(4):
                    cp = q * 4 + cpl
                    bank = b0 + cpl // 2
                    off = 256 * (cpl % 2)
                    for zi, (zT, ztok, gk) in enumerate(zin_list):
                        kb.op('pe', lambda p: p.matmul(PS(bank, 256, off=off), zT[:, 2 * cp:2 * cp + 2, :].rearrange("p a b -> p (a b)"),
                                                       C['G1hi'] if gk else C['G1'], start=(zi == 0), stop=(zi == len(zin_list) - 1)),
                              reads=(ztok,) + C['toks'], writes=(pst(bank),), inc=(zi == len(zin_list) - 1))
                kb.op('act', lambda a: a.activation(
                    out=B[:].rearrange("p k r cp cq -> p (k r) cp cq")[:, :, q * 4:q * 4 + 4, :],
                    in_=PS2(b0).rearrange("p (cp cq kr) -> p kr cp cq", cp=4, cq=4), func=AF.Identity),
                    reads=pst2(b0), writes=('B',))
                if q % 2 == 1:
                    filler()
            for c2 in range(2):
                for h in range(2):
                    b0 = next_pair()
                    for k1l in range(16):
                        k1 = 16 * h + k1l
                        bank = b0 + k1l // 8
                        off = 64 * (k1l % 8)
                        for ri in range(2):
                            kb.op('pe', lambda p: p.matmul(PS(bank, 64, off=off), C['T'][64 * c2:64 * c2 + 64, k1, ri, :],
                                                           B[64 * c2:64 * c2 + 64, k1, ri].rearrange("p a b -> p (a b)"),
                                                           start=(ri == 0), stop=(ri == 1)),
                                  reads=('B',) + C['toks'], writes=(pst(bank),), inc=(ri == 1))
                    spec_evac(c2, h, PS2(b0).rearrange("p (k m) -> p m k", k=16), pst2(b0))
                filler()

        def fft_inv(C, P1, P2, Dt, conv_out, conv_tok, filler=lambda: None):
            for c2 in range(2):
                for h in range(2):
                    b0 = next_pair()
                    for cpl in range(8):
                        cp = 8 * h + cpl
                        bank = b0 + cpl // 4
                        off = 128 * (cpl % 4)
                        kb.op('pe', lambda p: p.matmul(PS(bank, 128, off=off), P1[:, c2, cp * 128:(cp + 1) * 128], C['Ga'], start=True, stop=False),
                              reads=('P1',) + C['toks'], writes=(pst(bank),), inc=False)
                        kb.op('pe', lambda p: p.matmul(PS(bank, 128, off=off), P2[:, c2, cp * 128:(cp + 1) * 128], C['Gb'], start=False, stop=True),
                              reads=('P2',) + C['toks'], writes=(pst(bank),), inc=True)
                    kb.op('act', lambda a: a.activation(
                        out=Dt[:, c2].rearrange("p n r cp -> p (n r) cp")[:, :, 8 * h:8 * h + 8],
                        in_=PS2(b0).rearrange("p (cp nr) -> p nr cp", cp=8), func=AF.Identity),
                        reads=pst2(b0), writes=('B',))
                filler()
            for h in range(2):
                b0 = next_pair()
                for n2l in range(32):
                    n2 = 32 * h + n2l
                    bank = b0 + n2l // 16
                    off = 32 * (n2l % 16)
                    for r in range(2):
                        kb.op('pe', lambda p: p.matmul(PS(bank, 32, off=off), C['H'][:, n2, r, :],
                                                       Dt[:, :, n2, r, :].rearrange("p c2 cp -> p cp c2"), start=(r == 0), stop=(r == 1)),
                              reads=('B',) + C['toks'], writes=(pst(bank),), inc=(r == 1))
                kb.op('dve', lambda v: v.transpose(
                    out=conv_out.rearrange("p (n1 n2) -> p n2 n1", n2=64)[:, 32 * h:32 * h + 32, :],
                    in_=PS2(b0).rearrange("p (n c) -> p n c", c=32)),
                    reads=pst2(b0), writes=(conv_tok,))
            filler()

        def load_fft_consts(es, with_filter):
            E = es.enter_context
            fs = E(SBT("f_sm", [128, 896], BF16))
            Tb = E(SBT("f_T", [128, 32, 2, 128], BF16))
            kb.dma('sp', fs[:], fsmall, writes=('fc',))
            Tv = Tb[:].rearrange("p a b c -> p (a b c)")
            for i in range(2):
                kb.dma('sp', Tv[:, i * 4096:(i + 1) * 4096], fT[:, i * 4096:(i + 1) * 4096], writes=('fcT%d' % i,))
            C = {'G1': fs[:, 0:256], 'G1hi': fs[:, 256:512], 'Ga': fs[:, 512:640], 'Gb': fs[:, 640:768], 'Sw': fs[:, 768:896], 'T': Tb}
            C['toks'] = ('fc', 'fcT0', 'fcT1')
            if not with_filter:
                Hb = E(SBT("f_H", [128, 64, 2, 128], BF16))
                Hv = Hb[:].rearrange("p a b c -> p (a b c)")
                for i in range(4):
                    kb.dma('sp', Hv[:, i * 4096:(i + 1) * 4096], fH[:, i * 4096:(i + 1) * 4096], writes=('fcH%d' % i,))
                C['H'] = Hb
                C['toks'] = C['toks'] + tuple('fcH%d' % i for i in range(4))
            return C

        def prep_hy(l):
            with ExitStack() as es:
                E = es.enter_context
                C = load_fft_consts(es, True)
                w1 = E(SBT("q_w1", [33, 64], F32))
                w2 = E(SBT("q_w2", [64, 64], F32))
                w3 = E(SBT("q_w3", [64, 4096], F32))
                hv = E(SBT("q_hv", [64, 3], F32))
                bf = E(SBT("q_bf", [64, 2], F32))
                dec = E(SBT("q_dec", [128, 32], F32))
                ft = E(SBT("q_ft", [33, 2, L], F32))
                tt_ = E(SBT("q_tt", [128, 2, L], F32))
                h1 = E(SBT("q_h1", [64, L], F32))
                h2 = E(SBT("q_h2", [64, 2, L], F32))
                tmp = E(SBT("q_tmp", [64, L], F32))
                win = E(SBT("q_win", [128, 2, L], F32))
                hk = E(SBT("q_hk", [128, 2, 2, L], BF16))
                rsS = E(SBT("q_rs", [128, 16], F32))
                junk = E(SBT("q_junk", [128, L], BF16))
                ss = E(SBT("q_ss", [128, 2, 4], F32))
                zT = E(SBT("q_zT", [128, 2, 32, 64], BF16))
                B = E(SBT("q_B", [128, 32, 2, 16, 4], BF16))
                Kh = E(SBT("q_Kh", [128, 2, 2048], BF16))
                kb.dma('sp', w1[:], hw1[l], writes=('w1',))
                kb.dma('sp', w2[:], hw2[l], writes=('w2',))
                kb.dma('sp', w3[:], hw3[l], writes=('w3',))
                kb.dma('sp', hv[:], hvec[l], writes=('hv',))
                kb.dma('sp', dec[:], hdec[l], writes=('dec',))
                for i in range(2):
                    kb.dma('sp', ft[:, i, :], feats[i], writes=('ft',))
                    kb.dma('sp', tt_[:, i, :], ttab[i], writes=('tt',))
                V = lambda fn, r, w: kb.op('dve', fn, reads=r, writes=w)
                A = lambda fn, r, w: kb.op('act', fn, reads=r, writes=w)
                V(lambda v: v.tensor_tensor(out=bf[:, 0:1], in0=hv[:, 0:1], in1=hv[:, 2:3], op=ALU.mult), ('hv',), ('bf',))
                V(lambda v: v.tensor_tensor(out=bf[:, 1:2], in0=hv[:, 1:2], in1=hv[:, 2:3], op=ALU.mult), ('hv', 'bf'), ('bf',))
                A(lambda a: a.activation(out=dec[:], in_=dec[:], func=AF.Abs), ('dec',), ('dec',))
                V(lambda v: v.tensor_scalar(out=dec[:], in0=dec[:], scalar1=-1.0, scalar2=None, op0=ALU.mult), ('dec',), ('dec',))
                for i in range(2):
                    for stage in range(2):
                        wmat = w1 if stage == 0 else w2
                        dst = h1 if stage == 0 else h2[:, i, :]
                        dtok = 'h1' if stage == 0 else 'h2_%d' % i
                        for t4 in range(4):
                            ts = slice(t4 * 512, (t4 + 1) * 512)
                            b = kb.ps()
                            if stage == 0:
                                kb.op('pe', lambda p: p.matmul(PS(b, 512, 0, 64), w1[:], ft[:, i, ts], start=True, stop=True),
                                      reads=('w1', 'ft'), writes=(pst(b),))
                            else:
                                kb.op('pe', lambda p: p.matmul(PS(b, 512, 0, 64), w2[:], h1[:, ts], start=True, stop=True),
                                      reads=('w2', 'h1'), writes=(pst(b),))
                            dsl = dst[:, ts] if stage == 0 else h2[:, i, ts]
                            A(lambda a: a.activation(out=dsl, in_=PS(b, 512, 0, 64), func=AF.Identity, bias=bf[:, stage:stage + 1], scale=hv[:, 2:3]),
                              (pst(b), 'bf', 'hv'), (dtok,))
                        dfull = h1[:] if stage == 0 else h2[:, i, :]
                        range_reduce('dve', dfull, tmp[:], (dtok,), ('tmp',))
                        A(lambda a: a.activation(out=dfull, in_=dfull, func=AF.Sin), (dtok,), (dtok,))
                NIT = 16
                w3b = E(SBT("q_w3b", [64, 4096], BF16))
                h2b = E(SBT("q_h2b", [64, 2, L], BF16))
                for q_ in range(4):
                    A(lambda a: a.activation(out=w3b[:, q_ * 1024:(q_ + 1) * 1024], in_=w3[:, q_ * 1024:(q_ + 1) * 1024], func=AF.Identity), ('w3',), ('w3b',))
                V(lambda v: v.tensor_copy(out=h2b[:], in_=h2[:]), ('h2_0', 'h2_1'), ('h2b',))

                def make_A(n):
                    o, cc = n // 8, n % 8
                    par = n % 2
                    pieces = []
                    for dr in range(2):
                        for t4 in range(4):
                            def piece(dr=dr, t4=t4):
                                fc = o * 16 + dr * 8 + cc
                                if t4 == 0 and dr == 0:
                                    for d2 in range(2):
                                        fc2 = o * 16 + d2 * 8 + cc
                                        A(lambda a: a.activation(out=win[:, d2, :], in_=tt_[:, d2, :], func=AF.Exp, scale=dec[:, fc2:fc2 + 1]),
                                          ('tt', 'dec'), ('win%d' % d2,))
                                ts = slice(t4 * 512, (t4 + 1) * 512)
                                b = kb.ps_hi()
                                kb.op('pe', lambda p: p.matmul(PS(b), w3b[:, fc * 128:(fc + 1) * 128], h2b[:, dr, ts], start=True, stop=True),
                                      reads=('w3b', 'h2b'), writes=(pst(b),))
                                V(lambda v: v.scalar_tensor_tensor(out=hk[:, par, dr, ts], in0=win[:, dr, ts], scalar=0.05, in1=PS(b), op0=ALU.add, op1=ALU.mult),
                                  ('win%d' % dr, pst(b)), ('hk%d_%d' % (par, dr),))
                                if t4 == 3:
                                    if dr == 1:
                                        V(lambda v: v.memset(hk[:, par, 1, 0:1], 0.0), ('hk%d_1' % par,), ('hk%d_1' % par,))
                                    A(lambda a: a.activation(out=junk[:], in_=hk[:, par, dr, :], func=AF.Square, accum_out=ss[:, par, dr:dr + 1]),
                                      ('hk%d_%d' % (par, dr),), ('junk', 'ss%d_%d' % (par, dr)))
                            pieces.append(piece)
                    return pieces

                def run_B(n, filler):
                    o, cc = n // 8, n % 8
                    par = n % 2
                    V(lambda v: v.tensor_tensor(out=ss[:, par, 2:3], in0=ss[:, par, 0:1], in1=ss[:, par, 1:2], op=ALU.add),
                      ('ss%d_0' % par, 'ss%d_1' % par), ('ss%d_2' % par,))
                    A(lambda a: a.activation(out=ss[:, par, 2:3], in_=ss[:, par, 2:3], func=AF.Sqrt, bias=barscr[:, 1:2]), ('ss%d_2' % par,), ('ss%d_2' % par,))
                    V(lambda v: v.reciprocal(out=rsS[:, n:n + 1], in_=ss[:, par, 2:3]), ('ss%d_2' % par,), ('rsS',))
                    for dr in range(2):
                        V(lambda v: v.transpose(out=zT[:, dr].bitcast(U32).rearrange("p c m -> p m c"),
                                                in_=hk[:, par, dr, :].bitcast(U32).rearrange("p (n1 m) -> p m n1", m=32)),
                          ('hk%d_%d' % (par, dr),), ('zT%d' % dr,))
                        filler()

                    def spec_evac(c2, h, psv, ptoks):
                        V(lambda v: v.tensor_copy(out=Kh[:, c2, :].rearrange("p (m k) -> p m k", k=32)[:, :, 16 * h:16 * h + 16], in_=psv),
                          ptoks, ('Kh',))
                    fft_fwd(C, [(zT[:, 0], 'zT0', 0), (zT[:, 1], 'zT1', 1)], B, spec_evac, filler)
                    kb.dma('sp', khatD[o, cc], Kh[:].rearrange("p a b -> p (a b)"), reads=('Kh',), writes=('khatD',))

                cur = make_A(0)
                for pc in cur:
                    pc()
                for n in range(NIT):
                    nxt = make_A(n + 1) if n + 1 < NIT else []

                    def filler():
                        if nxt:
                            nxt.pop(0)()
                    run_B(n, filler)
                    while nxt:
                        nxt.pop(0)()
                kb.dma('sp', rsD, rsS[:], reads=('rsS',), writes=('rsD',))
                kb.barrier(barscr[:, 0:1])

        def phase_hy(l):
            with ExitStack() as es:
                E = es.enter_context
                C = load_fft_consts(es, False)
                cw = E(SBT("h_cw", [128, 3, 24], F32))
                cb_ = E(SBT("h_cb", [128, 24], F32))
                dd = E(SBT("h_dd", [128, 2, 8], F32))
                pp = E(SBT("h_ppb", [128, 3, L + 2], BF16))
                vx = E(SBT("h_vx", [128, 2, 4, L], BF16))
                z1 = E(SBT("h_z1", [128, L], BF16))
                zT = E(SBT("h_zT", [128, 32, 64], BF16))
                BD = E(SBT("h_BD", [128, 4096], BF16))
                B = BD[:].rearrange("p (k r cp cq) -> p k r cp cq", k=32, r=2, cp=16)
                Dt = BD[:].rearrange("p (c2 n r cp) -> p c2 n r cp", c2=2, n=64, r=2)
                P1 = E(SBT("h_P1", [128, 2, 2048], BF16))
                P2 = E(SBT("h_P2", [128, 2, 2048], BF16))
                Kt = E(SBT("h_Kt", [128, 2, 2, 4096], BF16))
                conv = E(SBT("h_conv", [128, L], F32))
                sc = conv
                kb.dma('sp', cw[:], convw[l], writes=('cw',))
                kb.dma('sp', cb_[:], convb[l], writes=('cb',))
                kb.dma('sp', dd[:], hyd[l], writes=('dd',))
                rs = E(SBT("h_rs", [128, 2, 16], F32))
                kb.dma('sp', rs[:, 0, :], rsD, reads=('rsD',), writes=('rs',))
                kb.op('dve', lambda v: v.reciprocal(out=rs[:, 1, :], in_=rs[:, 0, :]), reads=('rs',), writes=('rs',))
                kb.op('dve', lambda v: v.tensor_tensor(out=cw[:, :, 8:24], in0=cw[:, :, 8:24],
                                                       in1=rs[:, 0:1, :].to_broadcast([128, 3, 16]), op=ALU.mult), reads=('cw', 'rs'), writes=('cw',))
                kb.op('dve', lambda v: v.tensor_tensor(out=cb_[:, 8:24], in0=cb_[:, 8:24], in1=rs[:, 0, :], op=ALU.mult), reads=('cb', 'rs'), writes=('cb',))
                kb.op('dve', lambda v: v.tensor_tensor(out=dd[:].rearrange("p o c -> p (o c)"), in0=dd[:].rearrange("p o c -> p (o c)"), in1=rs[:, 1, :], op=ALU.mult),
                      reads=('dd', 'rs'), writes=('dd',))
                kb.op('pool', lambda g: g.memset(pp[:], 0.0), writes=('pp0', 'pp1', 'pp2'))

                def short_conv(cb, si):
                    par = cb % 2
                    sp_ = si
                    ch = si * 8 + cb
                    kb.op('dve', lambda v: v.tensor_scalar(out=sc[:], in0=pp[:, sp_, 1:L + 1], scalar1=cw[:, 1, ch:ch + 1], scalar2=cb_[:, ch:ch + 1],
                                                           op0=ALU.mult, op1=ALU.add),
                          reads=('pp%d' % sp_, 'cw', 'cb'), writes=('conv',))
                    kb.op('dve', lambda v: v.scalar_tensor_tensor(out=sc[:], in0=pp[:, sp_, 0:L], scalar=cw[:, 0, ch:ch + 1], in1=sc[:],
                                                                  op0=ALU.mult, op1=ALU.add),
                          reads=('pp%d' % sp_, 'cw', 'conv'), writes=('conv',))
                    kb.op('dve', lambda v: v.scalar_tensor_tensor(out=vx[:, par, si, :], in0=pp[:, sp_, 2:L + 2], scalar=cw[:, 2, ch:ch + 1], in1=sc[:],
                                                                  op0=ALU.mult, op1=ALU.add),
                          reads=('pp%d' % sp_, 'cw', 'conv'), writes=('vx%d_%d' % (par, si),))

                def make_A(cb):
                    par = cb % 2
                    base = [[] for _ in range(16)]
                    for si in range(3):
                        base[si].append(lambda si=si: kb.dma('sp', pp[:, si, 1:L + 1], pD[si, cb], writes=('pp%d' % si,)))
                    base[3].append(lambda: kb.dma('sp', vx[:, par, 3, :], pD[3, cb], writes=('vx%d_3' % par,)))
                    for si in range(3):
                        base[4 * si + 5].append(lambda si=si: short_conv(cb, si))
                    base[15].append(lambda: kb.op('pool', lambda g: g.tensor_tensor(out=vx[:, par, 2, :], in0=vx[:, par, 2, :], in1=vx[:, par, 3, :], op=ALU.mult),
                                                  reads=('vx%d_2' % par, 'vx%d_3' % par), writes=('vx%d_2' % par,)))
                    return [(lambda fs=fs: [f() for f in fs]) for fs in base]

                def load_kt(ci):
                    cb_i, o_i, kb_ = ci // 2, ci % 2, ci % 2
                    kb.dma('sp', Kt[:, kb_, 0, :], khatD[o_i, cb_i], writes=('Kt%d' % kb_,))
                    for r_ in range(2):
                        kb.dma('sp', Kt[r_:128:2, kb_, 1, :], khatD[o_i, cb_i][1 - r_:128:2, :], writes=('Kt%d' % kb_,))

                def run_B(cb, filler):
                    par = cb % 2
                    zin, ztok = vx[:, par, 0, :], 'vx%d_0' % par
                    for o in range(2):
                        ci = 2 * cb + o
                        kbuf = ci % 2
                        if ci + 1 < 16:
                            load_kt(ci + 1)
                        kb.op('dve', lambda v: v.transpose(out=zT[:].bitcast(U32).rearrange("p c m -> p m c"),
                                                           in_=zin.bitcast(U32).rearrange("p (n1 m) -> p m n1", m=32)),
                              reads=(ztok,), writes=('zT',))
                        filler()

                        def spec_evac(c2, h, psv, ptoks):
                            ksl = slice(16 * h, 16 * h + 16)
                            kb.op('dve', lambda v: v.tensor_tensor(out=P1[:, c2, :].rearrange("p (m k) -> p m k", k=32)[:, :, ksl], in0=psv,
                                                                   in1=Kt[:, kbuf, 0, c2 * 2048:(c2 + 1) * 2048].rearrange("p (m k) -> p m k", k=32)[:, :, ksl], op=ALU.mult),
                                  reads=ptoks + ('Kt%d' % kbuf,), writes=('P1',))
                            kb.op('dve', lambda v: v.tensor_tensor(out=P2[:, c2, :].rearrange("p (m k) -> p m k", k=32)[:, :, ksl], in0=psv,
                                                                   in1=Kt[:, kbuf, 1, c2 * 2048:(c2 + 1) * 2048].rearrange("p (m k) -> p m k", k=32)[:, :, ksl], op=ALU.mult),
                                  reads=ptoks + ('Kt%d' % kbuf,), writes=('P2',))
                        fft_fwd(C, [(zT[:], 'zT', 0)], B, spec_evac, filler)
                        fft_inv(C, P1, P2, Dt, conv[:], 'conv', filler)
                        kb.op('dve', lambda v: v.scalar_tensor_tensor(out=conv[:], in0=zin, scalar=dd[:, o, cb:cb + 1], in1=conv[:], op0=ALU.mult, op1=ALU.add),
                              reads=(ztok, 'dd', 'conv'), writes=('conv',))
                        if o == 0:
                            kb.op('dve', lambda v: v.tensor_tensor(out=z1[:], in0=conv[:], in1=vx[:, par, 1, :], op=ALU.mult),
                                  reads=('conv', 'vx%d_1' % par), writes=('z1',))
                            zin, ztok = z1[:], 'z1'
                        else:
                            yo = zT[:].rearrange("p a b -> p (a b)")
                            kb.op('dve', lambda v: v.tensor_tensor(out=yo, in0=conv[:], in1=vx[:, par, 2, :], op=ALU.mult),
                                  reads=('conv', 'vx%d_2' % par), writes=('zT',))
                            kb.dma('sp', yhyd[cb], yo, reads=('zT',), writes=('yhyd',))

                load_kt(0)
                cur = make_A(0)
                for pc in cur:
                    pc()
                def merge_weight_pieces():
                    if G.get('mw') is None:
                        return []
                    wh_, wo_, st_ = G['mw']
                    pcs = []
                    for wi, (wt, src) in enumerate(((wh_, w_bhy[l]), (wo_, w_out[l]))):
                        for k in range(8):
                            def pc(wt=wt, src=src, k=k, s_=(wi * 8 + k) % 2):
                                kb.dma('sp', st_[:, s_, :], src[k * 128:(k + 1) * 128, :], writes=('mst%d' % s_,))
                                kb.op('act', lambda a: a.activation(out=wt[:, k, :], in_=st_[:, s_, :], func=AF.Identity), reads=('mst%d' % s_,), writes=('mw',))
                            pcs.append(pc)
                    return pcs

                for cb in range(8):
                    nxt = make_A(cb + 1) if cb + 1 < 8 else merge_weight_pieces()

                    def filler():
                        if nxt:
                            nxt.pop(0)()
                    run_B(cb, filler)
                    while nxt:
                        nxt.pop(0)()
                kb.barrier(barscr[:, 0:1])

        def phase_merge(l, hsrc):
            with ExitStack() as es:
                E = es.enter_context
                wh, wo, st = G['mw']
                mm = E(SBT("m_mm", [128, 16, 512], BF16))
                ys = E(SBT("m_ys", [128, 8, 512], BF16))
                yh = E(SBT("m_yh", [128, 8, 512], BF16))
                mg = E(SBT("m_mg", [128, 8, 512], BF16))
                t1 = E(SBT("m_t1", [128, 2, 512], F32))
                ht = E(SBT("m_ht", [128, 8, 512], F32))
                for tt in range(4):
                    ts = slice(tt * 512, (tt + 1) * 512)
                    kb.dma('sp', mm[:], mmD.rearrange("c p t -> p c t")[:, :, ts], writes=('mm',))
                    kb.dma('sp', yh[:], yhyd.rearrange("c p t -> p c t")[:, :, ts], reads=('yhyd',), writes=('yh',))
                    kb.dma('sp', ys[:], ys5d.rearrange("c p t -> p c t")[:, :, ts], reads=('ys5d',), writes=('ys',))
                    kb.dma('sp', ht[:], hsrc.rearrange("c p t -> p c t")[:, :, ts], reads=('hsrc',), writes=('ht',))
                    for dc in range(8):
                        b = kb.ps()
                        for k in range(8):
                            kb.op('pe', lambda p: p.matmul(PS(b), wh[:, k, dc * 128:(dc + 1) * 128], yh[:, k, :], start=(k == 0), stop=(k == 7)),
                                  reads=('wh', 'yh'), writes=(pst(b),), inc=(k == 7))
                        s = dc % 2
                        kb.op('dve', lambda v: v.tensor_tensor(out=t1[:, s, :], in0=PS(b), in1=mm[:, 8 + dc, :], op=ALU.mult),
                              reads=(pst(b), 'mm'), writes=('t1_%d' % s,))
                        kb.op('pool', lambda g: g.tensor_tensor(out=mg[:, dc, :], in0=mm[:, dc, :], in1=ys[:, dc, :], op=ALU.mult),
                              reads=('mm', 'ys'), writes=('mg%d' % dc,))
                        kb.op('dve', lambda v: v.tensor_tensor(out=mg[:, dc, :], in0=mg[:, dc, :], in1=t1[:, s, :], op=ALU.add),
                              reads=('mg%d' % dc, 't1_%d' % s), writes=('mg%d' % dc, 'mg'))
                    for dc in range(8):
                        b = kb.ps()
                        for k in range(8):
                            kb.op('pe', lambda p: p.matmul(PS(b), wo[:, k, dc * 128:(dc + 1) * 128], mg[:, k, :], start=(k == 0), stop=(k == 7)),
                                  reads=('wo', 'mg'), writes=(pst(b),), inc=(k == 7))
                        kb.op('dve', lambda v: v.tensor_tensor(out=ht[:, dc, :], in0=PS(b), in1=ht[:, dc, :], op=ALU.add),
                              reads=(pst(b), 'ht'), writes=('ht',))
                        kb.dma('sp', hbuf[dc][:, ts], ht[:, dc, :], reads=('ht',), writes=('hbuf',))
                kb.barrier(barscr[:, 0:1])

        kb.op('pool', lambda g: g.memset(barscr[:, 1:2], 1e-6), writes=('epsq',))
        kb.barrier(barscr[:, 0:1])
        order = ['prep_s5', 'prep_hy', 'norm', 's5', 'hy', 'merge']
        nph = len(order) if stop_after is None else order.index(stop_after) + 1
        for l in range(layers):
            hsrc = xT if l == 0 else hbuf
            if nph >= 1:
                prep_s5(l)
            if nph >= 2:
                prep_hy(l)
            with ExitStack() as esl:
                G['xn'] = esl.enter_context(SBT("xn%d" % l, [128, 8, L], BF16))
                if nph >= 3:
                    phase_norm(hsrc, l, G['xn'], None)
                    if 'xnD' in dump:
                        xnD = dscr("xnD", [8, 128, L], BF16)
                        for c in range(8):
                            kb.dma('sp', xnD[c], G['xn'][:, c, :], reads=('xn%d' % c,), writes=('xnD',))
                if nph >= 4:
                    phase_s5(l)
            with ExitStack() as esm:
                if nph >= 6:
                    G['mw'] = (esm.enter_context(SBT("m_wh", [128, 8, 1024], BF16)),
                               esm.enter_context(SBT("m_wo", [128, 8, 1024], BF16)),
                               esm.enter_context(SBT("m_st", [128, 2, 1024], F32)))
                else:
                    G['mw'] = None
                if nph >= 5:
                    phase_hy(l)
                if nph >= 6:
                    phase_merge(l, hsrc)
        if stop_after is None:
            phase_norm(hbuf, DEPTH, None, outT)
        for i in range(NDS):
            if kb.dcnt[i]:
                kb._wait('sp', ('d', i), kb.dcnt[i])
    return nc, kb


_CACHE = {}


def _host_consts():
    if 'c' not in _CACHE:
        fsmall, fT, fH = fft_consts()
        ft, tt = hy_feats()
        import ml_dtypes
        bfc = lambda a: np.ascontiguousarray(a.astype(ml_dtypes.bfloat16))
        _CACHE['c'] = dict(fsmall=bfc(fsmall), fT=bfc(fT), fH=bfc(fH), feats=ft, ttab=tt, kvt=s5_kv(), masks=s5_masks(),
                           ident=np.eye(128, dtype=np.float32))
    return _CACHE['c']


def _layout_shared(inp):
    f32 = np.float32
    g = lambda k: np.asarray(inp[k], dtype=f32)
    m = dict(_host_consts())
    nw = np.concatenate([g("norm_w"), g("final_norm_w")[None]], 0)
    m["normw"] = np.ascontiguousarray(nw.reshape(3, 8, 128).transpose(0, 2, 1))
    m["w_in"] = g("w_in")
    m["w_glu"] = g("s5_w_glu")
    m["b_glu"] = np.ascontiguousarray(g("s5_b_glu").reshape(DEPTH, 4, 128).transpose(0, 2, 1))
    m["s5d"] = np.ascontiguousarray(g("s5_d").reshape(DEPTH, 4, 128).transpose(0, 2, 1))
    m["w_bs5"] = g("w_branch_s5")
    m["w_bhy"] = g("w_branch_hy")
    m["w_out"] = g("w_out")
    cwv = g("hy_conv_w").reshape(DEPTH, 3, 24, 128)
    m["convw"] = np.ascontiguousarray(cwv.transpose(0, 3, 1, 2))
    m["convb"] = np.ascontiguousarray(g("hy_conv_b").reshape(DEPTH, 24, 128).transpose(0, 2, 1))
    m["hyd"] = np.ascontiguousarray(g("hy_d").reshape(DEPTH, 2, 8, 128).transpose(0, 3, 1, 2))
    def pg(a):
        v = a.reshape(DEPTH, 2, 16, 2, 64)
        return v.transpose(0, 3, 4, 1, 2).reshape(DEPTH, 128, 32)
    ls = np.broadcast_to(g("s5_log_step")[..., None], (DEPTH, 2, 32, 64))
    m["s5lam"] = np.ascontiguousarray(np.stack([pg(g("s5_lam_re")), pg(g("s5_lam_im")), pg(ls)], 2))
    def pgB(a):
        v = a.reshape(DEPTH, 2, 16, 2, 64, 16)
        return v.transpose(0, 3, 4, 1, 2, 5).reshape(DEPTH, 128, 32, 16)
    m["s5B"] = np.ascontiguousarray(np.stack([pgB(g("s5_b_re")), pgB(g("s5_b_im"))], 2))
    ct = lambda a: a.transpose(0, 1, 2, 4, 3)
    m["s5C"] = np.ascontiguousarray(np.stack([pgB(ct(g("s5_c_re"))), pgB(ct(g("s5_c_im")))], 2))
    m["hw1"] = g("hy_w1"); m["hw2"] = g("hy_w2"); m["hw3"] = g("hy_w3")
    m["hvec"] = np.ascontiguousarray(np.stack([g("hy_b1"), g("hy_b2"), g("hy_freq")], -1))
    m["hdec"] = np.ascontiguousarray(g("hy_decay").reshape(DEPTH, 32, 128).transpose(0, 2, 1))
    return m


def kernel(**inputs):
    x = np.asarray(inputs["x"], dtype=np.float32)
    shared = _layout_shared(inputs)
    if 'nc' not in _CACHE:
        _CACHE['nc'] = build()[0]
    nc = _CACHE['nc']
    in_maps = []
    for b in range(NCORES):
        m = dict(shared)
        m["xT"] = np.ascontiguousarray(x[b].T.reshape(8, 128, L))
        in_maps.append(m)
    res = run_bass_kernel_spmd(nc, in_maps, core_ids=list(range(NCORES)))
    out = np.empty((NCORES, L, D), dtype=np.float32)
    for b in range(NCORES):
        out[b] = np.asarray(res.results[b]["outT"]).reshape(D, L).T
    return out
```
